# Optimizing a Trainium2 kernel written in Bass

```python
import math
import jax, jax.numpy as jnp
from jax import lax
import numpy as np

D_MODEL = 1024
BATCH = 8
SEQ = 4096
DEPTH = 2

PLE_DIM = 256
D_FF = 2816
N_NORMS = 4
EPS = 1e-6
NEG_INF = -1e30

ATT_HEADS = 4
ATT_HEAD_DIM = 64
ATT_WIDTH = ATT_HEADS * ATT_HEAD_DIM
Q_BLOCK = 128

POOL_WINDOWS = (2, 4, 8, 16)
POOL_GROUPS = len(POOL_WINDOWS)
POOL_GROUP_DIM = 64
POOL_WIDTH = POOL_GROUPS * POOL_GROUP_DIM

SSM_GROUP_DIM = 16
SSM_GROUPS = 16
SSM_WIDTH = SSM_GROUPS * SSM_GROUP_DIM
SSM_STATE = 64
SSM_DT_MIN = 0.001
SSM_DT_MAX = 0.1

CONV_CHANNELS = 256
CONV_WIDTH = 3

N_BRANCH = 4
BRANCH_WIDTH = 256

IN_SPLITS = (ATT_WIDTH, ATT_WIDTH, ATT_WIDTH, ATT_HEADS, POOL_WIDTH, SSM_WIDTH,
             CONV_CHANNELS, CONV_CHANNELS, CONV_CHANNELS, N_BRANCH * D_MODEL)
IN_COLS = sum(IN_SPLITS)

kernel_name = 'hybrid_gated_mixer_block'


def rmsnorm(x, g):
    xf = x.astype(jnp.float32)
    y = xf * lax.rsqrt(jnp.mean(xf * xf, axis=-1, keepdims=True) + EPS)
    return (y * g.astype(jnp.float32)).astype(x.dtype)


def swiglu(x, w_gate, w_up, w_down):
    return (jax.nn.silu(x @ w_gate) * (x @ w_up)) @ w_down


def forgetting_attention(q, k, v, f_logit, f_bias):
    B, S, H, Dh = q.shape
    nb = S // Q_BLOCK
    log_f = jax.nn.log_sigmoid(f_logit.astype(jnp.float32) + f_bias.astype(jnp.float32))
    c = jnp.cumsum(log_f, axis=1)
    qf = q.astype(jnp.float32) * (Dh ** -0.5)
    kf = k.astype(jnp.float32)
    vf = v.astype(jnp.float32)
    q_blk = qf.reshape(B, nb, Q_BLOCK, H, Dh).transpose(1, 0, 3, 2, 4)
    c_blk = c.reshape(B, nb, Q_BLOCK, H).transpose(1, 0, 3, 2)
    pos_blk = jnp.arange(S, dtype=jnp.int32).reshape(nb, Q_BLOCK)
    c_k = c.transpose(0, 2, 1)
    k_pos = jnp.arange(S, dtype=jnp.int32)

    def one_block(args):
        qb, cb, pb = args
        s = jnp.einsum('bhqd,bkhd->bhqk', qb, kf)
        s = s + (cb[..., :, None] - c_k[:, :, None, :])
        mask = k_pos[None, :] <= pb[:, None]
        s = jnp.where(mask[None, None], s, NEG_INF)
        pr = jax.nn.softmax(s, axis=-1)
        return jnp.einsum('bhqk,bkhd->bqhd', pr, vf)

    out = lax.map(one_block, (q_blk, c_blk, pos_blk))
    out = out.transpose(1, 0, 2, 3, 4).reshape(B, S, H * Dh)
    return out.astype(q.dtype)


def multiscale_pool(xp, pool_w, pool_scale):
    B, S, C = xp.shape
    xf = xp.astype(jnp.float32)
    csp = jnp.concatenate([jnp.zeros((B, 1, C), jnp.float32), jnp.cumsum(xf, axis=1)], axis=1)
    win = jnp.repeat(jnp.array(POOL_WINDOWS, jnp.int32), POOL_GROUP_DIM)
    t = jnp.arange(S, dtype=jnp.int32)[:, None]
    lo = jnp.maximum(t + 1 - win[None, :], 0)
    ch = jnp.arange(C, dtype=jnp.int32)[None, :]
    window_sum = csp[:, 1:, :] - csp[:, lo, ch]
    count = jnp.minimum(t + 1, win[None, :]).astype(jnp.float32)
    pooled = window_sum / count - xf
    grp = pooled.reshape(B, S, POOL_GROUPS, POOL_GROUP_DIM)
    y = jnp.einsum('bsgc,gcd->bsgd', grp, pool_w.astype(jnp.float32)).reshape(B, S, C)
    return (y * pool_scale.astype(jnp.float32)).astype(xp.dtype)


def _ssm_combine(left, right):
    a1, b1 = left
    a2, b2 = right
    return a1 * a2, a2 * b1 + b2


def s5_ssm(u, lam_re, lam_im, log_dt, b_re, b_im, c_re, c_im, d_skip, w_glu):
    B, S, _ = u.shape
    uf = u.astype(jnp.float32).reshape(B, S, SSM_GROUPS, SSM_GROUP_DIM)
    dt = jnp.exp(log_dt.astype(jnp.float32))[:, None]
    lam = lax.complex(jnp.minimum(lam_re.astype(jnp.float32), -1e-4), lam_im.astype(jnp.float32))
    lam_bar = jnp.exp(lam * dt)
    bmat = lax.complex(b_re.astype(jnp.float32), b_im.astype(jnp.float32))
    b_bar = ((lam_bar - 1.0) / lam)[..., None] * bmat
    bu = jnp.einsum('bsgh,gph->bsgp', uf.astype(jnp.complex64), b_bar)
    a = jnp.broadcast_to(lam_bar, bu.shape)
    _, states = lax.associative_scan(_ssm_combine, (a, bu), axis=1)
    cmat = lax.complex(c_re.astype(jnp.float32), c_im.astype(jnp.float32))
    y = jnp.real(jnp.einsum('bsgp,ghp->bsgh', states, cmat))
    y = y + d_skip.astype(jnp.float32).reshape(SSM_GROUPS, SSM_GROUP_DIM) * uf
    y = y.reshape(B, S, SSM_WIDTH)
    val, gate = jnp.split(y @ w_glu.astype(jnp.float32), 2, axis=-1)
    return (val * jax.nn.sigmoid(gate)).astype(u.dtype)


def short_conv(b_gate, c_gate, xin, conv_w):
    S = xin.shape[1]
    z = c_gate * xin
    zp = jnp.pad(z, ((0, 0), (CONV_WIDTH - 1, 0), (0, 0)))
    y = conv_w[0] * zp[:, 0:S]
    for j in range(1, CONV_WIDTH):
        y = y + conv_w[j] * zp[:, j:j + S]
    return b_gate * y


def hybrid_mixer(u, w_in, f_bias, pool_w, pool_scale, lam_re, lam_im, log_dt, b_re, b_im,
                 c_re, c_im, d_skip, w_glu, conv_w, w_branch, w_out):
    B, S, _ = u.shape
    z = u @ w_in
    q, k, v, f, xp, xs, cb, cc, cx, g = jnp.split(z, np.cumsum(IN_SPLITS)[:-1].tolist(), axis=-1)
    shp = (B, S, ATT_HEADS, ATT_HEAD_DIM)
    y_att = forgetting_attention(q.reshape(shp), k.reshape(shp), v.reshape(shp), f, f_bias)
    y_pool = multiscale_pool(xp, pool_w, pool_scale)
    y_ssm = s5_ssm(xs, lam_re, lam_im, log_dt, b_re, b_im, c_re, c_im, d_skip, w_glu)
    y_conv = short_conv(cb, cc, cx, conv_w)
    ys = jnp.stack([y_att.astype(u.dtype), y_pool.astype(u.dtype),
                    y_ssm.astype(u.dtype), y_conv.astype(u.dtype)], axis=2)
    proj = jnp.einsum('bsnc,ncd->bsnd', ys, w_branch)
    gates = jax.nn.sigmoid(g.reshape(B, S, N_BRANCH, D_MODEL))
    merged = jnp.sum(gates * proj, axis=2)
    return merged @ w_out


def _normal(k, shape, std):
    return std * jax.random.normal(k, shape, jnp.float32)


def setup_inputs(seed: int = 0) -> dict:
    key = jax.random.key(seed)
    ks = jax.random.split(key, 24)
    G, P, H = SSM_GROUPS, SSM_STATE, SSM_GROUP_DIM
    x = _normal(ks[0], (BATCH, SEQ, D_MODEL), 1.0)
    p = _normal(ks[1], (DEPTH, BATCH, SEQ, PLE_DIM), 1.0)
    norm_g = 1.0 + _normal(ks[2], (DEPTH, N_NORMS, D_MODEL), 0.05)
    ffn_w_gate = _normal(ks[3], (DEPTH, 2, D_MODEL, D_FF), D_MODEL ** -0.5)
    ffn_w_up = _normal(ks[4], (DEPTH, 2, D_MODEL, D_FF), D_MODEL ** -0.5)
    ffn_w_down = _normal(ks[5], (DEPTH, 2, D_FF, D_MODEL), D_FF ** -0.5)
    w_in = _normal(ks[6], (DEPTH, D_MODEL, IN_COLS), D_MODEL ** -0.5)
    f_bias = 1.0 + 4.0 * jax.random.uniform(ks[7], (DEPTH, ATT_HEADS), jnp.float32)
    pool_w = _normal(ks[8], (DEPTH, POOL_GROUPS, POOL_GROUP_DIM, POOL_GROUP_DIM), POOL_GROUP_DIM ** -0.5)
    pool_scale = 1.0 + _normal(ks[9], (DEPTH, POOL_WIDTH), 0.1)
    ssm_lam_re = -0.5 * (1.0 + _normal(ks[10], (DEPTH, G, P), 0.01))
    ssm_lam_im = jnp.tile(math.pi * jnp.arange(P, dtype=jnp.float32), (DEPTH, G, 1))
    ssm_log_dt = math.log(SSM_DT_MIN) + jax.random.uniform(ks[11], (DEPTH, G), jnp.float32) * (
        math.log(SSM_DT_MAX) - math.log(SSM_DT_MIN))
    ssm_b_re = _normal(ks[12], (DEPTH, G, P, H), (2 * H) ** -0.5)
    ssm_b_im = _normal(ks[13], (DEPTH, G, P, H), (2 * H) ** -0.5)
    ssm_c_re = _normal(ks[14], (DEPTH, G, H, P), P ** -0.5)
    ssm_c_im = _normal(ks[15], (DEPTH, G, H, P), P ** -0.5)
    ssm_d = _normal(ks[16], (DEPTH, SSM_WIDTH), 1.0)
    ssm_w_glu = _normal(ks[17], (DEPTH, SSM_WIDTH, 2 * SSM_WIDTH), SSM_WIDTH ** -0.5)
    conv_w = _normal(ks[18], (DEPTH, CONV_WIDTH, CONV_CHANNELS), CONV_WIDTH ** -0.5)
    w_branch = _normal(ks[19], (DEPTH, N_BRANCH, BRANCH_WIDTH, D_MODEL), BRANCH_WIDTH ** -0.5)
    w_out = _normal(ks[20], (DEPTH, D_MODEL, D_MODEL), D_MODEL ** -0.5)
    ple_w_gate = _normal(ks[21], (DEPTH, D_MODEL, D_MODEL), D_MODEL ** -0.5)
    ple_w_proj = _normal(ks[22], (DEPTH, PLE_DIM, D_MODEL), PLE_DIM ** -0.5)
    final_g = 1.0 + _normal(ks[23], (D_MODEL,), 0.05)
    return {'x': x, 'p': p, 'norm_g': norm_g, 'ffn_w_gate': ffn_w_gate, 'ffn_w_up': ffn_w_up,
            'ffn_w_down': ffn_w_down, 'w_in': w_in, 'f_bias': f_bias, 'pool_w': pool_w,
            'pool_scale': pool_scale, 'ssm_lam_re': ssm_lam_re, 'ssm_lam_im': ssm_lam_im,
            'ssm_log_dt': ssm_log_dt, 'ssm_b_re': ssm_b_re, 'ssm_b_im': ssm_b_im,
            'ssm_c_re': ssm_c_re, 'ssm_c_im': ssm_c_im, 'ssm_d': ssm_d, 'ssm_w_glu': ssm_w_glu,
            'conv_w': conv_w, 'w_branch': w_branch, 'w_out': w_out, 'ple_w_gate': ple_w_gate,
            'ple_w_proj': ple_w_proj, 'final_g': final_g}


def reference(x, p, norm_g, ffn_w_gate, ffn_w_up, ffn_w_down, w_in, f_bias, pool_w, pool_scale,
              ssm_lam_re, ssm_lam_im, ssm_log_dt, ssm_b_re, ssm_b_im, ssm_c_re, ssm_c_im, ssm_d,
              ssm_w_glu, conv_w, w_branch, w_out, ple_w_gate, ple_w_proj, final_g):
    h = x
    for i in range(DEPTH):
        h = h + 0.5 * swiglu(rmsnorm(h, norm_g[i, 0]), ffn_w_gate[i, 0], ffn_w_up[i, 0], ffn_w_down[i, 0])
        h = h + hybrid_mixer(rmsnorm(h, norm_g[i, 1]), w_in[i], f_bias[i], pool_w[i], pool_scale[i],
                             ssm_lam_re[i], ssm_lam_im[i], ssm_log_dt[i], ssm_b_re[i], ssm_b_im[i],
                             ssm_c_re[i], ssm_c_im[i], ssm_d[i], ssm_w_glu[i], conv_w[i],
                             w_branch[i], w_out[i])
        h = h + 0.5 * swiglu(rmsnorm(h, norm_g[i, 2]), ffn_w_gate[i, 1], ffn_w_up[i, 1], ffn_w_down[i, 1])
        gate = jax.nn.sigmoid(rmsnorm(h, norm_g[i, 3]) @ ple_w_gate[i])
        h = h + gate * (p[i] @ ple_w_proj[i])
    return rmsnorm(h, final_g)
```

```python
import contextlib
import numpy as np
import concourse.bass as bass
import concourse.mybir as mybir
from concourse.bass_utils import run_bass_kernel_spmd

F32 = mybir.dt.float32
BF16 = mybir.dt.bfloat16
AF = mybir.ActivationFunctionType
ALU = mybir.AluOpType

D = 1024
T = 4096
DEPTH = 2
DFF = 2816
NFC = DFF // 128
INC = 6148
PLE = 256
TT = 512
NQ = T // TT
EPS = 1e-6

COMPUTE = ("pe", "act", "dve", "pool")
NDMASEM = 56
NSP = 40


class Buf:
    __slots__ = ("name", "w", "r")

    def __init__(self, name=""):
        self.name = name
        self.w = None
        self.r = []


class Node:
    __slots__ = ("eng", "idx", "fn", "waits", "key", "val", "needs_inc", "clock", "is_dma")


class Prog:
    def __init__(self, nc):
        self.nc = nc
        self.ops = {e: [] for e in ("pe", "act", "dve", "pool", "sp")}
        self.clock = {e: {} for e in self.ops}
        self.dma_rr = 0
        self.dma_rr2 = 0
        self.dma_last = [None] * NDMASEM
        self.dma_cum = [0] * NDMASEM
        self.out_nodes = []

    def _record(self, eng, fn, reads, writes, is_dma, extra=()):
        n = Node()
        n.eng = eng
        n.idx = len(self.ops[eng])
        n.fn = fn
        n.is_dma = is_dma
        n.needs_inc = False
        deps = []
        for b in reads:
            if b.w is not None:
                deps.append(b.w)
        for b in writes:
            if b.w is not None:
                deps.append(b.w)
            deps.extend(b.r)
        deps.extend(extra)
        if is_dma:
            if eng == "sp":
                s = self.dma_rr
                self.dma_rr = (self.dma_rr + 1) % NSP
            else:
                s = NSP + self.dma_rr2
                self.dma_rr2 = (self.dma_rr2 + 1) % (NDMASEM - NSP)
            if self.dma_last[s] is not None:
                deps.append(self.dma_last[s])
            self.dma_cum[s] += 16
            n.key = ("d", s)
            n.val = self.dma_cum[s]
            self.dma_last[s] = n
        else:
            n.key = eng
            n.val = n.idx + 1
        ck = self.clock[eng]
        waits = {}
        for d in deps:
            if (not d.is_dma) and d.eng == eng and eng == "pe":
                continue
            if ck.get(d.key, 0) >= d.val:
                continue
            if waits.get(d.key, (0, None))[0] < d.val:
                waits[d.key] = (d.val, d)
        n.waits = [w[1] for w in waits.values()]
        if n.waits:
            ck = dict(ck)
            for d in n.waits:
                d.needs_inc = True
                for k, v in d.clock.items():
                    if ck.get(k, 0) < v:
                        ck[k] = v
                if ck.get(d.key, 0) < d.val:
                    ck[d.key] = d.val
            self.clock[eng] = ck
        n.clock = ck
        self.ops[eng].append(n)
        for b in reads:
            b.r.append(n)
        for b in writes:
            b.w = n
            b.r = []
        return n

    def op(self, eng, fn, reads=(), writes=()):
        return self._record(eng, fn, reads, writes, False)

    def dma(self, fn, reads=(), writes=(), q="sp", is_out=False):
        n = self._record(q, fn, reads, writes, True)
        if is_out:
            self.out_nodes.append(n)
        return n

    def barrier(self):
        last = [self.ops[e][-1] for e in COMPUTE if self.ops[e]]
        for e in COMPUTE:
            for n in reversed(self.ops[e]):
                if not n.is_dma:
                    last.append(n)
                    break
        last += [d for d in self.dma_last if d is not None]
        for e in ("pe", "act", "dve", "pool", "sp"):
            self._record(e, lambda eng: eng.nop(), (), (), False, extra=last)

    def emit(self, es):
        nc = self.nc
        sems = {}
        for e in COMPUTE:
            sems[e] = es.enter_context(nc.semaphore("S_" + e))
        for i in range(NDMASEM):
            sems[("d", i)] = es.enter_context(nc.semaphore("D%d" % i))
        for e in COMPUTE:
            c = 0
            for n in self.ops[e]:
                if n.is_dma:
                    continue
                if n.needs_inc:
                    c += 1
                    n.val = c
                else:
                    n.val = None
        block = es.enter_context(nc.Block())

        def run(ename):
            def body(eng):
                for n in self.ops[ename]:
                    for d in n.waits:
                        eng.wait_ge(sems[d.key], d.val)
                    ins = n.fn(eng)
                    if n.is_dma:
                        ins.then_inc(sems[n.key], 16)
                    elif n.needs_inc:
                        ins.then_inc(sems[n.key], 1)
                if ename == "sp":
                    for d in self.dma_last:
                        if d is not None:
                            eng.wait_ge(sems[d.key], d.val)
            return body

        block.tensor(run("pe"))
        block.scalar(run("act"))
        block.vector(run("dve"))
        block.gpsimd(run("pool"))
        block.sync(run("sp"))


C_ID = 0
C_ONES = 128
C_TRIU = 256
C_MNEG = 384
C_MH = 512
C_MS = 1024
C_RCW = 1536
C_RC0 = 1538
C_PM = C_RC0 + 1024
C_N = C_PM + 12 * 128
C_G = 512


def make_consts():
    c = np.zeros((128, C_N), np.float32)
    k = np.arange(128)
    c[:, C_ID:C_ID + 128] = np.eye(128, dtype=np.float32)
    c[:, C_ONES:C_ONES + 128] = 1.0
    c[:, C_TRIU:C_TRIU + 128] = (k[:, None] <= k[None, :]).astype(np.float32)
    c[:, C_MNEG:C_MNEG + 128] = np.where(k[None, :] < k[:, None], -30000.0, 0.0)
    rj4, rg2 = k // 32, (k // 16) % 2
    cg2 = k // 64
    for j4 in range(4):
        c[:, C_MH + j4 * 128:C_MH + (j4 + 1) * 128] = ((rj4[:, None] == j4) & (rg2[:, None] == cg2[None, :])).astype(np.float32)
        c[:, C_MS + j4 * 128:C_MS + (j4 + 1) * 128] = ((cg2[:, None] == rg2[None, :]) & (rj4[None, :] == j4)).astype(np.float32)
    wins = np.array([2, 4, 8, 16], np.float32)
    for ch in range(2):
        w = wins[2 * ch + (k // 64)]
        c[:, C_RCW + ch] = 1.0 / w
        t = np.arange(512, dtype=np.float32)
        c[:, C_RC0 + ch * 512:C_RC0 + (ch + 1) * 512] = 1.0 / np.minimum(t[None, :] + 1.0, w[:, None])
    tp = k[:, None].astype(np.float64)
    tq = k[None, :].astype(np.float64)
    for g in range(4):
        W = float(wins[g])
        main = np.where((tp <= tq) & (tp > tq - W), 1.0 / W, 0.0) - np.eye(128)
        corner = np.where((tp - 128 > tq - W), 1.0 / W, 0.0)
        cnt = np.minimum(tq + 1.0, W)
        main0 = np.where((tp <= tq) & (tp > tq - W), 1.0 / cnt, 0.0) - np.eye(128)
        for i, mtx in enumerate((main, corner, main0)):
            o = C_PM + (g * 3 + i) * 128
            c[:, o:o + 128] = mtx.astype(np.float32)
    return c


def build(phases=("in", "ffn", "mix", "ple", "out"), depth=DEPTH, mix_parts=("att", "pool", "ssm", "conv"), debug=False, nq=NQ):
    nc = bass.Bass("TRN2", target_bir_lowering=False)
    dt_in = lambda name, shape: nc.dram_tensor(name, list(shape), F32, kind="ExternalInput").ap()
    x = dt_in("x", [T, D])
    p_in = dt_in("p", [DEPTH, T, PLE])
    norm_g = dt_in("norm_g", [DEPTH, 4, D])
    w_gate = dt_in("ffn_w_gate", [DEPTH, 2, D, DFF])
    w_up = dt_in("ffn_w_up", [DEPTH, 2, D, DFF])
    w_down = dt_in("ffn_w_down", [DEPTH, 2, DFF, D])
    w_in = dt_in("w_in", [DEPTH, D, INC])
    f_bias = dt_in("f_bias", [DEPTH, 4])
    pool_w = dt_in("pool_w", [DEPTH, 4, 64, 64])
    pool_scale = dt_in("pool_scale", [DEPTH, 256])
    lam_re = dt_in("ssm_lam_re", [DEPTH, 16, 64])
    lam_im = dt_in("ssm_lam_im", [DEPTH, 16, 64])
    log_dt = dt_in("ssm_log_dt", [DEPTH, 16])
    b_re = dt_in("ssm_b_re", [DEPTH, 16, 64, 16])
    b_im = dt_in("ssm_b_im", [DEPTH, 16, 64, 16])
    c_re = dt_in("ssm_c_re", [DEPTH, 16, 16, 64])
    c_im = dt_in("ssm_c_im", [DEPTH, 16, 16, 64])
    ssm_d = dt_in("ssm_d", [DEPTH, 256])
    w_glu = dt_in("ssm_w_glu", [DEPTH, 256, 512])
    conv_w = dt_in("conv_w", [DEPTH, 3, 256])
    w_branch = dt_in("w_branch", [DEPTH, 4, 256, D])
    w_out = dt_in("w_out", [DEPTH, D, D])
    ple_wg = dt_in("ple_w_gate", [DEPTH, D, D])
    ple_wp = dt_in("ple_w_proj", [DEPTH, PLE, D])
    final_g = dt_in("final_g", [D])
    consts = dt_in("consts", [128, C_N])
    y_out = nc.dram_tensor("y", [T, D], F32, kind="ExternalOutput").ap()

    dk = dict(kind="ExternalOutput") if debug else {}
    hT = nc.dram_tensor("hT_scr", [D, T], F32, **dk).ap()
    xnT = nc.dram_tensor("xnT_scr", [D, T], BF16, **dk).ap()
    ybr = nc.dram_tensor("ybr_scr", [4, 256, T], BF16, **dk).ap()
    qaug = nc.dram_tensor("qaug_scr", [128, 4, T], BF16, **dk).ap()
    vtok = nc.dram_tensor("vtok_scr", [T, 256], BF16, **dk).ap()
    hTv = hT.rearrange("(c p) t -> p c t", p=128)
    xnTv = xnT.rearrange("(c p) t -> p c t", p=128)

    es = contextlib.ExitStack()
    P = Prog(nc)
    OP = lambda eng, reads, writes, fn: P.op(eng, fn, reads, writes)

    uid = [0]

    def sb(stack, name, shape, dt):
        uid[0] += 1
        return stack.enter_context(nc.sbuf_tensor("%s_u%d" % (name, uid[0]), list(shape), dt))

    def dbg_dump(name, ap, shape, dt, reads):
        if not debug:
            return
        t = nc.dram_tensor("dbg_" + name, list(shape), dt, kind="ExternalOutput").ap()
        P.dma(lambda e: e.dma_start(out=t, in_=ap), reads=reads, writes=[Buf()])

    HT = [[Buf("hT%d_%d" % (q, c)) for c in range(8)] for q in range(NQ)]
    XN = [Buf("xnT%d" % q) for q in range(NQ)]
    YB = [[Buf("ybr%d_%d" % (b, q)) for q in range(NQ)] for b in range(4)]
    OUTB = Buf("out")

    psb = [es.enter_context(nc.psum_tensor("psb%d" % i, [128, 512], F32)) for i in range(8)]
    psB = [Buf("psb%d" % i) for i in range(8)]
    ps_rr = [0]

    def next_ps():
        i = ps_rr[0]
        ps_rr[0] = (i + 1) % 6
        return psb[i], psB[i]

    cst = sb(es, "cst", [128, 512], F32)
    cstb = sb(es, "cstb", [128, 512], BF16)
    Bc = Buf("cst")
    Bcb = Buf("cstb")
    P.dma(lambda e: e.dma_start(out=cst[:], in_=consts[:, 0:512]), writes=[Bc])
    OP("dve", [Bc], [Bcb], lambda e: e.tensor_copy(out=cstb[:], in_=cst[:, 0:512]))
    ident_f = cst[:, C_ID:C_ID + 128]
    ones_f = cst[:, C_ONES:C_ONES + 128]
    triu_f = cst[:, C_TRIU:C_TRIU + 128]
    ident_b = cstb[:, C_ID:C_ID + 128]
    ones_b = cstb[:, C_ONES:C_ONES + 128]
    mneg_b = cstb[:, C_MNEG:C_MNEG + 128]
    epsc = sb(es, "epsc", [128, 2], F32)
    Beps = Buf("eps")
    OP("dve", [], [Beps], lambda e: e.memset(epsc[:, 0:1], EPS))
    OP("dve", [Beps], [Beps], lambda e: e.memset(epsc[:, 1:2], 1.0))
    gcol = sb(es, "gcol", [128, 9, 8], F32)
    Bg = Buf("gcol")
    P.dma(lambda e: e.dma_start(out=gcol[:, 0:8, :], in_=norm_g.rearrange("l n (c p) -> p (l n) c", p=128), allow_slow_non_contiguous=True), writes=[Bg])
    P.dma(lambda e: e.dma_start(out=gcol[:, 8, :], in_=final_g.rearrange("(c p) -> p c", p=128), allow_slow_non_contiguous=True), writes=[Bg])

    def rmsnorm(h, hB, gi, xn, xnB, tmp):
        ps, pB = next_ps()
        for c in range(8):
            sq, sqB = tmp["sq"][c % 2]
            OP("act", [hB[c]], [sqB], lambda e, c=c, sq=sq: e.activation(out=sq, in_=h[:, c, :], func=AF.Square))
            OP("pe", [sqB, Bcb], [pB], lambda e, c=c, sq=sq, ps=ps: e.matmul(ps[:, 0:TT], lhsT=ones_b, rhs=sq, start=(c == 0), stop=(c == 7)))
        lnv, lnB = tmp["lnv"]
        rstd, rsB = tmp["rstd"]
        OP("act", [pB, Beps], [lnB], lambda e, ps=ps: e.activation(out=lnv, in_=ps[:, 0:TT], func=AF.Ln, scale=1.0 / D, bias=epsc[:, 0:1]))
        OP("act", [lnB], [rsB], lambda e: e.activation(out=rstd, in_=lnv, func=AF.Exp, scale=-0.5))
        for c in range(8):
            OP("dve", [hB[c], rsB, Bg], [xnB[c]],
               lambda e, c=c: e.scalar_tensor_tensor(out=xn[:, c, :], in0=h[:, c, :], scalar=gcol[:, gi, c:c + 1], in1=rstd,
                                                     op0=ALU.mult, op1=ALU.mult))

    def norm_tmp(stack, tag):
        sq0 = sb(stack, "sq0" + tag, [128, TT], BF16)
        sq1 = sb(stack, "sq1" + tag, [128, TT], BF16)
        lnv = sb(stack, "lnv" + tag, [128, TT], F32)
        rstd = sb(stack, "rstd" + tag, [128, TT], F32)
        return {"sq": [(sq0[:], Buf()), (sq1[:], Buf())], "lnv": (lnv[:], Buf()), "rstd": (rstd[:], Buf())}

    def load_h(tile, tB, q):
        P.dma(lambda e: e.dma_start(out=tile, in_=hTv[:, :, q * TT:(q + 1) * TT]), reads=HT[q], writes=tB)

    def wload(dst, src, wB):
        P.dma(lambda e: e.dma_start(out=dst, in_=src), writes=wB, q="pool")

    def phase_in():
        P.barrier()
        with contextlib.ExitStack() as st:
            xt = [sb(st, "xt%d" % i, [128, D], F32) for i in range(2)]
            xtB = [Buf() for _ in range(2)]
            ho = [sb(st, "ho%d" % i, [128, 8, TT], F32) for i in range(2)]
            hoB = [Buf() for _ in range(2)]
            k = 0
            for q in range(NQ):
                for s in range(4):
                    tt = q * 4 + s
                    xb, xB = xt[tt % 2], xtB[tt % 2]
                    P.dma(lambda e, xb=xb, tt=tt: e.dma_start(out=xb[:], in_=x[tt * 128:(tt + 1) * 128, :]), writes=[xB])
                    for half in range(2):
                        ps, pB = next_ps()
                        for cc in range(4):
                            c = half * 4 + cc
                            OP("pe", [xB, Bc], [pB], lambda e, ps=ps, cc=cc, c=c, xb=xb: e.transpose(out=ps[:, cc * 128:(cc + 1) * 128], in_=xb[:, c * 128:(c + 1) * 128], identity=ident_f))
                        eng = "act" if (k % 2 == 0) else "dve"
                        k += 1
                        dst = ho[q % 2][:, half * 4:(half + 1) * 4, s * 128:(s + 1) * 128]
                        src = ps[:, :].rearrange("p (c t) -> p c t", c=4)
                        if eng == "act":
                            OP("act", [pB], [hoB[q % 2]], lambda e, dst=dst, src=src: e.activation(out=dst, in_=src, func=AF.Copy))
                        else:
                            OP("dve", [pB], [hoB[q % 2]], lambda e, dst=dst, src=src: e.tensor_copy(out=dst, in_=src))
                P.dma(lambda e, q=q: e.dma_start(out=hTv[:, :, q * TT:(q + 1) * TT], in_=ho[q % 2][:]), reads=[hoB[q % 2]], writes=HT[q])

    def phase_out():
        P.barrier()
        with contextlib.ExitStack() as st:
            hn2 = [sb(st, "fo_hn%d" % i, [128, 8, TT], F32) for i in range(2)]
            hnB2 = [[Buf() for _ in range(8)] for _ in range(2)]
            yn2 = [sb(st, "fo_yn%d" % i, [128, 8, TT], F32) for i in range(2)]
            ynB2 = [[Buf() for _ in range(8)] for _ in range(2)]
            ot = [sb(st, "fo_ot%d" % i, [128, D], F32) for i in range(4)]
            otB = [Buf() for _ in range(4)]
            tmp2 = [norm_tmp(st, "fo%d" % i) for i in range(2)]
            k = 0
            for q in range(NQ):
                hn, hnB, yn, ynB = hn2[q % 2], hnB2[q % 2], yn2[q % 2], ynB2[q % 2]
                load_h(hn[:], hnB, q)
                rmsnorm(hn, hnB, 8, yn, ynB, tmp2[q % 2])
                for s in range(4):
                    tt = q * 4 + s
                    o, oB = ot[tt % 4], otB[tt % 4]
                    for half in range(2):
                        ps, pB = next_ps()
                        for cc in range(4):
                            c = half * 4 + cc
                            OP("pe", [ynB[c], Bc], [pB], lambda e, ps=ps, cc=cc, c=c, s=s, yn=yn: e.transpose(out=ps[:, cc * 128:(cc + 1) * 128], in_=yn[:, c, s * 128:(s + 1) * 128], identity=ident_f))
                        dst = o[:, half * 512:(half + 1) * 512]
                        if k % 2 == 0:
                            OP("act", [pB], [oB], lambda e, dst=dst, ps=ps: e.activation(out=dst, in_=ps[:, :], func=AF.Copy))
                        else:
                            OP("dve", [pB], [oB], lambda e, dst=dst, ps=ps: e.tensor_copy(out=dst, in_=ps[:, :]))
                        k += 1
                    P.dma(lambda e, o=o, tt=tt: e.dma_start(out=y_out[tt * 128:(tt + 1) * 128, :], in_=o[:]), reads=[oB], writes=[Buf()], is_out=True)

    def phase_ffn(l, f):
        P.barrier()
        with contextlib.ExitStack() as st:
            wg = sb(st, "wg", [128, 8, DFF], BF16)
            wu = sb(st, "wu", [128, 8, DFF], BF16)
            wd = sb(st, "wd", [128, NFC, D], BF16)
            CB = [(0, 256), (256, 1024), (1024, 2048), (2048, DFF)]
            wgB = [Buf() for _ in range(4)]
            wuB = [Buf() for _ in range(4)]
            wdB = [Buf() for _ in range(NFC)]
            wgv = w_gate[l, f].rearrange("(kc p) n -> p kc n", p=128)
            wuv = w_up[l, f].rearrange("(kc p) n -> p kc n", p=128)
            wdv = w_down[l, f].rearrange("(fc p) n -> p fc n", p=128)
            for cb, (c0, c1) in enumerate(CB):
                wload(wg[:, :, c0:c1], wgv[:, :, c0:c1], [wgB[cb]])
                wload(wu[:, :, c0:c1], wuv[:, :, c0:c1], [wuB[cb]])
            for f0 in range(0, NFC, 6):
                f1 = min(NFC, f0 + 6)
                wload(wd[:, f0:f1, :], wdv[:, f0:f1, :], wdB[f0:f1])
            hn = sb(st, "ff_hn", [128, 8, TT], F32)
            hnB = [Buf() for _ in range(8)]
            xn = [sb(st, "ff_xn%d" % i, [128, 8, TT], BF16) for i in range(2)]
            xnB = [[Buf() for _ in range(8)] for _ in range(2)]
            act = sb(st, "ff_act", [128, NFC, TT], BF16)
            actB = [Buf() for _ in range(NFC)]
            sg = [sb(st, "ff_sg%d" % i, [128, TT], F32) for i in range(2)]
            sgB = [Buf() for _ in range(2)]
            hr = [sb(st, "ff_hr%d" % i, [128, TT], F32) for i in range(3)]
            hrB = [Buf() for _ in range(3)]
            tmp = norm_tmp(st, "ff")
            gi = l * 4 + (0 if f == 0 else 2)
            k = 0
            load_h(hn[:], hnB, 0)
            rmsnorm(hn, hnB, gi, xn[0], xnB[0], tmp)
            for q in range(nq):
                X, XB = xn[q % 2], xnB[q % 2]
                if q + 1 < nq:
                    load_h(hn[:], hnB, q + 1)
                if q == 0 and l == 0 and f == 0:
                    dbg_dump("xn", X[:], [128, 8, TT], BF16, XB)
                    dbg_dump("wg", wg[:], [128, 8, DFF], BF16, wgB)
                    dbg_dump("wd", wd[:], [128, NFC, D], BF16, wdB)
                for fc in range(NFC):
                    pg, pgB = next_ps()
                    for kc in range(8):
                        OP("pe", [wgB[0 if fc < 2 else 1 + fc // 8], XB[kc]], [pgB], lambda e, pg=pg, kc=kc, fc=fc, X=X: e.matmul(pg[:, 0:TT], lhsT=wg[:, kc, fc * 128:(fc + 1) * 128], rhs=X[:, kc, :], start=(kc == 0), stop=(kc == 7)))
                    pu, puB = next_ps()
                    for kc in range(8):
                        OP("pe", [wuB[0 if fc < 2 else 1 + fc // 8], XB[kc]], [puB], lambda e, pu=pu, kc=kc, fc=fc, X=X: e.matmul(pu[:, 0:TT], lhsT=wu[:, kc, fc * 128:(fc + 1) * 128], rhs=X[:, kc, :], start=(kc == 0), stop=(kc == 7)))
                    s_, sB_ = sg[fc % 2], sgB[fc % 2]
                    OP("act", [pgB], [sB_], lambda e, s_=s_, pg=pg: e.activation(out=s_[:], in_=pg[:, 0:TT], func=AF.Silu))
                    OP("dve", [sB_, puB], [actB[fc]], lambda e, s_=s_, pu=pu, fc=fc: e.tensor_tensor(out=act[:, fc, :], in0=pu[:, 0:TT], in1=s_[:], op=ALU.mult))
                    if fc == 11 and q + 1 < nq:
                        rmsnorm(hn, hnB, gi, xn[(q + 1) % 2], xnB[(q + 1) % 2], tmp)
                if q == 0 and l == 0 and f == 0:
                    dbg_dump("act", act[:], [128, NFC, TT], BF16, actB)
                for dc in range(8):
                    po, poB = next_ps()
                    for fc in range(NFC):
                        OP("pe", [wdB[fc], actB[fc]], [poB], lambda e, po=po, fc=fc, dc=dc: e.matmul(po[:, 0:TT], lhsT=wd[:, fc, dc * 128:(dc + 1) * 128], rhs=act[:, fc, :], start=(fc == 0), stop=(fc == NFC - 1)))
                    r, rB = hr[k % 3], hrB[k % 3]
                    k += 1
                    P.dma(lambda e, r=r, dc=dc, q=q: e.dma_start(out=r[:], in_=hTv[:, dc, q * TT:(q + 1) * TT]), reads=[HT[q][dc]], writes=[rB])
                    OP("dve", [poB, rB], [rB], lambda e, r=r, po=po: e.scalar_tensor_tensor(out=r[:], in0=po[:, 0:TT], scalar=0.5, in1=r[:], op0=ALU.mult, op1=ALU.add))
                    P.dma(lambda e, r=r, dc=dc, q=q: e.dma_start(out=hTv[:, dc, q * TT:(q + 1) * TT], in_=r[:]), reads=[rB], writes=[HT[q][dc]])

    def phase_ple(l, fuse_out=False):
        P.barrier()
        with contextlib.ExitStack() as st:
            wpg = sb(st, "wpg", [128, 8, D], BF16)
            wpp = sb(st, "wpp", [128, 2, D], BF16)
            wpgB = [Buf()]
            wppB = [Buf()]
            wpgv = ple_wg[l].rearrange("(kc p) n -> p kc n", p=128)
            wpgB = [Buf(), Buf()]
            wload(wpg[:, :, 0:256], wpgv[:, :, 0:256], [wpgB[0]])
            wload(wpg[:, :, 256:D], wpgv[:, :, 256:D], [wpgB[1]])
            wload(wpp[:], ple_wp[l].rearrange("(kc p) n -> p kc n", p=128), wppB)
            NH = 3 if fuse_out else 2
            hn2 = [sb(st, "pl_hn%d" % i, [128, 8, TT], F32) for i in range(NH)]
            hnB2 = [[Buf() for _ in range(8)] for _ in range(NH)]
            xn = [sb(st, "pl_xn%d" % i, [128, 8, TT], BF16) for i in range(2)]
            xnB = [[Buf() for _ in range(8)] for _ in range(2)]
            pt = [sb(st, "pl_pt%d" % i, [128, 4, PLE], F32) for i in range(2)]
            ptB = [Buf() for _ in range(2)]
            pT = [sb(st, "pl_pT%d" % i, [128, 2, TT], BF16) for i in range(2)]
            pTB = [Buf() for _ in range(2)]
            sg = [sb(st, "pl_sg%d" % i, [128, TT], F32) for i in range(2)]
            sgB = [Buf() for _ in range(2)]
            tg = [sb(st, "pl_tg%d" % i, [128, TT], F32) for i in range(2)]
            tgB = [Buf() for _ in range(2)]
            hr = [sb(st, "pl_hr%d" % i, [128, TT], F32) for i in range(3)]
            hrB = [Buf() for _ in range(3)]
            tmp2 = [norm_tmp(st, "pl%d" % i) for i in range(2)]
            gi = l * 4 + 3

            def ple_load(q):
                load_h(hn2[q % NH][:], hnB2[q % NH], q)
                pt_, ptB_ = pt[q % 2], ptB[q % 2]
                P.dma(lambda e, pt_=pt_, q=q: e.dma_start(out=pt_[:], in_=p_in[l, q * TT:(q + 1) * TT, :].rearrange("(s p) c -> p s c", p=128)), writes=[ptB_])

            def ple_prep(q):
                rmsnorm(hn2[q % NH], hnB2[q % NH], gi, xn[q % 2], xnB[q % 2], tmp2[q % 2])
                pt_, ptB_ = pt[q % 2], ptB[q % 2]
                pT_, pTB_ = pT[q % 2], pTB[q % 2]
                for c2 in range(2):
                    ps, pB = next_ps()
                    for s in range(4):
                        OP("pe", [ptB_, Bc], [pB], lambda e, ps=ps, s=s, c2=c2, pt_=pt_: e.transpose(out=ps[:, s * 128:(s + 1) * 128], in_=pt_[:, s, c2 * 128:(c2 + 1) * 128], identity=ident_f))
                    OP("act", [pB], [pTB_], lambda e, ps=ps, c2=c2, pT_=pT_: e.activation(out=pT_[:, c2, :], in_=ps[:, :], func=AF.Copy))

            if fuse_out:
                yn2 = [sb(st, "fo_yn%d" % i, [128, 8, TT], F32) for i in range(2)]
                ynB2 = [[Buf() for _ in range(8)] for _ in range(2)]
                ot = [sb(st, "fo_ot%d" % i, [128, D], F32) for i in range(4)]
                otB = [Buf() for _ in range(4)]
                tmpo = [norm_tmp(st, "fo%d" % i) for i in range(2)]
            kev = [0]

            def out_tile(q):
                hn, hnB, yn, ynB = hn2[q % NH], hnB2[q % NH], yn2[q % 2], ynB2[q % 2]
                rmsnorm(hn, hnB, 8, yn, ynB, tmpo[q % 2])
                for s in range(4):
                    tt = q * 4 + s
                    o, oB = ot[tt % 4], otB[tt % 4]
                    for half in range(2):
                        ps, pB = next_ps()
                        for cc in range(4):
                            c_ = half * 4 + cc
                            OP("pe", [ynB[c_], Bc], [pB], lambda e, ps=ps, cc=cc, c_=c_, s=s, yn=yn: e.transpose(out=ps[:, cc * 128:(cc + 1) * 128], in_=yn[:, c_, s * 128:(s + 1) * 128], identity=ident_f))
                        dst = o[:, half * 512:(half + 1) * 512]
                        if kev[0] % 2 == 0:
                            OP("act", [pB], [oB], lambda e, dst=dst, ps=ps: e.activation(out=dst, in_=ps[:, :], func=AF.Copy))
                        else:
                            OP("dve", [pB], [oB], lambda e, dst=dst, ps=ps: e.tensor_copy(out=dst, in_=ps[:, :]))
                        kev[0] += 1
                    P.dma(lambda e, o=o, tt=tt: e.dma_start(out=y_out[tt * 128:(tt + 1) * 128, :], in_=o[:]), reads=[oB], writes=[Buf()], is_out=True)

            ple_load(0)
            ple_prep(0)
            for q in range(NQ):
                hn, hnB = hn2[q % NH], hnB2[q % NH]
                X, XB = xn[q % 2], xnB[q % 2]
                pT_, pTB_ = pT[q % 2], pTB[q % 2]
                if q + 1 < NQ:
                    ple_load(q + 1)
                for dc in range(8):
                    pg, pgB = next_ps()
                    for kc in range(8):
                        OP("pe", [wpgB[0 if dc < 2 else 1], XB[kc]], [pgB], lambda e, pg=pg, kc=kc, dc=dc, X=X: e.matmul(pg[:, 0:TT], lhsT=wpg[:, kc, dc * 128:(dc + 1) * 128], rhs=X[:, kc, :], start=(kc == 0), stop=(kc == 7)))
                    pe_, peB = next_ps()
                    for c2 in range(2):
                        OP("pe", [wppB[0], pTB_], [peB], lambda e, pe_=pe_, c2=c2, dc=dc, pT_=pT_: e.matmul(pe_[:, 0:TT], lhsT=wpp[:, c2, dc * 128:(dc + 1) * 128], rhs=pT_[:, c2, :], start=(c2 == 0), stop=(c2 == 1)))
                    s_, sB_ = sg[dc % 2], sgB[dc % 2]
                    OP("act", [pgB], [sB_], lambda e, s_=s_, pg=pg: e.activation(out=s_[:], in_=pg[:, 0:TT], func=AF.Sigmoid))
                    t_, tB_ = tg[dc % 2], tgB[dc % 2]
                    OP("dve", [sB_, peB], [tB_], lambda e, s_=s_, pe_=pe_, t_=t_: e.tensor_tensor(out=t_[:], in0=pe_[:, 0:TT], in1=s_[:], op=ALU.mult))
                    OP("pool", [tB_, hnB[dc]], [hnB[dc]], lambda e, hn=hn, t_=t_, dc=dc: e.tensor_tensor(out=hn[:, dc, :], in0=t_[:], in1=hn[:, dc, :], op=ALU.add))
                    if not fuse_out:
                        P.dma(lambda e, hn=hn, dc=dc, q=q: e.dma_start(out=hTv[:, dc, q * TT:(q + 1) * TT], in_=hn[:, dc, :]), reads=[hnB[dc]], writes=[HT[q][dc]])
                    if dc == 3 and q + 1 < NQ:
                        ple_prep(q + 1)
                    if fuse_out and dc == 3 and q >= 1:
                        out_tile(q - 1)
            if fuse_out:
                out_tile(NQ - 1)

    ctx = dict(nc=nc, P=P, OP=OP, sb=sb, es=es, next_ps=next_ps, rmsnorm=rmsnorm, norm_tmp=norm_tmp, load_h=load_h,
               wload=wload, HT=HT, XN=XN, YB=YB, hTv=hTv, xnTv=xnTv, ybr=ybr, cst=cst, cstb=cstb, Bc=Bc, Bcb=Bcb,
               epsc=epsc, Beps=Beps, ident_f=ident_f, ones_f=ones_f, triu_f=triu_f, ident_b=ident_b, ones_b=ones_b,
               mneg_b=mneg_b, gcol=gcol, Bg=Bg,
               w_in=w_in, f_bias=f_bias, pool_w=pool_w, pool_scale=pool_scale, lam_re=lam_re, lam_im=lam_im,
               log_dt=log_dt, b_re=b_re, b_im=b_im, c_re=c_re, c_im=c_im, ssm_d=ssm_d, w_glu=w_glu, conv_w=conv_w,
               w_branch=w_branch, w_out=w_out, consts=consts, qaug=qaug, vtok=vtok, psb=psb, psB=psB, dbg_dump=dbg_dump, nq=nq)

    if "in" in phases:
        phase_in()
    for l in range(depth):
        if "ffn" in phases:
            phase_ffn(l, 0)
        if "mix" in phases:
            phase_mixer(ctx, l, mix_parts)
        if "ffn" in phases:
            phase_ffn(l, 1)
        fuse = ("out" in phases) and (l == depth - 1)
        if "ple" in phases:
            phase_ple(l, fuse_out=fuse)
    if "out" in phases and "ple" not in phases:
        phase_out()
    P.emit(es)
    es.close()
    return nc


from types import SimpleNamespace


def phase_mixer(ctx, l, parts):
    c = SimpleNamespace(**ctx)
    P, OP, sb, nc = c.P, c.OP, c.sb, c.nc
    psb, psB = c.psb, c.psB
    ybrv = c.ybr.rearrange("b (c p) t -> b p c t", p=128)
    QA = [Buf() for _ in range(NQ)]
    VT = [Buf() for _ in range(NQ)]
    P.barrier()
    with contextlib.ExitStack() as so:
        u_ssm = sb(so, "u_ssm", [128, 2, 8 + T], BF16)
        uB = [[Buf() for _ in range(NQ)] for _ in range(2)]
        spar = ssm_param_load(c, l, so) if "ssm" in parts else None
        spre = ssm_s_alloc(c, so) if "ssm" in parts else None
        with contextlib.ExitStack() as s1:
            k_aug = sb(s1, "k_aug", [128, 4, T], BF16)
            kB = [[Buf() for _ in range(NQ)] for _ in range(4)]
            kcB = Buf()
            cabs = sb(s1, "cabs", [128, 32, 4], F32)
            tots = sb(s1, "tots", [128, 33, 4], F32)
            cabsB = [Buf() for _ in range(32)]
            totsB = [Buf() for _ in range(33)]
            mixer_m1(c, l, parts, u_ssm, uB, k_aug, kB, kcB, cabs, cabsB, tots, totsB, QA, VT, ybrv)
            P.barrier()
            sch = ssm_s_chain(c, l, spre, spar) if "ssm" in parts else None
            if "att" in parts:
                mixer_m3(c, l, k_aug, kB, kcB, cabs, cabsB, tots, totsB, QA, VT, ybrv)
        P.barrier()
        if "ssm" in parts:
            mixer_m2(c, l, u_ssm, uB, ybrv, spar, sch)
    P.barrier()
    mixer_m4(c, l, parts, ybrv)


def mixer_m1(c, l, parts, u_ssm, uB, k_aug, kB, kcB, cabs, cabsB, tots, totsB, QA, VT, ybrv):
    P, OP, sb, nc, next_ps = c.P, c.OP, c.sb, c.nc, c.next_ps
    with contextlib.ExitStack() as st:
        win = sb(st, "win", [128, 8, 2052], BF16)
        WBLK = [(0, 772), (772, 1796), (1796, 2052)]
        winB = [Buf() for _ in WBLK]
        wiv = c.w_in[l].rearrange("(kc p) n -> p kc n", p=128)
        c.wload(win[:, :, 0:772], wiv[:, :, 0:772], [winB[0]])

        def wB(col):
            for i, (c0, c1) in enumerate(WBLK):
                if c0 <= col < c1:
                    return winB[i]

        wf_sb = sb(st, "wf_sb", [128, 8, 4, 64], BF16)
        wfB = Buf()
        OP("dve", [winB[0]], [wfB], lambda e: e.tensor_copy(out=wf_sb[:], in_=win[:, :, 768:772].unsqueeze(3).to_broadcast([128, 8, 4, 64])))
        fb = sb(st, "fb", [128, 8], F32)
        fbB = Buf()
        P.dma(lambda e: e.dma_start(out=fb[:, 0:4], in_=c.f_bias[l:l + 1, :].partition_broadcast(128), allow_slow_non_contiguous=True), writes=[fbB])
        OP("dve", [fbB], [fbB], lambda e: e.tensor_scalar(out=fb[:, 4:8], in0=fb[:, 0:4], scalar1=-1.0, scalar2=None, op0=ALU.mult))
        pwb = sb(st, "pwb", [128, 2, 128], BF16)
        pwB = Buf()
        OP("pool", [], [pwB], lambda e: e.memset(pwb[:], 0.0))
        for g in range(4):
            r0 = (g % 2) * 64
            P.dma(lambda e, g=g, r0=r0: e.dma_start(out=pwb[r0:r0 + 64, g // 2, r0:r0 + 64], in_=c.pool_w[l, g]), reads=[pwB], writes=[pwB], q="pool")
        pmb = sb(st, "pmb", [128, 12, 128], BF16)
        pmB = Buf()
        P.dma(lambda e: e.dma_start(out=pmb[:], in_=c.consts[:, C_PM:C_PM + 12 * 128].rearrange("p (a b) -> p a b", a=12)), writes=[pmB], q="pool")
        for i in (1, 2):
            c.wload(win[:, :, WBLK[i][0]:WBLK[i][1]], wiv[:, :, WBLK[i][0]:WBLK[i][1]], [winB[i]])
        scol = sb(st, "scol", [128, 8], F32)
        scB = Buf()
        P.dma(lambda e: e.dma_start(out=scol[:, 0:2], in_=c.pool_scale[l].rearrange("(c p) -> p c", p=128), allow_slow_non_contiguous=True), writes=[scB])
        P.dma(lambda e: e.dma_start(out=scol[:, 2:8].rearrange("p (j c) -> p j c", j=3), in_=c.conv_w[l].rearrange("j (c p) -> p j c", p=128), allow_slow_non_contiguous=True), writes=[scB])

        OP("pool", [], [totsB[0]], lambda e: e.memset(tots[:, 0, :], 0.0))

        hn = sb(st, "m1_hn", [128, 8, TT], F32)
        hnB = [Buf() for _ in range(8)]
        xn2 = [sb(st, "m1_xn%d" % i, [128, 8, TT], BF16) for i in range(2)]
        xnB2 = [[Buf() for _ in range(8)] for _ in range(2)]
        tmp = c.norm_tmp(st, "m1")
        qa = [sb(st, "m1_qa%d" % i, [128, 4, TT], BF16) for i in range(2)]
        qaB = [Buf() for _ in range(2)]
        et = [sb(st, "m1_et%d" % i, [128, TT], F32) for i in range(2)]
        etB = [Buf() for _ in range(2)]
        spt = [sb(st, "m1_sp%d" % i, [128, TT], F32) for i in range(2)]
        spB = [Buf() for _ in range(2)]
        crn = [sb(st, "m1_crn%d" % i, [128, TT], F32) for i in range(2)]
        crnB = [Buf() for _ in range(2)]
        hit = [sb(st, "m1_hit%d" % i, [128, TT], BF16) for i in range(2)]
        hitB = [Buf() for _ in range(2)]
        vst = [sb(st, "m1_vst%d" % i, [128, 4, 256], BF16) for i in range(2)]
        vstB = [Buf() for _ in range(2)]
        ftk = [sb(st, "m1_ftk%d" % i, [128, 12], F32) for i in range(2)]
        ftkB = [Buf() for _ in range(2)]
        xpt = sb(st, "m1_xpt", [128, 5, 256], BF16)
        xptB = [Buf() for _ in range(5)]
        pld = sb(st, "m1_pld", [128, 2, TT], BF16)
        pldB = [Buf() for _ in range(2)]
        yps = [sb(st, "m1_yps%d" % i, [128, 2, TT], BF16) for i in range(2)]
        ypsB = [Buf() for _ in range(2)]
        ycs = [sb(st, "m1_ycs%d" % i, [128, 2, TT], BF16) for i in range(2)]
        ycsB = [Buf() for _ in range(2)]
        ccs = [sb(st, "m1_ccs%d" % i, [128, TT], F32) for i in range(2)]
        ccsB = [Buf() for _ in range(2)]
        cbs = [sb(st, "m1_cbs%d" % i, [128, TT], F32) for i in range(2)]
        cbsB = [Buf() for _ in range(2)]
        zt = [[sb(st, "m1_z%d_%d" % (cc, i), [128, TT + 2], F32) for i in range(2)] for cc in range(2)]
        ztB = [[Buf() for _ in range(2)] for _ in range(2)]
        y1 = [sb(st, "m1_y1%d" % i, [128, TT], F32) for i in range(2)]
        y1B = [Buf() for _ in range(2)]
        ones1 = c.cst[:, C_ONES:C_ONES + 1]

        def proj_fm(col0, M, ps, pB, pslice, X, XB, tp=None):
            for kc in range(8):
                kw = {} if tp is None else {"tile_position": tp}
                OP("pe", [wB(col0), XB[kc]], [pB], lambda e, kc=kc, kw=kw: e.matmul(ps[pslice, 0:TT], lhsT=win[:, kc, col0:col0 + M], rhs=X[:, kc, :], start=(kc == 0), stop=(kc == 7), **kw))

        c.load_h(hn[:], hnB, 0)
        c.rmsnorm(hn, hnB, l * 4 + 1, xn2[0], xnB2[0], tmp)
        if c.nq > 1:
            c.load_h(hn[:], hnB, 1)
        for q in range(c.nq):
            qs = slice(q * TT, (q + 1) * TT)
            xn, xnB = xn2[q % 2], xnB2[q % 2]
            P.dma(lambda e, qs=qs, xn=xn: e.dma_start(out=c.xnTv[:, :, qs], in_=xn[:]), reads=xnB, writes=[c.XN[q]])
            Q_, QB_ = qa[q % 2], qaB[q % 2]
            if "att" in parts:
                for h in range(4):
                    ps, pB = next_ps()
                    proj_fm(h * 64, 64, ps, pB, slice(0, 64), xn, xnB)
                    for kc in range(8):
                        OP("pe", [wfB, xnB[kc]], [pB], lambda e, kc=kc, h=h, ps=ps, xn=xn: e.matmul(ps[64:128, 0:TT], lhsT=wf_sb[:, kc, h, :], rhs=xn[:, kc, :], start=(kc == 0), stop=(kc == 7), tile_position=(0, 64)))
                    OP("act", [pB], [QB_], lambda e, ps=ps, h=h, Q_=Q_: e.activation(out=Q_[0:64, h, :], in_=ps[0:64, 0:TT], func=AF.Copy, scale=0.125))
                    e_, eB_ = et[h % 2], etB[h % 2]
                    OP("act", [pB, fbB], [eB_], lambda e, ps=ps, h=h, e_=e_: e.activation(out=e_[64:128, :], in_=ps[64:128, 0:TT], func=AF.Exp, scale=-1.0, bias=fb[64:128, 4 + h:5 + h]))
                    s_, sB_ = spt[h % 2], spB[h % 2]
                    OP("act", [eB_, c.Beps], [sB_], lambda e, e_=e_, s_=s_: e.activation(out=s_[64:128, :], in_=e_[64:128, :], func=AF.Ln, bias=c.epsc[64:128, 1:2]))
                    r_, rB_ = crn[h % 2], crnB[h % 2]
                    OP("dve", [sB_, c.Bc], [rB_], lambda e, s_=s_, r_=r_: e.tensor_tensor_scan(out=r_[64:128, :], data0=ones1[64:128, :].to_broadcast([64, TT]), data1=s_[64:128, :], initial=0.0, op0=ALU.mult, op1=ALU.add))
                    h_, hB_ = hit[h % 2], hitB[h % 2]
                    OP("dve", [rB_], [hB_], lambda e, r_=r_, h_=h_: e.tensor_scalar(out=h_[64:128, :], in0=r_[64:128, :], scalar1=-1.0, scalar2=None, op0=ALU.mult))
                    OP("pool", [hB_], [QB_], lambda e, h_=h_, h=h, Q_=Q_: e.tensor_copy(out=Q_[64:96, h, :], in_=h_[64:96, :]))
                    OP("dve", [rB_, hB_], [QB_], lambda e, r_=r_, h_=h_, h=h, Q_=Q_: e.scalar_tensor_tensor(out=Q_[96:128, h, :], in0=r_[96:128, :], scalar=-1.0, in1=h_[96:128, :], op0=ALU.mult, op1=ALU.subtract))
                P.dma(lambda e, qs=qs, Q_=Q_: e.dma_start(out=c.qaug[:, :, qs], in_=Q_[:]), reads=[QB_], writes=[QA[q]])
                for h in range(4):
                    ps, pB = next_ps()
                    proj_fm(256 + h * 64, 64, ps, pB, slice(0, 64), xn, xnB)
                    OP("dve", [pB], [kB[h][q]], lambda e, ps=ps, h=h, qs=qs: e.tensor_copy(out=k_aug[0:64, h, qs], in_=ps[0:64, 0:TT]))
            if q == 0:
                OP("pool", [], [kcB], lambda e: e.memset(k_aug[64:128, :, :], 0.0))
                OP("pool", [kcB], [kcB], lambda e: e.memset(k_aug[64:65, :, :], 1.0))
                OP("pool", [kcB], [kcB], lambda e: e.memset(k_aug[96:97, :, :], 1.0))
            if q + 1 < c.nq:
                c.rmsnorm(hn, hnB, l * 4 + 1, xn2[(q + 1) % 2], xnB2[(q + 1) % 2], tmp)
                if q + 2 < c.nq:
                    c.load_h(hn[:], hnB, q + 2)
            V_, VB_ = vst[q % 2], vstB[q % 2]
            ppool = [(c.psb[6], c.psB[6]), (c.psb[7], c.psB[7])]

            def stA(s):
                tt = q * 4 + s
                ts_ = slice(s * 128, (s + 1) * 128)
                if "att" not in parts:
                    return
                psA, pBA = next_ps()
                for kc in range(8):
                    OP("pe", [winB[0], xnB[kc]], [pBA], lambda e, kc=kc, psA=psA, ts_=ts_, xn=xn: e.matmul(psA[:, 0:260], lhsT=xn[:, kc, ts_], rhs=win[:, kc, 512:772], start=(kc == 0), stop=(kc == 7)))
                OP("act", [pBA], [VB_], lambda e, psA=psA, s=s, V_=V_: e.activation(out=V_[:, s, :], in_=psA[:, 0:256], func=AF.Copy))
                f_, fB_ = ftk[tt % 2], ftkB[tt % 2]
                OP("dve", [pBA, fbB], [fB_], lambda e, psA=psA, f_=f_: e.tensor_tensor(out=f_[:, 0:4], in0=psA[:, 256:260], in1=fb[:, 0:4], op=ALU.add))
                OP("act", [fB_], [fB_], lambda e, f_=f_: e.activation(out=f_[:, 4:8], in_=f_[:, 0:4], func=AF.Exp, scale=-1.0))
                OP("act", [fB_, c.Beps], [fB_], lambda e, f_=f_: e.activation(out=f_[:, 8:12], in_=f_[:, 4:8], func=AF.Ln, bias=c.epsc[:, 1:2]))

            def stB(s):
                ts_ = slice(s * 128, (s + 1) * 128)
                if "pool" not in parts:
                    return
                psP, pBP = next_ps()
                for kc in range(8):
                    OP("pe", [winB[1], xnB[kc]], [pBP], lambda e, kc=kc, psP=psP, ts_=ts_, xn=xn: e.matmul(psP[:, 0:256], lhsT=xn[:, kc, ts_], rhs=win[:, kc, 772:1028], start=(kc == 0), stop=(kc == 7)))
                OP("act", [pBP], [xptB[1 + s]], lambda e, psP=psP, s=s: e.activation(out=xpt[:, 1 + s, :], in_=psP[:, 0:256], func=AF.Copy))

            def stC(s):
                tt = q * 4 + s
                ts_ = slice(s * 128, (s + 1) * 128)
                if "pool" not in parts:
                    return
                for g in range(4):
                    pp, ppB = ppool[g // 2]
                    r0 = (g % 2) * 64
                    first = (tt == 0)
                    mi = g * 3 + (2 if first else 0)
                    OP("pe", [xptB[1 + s], pmB], [ppB], lambda e, pp=pp, r0=r0, g=g, s=s, mi=mi, ts_=ts_, first=first: e.matmul(pp[r0:r0 + 64, ts_], lhsT=xpt[:, 1 + s, g * 64:(g + 1) * 64], rhs=pmb[:, mi, :], start=True, stop=first, tile_position=(0, r0)))
                    if not first:
                        OP("pe", [xptB[s], pmB], [ppB], lambda e, pp=pp, r0=r0, g=g, s=s, ts_=ts_: e.matmul(pp[r0:r0 + 64, ts_], lhsT=xpt[:, s, g * 64:(g + 1) * 64], rhs=pmb[:, g * 3 + 1, :], start=False, stop=True, tile_position=(0, r0)))

            def stD(s):
                tt = q * 4 + s
                if "att" not in parts:
                    return
                f_, fB_ = ftk[tt % 2], ftkB[tt % 2]
                psc, pBc = next_ps()
                OP("pe", [fB_, c.Bc], [pBc], lambda e, psc=psc, f_=f_: e.matmul(psc[:, 0:4], lhsT=c.triu_f, rhs=f_[:, 8:12], start=True, stop=True))
                OP("pe", [fB_, c.Bc], [pBc], lambda e, psc=psc, f_=f_: e.matmul(psc[:, 8:12], lhsT=c.ones_f, rhs=f_[:, 8:12], start=True, stop=True))
                OP("dve", [pBc, totsB[tt]], [cabsB[tt]], lambda e, psc=psc, tt=tt: e.tensor_tensor(out=cabs[:, tt, :], in0=psc[:, 0:4], in1=tots[:, tt, :], op=ALU.add))
                OP("dve", [pBc, totsB[tt]], [totsB[tt + 1]], lambda e, psc=psc, tt=tt: e.tensor_tensor(out=tots[:, tt + 1, :], in0=psc[:, 8:12], in1=tots[:, tt, :], op=ALU.add))

            stA(0); stB(0); stA(1); stB(1); stC(0); stD(0); stA(2); stB(2); stC(1); stD(1); stA(3); stB(3); stC(2); stD(2); stC(3); stD(3)
            if "att" in parts:
                P.dma(lambda e, q=q, V_=V_: e.dma_start(out=c.vtok[q * TT:(q + 1) * TT, :].rearrange("(s p) c -> p s c", p=128), in_=V_[:]), reads=[VB_], writes=[VT[q]])
            if "pool" in parts:
                OP("pool", [xptB[4]], [xptB[0]], lambda e: e.tensor_copy(out=xpt[:, 0, :], in_=xpt[:, 4, :]))
                Y_, YB_ = yps[q % 2], ypsB[q % 2]
                for cc in range(2):
                    pp, ppB = ppool[cc]
                    OP("act", [ppB], [pldB[cc]], lambda e, pp=pp, cc=cc: e.activation(out=pld[:, cc, :], in_=pp[:, 0:TT], func=AF.Copy))
                    ps, pB = next_ps()
                    OP("pe", [pldB[cc], pwB], [pB], lambda e, ps=ps, cc=cc: e.matmul(ps[:, 0:TT], lhsT=pwb[:, cc, :], rhs=pld[:, cc, :], start=True, stop=True))
                    OP("dve", [pB, scB], [YB_], lambda e, ps=ps, cc=cc, Y_=Y_: e.tensor_scalar(out=Y_[:, cc, :], in0=ps[:, 0:TT], scalar1=scol[:, cc:cc + 1], scalar2=None, op0=ALU.mult))
                P.dma(lambda e, qs=qs, Y_=Y_: e.dma_start(out=ybrv[1][:, :, qs], in_=Y_[:]), reads=[YB_], writes=[c.YB[1][q]])
            if "ssm" in parts:
                for cc in range(2):
                    ps, pB = next_ps()
                    proj_fm(1028 + cc * 128, 128, ps, pB, slice(0, 128), xn, xnB)
                    OP("act", [pB], [uB[cc][q]], lambda e, ps=ps, cc=cc, q=q: e.activation(out=u_ssm[:, cc, 8 + q * TT:8 + (q + 1) * TT], in_=ps[:, 0:TT], func=AF.Copy))
            if "conv" in parts:
                Y_, YB_ = ycs[q % 2], ycsB[q % 2]
                for cc in range(2):
                    pcc, pccB = next_ps()
                    proj_fm(1540 + cc * 128, 128, pcc, pccB, slice(0, 128), xn, xnB)
                    pcx, pcxB = next_ps()
                    proj_fm(1796 + cc * 128, 128, pcx, pcxB, slice(0, 128), xn, xnB)
                    pcb, pcbB = next_ps()
                    proj_fm(1284 + cc * 128, 128, pcb, pcbB, slice(0, 128), xn, xnB)
                    a_, aB_ = ccs[cc], ccsB[cc]
                    b_, bB_ = cbs[cc], cbsB[cc]
                    z_, zB_ = zt[cc][q % 2], ztB[cc][q % 2]
                    zp_, zpB_ = zt[cc][(q + 1) % 2], ztB[cc][(q + 1) % 2]
                    y_, yB_ = y1[cc], y1B[cc]
                    OP("act", [pccB], [aB_], lambda e, pcc=pcc, a_=a_: e.activation(out=a_[:], in_=pcc[:, 0:TT], func=AF.Copy))
                    OP("act", [pcbB], [bB_], lambda e, pcb=pcb, b_=b_: e.activation(out=b_[:], in_=pcb[:, 0:TT], func=AF.Copy))
                    if q == 0:
                        OP("pool", [], [zB_], lambda e, z_=z_: e.memset(z_[:, 0:2], 0.0))
                    else:
                        OP("pool", [zpB_], [zB_], lambda e, z_=z_, zp_=zp_: e.tensor_copy(out=z_[:, 0:2], in_=zp_[:, TT:TT + 2]))
                    OP("dve", [pcxB, aB_], [zB_], lambda e, pcx=pcx, a_=a_, z_=z_: e.tensor_tensor(out=z_[:, 2:TT + 2], in0=pcx[:, 0:TT], in1=a_[:], op=ALU.mult))
                    OP("dve", [zB_, scB], [yB_], lambda e, z_=z_, y_=y_, cc=cc: e.tensor_scalar(out=y_[:], in0=z_[:, 0:TT], scalar1=scol[:, 2 + cc:3 + cc], scalar2=None, op0=ALU.mult))
                    OP("dve", [zB_, scB, yB_], [yB_], lambda e, z_=z_, y_=y_, cc=cc: e.scalar_tensor_tensor(out=y_[:], in0=z_[:, 1:TT + 1], scalar=scol[:, 4 + cc:5 + cc], in1=y_[:], op0=ALU.mult, op1=ALU.add))
                    OP("dve", [zB_, scB, yB_], [yB_], lambda e, z_=z_, y_=y_, cc=cc: e.scalar_tensor_tensor(out=y_[:], in0=z_[:, 2:TT + 2], scalar=scol[:, 6 + cc:7 + cc], in1=y_[:], op0=ALU.mult, op1=ALU.add))
                    OP("pool", [yB_, bB_], [YB_], lambda e, y_=y_, b_=b_, Y_=Y_, cc=cc: e.tensor_tensor(out=Y_[:, cc, :], in0=y_[:], in1=b_[:], op=ALU.mult))
                P.dma(lambda e, qs=qs, Y_=Y_: e.dma_start(out=ybrv[3][:, :, qs], in_=Y_[:]), reads=[YB_], writes=[c.YB[3][q]])


def mixer_m3(c, l, k_aug, kB, kcB, cabs, cabsB, tots, totsB, QA, VT, ybrv):
    P, OP, sb, nc = c.P, c.OP, c.sb, c.nc
    psb, psB = c.psb, c.psB
    with contextlib.ExitStack() as st:
        V_aug = sb(st, "V_aug", [128, 32, 4, 2, 64], BF16)
        VB = [Buf() for _ in range(NQ)]
        VoB = Buf()
        OP("pool", [], [VoB], lambda e: e.memset(V_aug[:, :, :, 1, :], 1.0))
        for q in range(c.nq):
            for i4 in range(4):
                i = q * 4 + i4
                P.dma(lambda e, i=i: e.dma_start(out=V_aug[:, i, :, 0, :], in_=c.vtok[i * 128:(i + 1) * 128, :].rearrange("p (h d) -> p h d", h=4)), reads=[VT[q]], writes=[VB[q]])
        qt = [sb(st, "m3_qt%d" % i, [128, 4, TT], BF16) for i in range(2)]
        qtB = [Buf() for _ in range(2)]
        Pt = [sb(st, "m3_P%d" % i, [128, TT], BF16) for i in range(3)]
        PtB = [Buf() for _ in range(3)]
        bq = [sb(st, "m3_bq%d" % i, [128, 32, 4], F32) for i in range(2)]
        bqB = [Buf() for _ in range(2)]
        rd = [sb(st, "m3_rd%d" % i, [64, TT], F32) for i in range(2)]
        rdB = [Buf() for _ in range(2)]
        yst = [sb(st, "m3_y%d" % i, [64, 4, TT], BF16) for i in range(2)]
        ystB = [Buf() for _ in range(2)]
        yav = c.ybr[0].rearrange("(h p) t -> p h t", p=64)
        kP = 0
        kS = 0
        kO = 0
        for q in range(c.nq):
            Q_, QB_ = qt[q % 2], qtB[q % 2]
            P.dma(lambda e, q=q, Q_=Q_: e.dma_start(out=Q_[:], in_=c.qaug[:, :, q * TT:(q + 1) * TT]), reads=[QA[q]], writes=[QB_])
            n = 4 * q + 4
            b_, bB_ = bq[q % 2], bqB[q % 2]
            OP("dve", cabsB[0:n] + [totsB[4 * q]], [bB_], lambda e, b_=b_, n=n, q=q: e.tensor_tensor(out=b_[:, 0:n, :], in0=cabs[:, 0:n, :], in1=tots[:, 4 * q, :].unsqueeze(1).to_broadcast([128, n, 4]), op=ALU.subtract))
            Y_, YB_ = yst[q % 2], ystB[q % 2]
            for h in range(4):
                po, poB = psb[4 + kO % 2], psB[4 + kO % 2]
                kO += 1
                def emit_S(i):
                    d = i - 4 * q
                    c0 = max(0, d) * 128
                    ps, pB = psb[i % 4], psB[i % 4]
                    OP("pe", [kB[h][i // 4], kcB, QB_], [pB], lambda e, ps=ps, i=i, c0=c0, d=d, h=h, Q_=Q_: e.matmul(ps[:, c0:TT], lhsT=k_aug[:, h, i * 128:(i + 1) * 128], rhs=Q_[:, h, c0:TT], start=True, stop=(d < 0)))
                    if d >= 0:
                        OP("pe", [c.Bcb], [pB], lambda e, ps=ps, c0=c0: e.matmul(ps[:, c0:c0 + 128], lhsT=c.ident_b, rhs=c.mneg_b, start=False, stop=True))
                    return ps, pB, c0

                nxt = emit_S(0)
                for i in range(n):
                    ps, pB, c0 = nxt
                    if i + 1 < n:
                        nxt = emit_S(i + 1)
                    p_, pB_ = Pt[kP % 3], PtB[kP % 3]
                    kP += 1
                    OP("act", [pB, bB_], [pB_], lambda e, ps=ps, p_=p_, c0=c0, i=i, b_=b_, h=h: e.activation(out=p_[:, c0:TT], in_=ps[:, c0:TT], func=AF.Exp, bias=b_[:, i, h:h + 1]))
                    OP("pe", [VB[i // 4], VoB, pB_], [poB], lambda e, p_=p_, c0=c0, i=i, po=po, h=h, n=n: e.matmul(po[:, c0:TT], lhsT=V_aug[:, i, h, :, :].rearrange("p a b -> p (a b)"), rhs=p_[:, c0:TT], start=(i == 0), stop=(i == n - 1)))
                r_, rB_ = rd[h % 2], rdB[h % 2]
                OP("dve", [poB], [rB_], lambda e, po=po, r_=r_: e.reciprocal(out=r_[0:64, :], in_=po[64:128, 0:TT]))
                OP("dve", [poB, rB_], [YB_], lambda e, po=po, r_=r_, h=h, Y_=Y_: e.tensor_tensor(out=Y_[0:64, h, :], in0=po[0:64, 0:TT], in1=r_[0:64, :], op=ALU.mult))
            P.dma(lambda e, q=q, Y_=Y_: e.dma_start(out=yav[:, :, q * TT:(q + 1) * TT], in_=Y_[:]), reads=[YB_], writes=[c.YB[0][q]])


def ssm_param_load(c, l, stack):
    P, sb = c.P, c.sb

    def mk_(name, shape, dt=F32):
        return sb(stack, "spl_" + name, shape, dt), Buf()
    lamS, lamSB = mk_("lamS", [128, 2, 8])
    for ri, src in enumerate((c.lam_re, c.lam_im)):
        P.dma(lambda e, ri=ri, src=src: e.dma_start(out=lamS[:, ri, :], in_=src[l].rearrange("(j g) p -> (g p) j", g=2), allow_slow_non_contiguous=True), writes=[lamSB])
    ldtS, ldtSB = mk_("ldtS", [128, 8])
    for g in range(2):
        P.dma(lambda e, g=g: e.dma_start(out=ldtS[g * 64:(g + 1) * 64, :], in_=c.log_dt[l].rearrange("(j g) -> g j", g=2)[g:g + 1, :].partition_broadcast(64), allow_slow_non_contiguous=True), writes=[ldtSB])
    BS, BSB = mk_("BS", [128, 2, 8, 16])
    for ri, src in enumerate((c.b_re, c.b_im)):
        P.dma(lambda e, ri=ri, src=src: e.dma_start(out=BS[:, ri, :, :], in_=src[l].rearrange("(j g) p h -> (g p) j h", g=2)), writes=[BSB])
    Csrc, CsB = mk_("Csrc", [128, 2, 128])
    for ri, src in enumerate((c.c_re, c.c_im)):
        for j in range(8):
            P.dma(lambda e, ri=ri, src=src, j=j: e.dma_start(out=Csrc[16 * j:16 * j + 16, ri, :].rearrange("h (g p) -> h g p", g=2), in_=src[l, 2 * j:2 * j + 2].rearrange("g h p -> h g p")), writes=[CsB])
    dcol, dcB = mk_("dcol", [128, 2])
    P.dma(lambda e: e.dma_start(out=dcol[:], in_=c.ssm_d[l].rearrange("(c p) -> p c", p=128), allow_slow_non_contiguous=True), writes=[dcB])
    Bsrc, BsB = mk_("Bsrc", [128, 2, 2, 4, 2, 16])
    for ri, src in enumerate((c.b_re, c.b_im)):
        for hf in range(2):
            for g2p in range(2):
                P.dma(lambda e, ri=ri, src=src, hf=hf, g2p=g2p: e.dma_start(out=Bsrc[:, ri, hf, :, g2p, :], in_=src[l, 8 * hf:8 * hf + 8].rearrange("(j g) p h -> (g p) j h", g=2)), writes=[BsB])
    return SimpleNamespace(lamS=lamS, lamSB=lamSB, ldtS=ldtS, ldtSB=ldtSB, BS=BS, BSB=BSB, Csrc=Csrc, CsB=CsB, dcol=dcol, dcB=dcB,
                           Bsrc=Bsrc, BsB=BsB)


def ssm_s_alloc(c, stack):
    sb = c.sb
    names = ["lr", "dt", "th", "sn", "cs", "lg", "mag", "ar", "ai", "t1", "t2", "t3", "den", "am1", "cr", "ci", "p0r", "p0i"]
    names += ["p%d%s" % (n, x) for n in range(2, 9) for x in "ri"]
    tl = {nm: sb(stack, "ppS_%s" % nm, [128, 8], F32)[:] for nm in names}
    hp = sb(stack, "ssc_halfpi", [128, 1], F32)
    LV = sb(stack, "ssm_LV", [128, 3, 9, 8], F32)
    return SimpleNamespace(hp=hp, LV=LV, tl=tl)


def ssm_s_chain(c, l, pre, spar):
    P, OP, sb = c.P, c.OP, c.sb
    mul, add, sub = ALU.mult, ALU.add, ALU.subtract
    lamS, lamSB, ldtS, ldtSB = spar.lamS, spar.lamSB, spar.ldtS, spar.ldtSB

    def TT_(eng, out, a, b, op, rd, wr):
        OP(eng, rd, wr, lambda e: e.tensor_tensor(out=out, in0=a, in1=b, op=op))
    hp, LV, pre_tl = pre.hp, pre.LV, pre.tl
    hpB = Buf()
    OP("pool", [], [hpB], lambda e: e.memset(hp[:], float(np.pi / 2)))
    LVB = Buf()
    def cpow_prep(tag, F, lr_in, li, ldt_ap, deps, eng):
        tl = pre_tl

        def t_(nm):
            return tl[nm]
        B_ = Buf()
        D = list(deps) + [B_]
        OP(eng, D, [B_], lambda e: e.tensor_scalar(out=t_("lr"), in0=lr_in, scalar1=-1e-4, scalar2=None, op0=ALU.min))
        OP(eng, D, [B_], lambda e: e.tensor_copy(out=t_("dt"), in_=ldt_ap))
        OP("act", [B_], [B_], lambda e: e.activation(out=t_("dt"), in_=t_("dt"), func=AF.Exp))
        TT_(eng, t_("th"), li, t_("dt"), mul, D, [B_])
        OP("act", [B_], [B_], lambda e: e.activation(out=t_("sn"), in_=t_("th"), func=AF.Sin, scale=1.0 / 32))
        OP("act", [B_, hpB], [B_], lambda e: e.activation(out=t_("cs"), in_=t_("th"), func=AF.Sin, scale=1.0 / 32, bias=hp[:, 0:1]))
        TT_(eng, t_("lg"), t_("lr"), t_("dt"), mul, [B_], [B_])
        OP("act", [B_], [B_], lambda e: e.activation(out=t_("mag"), in_=t_("lg"), func=AF.Exp, scale=1.0 / 32))
        TT_(eng, t_("ar"), t_("mag"), t_("cs"), mul, [B_], [B_])
        TT_(eng, t_("ai"), t_("mag"), t_("sn"), mul, [B_], [B_])
        for _ in range(5):
            TT_(eng, t_("t1"), t_("ar"), t_("ar"), mul, [B_], [B_])
            TT_(eng, t_("t2"), t_("ai"), t_("ai"), mul, [B_], [B_])
            TT_(eng, t_("t3"), t_("ar"), t_("ai"), mul, [B_], [B_])
            TT_(eng, t_("ar"), t_("t1"), t_("t2"), sub, [B_], [B_])
            OP(eng, [B_], [B_], lambda e: e.tensor_scalar(out=t_("ai"), in0=t_("t3"), scalar1=2.0, scalar2=None, op0=mul))
        TT_(eng, t_("t1"), t_("lr"), t_("lr"), mul, [B_], [B_])
        TT_(eng, t_("t2"), li, li, mul, D, [B_])
        TT_(eng, t_("den"), t_("t1"), t_("t2"), add, [B_], [B_])
        OP("dve", [B_], [B_], lambda e: e.reciprocal(out=t_("den"), in_=t_("den")))
        OP(eng, [B_], [B_], lambda e: e.tensor_scalar(out=t_("am1"), in0=t_("ar"), scalar1=-1.0, scalar2=None, op0=add))
        TT_(eng, t_("t1"), t_("am1"), t_("lr"), mul, [B_], [B_])
        TT_(eng, t_("t2"), t_("ai"), li, mul, D, [B_])
        TT_(eng, t_("t3"), t_("t1"), t_("t2"), add, [B_], [B_])
        TT_(eng, t_("cr"), t_("t3"), t_("den"), mul, [B_], [B_])
        TT_(eng, t_("t1"), t_("ai"), t_("lr"), mul, [B_], [B_])
        TT_(eng, t_("t2"), t_("am1"), li, mul, D, [B_])
        TT_(eng, t_("t3"), t_("t1"), t_("t2"), sub, [B_], [B_])
        TT_(eng, t_("ci"), t_("t3"), t_("den"), mul, [B_], [B_])
        pw = []
        OP(eng, [B_], [B_], lambda e: e.memset(t_("p0r"), 1.0))
        OP(eng, [B_], [B_], lambda e: e.memset(t_("p0i"), 0.0))
        pw.append((t_("p0r"), t_("p0i")))
        pw.append((t_("ar"), t_("ai")))
        for n in range(2, 9):
            pr, pi = pw[-1]
            nr, ni = t_("p%dr" % n), t_("p%di" % n)
            cmul(nr, ni, pr, pi, t_("ar"), t_("ai"), t_("t1"), t_("t2"), [B_], [B_], eng)
            pw.append((nr, ni))
        return pw, (t_("cr"), t_("ci")), B_, t_

    def cmul(o_r, o_i, a_r, a_i, b_r, b_i, t1, t2, rd, wr, eng="dve"):
        TT_(eng, t1, a_r, b_r, mul, rd, wr)
        TT_(eng, t2, a_i, b_i, mul, rd, wr)
        TT_(eng, o_r, t1, t2, sub, rd, wr)
        TT_(eng, t1, a_r, b_i, mul, rd, wr)
        TT_(eng, t2, a_i, b_r, mul, rd, wr)
        TT_(eng, o_i, t1, t2, add, rd, wr)

    pwS, (crS, ciS), BS_, tS = cpow_prep("S", 8, lamS[:, 0, :], lamS[:, 1, :], ldtS[:], [lamSB, ldtSB], "dve")
    OP("dve", [BS_], [LVB], lambda e: e.tensor_copy(out=LV[:, 0, 0, :], in_=pwS[8][0]))
    OP("dve", [BS_], [LVB], lambda e: e.tensor_copy(out=LV[:, 1, 0, :], in_=pwS[8][1]))
    for k in range(1, 9):
        TT_("dve", tS("t1"), LV[:, 0, k - 1, :], LV[:, 0, k - 1, :], mul, [LVB, BS_], [BS_])
        TT_("dve", tS("t2"), LV[:, 1, k - 1, :], LV[:, 1, k - 1, :], mul, [LVB, BS_], [BS_])
        TT_("dve", tS("t3"), LV[:, 0, k - 1, :], LV[:, 1, k - 1, :], mul, [LVB, BS_], [BS_])
        TT_("dve", LV[:, 0, k, :], tS("t1"), tS("t2"), sub, [BS_], [LVB])
        OP("dve", [BS_], [LVB], lambda e, k=k: e.tensor_scalar(out=LV[:, 1, k, :], in0=tS("t3"), scalar1=2.0, scalar2=None, op0=mul))
    OP("dve", [LVB], [LVB], lambda e: e.tensor_scalar(out=LV[:, 2, :, :], in0=LV[:, 1, :, :], scalar1=-1.0, scalar2=None, op0=mul))
    return SimpleNamespace(pwS=pwS, crS=crS, ciS=ciS, BS_=BS_, tS=tS, LV=LV, LVB=LVB, cmul=cmul)


def mixer_m2(c, l, u_ssm, uB, ybrv, spar, sch):
    P, OP, sb, nc, next_ps = c.P, c.OP, c.sb, c.nc, c.next_ps
    mul, add, sub = ALU.mult, ALU.add, ALU.subtract
    NCH = T // 8

    def TT_(eng, out, a, b, op, rd, wr):
        OP(eng, rd, wr, lambda e: e.tensor_tensor(out=out, in0=a, in1=b, op=op))

    with contextlib.ExitStack() as st:
        WZ = sb(st, "ssm_WZ", [128, 8, 8, 2, 128], BF16)
        CA = sb(st, "ssm_CA", [128, 8, 9, 2, 128], BF16)
        BD = sb(st, "ssm_BD", [128, 2, 8, 128], BF16)
        LV, LVB = sch.LV, sch.LVB
        WZB, CAB, BDB = [Buf(), Buf()], Buf(), Buf()
        with contextlib.nullcontext():
            sp = st

            def mk_(name, shape, dt=F32):
                return sb(sp, "sp_" + name, shape, dt), Buf()
            mk, mkB = mk_("mk", [128, 8, 128])
            P.dma(lambda e: e.dma_start(out=mk[:], in_=c.consts[:, C_MH:C_MH + 1024].rearrange("p (a b) -> p a b", a=8)), writes=[mkB])
            msall, msB = mk_("msall", [128, 8, 128])
            for j in range(8):
                OP("pool", [mkB], [msB], lambda e, j=j: e.tensor_copy(out=msall[:, j, :], in_=mk[:, 4 + j % 4, :]))
            lamS, lamSB, ldtS, ldtSB, BS, BSB, Csrc, CsB = spar.lamS, spar.lamSB, spar.ldtS, spar.ldtSB, spar.BS, spar.BSB, spar.Csrc, spar.CsB
            dcol, dcB, Bsrc, BsB = spar.dcol, spar.dcB, spar.Bsrc, spar.BsB
            CT, CTB = mk_("CT", [128, 2, 128])
            ps, pB = next_ps()
            for ri in range(2):
                OP("pe", [CsB, c.Bc], [pB], lambda e, ps=ps, ri=ri: e.transpose(out=ps[:, ri * 128:(ri + 1) * 128], in_=Csrc[:, ri, :], identity=c.ident_f))
            OP("act", [pB], [CTB], lambda e, ps=ps: e.activation(out=CT[:], in_=ps[:, 0:256].rearrange("p (a b) -> p a b", a=2), func=AF.Copy))

            pwS, crS, ciS, BS_, tS, cmul = sch.pwS, sch.crS, sch.ciS, sch.BS_, sch.tS, sch.cmul
            cw1, cw1B = mk_("cw1", [128, 8, 16])
            cw2, cw2B = mk_("cw2", [128, 8, 16])
            cw3, cw3B = mk_("cw3", [128, 8, 16])
            CTr = CT[:, 0, :].rearrange("p (j h) -> p j h", j=8)
            CTi = CT[:, 1, :].rearrange("p (j h) -> p j h", j=8)
            bc16 = lambda ap: ap.unsqueeze(2).to_broadcast([128, 8, 16])
            msv = msall[:].rearrange("p j (a h) -> p j a h", a=8)
            for n in range(9):
                pr, pi = pwS[n]
                TT_("dve", cw1[:], CTr, bc16(pr), mul, [CTB, BS_, cw1B], [cw1B])
                TT_("dve", cw2[:], CTi, bc16(pi), mul, [CTB, BS_, cw2B], [cw2B])
                TT_("dve", cw3[:], cw1[:], cw2[:], sub, [cw1B, cw2B, cw3B], [cw3B])
                OP("dve", [cw3B, msB, CAB], [CAB], lambda e, n=n: e.tensor_tensor(out=CA[:, :, n, 0, :].rearrange("p j (a h) -> p j a h", a=8), in0=cw3[:].unsqueeze(2).to_broadcast([128, 8, 8, 16]), in1=msv, op=mul))
                TT_("dve", cw1[:], CTr, bc16(pi), mul, [CTB, BS_, cw1B], [cw1B])
                TT_("dve", cw2[:], CTi, bc16(pr), mul, [CTB, BS_, cw2B], [cw2B])
                OP("dve", [cw1B, cw2B, cw3B], [cw3B], lambda e: e.scalar_tensor_tensor(out=cw3[:], in0=cw1[:], scalar=-1.0, in1=cw2[:], op0=mul, op1=sub))
                OP("dve", [cw3B, msB, CAB], [CAB], lambda e, n=n: e.tensor_tensor(out=CA[:, :, n, 1, :].rearrange("p j (a h) -> p j a h", a=8), in0=cw3[:].unsqueeze(2).to_broadcast([128, 8, 8, 16]), in1=msv, op=mul))
            cB, cBB = mk_("cB", [128, 8, 2, 32], BF16)
            m2 = mk[:, 4, 0:32].rearrange("p (a h) -> p a h", a=2).unsqueeze(1).to_broadcast([128, 8, 2, 16])
            BSr, BSi = BS[:, 0, :, :], BS[:, 1, :, :]
            TT_("dve", cw1[:], BSr, bc16(crS), mul, [BSB, BS_, cw1B], [cw1B])
            TT_("dve", cw2[:], BSi, bc16(ciS), mul, [BSB, BS_, cw2B], [cw2B])
            TT_("dve", cw3[:], cw1[:], cw2[:], sub, [cw1B, cw2B, cw3B], [cw3B])
            OP("dve", [cw3B, mkB], [cBB], lambda e: e.tensor_tensor(out=cB[:, :, 0, :].rearrange("p j (a h) -> p j a h", a=2), in0=cw3[:].unsqueeze(2).to_broadcast([128, 8, 2, 16]), in1=m2, op=mul))
            TT_("dve", cw1[:], BSr, bc16(ciS), mul, [BSB, BS_, cw1B], [cw1B])
            TT_("dve", cw2[:], BSi, bc16(crS), mul, [BSB, BS_, cw2B], [cw2B])
            TT_("dve", cw3[:], cw1[:], cw2[:], add, [cw1B, cw2B, cw3B], [cw3B])
            OP("dve", [cw3B, mkB], [cBB], lambda e: e.tensor_tensor(out=cB[:, :, 1, :].rearrange("p j (a h) -> p j a h", a=2), in0=cw3[:].unsqueeze(2).to_broadcast([128, 8, 2, 16]), in1=m2, op=mul))
            for hf in range(2):
                for tau in range(8):
                    ps, pB = next_ps()
                    for j4 in range(4):
                        j = 4 * hf + j4
                        OP("pe", [cBB, CAB], [pB], lambda e, ps=ps, j=j, j4=j4, tau=tau: e.matmul(ps[32 * j4:32 * j4 + 32, 0:128], lhsT=cB[:, j, 0, :], rhs=CA[:, j, tau, 0, :], start=True, stop=False, tile_position=(0, 32 * j4)))
                        OP("pe", [cBB, CAB], [pB], lambda e, ps=ps, j=j, j4=j4, tau=tau: e.matmul(ps[32 * j4:32 * j4 + 32, 0:128], lhsT=cB[:, j, 1, :], rhs=CA[:, j, tau, 1, :], start=False, stop=True, tile_position=(0, 32 * j4)))
                    if tau == 0:
                        OP("dve", [pB, dcB, c.Bc], [BDB], lambda e, ps=ps, hf=hf: e.scalar_tensor_tensor(out=BD[:, hf, 0, :], in0=c.ident_f, scalar=dcol[:, hf:hf + 1], in1=ps[:, 0:128], op0=mul, op1=add))
                    else:
                        OP("act", [pB], [BDB], lambda e, ps=ps, hf=hf, tau=tau: e.activation(out=BD[:, hf, tau, :], in_=ps[:, 0:128], func=AF.Copy))
            Gt = [mk_("Gt%d" % i, [128, 2, 8]) for i in range(2)]
            Ws = [mk_("Ws%d" % i, [128, 2, 2, 128]) for i in range(2)]
            wt1, wt1B = mk_("wt1", [128, 2, 128])
            wt2, wt2B = mk_("wt2", [128, 2, 128])
            v4 = lambda ap: ap.rearrange("p a (b c) -> p a b c", b=4)
            Brv = Bsrc[:, 0].rearrange("p a b c d -> p a b (c d)")
            Biv = Bsrc[:, 1].rearrange("p a b c d -> p a b (c d)")
            gb = lambda ap: ap.rearrange("p (a b) -> p a b", a=2).unsqueeze(3).to_broadcast([128, 2, 4, 32])
            mh = mk[:, 0:4, :]
            for tau in range(8):
                G_, GB_ = Gt[tau % 2]
                if tau == 0:
                    OP("dve", [BS_, GB_], [GB_], lambda e, G_=G_: e.tensor_copy(out=G_[:, 0, :], in_=crS))
                    OP("dve", [BS_, GB_], [GB_], lambda e, G_=G_: e.tensor_copy(out=G_[:, 1, :], in_=ciS))
                else:
                    Gp, GpB = Gt[(tau - 1) % 2]
                    cmul(G_[:, 0, :], G_[:, 1, :], Gp[:, 0, :], Gp[:, 1, :], pwS[1][0], pwS[1][1], tS("t1"), tS("t2"), [BS_, GpB, GB_], [BS_, GB_])
                W_, WB_ = Ws[tau % 2]
                TT_("dve", v4(wt1[:]), Brv, gb(G_[:, 0, :]), mul, [BsB, GB_, wt1B], [wt1B])
                TT_("dve", v4(wt2[:]), Biv, gb(G_[:, 1, :]), mul, [BsB, GB_, wt2B], [wt2B])
                TT_("dve", W_[:, 0, :, :], wt1[:], wt2[:], sub, [wt1B, wt2B, WB_], [WB_])
                TT_("dve", v4(wt1[:]), Biv, gb(G_[:, 0, :]), mul, [BsB, GB_, wt1B], [wt1B])
                TT_("dve", v4(wt2[:]), Brv, gb(G_[:, 1, :]), mul, [BsB, GB_, wt2B], [wt2B])
                TT_("dve", W_[:, 1, :, :], wt1[:], wt2[:], add, [wt1B, wt2B, WB_], [WB_])
                ps, pB = next_ps()
                for ri in range(2):
                    for hf in range(2):
                        k4 = ri * 2 + hf
                        OP("pe", [WB_, c.Bc], [pB], lambda e, ps=ps, ri=ri, hf=hf, k4=k4, W_=W_: e.transpose(out=ps[:, k4 * 128:(k4 + 1) * 128], in_=W_[:, ri, hf, :], identity=c.ident_f))
                for ri in range(2):
                    for hf in range(2):
                        k4 = ri * 2 + hf
                        OP("dve", [pB, mkB, WZB[hf]], [WZB[hf]], lambda e, ps=ps, ri=ri, hf=hf, k4=k4, tau=tau: e.tensor_tensor(out=WZ[:, 4 * hf:4 * hf + 4, tau, ri, :], in0=ps[:, k4 * 128:(k4 + 1) * 128].unsqueeze(1).to_broadcast([128, 4, 128]), in1=mh, op=mul))
        with contextlib.nullcontext():
            sr = st
            Xb = [[[sb(sr, "X%d%d%d" % (s_, ri, ab), [128, 256 + NCH], F32) for ab in range(2)] for ri in range(2)] for s_ in range(2)]
            XB = [[[Buf() for ab in range(2)] for ri in range(2)] for s_ in range(2)]
            for s_ in range(2):
                for ri in range(2):
                    for ab in range(2):
                        OP("pool", [], [XB[s_][ri][ab]], lambda e, t=Xb[s_][ri][ab]: e.memset(t[:, 0:256], 0.0))
            Xp = sb(sr, "Xp", [128, 8, 2, NCH], BF16)
            XpB = [Buf() for _ in range(8)]
            ysT = sb(sr, "ysT", [128, 2, T], BF16)
            ysB = [Buf() for _ in range(2)]
            wgl = sb(sr, "wgl", [128, 2, 512], BF16)
            wglB = [Buf()]
            c.wload(wgl[:], c.w_glu[l].rearrange("(kc p) n -> p kc n", p=128), wglB)
            sg = [sb(sr, "s_sg%d" % i, [128, TT], F32) for i in range(2)]
            sgB = [Buf() for _ in range(2)]
            yst = [sb(sr, "s_yst%d" % i, [128, 2, TT], BF16) for i in range(2)]
            ystB = [Buf() for _ in range(2)]
            for j in range(8):
                hf = j // 4
                s_ = j % 2
                for ri in range(2):
                    ps, pB = next_ps()
                    for tau in range(8):
                        OP("pe", [WZB[hf]] + uB[hf], [pB], lambda e, ps=ps, j=j, tau=tau, ri=ri, hf=hf: e.matmul(ps[:, 0:NCH], lhsT=WZ[:, j, tau, ri, :], rhs=u_ssm[:, hf, 15 - tau:15 - tau + (NCH - 1) * 8 + 1:8], start=(tau == 0), stop=(tau == 7)))
                    OP("act", [pB], [XB[s_][ri][0]], lambda e, ps=ps, t=Xb[s_][ri][0]: e.activation(out=t[:, 256:256 + NCH], in_=ps[:, 0:NCH], func=AF.Copy))
                cur = 0
                for k in range(9):
                    sh = 1 << k
                    sr_, si_ = Xb[s_][0][cur], Xb[s_][1][cur]
                    dr_, di_ = Xb[s_][0][1 - cur], Xb[s_][1][1 - cur]
                    sBr, sBi = XB[s_][0][cur], XB[s_][1][cur]
                    dBr, dBi = XB[s_][0][1 - cur], XB[s_][1][1 - cur]
                    lo, hi = 256 - sh, 256 + NCH - sh
                    OP("dve", [sBr, LVB], [dBr], lambda e, sr_=sr_, dr_=dr_, lo=lo, hi=hi, k=k, j=j: e.scalar_tensor_tensor(out=dr_[:, 256:256 + NCH], in0=sr_[:, lo:hi], scalar=LV[:, 0, k, j:j + 1], in1=sr_[:, 256:256 + NCH], op0=mul, op1=add))
                    OP("dve", [sBi, LVB, dBr], [dBr], lambda e, si_=si_, dr_=dr_, lo=lo, hi=hi, k=k, j=j: e.scalar_tensor_tensor(out=dr_[:, 256:256 + NCH], in0=si_[:, lo:hi], scalar=LV[:, 2, k, j:j + 1], in1=dr_[:, 256:256 + NCH], op0=mul, op1=add))
                    OP("dve", [sBr, sBi, LVB], [dBi], lambda e, sr_=sr_, si_=si_, di_=di_, lo=lo, hi=hi, k=k, j=j: e.scalar_tensor_tensor(out=di_[:, 256:256 + NCH], in0=sr_[:, lo:hi], scalar=LV[:, 1, k, j:j + 1], in1=si_[:, 256:256 + NCH], op0=mul, op1=add))
                    OP("dve", [sBi, LVB, dBi], [dBi], lambda e, si_=si_, di_=di_, lo=lo, hi=hi, k=k, j=j: e.scalar_tensor_tensor(out=di_[:, 256:256 + NCH], in0=si_[:, lo:hi], scalar=LV[:, 0, k, j:j + 1], in1=di_[:, 256:256 + NCH], op0=mul, op1=add))
                    cur = 1 - cur
                for ri in range(2):
                    OP("act", [XB[s_][ri][cur]], [XpB[j]], lambda e, t=Xb[s_][ri][cur], j=j, ri=ri: e.activation(out=Xp[:, j, ri, :], in_=t[:, 255:255 + NCH], func=AF.Copy))
            kk = 0
            for hf in range(2):
                for s in range(8):
                    ps, pB = next_ps()
                    mms = []
                    for j4 in range(4):
                        j = 4 * hf + j4
                        for ri in range(2):
                            mms.append(([CAB, XpB[j]], CA[:, j, s + 1, ri, :], Xp[:, j, ri, :]))
                    for tau in range(s + 1):
                        mms.append(([BDB] + uB[hf], BD[:, hf, tau, :], u_ssm[:, hf, 8 + s - tau:8 + s - tau + (NCH - 1) * 8 + 1:8]))
                    for i, (rd, lh, rh) in enumerate(mms):
                        OP("pe", rd, [pB], lambda e, ps=ps, lh=lh, rh=rh, i=i, n=len(mms): e.matmul(ps[:, 0:NCH], lhsT=lh, rhs=rh, start=(i == 0), stop=(i == n - 1)))
                    if kk % 2 == 0:
                        OP("act", [pB], [ysB[hf]], lambda e, ps=ps, hf=hf, s=s: e.activation(out=ysT[:, hf, s:T:8], in_=ps[:, 0:NCH], func=AF.Copy))
                    else:
                        OP("dve", [pB], [ysB[hf]], lambda e, ps=ps, hf=hf, s=s: e.tensor_copy(out=ysT[:, hf, s:T:8], in_=ps[:, 0:NCH]))
                    kk += 1
            for q in range(NQ):
                qs = slice(q * TT, (q + 1) * TT)
                Y_, YB_ = yst[q % 2], ystB[q % 2]
                for cc in range(2):
                    pv, pvB = next_ps()
                    pg, pgB = next_ps()
                    for kc in range(2):
                        OP("pe", [wglB[0], ysB[kc]], [pvB], lambda e, pv=pv, kc=kc, cc=cc, qs=qs: e.matmul(pv[:, 0:TT], lhsT=wgl[:, kc, cc * 128:(cc + 1) * 128], rhs=ysT[:, kc, qs], start=(kc == 0), stop=(kc == 1)))
                    for kc in range(2):
                        OP("pe", [wglB[0], ysB[kc]], [pgB], lambda e, pg=pg, kc=kc, cc=cc, qs=qs: e.matmul(pg[:, 0:TT], lhsT=wgl[:, kc, 256 + cc * 128:256 + (cc + 1) * 128], rhs=ysT[:, kc, qs], start=(kc == 0), stop=(kc == 1)))
                    s2, s2B = sg[cc], sgB[cc]
                    OP("act", [pgB], [s2B], lambda e, pg=pg, s2=s2: e.activation(out=s2[:], in_=pg[:, 0:TT], func=AF.Sigmoid))
                    OP("dve", [pvB, s2B], [YB_], lambda e, pv=pv, s2=s2, Y_=Y_, cc=cc: e.tensor_tensor(out=Y_[:, cc, :], in0=pv[:, 0:TT], in1=s2[:], op=mul))
                P.dma(lambda e, qs=qs, Y_=Y_: e.dma_start(out=ybrv[2][:, :, qs], in_=Y_[:]), reads=[YB_], writes=[c.YB[2][q]])


def mixer_m4(c, l, parts, ybrv):
    P, OP, sb, nc, next_ps = c.P, c.OP, c.sb, c.nc, c.next_ps
    order = [b for b, nm in enumerate(("att", "pool", "ssm", "conv")) if nm in parts]
    with contextlib.ExitStack() as st:
        wgt = sb(st, "m4_wg", [128, 8, 4096], BF16)
        wgtB = [[Buf(), Buf()] for _ in range(4)]
        wiv = c.w_in[l].rearrange("(kc p) n -> p kc n", p=128)
        wbr = sb(st, "m4_wbr", [128, 4, 2, D], BF16)
        wbrB = [Buf() for _ in range(4)]
        for b in order:
            c.wload(wgt[:, :, b * 1024:b * 1024 + 256], wiv[:, :, 2052 + b * 1024:2052 + b * 1024 + 256], [wgtB[b][0]])
            c.wload(wbr[:, b, :, :], c.w_branch[l, b].rearrange("(kc p) n -> p kc n", p=128), [wbrB[b]])
        for b in order:
            c.wload(wgt[:, :, b * 1024 + 256:(b + 1) * 1024], wiv[:, :, 2052 + b * 1024 + 256:2052 + (b + 1) * 1024], [wgtB[b][1]])
        wo = sb(st, "m4_wo", [128, 8, D], BF16)
        woB = [Buf()]
        c.wload(wo[:], c.w_out[l].rearrange("(kc p) n -> p kc n", p=128), woB)
        xn = [sb(st, "m4_xn%d" % i, [128, 8, TT], BF16) for i in range(2)]
        xnB = [Buf() for _ in range(2)]
        ysb = [sb(st, "m4_ys%d" % i, [128, 4, 2, TT], BF16) for i in range(2)]
        ysB = [[Buf() for _ in range(4)] for _ in range(2)]
        m = sb(st, "m4_m", [128, 8, TT], F32)
        mB = [Buf() for _ in range(8)]
        mb = sb(st, "m4_mb", [128, 8, TT], BF16)
        mbB = [Buf() for _ in range(8)]
        sg = [sb(st, "m4_sg%d" % i, [128, TT], F32) for i in range(2)]
        sgB = [Buf() for _ in range(2)]
        tm = [sb(st, "m4_tm%d" % i, [128, TT], F32) for i in range(2)]
        tmB = [Buf() for _ in range(2)]
        hr = [sb(st, "m4_hr%d" % i, [128, TT], F32) for i in range(3)]
        hrB = [Buf() for _ in range(3)]
        k = 0
        kk = 0
        def m4_load(q):
            qs = slice(q * TT, (q + 1) * TT)
            X, XB = xn[q % 2], xnB[q % 2]
            P.dma(lambda e, qs=qs, X=X: e.dma_start(out=X[:], in_=c.xnTv[:, :, qs]), reads=[c.XN[q]], writes=[XB])
            Ys, YsB = ysb[q % 2], ysB[q % 2]
            for b in order:
                P.dma(lambda e, qs=qs, b=b, Ys=Ys: e.dma_start(out=Ys[:, b, :, :], in_=ybrv[b][:, :, qs]), reads=[c.YB[b][q]], writes=[YsB[b]])

        m4_load(0)
        for q in range(c.nq):
            qs = slice(q * TT, (q + 1) * TT)
            X, XB = xn[q % 2], xnB[q % 2]
            Ys, YsB = ysb[q % 2], ysB[q % 2]
            if q + 1 < c.nq:
                m4_load(q + 1)
            for dc in range(8):
                ds = slice(dc * 128, (dc + 1) * 128)
                for bi, b in enumerate(order):
                    pg, pgB = next_ps()
                    for kc in range(8):
                        OP("pe", [wgtB[b][0 if dc < 2 else 1], XB], [pgB], lambda e, pg=pg, kc=kc, b=b, dc=dc, X=X: e.matmul(pg[:, 0:TT], lhsT=wgt[:, kc, b * 1024 + dc * 128:b * 1024 + (dc + 1) * 128], rhs=X[:, kc, :], start=(kc == 0), stop=(kc == 7)))
                    pp, ppB = next_ps()
                    for kc in range(2):
                        OP("pe", [wbrB[b], YsB[b]], [ppB], lambda e, pp=pp, kc=kc, b=b, ds=ds, Ys=Ys: e.matmul(pp[:, 0:TT], lhsT=wbr[:, b, kc, ds], rhs=Ys[:, b, kc, :], start=(kc == 0), stop=(kc == 1)))
                    s_, sB_ = sg[kk % 2], sgB[kk % 2]
                    OP("act", [pgB], [sB_], lambda e, s_=s_, pg=pg: e.activation(out=s_[:], in_=pg[:, 0:TT], func=AF.Sigmoid))
                    last = (bi == len(order) - 1)
                    if bi == 0 and last:
                        OP("dve", [sB_, ppB], [mbB[dc]], lambda e, s_=s_, pp=pp, dc=dc: e.tensor_tensor(out=mb[:, dc, :], in0=pp[:, 0:TT], in1=s_[:], op=ALU.mult))
                    elif bi == 0:
                        OP("dve", [sB_, ppB], [mB[dc]], lambda e, s_=s_, pp=pp, dc=dc: e.tensor_tensor(out=m[:, dc, :], in0=pp[:, 0:TT], in1=s_[:], op=ALU.mult))
                    else:
                        t_, tB_ = tm[kk % 2], tmB[kk % 2]
                        OP("dve", [sB_, ppB], [tB_], lambda e, s_=s_, pp=pp, t_=t_: e.tensor_tensor(out=t_[:], in0=pp[:, 0:TT], in1=s_[:], op=ALU.mult))
                        if last:
                            OP("pool", [tB_, mB[dc]], [mbB[dc]], lambda e, t_=t_, dc=dc: e.tensor_tensor(out=mb[:, dc, :], in0=m[:, dc, :], in1=t_[:], op=ALU.add))
                        else:
                            OP("pool", [tB_, mB[dc]], [mB[dc]], lambda e, t_=t_, dc=dc: e.tensor_tensor(out=m[:, dc, :], in0=m[:, dc, :], in1=t_[:], op=ALU.add))
                    kk += 1
            for d2 in range(8):
                po, poB = next_ps()
                for dc in range(8):
                    OP("pe", [woB[0], mbB[dc]], [poB], lambda e, po=po, dc=dc, d2=d2: e.matmul(po[:, 0:TT], lhsT=wo[:, dc, d2 * 128:(d2 + 1) * 128], rhs=mb[:, dc, :], start=(dc == 0), stop=(dc == 7)))
                r, rB = hr[k % 3], hrB[k % 3]
                k += 1
                P.dma(lambda e, r=r, d2=d2, qs=qs: e.dma_start(out=r[:], in_=c.hTv[:, d2, qs]), reads=[c.HT[q][d2]], writes=[rB])
                OP("dve", [poB, rB], [rB], lambda e, r=r, po=po: e.tensor_tensor(out=r[:], in0=po[:, 0:TT], in1=r[:], op=ALU.add))
                P.dma(lambda e, r=r, d2=d2, qs=qs: e.dma_start(out=c.hTv[:, d2, qs], in_=r[:]), reads=[rB], writes=[c.HT[q][d2]])


_NAMES = ["norm_g", "ffn_w_gate", "ffn_w_up", "ffn_w_down", "w_in", "f_bias", "pool_w", "pool_scale", "ssm_lam_re",
          "ssm_lam_im", "ssm_log_dt", "ssm_b_re", "ssm_b_im", "ssm_c_re", "ssm_c_im", "ssm_d", "ssm_w_glu", "conv_w",
          "w_branch", "w_out", "ple_w_gate", "ple_w_proj", "final_g"]


def run(inputs, n_cores=8, ret_all=False, **bk):
    nc = build(**bk)
    cst = make_consts()
    shared = {k: np.ascontiguousarray(np.asarray(inputs[k], dtype=np.float32)) for k in _NAMES}
    xs = np.asarray(inputs["x"], dtype=np.float32)
    ps = np.asarray(inputs["p"], dtype=np.float32)
    in_maps = []
    for b in range(n_cores):
        m = dict(shared)
        m["x"] = np.ascontiguousarray(xs[b])
        m["p"] = np.ascontiguousarray(ps[:, b])
        m["consts"] = cst
        in_maps.append(m)
    res = run_bass_kernel_spmd(nc, in_maps, core_ids=list(range(n_cores)))
    if ret_all:
        return res.results
    return np.stack([np.asarray(r["y"]) for r in res.results], axis=0)


def kernel(**inputs):
    return run(inputs, n_cores=8).astype(np.float32)
```

```python
import contextlib
import numpy as np
import concourse.bass as bass
import concourse.mybir as mybir
from concourse.bass_utils import run_bass_kernel_spmd

F32 = mybir.dt.float32
BF16 = mybir.dt.bfloat16
AF = mybir.ActivationFunctionType
ALU = mybir.AluOpType

D = 1024
T = 4096
DEPTH = 2
DFF = 2816
NFC = DFF // 128
INC = 6148
PLE = 256
TT = 512
NQ = T // TT
EPS = 1e-6

COMPUTE = ("pe", "act", "dve", "pool")
NDMASEM = 56
NSP = 40


class Buf:
    __slots__ = ("name", "w", "r")

    def __init__(self, name=""):
        self.name = name
        self.w = None
        self.r = []


class Node:
    __slots__ = ("eng", "idx", "fn", "waits", "key", "val", "needs_inc", "clock", "is_dma")


class Prog:
    def __init__(self, nc):
        self.nc = nc
        self.ops = {e: [] for e in ("pe", "act", "dve", "pool", "sp")}
        self.clock = {e: {} for e in self.ops}
        self.dma_rr = 0
        self.dma_rr2 = 0
        self.dma_last = [None] * NDMASEM
        self.dma_cum = [0] * NDMASEM
        self.out_nodes = []

    def _record(self, eng, fn, reads, writes, is_dma, extra=()):
        n = Node()
        n.eng = eng
        n.idx = len(self.ops[eng])
        n.fn = fn
        n.is_dma = is_dma
        n.needs_inc = False
        deps = []
        for b in reads:
            if b.w is not None:
                deps.append(b.w)
        for b in writes:
            if b.w is not None:
                deps.append(b.w)
            deps.extend(b.r)
        deps.extend(extra)
        if is_dma:
            if eng == "sp":
                s = self.dma_rr
                self.dma_rr = (self.dma_rr + 1) % NSP
            else:
                s = NSP + self.dma_rr2
                self.dma_rr2 = (self.dma_rr2 + 1) % (NDMASEM - NSP)
            if self.dma_last[s] is not None:
                deps.append(self.dma_last[s])
            self.dma_cum[s] += 16
            n.key = ("d", s)
            n.val = self.dma_cum[s]
            self.dma_last[s] = n
        else:
            n.key = eng
            n.val = n.idx + 1
        ck = self.clock[eng]
        waits = {}
        for d in deps:
            if (not d.is_dma) and d.eng == eng and eng == "pe":
                continue
            if ck.get(d.key, 0) >= d.val:
                continue
            if waits.get(d.key, (0, None))[0] < d.val:
                waits[d.key] = (d.val, d)
        n.waits = [w[1] for w in waits.values()]
        if n.waits:
            ck = dict(ck)
            for d in n.waits:
                d.needs_inc = True
                for k, v in d.clock.items():
                    if ck.get(k, 0) < v:
                        ck[k] = v
                if ck.get(d.key, 0) < d.val:
                    ck[d.key] = d.val
            self.clock[eng] = ck
        n.clock = ck
        self.ops[eng].append(n)
        for b in reads:
            b.r.append(n)
        for b in writes:
            b.w = n
            b.r = []
        return n

    def op(self, eng, fn, reads=(), writes=()):
        return self._record(eng, fn, reads, writes, False)

    def dma(self, fn, reads=(), writes=(), q="sp", is_out=False):
        n = self._record(q, fn, reads, writes, True)
        if is_out:
            self.out_nodes.append(n)
        return n

    def barrier(self):
        last = [self.ops[e][-1] for e in COMPUTE if self.ops[e]]
        for e in COMPUTE:
            for n in reversed(self.ops[e]):
                if not n.is_dma:
                    last.append(n)
                    break
        last += [d for d in self.dma_last if d is not None]
        for e in ("pe", "act", "dve", "pool", "sp"):
            self._record(e, lambda eng: eng.nop(), (), (), False, extra=last)

    def emit(self, es):
        nc = self.nc
        sems = {}
        for e in COMPUTE:
            sems[e] = es.enter_context(nc.semaphore("S_" + e))
        for i in range(NDMASEM):
            sems[("d", i)] = es.enter_context(nc.semaphore("D%d" % i))
        for e in COMPUTE:
            c = 0
            for n in self.ops[e]:
                if n.is_dma:
                    continue
                if n.needs_inc:
                    c += 1
                    n.val = c
                else:
                    n.val = None
        block = es.enter_context(nc.Block())

        def run(ename):
            def body(eng):
                for n in self.ops[ename]:
                    for d in n.waits:
                        eng.wait_ge(sems[d.key], d.val)
                    ins = n.fn(eng)
                    if n.is_dma:
                        ins.then_inc(sems[n.key], 16)
                    elif n.needs_inc:
                        ins.then_inc(sems[n.key], 1)
                if ename == "sp":
                    for d in self.dma_last:
                        if d is not None:
                            eng.wait_ge(sems[d.key], d.val)
            return body

        block.tensor(run("pe"))
        block.scalar(run("act"))
        block.vector(run("dve"))
        block.gpsimd(run("pool"))
        block.sync(run("sp"))


C_ID = 0
C_ONES = 128
C_TRIU = 256
C_MNEG = 384
C_MH = 512
C_MS = 1024
C_RCW = 1536
C_RC0 = 1538
C_PM = C_RC0 + 1024
C_N = C_PM + 12 * 128
C_G = 512


def make_consts():
    c = np.zeros((128, C_N), np.float32)
    k = np.arange(128)
    c[:, C_ID:C_ID + 128] = np.eye(128, dtype=np.float32)
    c[:, C_ONES:C_ONES + 128] = 1.0
    c[:, C_TRIU:C_TRIU + 128] = (k[:, None] <= k[None, :]).astype(np.float32)
    c[:, C_MNEG:C_MNEG + 128] = np.where(k[None, :] < k[:, None], -30000.0, 0.0)
    rj4, rg2 = k // 32, (k // 16) % 2
    cg2 = k // 64
    for j4 in range(4):
        c[:, C_MH + j4 * 128:C_MH + (j4 + 1) * 128] = ((rj4[:, None] == j4) & (rg2[:, None] == cg2[None, :])).astype(np.float32)
        c[:, C_MS + j4 * 128:C_MS + (j4 + 1) * 128] = ((cg2[:, None] == rg2[None, :]) & (rj4[None, :] == j4)).astype(np.float32)
    wins = np.array([2, 4, 8, 16], np.float32)
    for ch in range(2):
        w = wins[2 * ch + (k // 64)]
        c[:, C_RCW + ch] = 1.0 / w
        t = np.arange(512, dtype=np.float32)
        c[:, C_RC0 + ch * 512:C_RC0 + (ch + 1) * 512] = 1.0 / np.minimum(t[None, :] + 1.0, w[:, None])
    tp = k[:, None].astype(np.float64)
    tq = k[None, :].astype(np.float64)
    for g in range(4):
        W = float(wins[g])
        main = np.where((tp <= tq) & (tp > tq - W), 1.0 / W, 0.0) - np.eye(128)
        corner = np.where((tp - 128 > tq - W), 1.0 / W, 0.0)
        cnt = np.minimum(tq + 1.0, W)
        main0 = np.where((tp <= tq) & (tp > tq - W), 1.0 / cnt, 0.0) - np.eye(128)
        for i, mtx in enumerate((main, corner, main0)):
            o = C_PM + (g * 3 + i) * 128
            c[:, o:o + 128] = mtx.astype(np.float32)
    return c


def build(phases=("in", "ffn", "mix", "ple", "out"), depth=DEPTH, mix_parts=("att", "pool", "ssm", "conv"), debug=False, nq=NQ):
    nc = bass.Bass("TRN2", target_bir_lowering=False)
    dt_in = lambda name, shape: nc.dram_tensor(name, list(shape), F32, kind="ExternalInput").ap()
    x = dt_in("x", [T, D])
    p_in = dt_in("p", [DEPTH, T, PLE])
    norm_g = dt_in("norm_g", [DEPTH, 4, D])
    w_gate = dt_in("ffn_w_gate", [DEPTH, 2, D, DFF])
    w_up = dt_in("ffn_w_up", [DEPTH, 2, D, DFF])
    w_down = dt_in("ffn_w_down", [DEPTH, 2, DFF, D])
    w_in = dt_in("w_in", [DEPTH, D, INC])
    f_bias = dt_in("f_bias", [DEPTH, 4])
    pool_w = dt_in("pool_w", [DEPTH, 4, 64, 64])
    pool_scale = dt_in("pool_scale", [DEPTH, 256])
    lam_re = dt_in("ssm_lam_re", [DEPTH, 16, 64])
    lam_im = dt_in("ssm_lam_im", [DEPTH, 16, 64])
    log_dt = dt_in("ssm_log_dt", [DEPTH, 16])
    b_re = dt_in("ssm_b_re", [DEPTH, 16, 64, 16])
    b_im = dt_in("ssm_b_im", [DEPTH, 16, 64, 16])
    c_re = dt_in("ssm_c_re", [DEPTH, 16, 16, 64])
    c_im = dt_in("ssm_c_im", [DEPTH, 16, 16, 64])
    ssm_d = dt_in("ssm_d", [DEPTH, 256])
    w_glu = dt_in("ssm_w_glu", [DEPTH, 256, 512])
    conv_w = dt_in("conv_w", [DEPTH, 3, 256])
    w_branch = dt_in("w_branch", [DEPTH, 4, 256, D])
    w_out = dt_in("w_out", [DEPTH, D, D])
    ple_wg = dt_in("ple_w_gate", [DEPTH, D, D])
    ple_wp = dt_in("ple_w_proj", [DEPTH, PLE, D])
    final_g = dt_in("final_g", [D])
    consts = dt_in("consts", [128, C_N])
    y_out = nc.dram_tensor("y", [T, D], F32, kind="ExternalOutput").ap()

    dk = dict(kind="ExternalOutput") if debug else {}
    hT = nc.dram_tensor("hT_scr", [D, T], F32, **dk).ap()
    xnT = nc.dram_tensor("xnT_scr", [D, T], BF16, **dk).ap()
    ybr = nc.dram_tensor("ybr_scr", [4, 256, T], BF16, **dk).ap()
    qaug = nc.dram_tensor("qaug_scr", [128, 4, T], BF16, **dk).ap()
    vtok = nc.dram_tensor("vtok_scr", [T, 256], BF16, **dk).ap()
    hTv = hT.rearrange("(c p) t -> p c t", p=128)
    xnTv = xnT.rearrange("(c p) t -> p c t", p=128)

    es = contextlib.ExitStack()
    P = Prog(nc)
    OP = lambda eng, reads, writes, fn: P.op(eng, fn, reads, writes)

    uid = [0]

    def sb(stack, name, shape, dt):
        uid[0] += 1
        return stack.enter_context(nc.sbuf_tensor("%s_u%d" % (name, uid[0]), list(shape), dt))

    def dbg_dump(name, ap, shape, dt, reads):
        if not debug:
            return
        t = nc.dram_tensor("dbg_" + name, list(shape), dt, kind="ExternalOutput").ap()
        P.dma(lambda e: e.dma_start(out=t, in_=ap), reads=reads, writes=[Buf()])

    HT = [[Buf("hT%d_%d" % (q, c)) for c in range(8)] for q in range(NQ)]
    XN = [Buf("xnT%d" % q) for q in range(NQ)]
    YB = [[Buf("ybr%d_%d" % (b, q)) for q in range(NQ)] for b in range(4)]
    OUTB = Buf("out")

    psb = [es.enter_context(nc.psum_tensor("psb%d" % i, [128, 512], F32)) for i in range(8)]
    psB = [Buf("psb%d" % i) for i in range(8)]
    ps_rr = [0]

    def next_ps():
        i = ps_rr[0]
        ps_rr[0] = (i + 1) % 6
        return psb[i], psB[i]

    cst = sb(es, "cst", [128, 512], F32)
    cstb = sb(es, "cstb", [128, 512], BF16)
    Bc = Buf("cst")
    Bcb = Buf("cstb")
    P.dma(lambda e: e.dma_start(out=cst[:], in_=consts[:, 0:512]), writes=[Bc])
    OP("dve", [Bc], [Bcb], lambda e: e.tensor_copy(out=cstb[:], in_=cst[:, 0:512]))
    ident_f = cst[:, C_ID:C_ID + 128]
    ones_f = cst[:, C_ONES:C_ONES + 128]
    triu_f = cst[:, C_TRIU:C_TRIU + 128]
    ident_b = cstb[:, C_ID:C_ID + 128]
    ones_b = cstb[:, C_ONES:C_ONES + 128]
    mneg_b = cstb[:, C_MNEG:C_MNEG + 128]
    epsc = sb(es, "epsc", [128, 2], F32)
    Beps = Buf("eps")
    OP("dve", [], [Beps], lambda e: e.memset(epsc[:, 0:1], EPS))
    OP("dve", [Beps], [Beps], lambda e: e.memset(epsc[:, 1:2], 1.0))
    gcol = sb(es, "gcol", [128, 9, 8], F32)
    Bg = Buf("gcol")
    P.dma(lambda e: e.dma_start(out=gcol[:, 0:8, :], in_=norm_g.rearrange("l n (c p) -> p (l n) c", p=128), allow_slow_non_contiguous=True), writes=[Bg])
    P.dma(lambda e: e.dma_start(out=gcol[:, 8, :], in_=final_g.rearrange("(c p) -> p c", p=128), allow_slow_non_contiguous=True), writes=[Bg])

    def rmsnorm(h, hB, gi, xn, xnB, tmp):
        ps, pB = next_ps()
        for c in range(8):
            sq, sqB = tmp["sq"][c % 2]
            OP("act", [hB[c]], [sqB], lambda e, c=c, sq=sq: e.activation(out=sq, in_=h[:, c, :], func=AF.Square))
            OP("pe", [sqB, Bcb], [pB], lambda e, c=c, sq=sq, ps=ps: e.matmul(ps[:, 0:TT], lhsT=ones_b, rhs=sq, start=(c == 0), stop=(c == 7)))
        lnv, lnB = tmp["lnv"]
        rstd, rsB = tmp["rstd"]
        OP("act", [pB, Beps], [lnB], lambda e, ps=ps: e.activation(out=lnv, in_=ps[:, 0:TT], func=AF.Ln, scale=1.0 / D, bias=epsc[:, 0:1]))
        OP("act", [lnB], [rsB], lambda e: e.activation(out=rstd, in_=lnv, func=AF.Exp, scale=-0.5))
        for c in range(8):
            OP("dve", [hB[c], rsB, Bg], [xnB[c]],
               lambda e, c=c: e.scalar_tensor_tensor(out=xn[:, c, :], in0=h[:, c, :], scalar=gcol[:, gi, c:c + 1], in1=rstd,
                                                     op0=ALU.mult, op1=ALU.mult))

    def norm_tmp(stack, tag):
        sq0 = sb(stack, "sq0" + tag, [128, TT], BF16)
        sq1 = sb(stack, "sq1" + tag, [128, TT], BF16)
        lnv = sb(stack, "lnv" + tag, [128, TT], F32)
        rstd = sb(stack, "rstd" + tag, [128, TT], F32)
        return {"sq": [(sq0[:], Buf()), (sq1[:], Buf())], "lnv": (lnv[:], Buf()), "rstd": (rstd[:], Buf())}

    def load_h(tile, tB, q):
        P.dma(lambda e: e.dma_start(out=tile, in_=hTv[:, :, q * TT:(q + 1) * TT]), reads=HT[q], writes=tB)

    def wload(dst, src, wB):
        P.dma(lambda e: e.dma_start(out=dst, in_=src), writes=wB, q="pool")

    def phase_in():
        P.barrier()
        with contextlib.ExitStack() as st:
            xt = [sb(st, "xt%d" % i, [128, D], F32) for i in range(2)]
            xtB = [Buf() for _ in range(2)]
            ho = [sb(st, "ho%d" % i, [128, 8, TT], F32) for i in range(2)]
            hoB = [Buf() for _ in range(2)]
            k = 0
            for q in range(NQ):
                for s in range(4):
                    tt = q * 4 + s
                    xb, xB = xt[tt % 2], xtB[tt % 2]
                    P.dma(lambda e, xb=xb, tt=tt: e.dma_start(out=xb[:], in_=x[tt * 128:(tt + 1) * 128, :]), writes=[xB])
                    for half in range(2):
                        ps, pB = next_ps()
                        for cc in range(4):
                            c = half * 4 + cc
                            OP("pe", [xB, Bc], [pB], lambda e, ps=ps, cc=cc, c=c, xb=xb: e.transpose(out=ps[:, cc * 128:(cc + 1) * 128], in_=xb[:, c * 128:(c + 1) * 128], identity=ident_f))
                        eng = "act" if (k % 2 == 0) else "dve"
                        k += 1
                        dst = ho[q % 2][:, half * 4:(half + 1) * 4, s * 128:(s + 1) * 128]
                        src = ps[:, :].rearrange("p (c t) -> p c t", c=4)
                        if eng == "act":
                            OP("act", [pB], [hoB[q % 2]], lambda e, dst=dst, src=src: e.activation(out=dst, in_=src, func=AF.Copy))
                        else:
                            OP("dve", [pB], [hoB[q % 2]], lambda e, dst=dst, src=src: e.tensor_copy(out=dst, in_=src))
                P.dma(lambda e, q=q: e.dma_start(out=hTv[:, :, q * TT:(q + 1) * TT], in_=ho[q % 2][:]), reads=[hoB[q % 2]], writes=HT[q])

    def phase_out():
        P.barrier()
        with contextlib.ExitStack() as st:
            hn2 = [sb(st, "fo_hn%d" % i, [128, 8, TT], F32) for i in range(2)]
            hnB2 = [[Buf() for _ in range(8)] for _ in range(2)]
            yn2 = [sb(st, "fo_yn%d" % i, [128, 8, TT], F32) for i in range(2)]
            ynB2 = [[Buf() for _ in range(8)] for _ in range(2)]
            ot = [sb(st, "fo_ot%d" % i, [128, D], F32) for i in range(4)]
            otB = [Buf() for _ in range(4)]
            tmp2 = [norm_tmp(st, "fo%d" % i) for i in range(2)]
            k = 0
            for q in range(NQ):
                hn, hnB, yn, ynB = hn2[q % 2], hnB2[q % 2], yn2[q % 2], ynB2[q % 2]
                load_h(hn[:], hnB, q)
                rmsnorm(hn, hnB, 8, yn, ynB, tmp2[q % 2])
                for s in range(4):
                    tt = q * 4 + s
                    o, oB = ot[tt % 4], otB[tt % 4]
                    for half in range(2):
                        ps, pB = next_ps()
                        for cc in range(4):
                            c = half * 4 + cc
                            OP("pe", [ynB[c], Bc], [pB], lambda e, ps=ps, cc=cc, c=c, s=s, yn=yn: e.transpose(out=ps[:, cc * 128:(cc + 1) * 128], in_=yn[:, c, s * 128:(s + 1) * 128], identity=ident_f))
                        dst = o[:, half * 512:(half + 1) * 512]
                        if k % 2 == 0:
                            OP("act", [pB], [oB], lambda e, dst=dst, ps=ps: e.activation(out=dst, in_=ps[:, :], func=AF.Copy))
                        else:
                            OP("dve", [pB], [oB], lambda e, dst=dst, ps=ps: e.tensor_copy(out=dst, in_=ps[:, :]))
                        k += 1
                    P.dma(lambda e, o=o, tt=tt: e.dma_start(out=y_out[tt * 128:(tt + 1) * 128, :], in_=o[:]), reads=[oB], writes=[Buf()], is_out=True)

    def phase_ffn(l, f):
        P.barrier()
        with contextlib.ExitStack() as st:
            wg = sb(st, "wg", [128, 8, DFF], BF16)
            wu = sb(st, "wu", [128, 8, DFF], BF16)
            wd = sb(st, "wd", [128, NFC, D], BF16)
            CB = [(0, 256), (256, 1024), (1024, 2048), (2048, DFF)]
            wgB = [Buf() for _ in range(4)]
            wuB = [Buf() for _ in range(4)]
            wdB = [Buf() for _ in range(NFC)]
            wgv = w_gate[l, f].rearrange("(kc p) n -> p kc n", p=128)
            wuv = w_up[l, f].rearrange("(kc p) n -> p kc n", p=128)
            wdv = w_down[l, f].rearrange("(fc p) n -> p fc n", p=128)
            for cb, (c0, c1) in enumerate(CB):
                wload(wg[:, :, c0:c1], wgv[:, :, c0:c1], [wgB[cb]])
                wload(wu[:, :, c0:c1], wuv[:, :, c0:c1], [wuB[cb]])
            for f0 in range(0, NFC, 6):
                f1 = min(NFC, f0 + 6)
                wload(wd[:, f0:f1, :], wdv[:, f0:f1, :], wdB[f0:f1])
            hn = sb(st, "ff_hn", [128, 8, TT], F32)
            hnB = [Buf() for _ in range(8)]
            xn = [sb(st, "ff_xn%d" % i, [128, 8, TT], BF16) for i in range(2)]
            xnB = [[Buf() for _ in range(8)] for _ in range(2)]
            act = sb(st, "ff_act", [128, NFC, TT], BF16)
            actB = [Buf() for _ in range(NFC)]
            sg = [sb(st, "ff_sg%d" % i, [128, TT], F32) for i in range(2)]
            sgB = [Buf() for _ in range(2)]
            hr = [sb(st, "ff_hr%d" % i, [128, TT], F32) for i in range(3)]
            hrB = [Buf() for _ in range(3)]
            tmp = norm_tmp(st, "ff")
            gi = l * 4 + (0 if f == 0 else 2)
            k = 0
            load_h(hn[:], hnB, 0)
            rmsnorm(hn, hnB, gi, xn[0], xnB[0], tmp)
            for q in range(nq):
                X, XB = xn[q % 2], xnB[q % 2]
                if q + 1 < nq:
                    load_h(hn[:], hnB, q + 1)
                if q == 0 and l == 0 and f == 0:
                    dbg_dump("xn", X[:], [128, 8, TT], BF16, XB)
                    dbg_dump("wg", wg[:], [128, 8, DFF], BF16, wgB)
                    dbg_dump("wd", wd[:], [128, NFC, D], BF16, wdB)
                for fc in range(NFC):
                    pg, pgB = next_ps()
                    for kc in range(8):
                        OP("pe", [wgB[0 if fc < 2 else 1 + fc // 8], XB[kc]], [pgB], lambda e, pg=pg, kc=kc, fc=fc, X=X: e.matmul(pg[:, 0:TT], lhsT=wg[:, kc, fc * 128:(fc + 1) * 128], rhs=X[:, kc, :], start=(kc == 0), stop=(kc == 7)))
                    pu, puB = next_ps()
                    for kc in range(8):
                        OP("pe", [wuB[0 if fc < 2 else 1 + fc // 8], XB[kc]], [puB], lambda e, pu=pu, kc=kc, fc=fc, X=X: e.matmul(pu[:, 0:TT], lhsT=wu[:, kc, fc * 128:(fc + 1) * 128], rhs=X[:, kc, :], start=(kc == 0), stop=(kc == 7)))
                    s_, sB_ = sg[fc % 2], sgB[fc % 2]
                    OP("act", [pgB], [sB_], lambda e, s_=s_, pg=pg: e.activation(out=s_[:], in_=pg[:, 0:TT], func=AF.Silu))
                    OP("dve", [sB_, puB], [actB[fc]], lambda e, s_=s_, pu=pu, fc=fc: e.tensor_tensor(out=act[:, fc, :], in0=pu[:, 0:TT], in1=s_[:], op=ALU.mult))
                    if fc == 11 and q + 1 < nq:
                        rmsnorm(hn, hnB, gi, xn[(q + 1) % 2], xnB[(q + 1) % 2], tmp)
                if q == 0 and l == 0 and f == 0:
                    dbg_dump("act", act[:], [128, NFC, TT], BF16, actB)
                for dc in range(8):
                    po, poB = next_ps()
                    for fc in range(NFC):
                        OP("pe", [wdB[fc], actB[fc]], [poB], lambda e, po=po, fc=fc, dc=dc: e.matmul(po[:, 0:TT], lhsT=wd[:, fc, dc * 128:(dc + 1) * 128], rhs=act[:, fc, :], start=(fc == 0), stop=(fc == NFC - 1)))
                    r, rB = hr[k % 3], hrB[k % 3]
                    k += 1
                    P.dma(lambda e, r=r, dc=dc, q=q: e.dma_start(out=r[:], in_=hTv[:, dc, q * TT:(q + 1) * TT]), reads=[HT[q][dc]], writes=[rB])
                    OP("dve", [poB, rB], [rB], lambda e, r=r, po=po: e.scalar_tensor_tensor(out=r[:], in0=po[:, 0:TT], scalar=0.5, in1=r[:], op0=ALU.mult, op1=ALU.add))
                    P.dma(lambda e, r=r, dc=dc, q=q: e.dma_start(out=hTv[:, dc, q * TT:(q + 1) * TT], in_=r[:]), reads=[rB], writes=[HT[q][dc]])

    def phase_ple(l, fuse_out=False):
        P.barrier()
        with contextlib.ExitStack() as st:
            wpg = sb(st, "wpg", [128, 8, D], BF16)
            wpp = sb(st, "wpp", [128, 2, D], BF16)
            wpgB = [Buf()]
            wppB = [Buf()]
            wpgv = ple_wg[l].rearrange("(kc p) n -> p kc n", p=128)
            wpgB = [Buf(), Buf()]
            wload(wpg[:, :, 0:256], wpgv[:, :, 0:256], [wpgB[0]])
            wload(wpg[:, :, 256:D], wpgv[:, :, 256:D], [wpgB[1]])
            wload(wpp[:], ple_wp[l].rearrange("(kc p) n -> p kc n", p=128), wppB)
            NH = 3 if fuse_out else 2
            hn2 = [sb(st, "pl_hn%d" % i, [128, 8, TT], F32) for i in range(NH)]
            hnB2 = [[Buf() for _ in range(8)] for _ in range(NH)]
            xn = [sb(st, "pl_xn%d" % i, [128, 8, TT], BF16) for i in range(2)]
            xnB = [[Buf() for _ in range(8)] for _ in range(2)]
            pt = [sb(st, "pl_pt%d" % i, [128, 4, PLE], F32) for i in range(2)]
            ptB = [Buf() for _ in range(2)]
            pT = [sb(st, "pl_pT%d" % i, [128, 2, TT], BF16) for i in range(2)]
            pTB = [Buf() for _ in range(2)]
            sg = [sb(st, "pl_sg%d" % i, [128, TT], F32) for i in range(2)]
            sgB = [Buf() for _ in range(2)]
            tg = [sb(st, "pl_tg%d" % i, [128, TT], F32) for i in range(2)]
            tgB = [Buf() for _ in range(2)]
            hr = [sb(st, "pl_hr%d" % i, [128, TT], F32) for i in range(3)]
            hrB = [Buf() for _ in range(3)]
            tmp2 = [norm_tmp(st, "pl%d" % i) for i in range(2)]
            gi = l * 4 + 3

            def ple_load(q):
                load_h(hn2[q % NH][:], hnB2[q % NH], q)
                pt_, ptB_ = pt[q % 2], ptB[q % 2]
                P.dma(lambda e, pt_=pt_, q=q: e.dma_start(out=pt_[:], in_=p_in[l, q * TT:(q + 1) * TT, :].rearrange("(s p) c -> p s c", p=128)), writes=[ptB_])

            def ple_prep(q):
                rmsnorm(hn2[q % NH], hnB2[q % NH], gi, xn[q % 2], xnB[q % 2], tmp2[q % 2])
                pt_, ptB_ = pt[q % 2], ptB[q % 2]
                pT_, pTB_ = pT[q % 2], pTB[q % 2]
                for c2 in range(2):
                    ps, pB = next_ps()
                    for s in range(4):
                        OP("pe", [ptB_, Bc], [pB], lambda e, ps=ps, s=s, c2=c2, pt_=pt_: e.transpose(out=ps[:, s * 128:(s + 1) * 128], in_=pt_[:, s, c2 * 128:(c2 + 1) * 128], identity=ident_f))
                    OP("act", [pB], [pTB_], lambda e, ps=ps, c2=c2, pT_=pT_: e.activation(out=pT_[:, c2, :], in_=ps[:, :], func=AF.Copy))

            if fuse_out:
                yn2 = [sb(st, "fo_yn%d" % i, [128, 8, TT], F32) for i in range(2)]
                ynB2 = [[Buf() for _ in range(8)] for _ in range(2)]
                ot = [sb(st, "fo_ot%d" % i, [128, D], F32) for i in range(4)]
                otB = [Buf() for _ in range(4)]
                tmpo = [norm_tmp(st, "fo%d" % i) for i in range(2)]
            kev = [0]

            def out_tile(q):
                hn, hnB, yn, ynB = hn2[q % NH], hnB2[q % NH], yn2[q % 2], ynB2[q % 2]
                rmsnorm(hn, hnB, 8, yn, ynB, tmpo[q % 2])
                for s in range(4):
                    tt = q * 4 + s
                    o, oB = ot[tt % 4], otB[tt % 4]
                    for half in range(2):
                        ps, pB = next_ps()
                        for cc in range(4):
                            c_ = half * 4 + cc
                            OP("pe", [ynB[c_], Bc], [pB], lambda e, ps=ps, cc=cc, c_=c_, s=s, yn=yn: e.transpose(out=ps[:, cc * 128:(cc + 1) * 128], in_=yn[:, c_, s * 128:(s + 1) * 128], identity=ident_f))
                        dst = o[:, half * 512:(half + 1) * 512]
                        if kev[0] % 2 == 0:
                            OP("act", [pB], [oB], lambda e, dst=dst, ps=ps: e.activation(out=dst, in_=ps[:, :], func=AF.Copy))
                        else:
                            OP("dve", [pB], [oB], lambda e, dst=dst, ps=ps: e.tensor_copy(out=dst, in_=ps[:, :]))
                        kev[0] += 1
                    P.dma(lambda e, o=o, tt=tt: e.dma_start(out=y_out[tt * 128:(tt + 1) * 128, :], in_=o[:]), reads=[oB], writes=[Buf()], is_out=True)

            ple_load(0)
            ple_prep(0)
            for q in range(NQ):
                hn, hnB = hn2[q % NH], hnB2[q % NH]
                X, XB = xn[q % 2], xnB[q % 2]
                pT_, pTB_ = pT[q % 2], pTB[q % 2]
                if q + 1 < NQ:
                    ple_load(q + 1)
                for dc in range(8):
                    pg, pgB = next_ps()
                    for kc in range(8):
                        OP("pe", [wpgB[0 if dc < 2 else 1], XB[kc]], [pgB], lambda e, pg=pg, kc=kc, dc=dc, X=X: e.matmul(pg[:, 0:TT], lhsT=wpg[:, kc, dc * 128:(dc + 1) * 128], rhs=X[:, kc, :], start=(kc == 0), stop=(kc == 7)))
                    pe_, peB = next_ps()
                    for c2 in range(2):
                        OP("pe", [wppB[0], pTB_], [peB], lambda e, pe_=pe_, c2=c2, dc=dc, pT_=pT_: e.matmul(pe_[:, 0:TT], lhsT=wpp[:, c2, dc * 128:(dc + 1) * 128], rhs=pT_[:, c2, :], start=(c2 == 0), stop=(c2 == 1)))
                    s_, sB_ = sg[dc % 2], sgB[dc % 2]
                    OP("act", [pgB], [sB_], lambda e, s_=s_, pg=pg: e.activation(out=s_[:], in_=pg[:, 0:TT], func=AF.Sigmoid))
                    t_, tB_ = tg[dc % 2], tgB[dc % 2]
                    OP("dve", [sB_, peB], [tB_], lambda e, s_=s_, pe_=pe_, t_=t_: e.tensor_tensor(out=t_[:], in0=pe_[:, 0:TT], in1=s_[:], op=ALU.mult))
                    OP("pool", [tB_, hnB[dc]], [hnB[dc]], lambda e, hn=hn, t_=t_, dc=dc: e.tensor_tensor(out=hn[:, dc, :], in0=t_[:], in1=hn[:, dc, :], op=ALU.add))
                    if not fuse_out:
                        P.dma(lambda e, hn=hn, dc=dc, q=q: e.dma_start(out=hTv[:, dc, q * TT:(q + 1) * TT], in_=hn[:, dc, :]), reads=[hnB[dc]], writes=[HT[q][dc]])
                    if dc == 3 and q + 1 < NQ:
                        ple_prep(q + 1)
                    if fuse_out and dc == 3 and q >= 1:
                        out_tile(q - 1)
            if fuse_out:
                out_tile(NQ - 1)

    ctx = dict(nc=nc, P=P, OP=OP, sb=sb, es=es, next_ps=next_ps, rmsnorm=rmsnorm, norm_tmp=norm_tmp, load_h=load_h,
               wload=wload, HT=HT, XN=XN, YB=YB, hTv=hTv, xnTv=xnTv, ybr=ybr, cst=cst, cstb=cstb, Bc=Bc, Bcb=Bcb,
               epsc=epsc, Beps=Beps, ident_f=ident_f, ones_f=ones_f, triu_f=triu_f, ident_b=ident_b, ones_b=ones_b,
               mneg_b=mneg_b, gcol=gcol, Bg=Bg,
               w_in=w_in, f_bias=f_bias, pool_w=pool_w, pool_scale=pool_scale, lam_re=lam_re, lam_im=lam_im,
               log_dt=log_dt, b_re=b_re, b_im=b_im, c_re=c_re, c_im=c_im, ssm_d=ssm_d, w_glu=w_glu, conv_w=conv_w,
               w_branch=w_branch, w_out=w_out, consts=consts, qaug=qaug, vtok=vtok, psb=psb, psB=psB, dbg_dump=dbg_dump, nq=nq)

    if "in" in phases:
        phase_in()
    for l in range(depth):
        if "ffn" in phases:
            phase_ffn(l, 0)
        if "mix" in phases:
            phase_mixer(ctx, l, mix_parts)
        if "ffn" in phases:
            phase_ffn(l, 1)
        fuse = ("out" in phases) and (l == depth - 1)
        if "ple" in phases:
            phase_ple(l, fuse_out=fuse)
    if "out" in phases and "ple" not in phases:
        phase_out()
    P.emit(es)
    es.close()
    return nc


from types import SimpleNamespace


def phase_mixer(ctx, l, parts):
    c = SimpleNamespace(**ctx)
    P, OP, sb, nc = c.P, c.OP, c.sb, c.nc
    psb, psB = c.psb, c.psB
    ybrv = c.ybr.rearrange("b (c p) t -> b p c t", p=128)
    QA = [Buf() for _ in range(NQ)]
    VT = [Buf() for _ in range(NQ)]
    P.barrier()
    with contextlib.ExitStack() as so:
        u_ssm = sb(so, "u_ssm", [128, 2, 8 + T], BF16)
        uB = [[Buf() for _ in range(NQ)] for _ in range(2)]
        spar = ssm_param_load(c, l, so) if "ssm" in parts else None
        spre = ssm_s_alloc(c, so) if "ssm" in parts else None
        with contextlib.ExitStack() as s1:
            k_aug = sb(s1, "k_aug", [128, 4, T], BF16)
            kB = [[Buf() for _ in range(NQ)] for _ in range(4)]
            kcB = Buf()
            cabs = sb(s1, "cabs", [128, 32, 4], F32)
            tots = sb(s1, "tots", [128, 33, 4], F32)
            cabsB = [Buf() for _ in range(32)]
            totsB = [Buf() for _ in range(33)]
            mixer_m1(c, l, parts, u_ssm, uB, k_aug, kB, kcB, cabs, cabsB, tots, totsB, QA, VT, ybrv)
            P.barrier()
            sch = ssm_s_chain(c, l, spre, spar) if "ssm" in parts else None
            dq = sch.dq if sch is not None else []

            def drain(n):
                for _ in range(min(n, len(dq))):
                    eng, rd, wr, fn = dq.pop(0)
                    OP(eng, rd, wr, fn)
            if "att" in parts:
                mixer_m3(c, l, k_aug, kB, kcB, cabs, cabsB, tots, totsB, QA, VT, ybrv, drain)
            drain(len(dq))
        P.barrier()
        if "ssm" in parts:
            mixer_m2(c, l, u_ssm, uB, ybrv, spar, sch)
    P.barrier()
    mixer_m4(c, l, parts, ybrv)


def mixer_m1(c, l, parts, u_ssm, uB, k_aug, kB, kcB, cabs, cabsB, tots, totsB, QA, VT, ybrv):
    P, OP, sb, nc, next_ps = c.P, c.OP, c.sb, c.nc, c.next_ps
    with contextlib.ExitStack() as st:
        win = sb(st, "win", [128, 8, 2052], BF16)
        WBLK = [(0, 772), (772, 1796), (1796, 2052)]
        winB = [Buf() for _ in WBLK]
        wiv = c.w_in[l].rearrange("(kc p) n -> p kc n", p=128)
        c.wload(win[:, :, 0:772], wiv[:, :, 0:772], [winB[0]])

        def wB(col):
            for i, (c0, c1) in enumerate(WBLK):
                if c0 <= col < c1:
                    return winB[i]

        wf_sb = sb(st, "wf_sb", [128, 8, 4, 64], BF16)
        wfB = Buf()
        OP("dve", [winB[0]], [wfB], lambda e: e.tensor_copy(out=wf_sb[:], in_=win[:, :, 768:772].unsqueeze(3).to_broadcast([128, 8, 4, 64])))
        fb = sb(st, "fb", [128, 8], F32)
        fbB = Buf()
        P.dma(lambda e: e.dma_start(out=fb[:, 0:4], in_=c.f_bias[l:l + 1, :].partition_broadcast(128), allow_slow_non_contiguous=True), writes=[fbB])
        OP("dve", [fbB], [fbB], lambda e: e.tensor_scalar(out=fb[:, 4:8], in0=fb[:, 0:4], scalar1=-1.0, scalar2=None, op0=ALU.mult))
        pwb = sb(st, "pwb", [128, 2, 128], BF16)
        pwB = Buf()
        OP("pool", [], [pwB], lambda e: e.memset(pwb[:], 0.0))
        for g in range(4):
            r0 = (g % 2) * 64
            P.dma(lambda e, g=g, r0=r0: e.dma_start(out=pwb[r0:r0 + 64, g // 2, r0:r0 + 64], in_=c.pool_w[l, g]), reads=[pwB], writes=[pwB], q="pool")
        pmb = sb(st, "pmb", [128, 12, 128], BF16)
        pmB = Buf()
        P.dma(lambda e: e.dma_start(out=pmb[:], in_=c.consts[:, C_PM:C_PM + 12 * 128].rearrange("p (a b) -> p a b", a=12)), writes=[pmB], q="pool")
        for i in (1, 2):
            c.wload(win[:, :, WBLK[i][0]:WBLK[i][1]], wiv[:, :, WBLK[i][0]:WBLK[i][1]], [winB[i]])
        scol = sb(st, "scol", [128, 8], F32)
        scB = Buf()
        P.dma(lambda e: e.dma_start(out=scol[:, 0:2], in_=c.pool_scale[l].rearrange("(c p) -> p c", p=128), allow_slow_non_contiguous=True), writes=[scB])
        P.dma(lambda e: e.dma_start(out=scol[:, 2:8].rearrange("p (j c) -> p j c", j=3), in_=c.conv_w[l].rearrange("j (c p) -> p j c", p=128), allow_slow_non_contiguous=True), writes=[scB])

        OP("pool", [], [totsB[0]], lambda e: e.memset(tots[:, 0, :], 0.0))

        hn = sb(st, "m1_hn", [128, 8, TT], F32)
        hnB = [Buf() for _ in range(8)]
        xn2 = [sb(st, "m1_xn%d" % i, [128, 8, TT], BF16) for i in range(2)]
        xnB2 = [[Buf() for _ in range(8)] for _ in range(2)]
        tmp = c.norm_tmp(st, "m1")
        qa = [sb(st, "m1_qa%d" % i, [128, 4, TT], BF16) for i in range(2)]
        qaB = [Buf() for _ in range(2)]
        et = [sb(st, "m1_et%d" % i, [128, TT], F32) for i in range(2)]
        etB = [Buf() for _ in range(2)]
        spt = [sb(st, "m1_sp%d" % i, [128, TT], F32) for i in range(2)]
        spB = [Buf() for _ in range(2)]
        crn = [sb(st, "m1_crn%d" % i, [128, TT], F32) for i in range(2)]
        crnB = [Buf() for _ in range(2)]
        hit = [sb(st, "m1_hit%d" % i, [128, TT], BF16) for i in range(2)]
        hitB = [Buf() for _ in range(2)]
        vst = [sb(st, "m1_vst%d" % i, [128, 4, 256], BF16) for i in range(2)]
        vstB = [Buf() for _ in range(2)]
        ftk = [sb(st, "m1_ftk%d" % i, [128, 12], F32) for i in range(2)]
        ftkB = [Buf() for _ in range(2)]
        xpt = sb(st, "m1_xpt", [128, 5, 256], BF16)
        xptB = [Buf() for _ in range(5)]
        pld = sb(st, "m1_pld", [128, 2, TT], BF16)
        pldB = [Buf() for _ in range(2)]
        yps = [sb(st, "m1_yps%d" % i, [128, 2, TT], BF16) for i in range(2)]
        ypsB = [Buf() for _ in range(2)]
        ycs = [sb(st, "m1_ycs%d" % i, [128, 2, TT], BF16) for i in range(2)]
        ycsB = [Buf() for _ in range(2)]
        ccs = [sb(st, "m1_ccs%d" % i, [128, TT], F32) for i in range(2)]
        ccsB = [Buf() for _ in range(2)]
        cbs = [sb(st, "m1_cbs%d" % i, [128, TT], F32) for i in range(2)]
        cbsB = [Buf() for _ in range(2)]
        zt = [[sb(st, "m1_z%d_%d" % (cc, i), [128, TT + 2], F32) for i in range(2)] for cc in range(2)]
        ztB = [[Buf() for _ in range(2)] for _ in range(2)]
        y1 = [sb(st, "m1_y1%d" % i, [128, TT], F32) for i in range(2)]
        y1B = [Buf() for _ in range(2)]
        ones1 = c.cst[:, C_ONES:C_ONES + 1]

        def proj_fm(col0, M, ps, pB, pslice, X, XB, tp=None):
            for kc in range(8):
                kw = {} if tp is None else {"tile_position": tp}
                OP("pe", [wB(col0), XB[kc]], [pB], lambda e, kc=kc, kw=kw: e.matmul(ps[pslice, 0:TT], lhsT=win[:, kc, col0:col0 + M], rhs=X[:, kc, :], start=(kc == 0), stop=(kc == 7), **kw))

        c.load_h(hn[:], hnB, 0)
        c.rmsnorm(hn, hnB, l * 4 + 1, xn2[0], xnB2[0], tmp)
        if c.nq > 1:
            c.load_h(hn[:], hnB, 1)
        for q in range(c.nq):
            qs = slice(q * TT, (q + 1) * TT)
            xn, xnB = xn2[q % 2], xnB2[q % 2]
            P.dma(lambda e, qs=qs, xn=xn: e.dma_start(out=c.xnTv[:, :, qs], in_=xn[:]), reads=xnB, writes=[c.XN[q]])
            Q_, QB_ = qa[q % 2], qaB[q % 2]
            if "att" in parts:
                for h in range(4):
                    ps, pB = next_ps()
                    proj_fm(h * 64, 64, ps, pB, slice(0, 64), xn, xnB)
                    for kc in range(8):
                        OP("pe", [wfB, xnB[kc]], [pB], lambda e, kc=kc, h=h, ps=ps, xn=xn: e.matmul(ps[64:128, 0:TT], lhsT=wf_sb[:, kc, h, :], rhs=xn[:, kc, :], start=(kc == 0), stop=(kc == 7), tile_position=(0, 64)))
                    OP("act", [pB], [QB_], lambda e, ps=ps, h=h, Q_=Q_: e.activation(out=Q_[0:64, h, :], in_=ps[0:64, 0:TT], func=AF.Copy, scale=0.125))
                    e_, eB_ = et[h % 2], etB[h % 2]
                    OP("act", [pB, fbB], [eB_], lambda e, ps=ps, h=h, e_=e_: e.activation(out=e_[64:128, :], in_=ps[64:128, 0:TT], func=AF.Exp, scale=-1.0, bias=fb[64:128, 4 + h:5 + h]))
                    s_, sB_ = spt[h % 2], spB[h % 2]
                    OP("act", [eB_, c.Beps], [sB_], lambda e, e_=e_, s_=s_: e.activation(out=s_[64:128, :], in_=e_[64:128, :], func=AF.Ln, bias=c.epsc[64:128, 1:2]))
                    r_, rB_ = crn[h % 2], crnB[h % 2]
                    OP("dve", [sB_, c.Bc], [rB_], lambda e, s_=s_, r_=r_: e.tensor_tensor_scan(out=r_[64:128, :], data0=ones1[64:128, :].to_broadcast([64, TT]), data1=s_[64:128, :], initial=0.0, op0=ALU.mult, op1=ALU.add))
                    h_, hB_ = hit[h % 2], hitB[h % 2]
                    OP("dve", [rB_], [hB_], lambda e, r_=r_, h_=h_: e.tensor_scalar(out=h_[64:128, :], in0=r_[64:128, :], scalar1=-1.0, scalar2=None, op0=ALU.mult))
                    OP("pool", [hB_], [QB_], lambda e, h_=h_, h=h, Q_=Q_: e.tensor_copy(out=Q_[64:96, h, :], in_=h_[64:96, :]))
                    OP("dve", [rB_, hB_], [QB_], lambda e, r_=r_, h_=h_, h=h, Q_=Q_: e.scalar_tensor_tensor(out=Q_[96:128, h, :], in0=r_[96:128, :], scalar=-1.0, in1=h_[96:128, :], op0=ALU.mult, op1=ALU.subtract))
                P.dma(lambda e, qs=qs, Q_=Q_: e.dma_start(out=c.qaug[:, :, qs], in_=Q_[:]), reads=[QB_], writes=[QA[q]])
                for h in range(4):
                    ps, pB = next_ps()
                    proj_fm(256 + h * 64, 64, ps, pB, slice(0, 64), xn, xnB)
                    OP("dve", [pB], [kB[h][q]], lambda e, ps=ps, h=h, qs=qs: e.tensor_copy(out=k_aug[0:64, h, qs], in_=ps[0:64, 0:TT]))
            if q == 0:
                OP("pool", [], [kcB], lambda e: e.memset(k_aug[64:128, :, :], 0.0))
                OP("pool", [kcB], [kcB], lambda e: e.memset(k_aug[64:65, :, :], 1.0))
                OP("pool", [kcB], [kcB], lambda e: e.memset(k_aug[96:97, :, :], 1.0))
            if q + 1 < c.nq:
                c.rmsnorm(hn, hnB, l * 4 + 1, xn2[(q + 1) % 2], xnB2[(q + 1) % 2], tmp)
                if q + 2 < c.nq:
                    c.load_h(hn[:], hnB, q + 2)
            V_, VB_ = vst[q % 2], vstB[q % 2]
            ppool = [(c.psb[6], c.psB[6]), (c.psb[7], c.psB[7])]

            def stA(s):
                tt = q * 4 + s
                ts_ = slice(s * 128, (s + 1) * 128)
                if "att" not in parts:
                    return
                psA, pBA = next_ps()
                for kc in range(8):
                    OP("pe", [winB[0], xnB[kc]], [pBA], lambda e, kc=kc, psA=psA, ts_=ts_, xn=xn: e.matmul(psA[:, 0:260], lhsT=xn[:, kc, ts_], rhs=win[:, kc, 512:772], start=(kc == 0), stop=(kc == 7)))
                OP("act", [pBA], [VB_], lambda e, psA=psA, s=s, V_=V_: e.activation(out=V_[:, s, :], in_=psA[:, 0:256], func=AF.Copy))
                f_, fB_ = ftk[tt % 2], ftkB[tt % 2]
                OP("dve", [pBA, fbB], [fB_], lambda e, psA=psA, f_=f_: e.tensor_tensor(out=f_[:, 0:4], in0=psA[:, 256:260], in1=fb[:, 0:4], op=ALU.add))
                OP("act", [fB_], [fB_], lambda e, f_=f_: e.activation(out=f_[:, 4:8], in_=f_[:, 0:4], func=AF.Exp, scale=-1.0))
                OP("act", [fB_, c.Beps], [fB_], lambda e, f_=f_: e.activation(out=f_[:, 8:12], in_=f_[:, 4:8], func=AF.Ln, bias=c.epsc[:, 1:2]))

            def stB(s):
                ts_ = slice(s * 128, (s + 1) * 128)
                if "pool" not in parts:
                    return
                psP, pBP = next_ps()
                for kc in range(8):
                    OP("pe", [winB[1], xnB[kc]], [pBP], lambda e, kc=kc, psP=psP, ts_=ts_, xn=xn: e.matmul(psP[:, 0:256], lhsT=xn[:, kc, ts_], rhs=win[:, kc, 772:1028], start=(kc == 0), stop=(kc == 7)))
                OP("act", [pBP], [xptB[1 + s]], lambda e, psP=psP, s=s: e.activation(out=xpt[:, 1 + s, :], in_=psP[:, 0:256], func=AF.Copy))

            def stC(s):
                tt = q * 4 + s
                ts_ = slice(s * 128, (s + 1) * 128)
                if "pool" not in parts:
                    return
                for g in range(4):
                    pp, ppB = ppool[g // 2]
                    r0 = (g % 2) * 64
                    first = (tt == 0)
                    mi = g * 3 + (2 if first else 0)
                    OP("pe", [xptB[1 + s], pmB], [ppB], lambda e, pp=pp, r0=r0, g=g, s=s, mi=mi, ts_=ts_, first=first: e.matmul(pp[r0:r0 + 64, ts_], lhsT=xpt[:, 1 + s, g * 64:(g + 1) * 64], rhs=pmb[:, mi, :], start=True, stop=first, tile_position=(0, r0)))
                    if not first:
                        OP("pe", [xptB[s], pmB], [ppB], lambda e, pp=pp, r0=r0, g=g, s=s, ts_=ts_: e.matmul(pp[r0:r0 + 64, ts_], lhsT=xpt[:, s, g * 64:(g + 1) * 64], rhs=pmb[:, g * 3 + 1, :], start=False, stop=True, tile_position=(0, r0)))

            def stD(s):
                tt = q * 4 + s
                if "att" not in parts:
                    return
                f_, fB_ = ftk[tt % 2], ftkB[tt % 2]
                psc, pBc = next_ps()
                OP("pe", [fB_, c.Bc], [pBc], lambda e, psc=psc, f_=f_: e.matmul(psc[:, 0:4], lhsT=c.triu_f, rhs=f_[:, 8:12], start=True, stop=True))
                OP("pe", [fB_, c.Bc], [pBc], lambda e, psc=psc, f_=f_: e.matmul(psc[:, 8:12], lhsT=c.ones_f, rhs=f_[:, 8:12], start=True, stop=True))
                OP("dve", [pBc, totsB[tt]], [cabsB[tt]], lambda e, psc=psc, tt=tt: e.tensor_tensor(out=cabs[:, tt, :], in0=psc[:, 0:4], in1=tots[:, tt, :], op=ALU.add))
                OP("dve", [pBc, totsB[tt]], [totsB[tt + 1]], lambda e, psc=psc, tt=tt: e.tensor_tensor(out=tots[:, tt + 1, :], in0=psc[:, 8:12], in1=tots[:, tt, :], op=ALU.add))

            stA(0); stB(0); stA(1); stB(1); stC(0); stD(0); stA(2); stB(2); stC(1); stD(1); stA(3); stB(3); stC(2); stD(2); stC(3); stD(3)
            if "att" in parts:
                P.dma(lambda e, q=q, V_=V_: e.dma_start(out=c.vtok[q * TT:(q + 1) * TT, :].rearrange("(s p) c -> p s c", p=128), in_=V_[:]), reads=[VB_], writes=[VT[q]])
            if "pool" in parts:
                OP("pool", [xptB[4]], [xptB[0]], lambda e: e.tensor_copy(out=xpt[:, 0, :], in_=xpt[:, 4, :]))
                Y_, YB_ = yps[q % 2], ypsB[q % 2]
                for cc in range(2):
                    pp, ppB = ppool[cc]
                    OP("act", [ppB], [pldB[cc]], lambda e, pp=pp, cc=cc: e.activation(out=pld[:, cc, :], in_=pp[:, 0:TT], func=AF.Copy))
                    ps, pB = next_ps()
                    OP("pe", [pldB[cc], pwB], [pB], lambda e, ps=ps, cc=cc: e.matmul(ps[:, 0:TT], lhsT=pwb[:, cc, :], rhs=pld[:, cc, :], start=True, stop=True))
                    OP("dve", [pB, scB], [YB_], lambda e, ps=ps, cc=cc, Y_=Y_: e.tensor_scalar(out=Y_[:, cc, :], in0=ps[:, 0:TT], scalar1=scol[:, cc:cc + 1], scalar2=None, op0=ALU.mult))
                P.dma(lambda e, qs=qs, Y_=Y_: e.dma_start(out=ybrv[1][:, :, qs], in_=Y_[:]), reads=[YB_], writes=[c.YB[1][q]])
            if "ssm" in parts:
                for cc in range(2):
                    ps, pB = next_ps()
                    proj_fm(1028 + cc * 128, 128, ps, pB, slice(0, 128), xn, xnB)
                    OP("act", [pB], [uB[cc][q]], lambda e, ps=ps, cc=cc, q=q: e.activation(out=u_ssm[:, cc, 8 + q * TT:8 + (q + 1) * TT], in_=ps[:, 0:TT], func=AF.Copy))
            if "conv" in parts:
                Y_, YB_ = ycs[q % 2], ycsB[q % 2]
                for cc in range(2):
                    pcc, pccB = next_ps()
                    proj_fm(1540 + cc * 128, 128, pcc, pccB, slice(0, 128), xn, xnB)
                    pcx, pcxB = next_ps()
                    proj_fm(1796 + cc * 128, 128, pcx, pcxB, slice(0, 128), xn, xnB)
                    pcb, pcbB = next_ps()
                    proj_fm(1284 + cc * 128, 128, pcb, pcbB, slice(0, 128), xn, xnB)
                    a_, aB_ = ccs[cc], ccsB[cc]
                    b_, bB_ = cbs[cc], cbsB[cc]
                    z_, zB_ = zt[cc][q % 2], ztB[cc][q % 2]
                    zp_, zpB_ = zt[cc][(q + 1) % 2], ztB[cc][(q + 1) % 2]
                    y_, yB_ = y1[cc], y1B[cc]
                    OP("act", [pccB], [aB_], lambda e, pcc=pcc, a_=a_: e.activation(out=a_[:], in_=pcc[:, 0:TT], func=AF.Copy))
                    OP("act", [pcbB], [bB_], lambda e, pcb=pcb, b_=b_: e.activation(out=b_[:], in_=pcb[:, 0:TT], func=AF.Copy))
                    if q == 0:
                        OP("pool", [], [zB_], lambda e, z_=z_: e.memset(z_[:, 0:2], 0.0))
                    else:
                        OP("pool", [zpB_], [zB_], lambda e, z_=z_, zp_=zp_: e.tensor_copy(out=z_[:, 0:2], in_=zp_[:, TT:TT + 2]))
                    OP("dve", [pcxB, aB_], [zB_], lambda e, pcx=pcx, a_=a_, z_=z_: e.tensor_tensor(out=z_[:, 2:TT + 2], in0=pcx[:, 0:TT], in1=a_[:], op=ALU.mult))
                    OP("dve", [zB_, scB], [yB_], lambda e, z_=z_, y_=y_, cc=cc: e.tensor_scalar(out=y_[:], in0=z_[:, 0:TT], scalar1=scol[:, 2 + cc:3 + cc], scalar2=None, op0=ALU.mult))
                    OP("dve", [zB_, scB, yB_], [yB_], lambda e, z_=z_, y_=y_, cc=cc: e.scalar_tensor_tensor(out=y_[:], in0=z_[:, 1:TT + 1], scalar=scol[:, 4 + cc:5 + cc], in1=y_[:], op0=ALU.mult, op1=ALU.add))
                    OP("dve", [zB_, scB, yB_], [yB_], lambda e, z_=z_, y_=y_, cc=cc: e.scalar_tensor_tensor(out=y_[:], in0=z_[:, 2:TT + 2], scalar=scol[:, 6 + cc:7 + cc], in1=y_[:], op0=ALU.mult, op1=ALU.add))
                    OP("pool", [yB_, bB_], [YB_], lambda e, y_=y_, b_=b_, Y_=Y_, cc=cc: e.tensor_tensor(out=Y_[:, cc, :], in0=y_[:], in1=b_[:], op=ALU.mult))
                P.dma(lambda e, qs=qs, Y_=Y_: e.dma_start(out=ybrv[3][:, :, qs], in_=Y_[:]), reads=[YB_], writes=[c.YB[3][q]])


def mixer_m3(c, l, k_aug, kB, kcB, cabs, cabsB, tots, totsB, QA, VT, ybrv, drain=lambda n: None):
    P, OP, sb, nc = c.P, c.OP, c.sb, c.nc
    psb, psB = c.psb, c.psB
    with contextlib.ExitStack() as st:
        V_aug = sb(st, "V_aug", [128, 32, 4, 2, 64], BF16)
        VB = [Buf() for _ in range(NQ)]
        VoB = Buf()
        OP("pool", [], [VoB], lambda e: e.memset(V_aug[:, :, :, 1, :], 1.0))
        for q in range(c.nq):
            for i4 in range(4):
                i = q * 4 + i4
                P.dma(lambda e, i=i: e.dma_start(out=V_aug[:, i, :, 0, :], in_=c.vtok[i * 128:(i + 1) * 128, :].rearrange("p (h d) -> p h d", h=4)), reads=[VT[q]], writes=[VB[q]])
        qt = [sb(st, "m3_qt%d" % i, [128, 4, TT], BF16) for i in range(2)]
        qtB = [Buf() for _ in range(2)]
        Pt = [sb(st, "m3_P%d" % i, [128, TT], BF16) for i in range(3)]
        PtB = [Buf() for _ in range(3)]
        bq = [sb(st, "m3_bq%d" % i, [128, 32, 4], F32) for i in range(2)]
        bqB = [Buf() for _ in range(2)]
        rd = [sb(st, "m3_rd%d" % i, [64, TT], F32) for i in range(2)]
        rdB = [Buf() for _ in range(2)]
        yst = [sb(st, "m3_y%d" % i, [64, 4, TT], BF16) for i in range(2)]
        ystB = [Buf() for _ in range(2)]
        yav = c.ybr[0].rearrange("(h p) t -> p h t", p=64)
        kP = 0
        kS = 0
        kO = 0
        for q in range(c.nq):
            Q_, QB_ = qt[q % 2], qtB[q % 2]
            P.dma(lambda e, q=q, Q_=Q_: e.dma_start(out=Q_[:], in_=c.qaug[:, :, q * TT:(q + 1) * TT]), reads=[QA[q]], writes=[QB_])
            n = 4 * q + 4
            b_, bB_ = bq[q % 2], bqB[q % 2]
            OP("dve", cabsB[0:n] + [totsB[4 * q]], [bB_], lambda e, b_=b_, n=n, q=q: e.tensor_tensor(out=b_[:, 0:n, :], in0=cabs[:, 0:n, :], in1=tots[:, 4 * q, :].unsqueeze(1).to_broadcast([128, n, 4]), op=ALU.subtract))
            Y_, YB_ = yst[q % 2], ystB[q % 2]
            for h in range(4):
                po, poB = psb[4 + kO % 2], psB[4 + kO % 2]
                kO += 1
                def emit_S(i):
                    d = i - 4 * q
                    c0 = max(0, d) * 128
                    ps, pB = psb[i % 4], psB[i % 4]
                    OP("pe", [kB[h][i // 4], kcB, QB_], [pB], lambda e, ps=ps, i=i, c0=c0, d=d, h=h, Q_=Q_: e.matmul(ps[:, c0:TT], lhsT=k_aug[:, h, i * 128:(i + 1) * 128], rhs=Q_[:, h, c0:TT], start=True, stop=(d < 0)))
                    if d >= 0:
                        OP("pe", [c.Bcb], [pB], lambda e, ps=ps, c0=c0: e.matmul(ps[:, c0:c0 + 128], lhsT=c.ident_b, rhs=c.mneg_b, start=False, stop=True))
                    return ps, pB, c0

                nxt = emit_S(0)
                for i in range(n):
                    ps, pB, c0 = nxt
                    if i + 1 < n:
                        nxt = emit_S(i + 1)
                    p_, pB_ = Pt[kP % 3], PtB[kP % 3]
                    kP += 1
                    OP("act", [pB, bB_], [pB_], lambda e, ps=ps, p_=p_, c0=c0, i=i, b_=b_, h=h: e.activation(out=p_[:, c0:TT], in_=ps[:, c0:TT], func=AF.Exp, bias=b_[:, i, h:h + 1]))
                    OP("pe", [VB[i // 4], VoB, pB_], [poB], lambda e, p_=p_, c0=c0, i=i, po=po, h=h, n=n: e.matmul(po[:, c0:TT], lhsT=V_aug[:, i, h, :, :].rearrange("p a b -> p (a b)"), rhs=p_[:, c0:TT], start=(i == 0), stop=(i == n - 1)))
                r_, rB_ = rd[h % 2], rdB[h % 2]
                OP("dve", [poB], [rB_], lambda e, po=po, r_=r_: e.reciprocal(out=r_[0:64, :], in_=po[64:128, 0:TT]))
                OP("dve", [poB, rB_], [YB_], lambda e, po=po, r_=r_, h=h, Y_=Y_: e.tensor_tensor(out=Y_[0:64, h, :], in0=po[0:64, 0:TT], in1=r_[0:64, :], op=ALU.mult))
                drain(12)
            P.dma(lambda e, q=q, Y_=Y_: e.dma_start(out=yav[:, :, q * TT:(q + 1) * TT], in_=Y_[:]), reads=[YB_], writes=[c.YB[0][q]])


def ssm_param_load(c, l, stack):
    P, sb = c.P, c.sb

    def mk_(name, shape, dt=F32):
        return sb(stack, "spl_" + name, shape, dt), Buf()
    lamS, lamSB = mk_("lamS", [128, 2, 8])
    for ri, src in enumerate((c.lam_re, c.lam_im)):
        P.dma(lambda e, ri=ri, src=src: e.dma_start(out=lamS[:, ri, :], in_=src[l].rearrange("(j g) p -> (g p) j", g=2), allow_slow_non_contiguous=True), writes=[lamSB])
    ldtS, ldtSB = mk_("ldtS", [128, 8])
    for g in range(2):
        P.dma(lambda e, g=g: e.dma_start(out=ldtS[g * 64:(g + 1) * 64, :], in_=c.log_dt[l].rearrange("(j g) -> g j", g=2)[g:g + 1, :].partition_broadcast(64), allow_slow_non_contiguous=True), writes=[ldtSB])
    BS, BSB = mk_("BS", [128, 2, 8, 16])
    for ri, src in enumerate((c.b_re, c.b_im)):
        P.dma(lambda e, ri=ri, src=src: e.dma_start(out=BS[:, ri, :, :], in_=src[l].rearrange("(j g) p h -> (g p) j h", g=2)), writes=[BSB])
    Csrc, CsB = mk_("Csrc", [128, 2, 128])
    for ri, src in enumerate((c.c_re, c.c_im)):
        for j in range(8):
            P.dma(lambda e, ri=ri, src=src, j=j: e.dma_start(out=Csrc[16 * j:16 * j + 16, ri, :].rearrange("h (g p) -> h g p", g=2), in_=src[l, 2 * j:2 * j + 2].rearrange("g h p -> h g p")), writes=[CsB])
    dcol, dcB = mk_("dcol", [128, 2])
    P.dma(lambda e: e.dma_start(out=dcol[:], in_=c.ssm_d[l].rearrange("(c p) -> p c", p=128), allow_slow_non_contiguous=True), writes=[dcB])
    Bsrc, BsB = mk_("Bsrc", [128, 2, 2, 4, 2, 16])
    for ri, src in enumerate((c.b_re, c.b_im)):
        for hf in range(2):
            for g2p in range(2):
                P.dma(lambda e, ri=ri, src=src, hf=hf, g2p=g2p: e.dma_start(out=Bsrc[:, ri, hf, :, g2p, :], in_=src[l, 8 * hf:8 * hf + 8].rearrange("(j g) p h -> (g p) j h", g=2)), writes=[BsB])
    return SimpleNamespace(lamS=lamS, lamSB=lamSB, ldtS=ldtS, ldtSB=ldtSB, BS=BS, BSB=BSB, Csrc=Csrc, CsB=CsB, dcol=dcol, dcB=dcB,
                           Bsrc=Bsrc, BsB=BsB)


def ssm_s_alloc(c, stack):
    sb = c.sb
    names = ["lr", "dt", "th", "sn", "cs", "lg", "mag", "ar", "ai", "t1", "t2", "t3", "den", "am1", "cr", "ci", "p0r", "p0i"]
    names += ["p%d%s" % (n, x) for n in range(2, 9) for x in "ri"]
    tl = {nm: sb(stack, "ppS_%s" % nm, [128, 8], F32)[:] for nm in names}
    hp = sb(stack, "ssc_halfpi", [128, 1], F32)
    LV = sb(stack, "ssm_LV", [128, 3, 9, 8], F32)
    return SimpleNamespace(hp=hp, LV=LV, tl=tl)


def ssm_s_chain(c, l, pre, spar):
    P, sb = c.P, c.sb
    dq = []
    OP = lambda eng, rd, wr, fn: dq.append((eng, list(rd), list(wr), fn))
    mul, add, sub = ALU.mult, ALU.add, ALU.subtract
    lamS, lamSB, ldtS, ldtSB = spar.lamS, spar.lamSB, spar.ldtS, spar.ldtSB

    def TT_(eng, out, a, b, op, rd, wr):
        OP(eng, rd, wr, lambda e: e.tensor_tensor(out=out, in0=a, in1=b, op=op))
    hp, LV, pre_tl = pre.hp, pre.LV, pre.tl
    hpB = Buf()
    OP("pool", [], [hpB], lambda e: e.memset(hp[:], float(np.pi / 2)))
    LVB = Buf()
    def cpow_prep(tag, F, lr_in, li, ldt_ap, deps, eng):
        tl = pre_tl

        def t_(nm):
            return tl[nm]
        B_ = Buf()
        D = list(deps) + [B_]
        OP(eng, D, [B_], lambda e: e.tensor_scalar(out=t_("lr"), in0=lr_in, scalar1=-1e-4, scalar2=None, op0=ALU.min))
        OP(eng, D, [B_], lambda e: e.tensor_copy(out=t_("dt"), in_=ldt_ap))
        OP("act", [B_], [B_], lambda e: e.activation(out=t_("dt"), in_=t_("dt"), func=AF.Exp))
        TT_(eng, t_("th"), li, t_("dt"), mul, D, [B_])
        OP("act", [B_], [B_], lambda e: e.activation(out=t_("sn"), in_=t_("th"), func=AF.Sin, scale=1.0 / 32))
        OP("act", [B_, hpB], [B_], lambda e: e.activation(out=t_("cs"), in_=t_("th"), func=AF.Sin, scale=1.0 / 32, bias=hp[:, 0:1]))
        TT_(eng, t_("lg"), t_("lr"), t_("dt"), mul, [B_], [B_])
        OP("act", [B_], [B_], lambda e: e.activation(out=t_("mag"), in_=t_("lg"), func=AF.Exp, scale=1.0 / 32))
        TT_(eng, t_("ar"), t_("mag"), t_("cs"), mul, [B_], [B_])
        TT_(eng, t_("ai"), t_("mag"), t_("sn"), mul, [B_], [B_])
        for _ in range(5):
            TT_(eng, t_("t1"), t_("ar"), t_("ar"), mul, [B_], [B_])
            TT_(eng, t_("t2"), t_("ai"), t_("ai"), mul, [B_], [B_])
            TT_(eng, t_("t3"), t_("ar"), t_("ai"), mul, [B_], [B_])
            TT_(eng, t_("ar"), t_("t1"), t_("t2"), sub, [B_], [B_])
            OP(eng, [B_], [B_], lambda e: e.tensor_scalar(out=t_("ai"), in0=t_("t3"), scalar1=2.0, scalar2=None, op0=mul))
        TT_(eng, t_("t1"), t_("lr"), t_("lr"), mul, [B_], [B_])
        TT_(eng, t_("t2"), li, li, mul, D, [B_])
        TT_(eng, t_("den"), t_("t1"), t_("t2"), add, [B_], [B_])
        OP("dve", [B_], [B_], lambda e: e.reciprocal(out=t_("den"), in_=t_("den")))
        OP(eng, [B_], [B_], lambda e: e.tensor_scalar(out=t_("am1"), in0=t_("ar"), scalar1=-1.0, scalar2=None, op0=add))
        TT_(eng, t_("t1"), t_("am1"), t_("lr"), mul, [B_], [B_])
        TT_(eng, t_("t2"), t_("ai"), li, mul, D, [B_])
        TT_(eng, t_("t3"), t_("t1"), t_("t2"), add, [B_], [B_])
        TT_(eng, t_("cr"), t_("t3"), t_("den"), mul, [B_], [B_])
        TT_(eng, t_("t1"), t_("ai"), t_("lr"), mul, [B_], [B_])
        TT_(eng, t_("t2"), t_("am1"), li, mul, D, [B_])
        TT_(eng, t_("t3"), t_("t1"), t_("t2"), sub, [B_], [B_])
        TT_(eng, t_("ci"), t_("t3"), t_("den"), mul, [B_], [B_])
        pw = []
        OP(eng, [B_], [B_], lambda e: e.memset(t_("p0r"), 1.0))
        OP(eng, [B_], [B_], lambda e: e.memset(t_("p0i"), 0.0))
        pw.append((t_("p0r"), t_("p0i")))
        pw.append((t_("ar"), t_("ai")))
        for n in range(2, 9):
            pr, pi = pw[-1]
            nr, ni = t_("p%dr" % n), t_("p%di" % n)
            cmul(nr, ni, pr, pi, t_("ar"), t_("ai"), t_("t1"), t_("t2"), [B_], [B_], eng)
            pw.append((nr, ni))
        return pw, (t_("cr"), t_("ci")), B_, t_

    def cmul(o_r, o_i, a_r, a_i, b_r, b_i, t1, t2, rd, wr, eng="dve"):
        TT_(eng, t1, a_r, b_r, mul, rd, wr)
        TT_(eng, t2, a_i, b_i, mul, rd, wr)
        TT_(eng, o_r, t1, t2, sub, rd, wr)
        TT_(eng, t1, a_r, b_i, mul, rd, wr)
        TT_(eng, t2, a_i, b_r, mul, rd, wr)
        TT_(eng, o_i, t1, t2, add, rd, wr)

    pwS, (crS, ciS), BS_, tS = cpow_prep("S", 8, lamS[:, 0, :], lamS[:, 1, :], ldtS[:], [lamSB, ldtSB], "dve")
    OP("dve", [BS_], [LVB], lambda e: e.tensor_copy(out=LV[:, 0, 0, :], in_=pwS[8][0]))
    OP("dve", [BS_], [LVB], lambda e: e.tensor_copy(out=LV[:, 1, 0, :], in_=pwS[8][1]))
    for k in range(1, 9):
        TT_("dve", tS("t1"), LV[:, 0, k - 1, :], LV[:, 0, k - 1, :], mul, [LVB, BS_], [BS_])
        TT_("dve", tS("t2"), LV[:, 1, k - 1, :], LV[:, 1, k - 1, :], mul, [LVB, BS_], [BS_])
        TT_("dve", tS("t3"), LV[:, 0, k - 1, :], LV[:, 1, k - 1, :], mul, [LVB, BS_], [BS_])
        TT_("dve", LV[:, 0, k, :], tS("t1"), tS("t2"), sub, [BS_], [LVB])
        OP("dve", [BS_], [LVB], lambda e, k=k: e.tensor_scalar(out=LV[:, 1, k, :], in0=tS("t3"), scalar1=2.0, scalar2=None, op0=mul))
    OP("dve", [LVB], [LVB], lambda e: e.tensor_scalar(out=LV[:, 2, :, :], in0=LV[:, 1, :, :], scalar1=-1.0, scalar2=None, op0=mul))
    return SimpleNamespace(pwS=pwS, crS=crS, ciS=ciS, BS_=BS_, tS=tS, LV=LV, LVB=LVB, cmul=cmul, dq=dq)


def mixer_m2(c, l, u_ssm, uB, ybrv, spar, sch):
    P, OP, sb, nc, next_ps = c.P, c.OP, c.sb, c.nc, c.next_ps
    mul, add, sub = ALU.mult, ALU.add, ALU.subtract
    NCH = T // 8

    def TT_(eng, out, a, b, op, rd, wr):
        OP(eng, rd, wr, lambda e: e.tensor_tensor(out=out, in0=a, in1=b, op=op))

    with contextlib.ExitStack() as st:
        WZ = sb(st, "ssm_WZ", [128, 8, 8, 2, 128], BF16)
        CA = sb(st, "ssm_CA", [128, 8, 9, 2, 128], BF16)
        BD = sb(st, "ssm_BD", [128, 2, 8, 128], BF16)
        LV, LVB = sch.LV, sch.LVB
        WZB, CAB, BDB = [Buf(), Buf()], Buf(), Buf()
        with contextlib.nullcontext():
            sp = st

            def mk_(name, shape, dt=F32):
                return sb(sp, "sp_" + name, shape, dt), Buf()
            mk, mkB = mk_("mk", [128, 8, 128])
            P.dma(lambda e: e.dma_start(out=mk[:], in_=c.consts[:, C_MH:C_MH + 1024].rearrange("p (a b) -> p a b", a=8)), writes=[mkB])
            msall, msB = mk_("msall", [128, 8, 128])
            for j in range(8):
                OP("pool", [mkB], [msB], lambda e, j=j: e.tensor_copy(out=msall[:, j, :], in_=mk[:, 4 + j % 4, :]))
            lamS, lamSB, ldtS, ldtSB, BS, BSB, Csrc, CsB = spar.lamS, spar.lamSB, spar.ldtS, spar.ldtSB, spar.BS, spar.BSB, spar.Csrc, spar.CsB
            dcol, dcB, Bsrc, BsB = spar.dcol, spar.dcB, spar.Bsrc, spar.BsB
            CT, CTB = mk_("CT", [128, 2, 128])
            ps, pB = next_ps()
            for ri in range(2):
                OP("pe", [CsB, c.Bc], [pB], lambda e, ps=ps, ri=ri: e.transpose(out=ps[:, ri * 128:(ri + 1) * 128], in_=Csrc[:, ri, :], identity=c.ident_f))
            OP("act", [pB], [CTB], lambda e, ps=ps: e.activation(out=CT[:], in_=ps[:, 0:256].rearrange("p (a b) -> p a b", a=2), func=AF.Copy))

            pwS, crS, ciS, BS_, tS = sch.pwS, sch.crS, sch.ciS, sch.BS_, sch.tS

            def cmul(o_r, o_i, a_r, a_i, b_r, b_i, t1, t2, rd, wr, eng="dve"):
                TT_(eng, t1, a_r, b_r, mul, rd, wr)
                TT_(eng, t2, a_i, b_i, mul, rd, wr)
                TT_(eng, o_r, t1, t2, sub, rd, wr)
                TT_(eng, t1, a_r, b_i, mul, rd, wr)
                TT_(eng, t2, a_i, b_r, mul, rd, wr)
                TT_(eng, o_i, t1, t2, add, rd, wr)
            cw1, cw1B = mk_("cw1", [128, 8, 16])
            cw2, cw2B = mk_("cw2", [128, 8, 16])
            cw3, cw3B = mk_("cw3", [128, 8, 16])
            CTr = CT[:, 0, :].rearrange("p (j h) -> p j h", j=8)
            CTi = CT[:, 1, :].rearrange("p (j h) -> p j h", j=8)
            bc16 = lambda ap: ap.unsqueeze(2).to_broadcast([128, 8, 16])
            msv = msall[:].rearrange("p j (a h) -> p j a h", a=8)
            for n in range(9):
                pr, pi = pwS[n]
                TT_("dve", cw1[:], CTr, bc16(pr), mul, [CTB, BS_, cw1B], [cw1B])
                TT_("dve", cw2[:], CTi, bc16(pi), mul, [CTB, BS_, cw2B], [cw2B])
                TT_("dve", cw3[:], cw1[:], cw2[:], sub, [cw1B, cw2B, cw3B], [cw3B])
                OP("dve", [cw3B, msB, CAB], [CAB], lambda e, n=n: e.tensor_tensor(out=CA[:, :, n, 0, :].rearrange("p j (a h) -> p j a h", a=8), in0=cw3[:].unsqueeze(2).to_broadcast([128, 8, 8, 16]), in1=msv, op=mul))
                TT_("dve", cw1[:], CTr, bc16(pi), mul, [CTB, BS_, cw1B], [cw1B])
                TT_("dve", cw2[:], CTi, bc16(pr), mul, [CTB, BS_, cw2B], [cw2B])
                OP("dve", [cw1B, cw2B, cw3B], [cw3B], lambda e: e.scalar_tensor_tensor(out=cw3[:], in0=cw1[:], scalar=-1.0, in1=cw2[:], op0=mul, op1=sub))
                OP("dve", [cw3B, msB, CAB], [CAB], lambda e, n=n: e.tensor_tensor(out=CA[:, :, n, 1, :].rearrange("p j (a h) -> p j a h", a=8), in0=cw3[:].unsqueeze(2).to_broadcast([128, 8, 8, 16]), in1=msv, op=mul))
            cB, cBB = mk_("cB", [128, 8, 2, 32], BF16)
            m2 = mk[:, 4, 0:32].rearrange("p (a h) -> p a h", a=2).unsqueeze(1).to_broadcast([128, 8, 2, 16])
            BSr, BSi = BS[:, 0, :, :], BS[:, 1, :, :]
            TT_("dve", cw1[:], BSr, bc16(crS), mul, [BSB, BS_, cw1B], [cw1B])
            TT_("dve", cw2[:], BSi, bc16(ciS), mul, [BSB, BS_, cw2B], [cw2B])
            TT_("dve", cw3[:], cw1[:], cw2[:], sub, [cw1B, cw2B, cw3B], [cw3B])
            OP("dve", [cw3B, mkB], [cBB], lambda e: e.tensor_tensor(out=cB[:, :, 0, :].rearrange("p j (a h) -> p j a h", a=2), in0=cw3[:].unsqueeze(2).to_broadcast([128, 8, 2, 16]), in1=m2, op=mul))
            TT_("dve", cw1[:], BSr, bc16(ciS), mul, [BSB, BS_, cw1B], [cw1B])
            TT_("dve", cw2[:], BSi, bc16(crS), mul, [BSB, BS_, cw2B], [cw2B])
            TT_("dve", cw3[:], cw1[:], cw2[:], add, [cw1B, cw2B, cw3B], [cw3B])
            OP("dve", [cw3B, mkB], [cBB], lambda e: e.tensor_tensor(out=cB[:, :, 1, :].rearrange("p j (a h) -> p j a h", a=2), in0=cw3[:].unsqueeze(2).to_broadcast([128, 8, 2, 16]), in1=m2, op=mul))
            for hf in range(2):
                for tau in range(8):
                    ps, pB = next_ps()
                    for j4 in range(4):
                        j = 4 * hf + j4
                        OP("pe", [cBB, CAB], [pB], lambda e, ps=ps, j=j, j4=j4, tau=tau: e.matmul(ps[32 * j4:32 * j4 + 32, 0:128], lhsT=cB[:, j, 0, :], rhs=CA[:, j, tau, 0, :], start=True, stop=False, tile_position=(0, 32 * j4)))
                        OP("pe", [cBB, CAB], [pB], lambda e, ps=ps, j=j, j4=j4, tau=tau: e.matmul(ps[32 * j4:32 * j4 + 32, 0:128], lhsT=cB[:, j, 1, :], rhs=CA[:, j, tau, 1, :], start=False, stop=True, tile_position=(0, 32 * j4)))
                    if tau == 0:
                        OP("dve", [pB, dcB, c.Bc], [BDB], lambda e, ps=ps, hf=hf: e.scalar_tensor_tensor(out=BD[:, hf, 0, :], in0=c.ident_f, scalar=dcol[:, hf:hf + 1], in1=ps[:, 0:128], op0=mul, op1=add))
                    else:
                        OP("act", [pB], [BDB], lambda e, ps=ps, hf=hf, tau=tau: e.activation(out=BD[:, hf, tau, :], in_=ps[:, 0:128], func=AF.Copy))
            Gt = [mk_("Gt%d" % i, [128, 2, 8]) for i in range(2)]
            Ws = [mk_("Ws%d" % i, [128, 2, 2, 128]) for i in range(2)]
            wt1, wt1B = mk_("wt1", [128, 2, 128])
            wt2, wt2B = mk_("wt2", [128, 2, 128])
            v4 = lambda ap: ap.rearrange("p a (b c) -> p a b c", b=4)
            Brv = Bsrc[:, 0].rearrange("p a b c d -> p a b (c d)")
            Biv = Bsrc[:, 1].rearrange("p a b c d -> p a b (c d)")
            gb = lambda ap: ap.rearrange("p (a b) -> p a b", a=2).unsqueeze(3).to_broadcast([128, 2, 4, 32])
            mh = mk[:, 0:4, :]
            for tau in range(8):
                G_, GB_ = Gt[tau % 2]
                if tau == 0:
                    OP("dve", [BS_, GB_], [GB_], lambda e, G_=G_: e.tensor_copy(out=G_[:, 0, :], in_=crS))
                    OP("dve", [BS_, GB_], [GB_], lambda e, G_=G_: e.tensor_copy(out=G_[:, 1, :], in_=ciS))
                else:
                    Gp, GpB = Gt[(tau - 1) % 2]
                    cmul(G_[:, 0, :], G_[:, 1, :], Gp[:, 0, :], Gp[:, 1, :], pwS[1][0], pwS[1][1], tS("t1"), tS("t2"), [BS_, GpB, GB_], [BS_, GB_])
                W_, WB_ = Ws[tau % 2]
                TT_("dve", v4(wt1[:]), Brv, gb(G_[:, 0, :]), mul, [BsB, GB_, wt1B], [wt1B])
                TT_("dve", v4(wt2[:]), Biv, gb(G_[:, 1, :]), mul, [BsB, GB_, wt2B], [wt2B])
                TT_("dve", W_[:, 0, :, :], wt1[:], wt2[:], sub, [wt1B, wt2B, WB_], [WB_])
                TT_("dve", v4(wt1[:]), Biv, gb(G_[:, 0, :]), mul, [BsB, GB_, wt1B], [wt1B])
                TT_("dve", v4(wt2[:]), Brv, gb(G_[:, 1, :]), mul, [BsB, GB_, wt2B], [wt2B])
                TT_("dve", W_[:, 1, :, :], wt1[:], wt2[:], add, [wt1B, wt2B, WB_], [WB_])
                ps, pB = next_ps()
                for ri in range(2):
                    for hf in range(2):
                        k4 = ri * 2 + hf
                        OP("pe", [WB_, c.Bc], [pB], lambda e, ps=ps, ri=ri, hf=hf, k4=k4, W_=W_: e.transpose(out=ps[:, k4 * 128:(k4 + 1) * 128], in_=W_[:, ri, hf, :], identity=c.ident_f))
                for ri in range(2):
                    for hf in range(2):
                        k4 = ri * 2 + hf
                        OP("dve", [pB, mkB, WZB[hf]], [WZB[hf]], lambda e, ps=ps, ri=ri, hf=hf, k4=k4, tau=tau: e.tensor_tensor(out=WZ[:, 4 * hf:4 * hf + 4, tau, ri, :], in0=ps[:, k4 * 128:(k4 + 1) * 128].unsqueeze(1).to_broadcast([128, 4, 128]), in1=mh, op=mul))
        with contextlib.nullcontext():
            sr = st
            Xb = [[[sb(sr, "X%d%d%d" % (s_, ri, ab), [128, 256 + NCH], F32) for ab in range(2)] for ri in range(2)] for s_ in range(2)]
            XB = [[[Buf() for ab in range(2)] for ri in range(2)] for s_ in range(2)]
            for s_ in range(2):
                for ri in range(2):
                    for ab in range(2):
                        OP("pool", [], [XB[s_][ri][ab]], lambda e, t=Xb[s_][ri][ab]: e.memset(t[:, 0:256], 0.0))
            Xp = sb(sr, "Xp", [128, 8, 2, NCH], BF16)
            XpB = [Buf() for _ in range(8)]
            ysT = sb(sr, "ysT", [128, 2, T], BF16)
            ysB = [Buf() for _ in range(2)]
            wgl = sb(sr, "wgl", [128, 2, 512], BF16)
            wglB = [Buf()]
            c.wload(wgl[:], c.w_glu[l].rearrange("(kc p) n -> p kc n", p=128), wglB)
            sg = [sb(sr, "s_sg%d" % i, [128, TT], F32) for i in range(2)]
            sgB = [Buf() for _ in range(2)]
            yst = [sb(sr, "s_yst%d" % i, [128, 2, TT], BF16) for i in range(2)]
            ystB = [Buf() for _ in range(2)]
            for j in range(8):
                hf = j // 4
                s_ = j % 2
                for ri in range(2):
                    ps, pB = next_ps()
                    for tau in range(8):
                        OP("pe", [WZB[hf]] + uB[hf], [pB], lambda e, ps=ps, j=j, tau=tau, ri=ri, hf=hf: e.matmul(ps[:, 0:NCH], lhsT=WZ[:, j, tau, ri, :], rhs=u_ssm[:, hf, 15 - tau:15 - tau + (NCH - 1) * 8 + 1:8], start=(tau == 0), stop=(tau == 7)))
                    OP("act", [pB], [XB[s_][ri][0]], lambda e, ps=ps, t=Xb[s_][ri][0]: e.activation(out=t[:, 256:256 + NCH], in_=ps[:, 0:NCH], func=AF.Copy))
                cur = 0
                for k in range(9):
                    sh = 1 << k
                    sr_, si_ = Xb[s_][0][cur], Xb[s_][1][cur]
                    dr_, di_ = Xb[s_][0][1 - cur], Xb[s_][1][1 - cur]
                    sBr, sBi = XB[s_][0][cur], XB[s_][1][cur]
                    dBr, dBi = XB[s_][0][1 - cur], XB[s_][1][1 - cur]
                    lo, hi = 256 - sh, 256 + NCH - sh
                    OP("dve", [sBr, LVB], [dBr], lambda e, sr_=sr_, dr_=dr_, lo=lo, hi=hi, k=k, j=j: e.scalar_tensor_tensor(out=dr_[:, 256:256 + NCH], in0=sr_[:, lo:hi], scalar=LV[:, 0, k, j:j + 1], in1=sr_[:, 256:256 + NCH], op0=mul, op1=add))
                    OP("dve", [sBi, LVB, dBr], [dBr], lambda e, si_=si_, dr_=dr_, lo=lo, hi=hi, k=k, j=j: e.scalar_tensor_tensor(out=dr_[:, 256:256 + NCH], in0=si_[:, lo:hi], scalar=LV[:, 2, k, j:j + 1], in1=dr_[:, 256:256 + NCH], op0=mul, op1=add))
                    OP("dve", [sBr, sBi, LVB], [dBi], lambda e, sr_=sr_, si_=si_, di_=di_, lo=lo, hi=hi, k=k, j=j: e.scalar_tensor_tensor(out=di_[:, 256:256 + NCH], in0=sr_[:, lo:hi], scalar=LV[:, 1, k, j:j + 1], in1=si_[:, 256:256 + NCH], op0=mul, op1=add))
                    OP("dve", [sBi, LVB, dBi], [dBi], lambda e, si_=si_, di_=di_, lo=lo, hi=hi, k=k, j=j: e.scalar_tensor_tensor(out=di_[:, 256:256 + NCH], in0=si_[:, lo:hi], scalar=LV[:, 0, k, j:j + 1], in1=di_[:, 256:256 + NCH], op0=mul, op1=add))
                    cur = 1 - cur
                for ri in range(2):
                    OP("act", [XB[s_][ri][cur]], [XpB[j]], lambda e, t=Xb[s_][ri][cur], j=j, ri=ri: e.activation(out=Xp[:, j, ri, :], in_=t[:, 255:255 + NCH], func=AF.Copy))
            kk = 0
            for hf in range(2):
                for s in range(8):
                    ps, pB = next_ps()
                    mms = []
                    for j4 in range(4):
                        j = 4 * hf + j4
                        for ri in range(2):
                            mms.append(([CAB, XpB[j]], CA[:, j, s + 1, ri, :], Xp[:, j, ri, :]))
                    for tau in range(s + 1):
                        mms.append(([BDB] + uB[hf], BD[:, hf, tau, :], u_ssm[:, hf, 8 + s - tau:8 + s - tau + (NCH - 1) * 8 + 1:8]))
                    for i, (rd, lh, rh) in enumerate(mms):
                        OP("pe", rd, [pB], lambda e, ps=ps, lh=lh, rh=rh, i=i, n=len(mms): e.matmul(ps[:, 0:NCH], lhsT=lh, rhs=rh, start=(i == 0), stop=(i == n - 1)))
                    if kk % 2 == 0:
                        OP("act", [pB], [ysB[hf]], lambda e, ps=ps, hf=hf, s=s: e.activation(out=ysT[:, hf, s:T:8], in_=ps[:, 0:NCH], func=AF.Copy))
                    else:
                        OP("dve", [pB], [ysB[hf]], lambda e, ps=ps, hf=hf, s=s: e.tensor_copy(out=ysT[:, hf, s:T:8], in_=ps[:, 0:NCH]))
                    kk += 1
            for q in range(NQ):
                qs = slice(q * TT, (q + 1) * TT)
                Y_, YB_ = yst[q % 2], ystB[q % 2]
                for cc in range(2):
                    pv, pvB = next_ps()
                    pg, pgB = next_ps()
                    for kc in range(2):
                        OP("pe", [wglB[0], ysB[kc]], [pvB], lambda e, pv=pv, kc=kc, cc=cc, qs=qs: e.matmul(pv[:, 0:TT], lhsT=wgl[:, kc, cc * 128:(cc + 1) * 128], rhs=ysT[:, kc, qs], start=(kc == 0), stop=(kc == 1)))
                    for kc in range(2):
                        OP("pe", [wglB[0], ysB[kc]], [pgB], lambda e, pg=pg, kc=kc, cc=cc, qs=qs: e.matmul(pg[:, 0:TT], lhsT=wgl[:, kc, 256 + cc * 128:256 + (cc + 1) * 128], rhs=ysT[:, kc, qs], start=(kc == 0), stop=(kc == 1)))
                    s2, s2B = sg[cc], sgB[cc]
                    OP("act", [pgB], [s2B], lambda e, pg=pg, s2=s2: e.activation(out=s2[:], in_=pg[:, 0:TT], func=AF.Sigmoid))
                    OP("dve", [pvB, s2B], [YB_], lambda e, pv=pv, s2=s2, Y_=Y_, cc=cc: e.tensor_tensor(out=Y_[:, cc, :], in0=pv[:, 0:TT], in1=s2[:], op=mul))
                P.dma(lambda e, qs=qs, Y_=Y_: e.dma_start(out=ybrv[2][:, :, qs], in_=Y_[:]), reads=[YB_], writes=[c.YB[2][q]])


def mixer_m4(c, l, parts, ybrv):
    P, OP, sb, nc, next_ps = c.P, c.OP, c.sb, c.nc, c.next_ps
    order = [b for b, nm in enumerate(("att", "pool", "ssm", "conv")) if nm in parts]
    with contextlib.ExitStack() as st:
        wgt = sb(st, "m4_wg", [128, 8, 4096], BF16)
        wgtB = [[Buf(), Buf()] for _ in range(4)]
        wiv = c.w_in[l].rearrange("(kc p) n -> p kc n", p=128)
        wbr = sb(st, "m4_wbr", [128, 4, 2, D], BF16)
        wbrB = [Buf() for _ in range(4)]
        for b in order:
            c.wload(wgt[:, :, b * 1024:b * 1024 + 256], wiv[:, :, 2052 + b * 1024:2052 + b * 1024 + 256], [wgtB[b][0]])
            c.wload(wbr[:, b, :, :], c.w_branch[l, b].rearrange("(kc p) n -> p kc n", p=128), [wbrB[b]])
        for b in order:
            c.wload(wgt[:, :, b * 1024 + 256:(b + 1) * 1024], wiv[:, :, 2052 + b * 1024 + 256:2052 + (b + 1) * 1024], [wgtB[b][1]])
        wo = sb(st, "m4_wo", [128, 8, D], BF16)
        woB = [Buf()]
        c.wload(wo[:], c.w_out[l].rearrange("(kc p) n -> p kc n", p=128), woB)
        xn = [sb(st, "m4_xn%d" % i, [128, 8, TT], BF16) for i in range(2)]
        xnB = [Buf() for _ in range(2)]
        ysb = [sb(st, "m4_ys%d" % i, [128, 4, 2, TT], BF16) for i in range(2)]
        ysB = [[Buf() for _ in range(4)] for _ in range(2)]
        m = sb(st, "m4_m", [128, 8, TT], F32)
        mB = [Buf() for _ in range(8)]
        mb = sb(st, "m4_mb", [128, 8, TT], BF16)
        mbB = [Buf() for _ in range(8)]
        sg = [sb(st, "m4_sg%d" % i, [128, TT], F32) for i in range(2)]
        sgB = [Buf() for _ in range(2)]
        tm = [sb(st, "m4_tm%d" % i, [128, TT], F32) for i in range(2)]
        tmB = [Buf() for _ in range(2)]
        hr = [sb(st, "m4_hr%d" % i, [128, TT], F32) for i in range(3)]
        hrB = [Buf() for _ in range(3)]
        k = 0
        kk = 0
        def m4_load(q):
            qs = slice(q * TT, (q + 1) * TT)
            X, XB = xn[q % 2], xnB[q % 2]
            P.dma(lambda e, qs=qs, X=X: e.dma_start(out=X[:], in_=c.xnTv[:, :, qs]), reads=[c.XN[q]], writes=[XB])
            Ys, YsB = ysb[q % 2], ysB[q % 2]
            for b in order:
                P.dma(lambda e, qs=qs, b=b, Ys=Ys: e.dma_start(out=Ys[:, b, :, :], in_=ybrv[b][:, :, qs]), reads=[c.YB[b][q]], writes=[YsB[b]])

        m4_load(0)
        for q in range(c.nq):
            qs = slice(q * TT, (q + 1) * TT)
            X, XB = xn[q % 2], xnB[q % 2]
            Ys, YsB = ysb[q % 2], ysB[q % 2]
            if q + 1 < c.nq:
                m4_load(q + 1)
            for dc in range(8):
                ds = slice(dc * 128, (dc + 1) * 128)
                for bi, b in enumerate(order):
                    pg, pgB = next_ps()
                    for kc in range(8):
                        OP("pe", [wgtB[b][0 if dc < 2 else 1], XB], [pgB], lambda e, pg=pg, kc=kc, b=b, dc=dc, X=X: e.matmul(pg[:, 0:TT], lhsT=wgt[:, kc, b * 1024 + dc * 128:b * 1024 + (dc + 1) * 128], rhs=X[:, kc, :], start=(kc == 0), stop=(kc == 7)))
                    pp, ppB = next_ps()
                    for kc in range(2):
                        OP("pe", [wbrB[b], YsB[b]], [ppB], lambda e, pp=pp, kc=kc, b=b, ds=ds, Ys=Ys: e.matmul(pp[:, 0:TT], lhsT=wbr[:, b, kc, ds], rhs=Ys[:, b, kc, :], start=(kc == 0), stop=(kc == 1)))
                    s_, sB_ = sg[kk % 2], sgB[kk % 2]
                    OP("act", [pgB], [sB_], lambda e, s_=s_, pg=pg: e.activation(out=s_[:], in_=pg[:, 0:TT], func=AF.Sigmoid))
                    last = (bi == len(order) - 1)
                    if bi == 0 and last:
                        OP("dve", [sB_, ppB], [mbB[dc]], lambda e, s_=s_, pp=pp, dc=dc: e.tensor_tensor(out=mb[:, dc, :], in0=pp[:, 0:TT], in1=s_[:], op=ALU.mult))
                    elif bi == 0:
                        OP("dve", [sB_, ppB], [mB[dc]], lambda e, s_=s_, pp=pp, dc=dc: e.tensor_tensor(out=m[:, dc, :], in0=pp[:, 0:TT], in1=s_[:], op=ALU.mult))
                    else:
                        t_, tB_ = tm[kk % 2], tmB[kk % 2]
                        OP("dve", [sB_, ppB], [tB_], lambda e, s_=s_, pp=pp, t_=t_: e.tensor_tensor(out=t_[:], in0=pp[:, 0:TT], in1=s_[:], op=ALU.mult))
                        if last:
                            OP("pool", [tB_, mB[dc]], [mbB[dc]], lambda e, t_=t_, dc=dc: e.tensor_tensor(out=mb[:, dc, :], in0=m[:, dc, :], in1=t_[:], op=ALU.add))
                        else:
                            OP("pool", [tB_, mB[dc]], [mB[dc]], lambda e, t_=t_, dc=dc: e.tensor_tensor(out=m[:, dc, :], in0=m[:, dc, :], in1=t_[:], op=ALU.add))
                    kk += 1
            for d2 in range(8):
                po, poB = next_ps()
                for dc in range(8):
                    OP("pe", [woB[0], mbB[dc]], [poB], lambda e, po=po, dc=dc, d2=d2: e.matmul(po[:, 0:TT], lhsT=wo[:, dc, d2 * 128:(d2 + 1) * 128], rhs=mb[:, dc, :], start=(dc == 0), stop=(dc == 7)))
                r, rB = hr[k % 3], hrB[k % 3]
                k += 1
                P.dma(lambda e, r=r, d2=d2, qs=qs: e.dma_start(out=r[:], in_=c.hTv[:, d2, qs]), reads=[c.HT[q][d2]], writes=[rB])
                OP("dve", [poB, rB], [rB], lambda e, r=r, po=po: e.tensor_tensor(out=r[:], in0=po[:, 0:TT], in1=r[:], op=ALU.add))
                P.dma(lambda e, r=r, d2=d2, qs=qs: e.dma_start(out=c.hTv[:, d2, qs], in_=r[:]), reads=[rB], writes=[c.HT[q][d2]])


_NAMES = ["norm_g", "ffn_w_gate", "ffn_w_up", "ffn_w_down", "w_in", "f_bias", "pool_w", "pool_scale", "ssm_lam_re",
          "ssm_lam_im", "ssm_log_dt", "ssm_b_re", "ssm_b_im", "ssm_c_re", "ssm_c_im", "ssm_d", "ssm_w_glu", "conv_w",
          "w_branch", "w_out", "ple_w_gate", "ple_w_proj", "final_g"]


def run(inputs, n_cores=8, ret_all=False, **bk):
    nc = build(**bk)
    cst = make_consts()
    shared = {k: np.ascontiguousarray(np.asarray(inputs[k], dtype=np.float32)) for k in _NAMES}
    xs = np.asarray(inputs["x"], dtype=np.float32)
    ps = np.asarray(inputs["p"], dtype=np.float32)
    in_maps = []
    for b in range(n_cores):
        m = dict(shared)
        m["x"] = np.ascontiguousarray(xs[b])
        m["p"] = np.ascontiguousarray(ps[:, b])
        m["consts"] = cst
        in_maps.append(m)
    res = run_bass_kernel_spmd(nc, in_maps, core_ids=list(range(n_cores)))
    if ret_all:
        return res.results
    return np.stack([np.asarray(r["y"]) for r in res.results], axis=0)


def kernel(**inputs):
    return run(inputs, n_cores=8).astype(np.float32)
```

```python
import contextlib
import numpy as np
import concourse.bass as bass
import concourse.mybir as mybir
from concourse.bass_utils import run_bass_kernel_spmd

F32 = mybir.dt.float32
BF16 = mybir.dt.bfloat16
AF = mybir.ActivationFunctionType
ALU = mybir.AluOpType

D = 1024
T = 4096
DEPTH = 2
DFF = 2816
NFC = DFF // 128
INC = 6148
PLE = 256
TT = 512
NQ = T // TT
EPS = 1e-6

COMPUTE = ("pe", "act", "dve", "pool")
NDMASEM = 56
NSP = 40


class Buf:
    __slots__ = ("name", "w", "r", "wl")

    def __init__(self, name=""):
        self.name = name
        self.w = None
        self.r = []
        self.wl = []


class Node:
    __slots__ = ("eng", "idx", "fn", "waits", "key", "val", "needs_inc", "clock", "is_dma")


class Prog:
    def __init__(self, nc):
        self.nc = nc
        self.ops = {e: [] for e in ("pe", "act", "dve", "pool", "sp")}
        self.clock = {e: {} for e in self.ops}
        self.dma_rr = 0
        self.dma_rr2 = 0
        self.dma_last = [None] * NDMASEM
        self.dma_cum = [0] * NDMASEM
        self.out_nodes = []

    def _record(self, eng, fn, reads, writes, is_dma, extra=(), shared=False):
        n = Node()
        n.eng = eng
        n.idx = len(self.ops[eng])
        n.fn = fn
        n.is_dma = is_dma
        n.needs_inc = False
        deps = []
        for b in reads:
            if b.w is not None:
                deps.append(b.w)
            deps.extend(b.wl)
        for b in writes:
            if not shared:
                if b.w is not None:
                    deps.append(b.w)
                deps.extend(b.wl)
            deps.extend(b.r)
        deps.extend(extra)
        if is_dma:
            if eng == "sp":
                s = self.dma_rr
                self.dma_rr = (self.dma_rr + 1) % NSP
            else:
                s = NSP + self.dma_rr2
                self.dma_rr2 = (self.dma_rr2 + 1) % (NDMASEM - NSP)
            if self.dma_last[s] is not None:
                deps.append(self.dma_last[s])
            self.dma_cum[s] += 16
            n.key = ("d", s)
            n.val = self.dma_cum[s]
            self.dma_last[s] = n
        else:
            n.key = eng
            n.val = n.idx + 1
        ck = self.clock[eng]
        waits = {}
        for d in deps:
            if (not d.is_dma) and d.eng == eng and eng == "pe":
                continue
            if ck.get(d.key, 0) >= d.val:
                continue
            if waits.get(d.key, (0, None))[0] < d.val:
                waits[d.key] = (d.val, d)
        n.waits = [w[1] for w in waits.values()]
        if n.waits:
            ck = dict(ck)
            for d in n.waits:
                d.needs_inc = True
                for k, v in d.clock.items():
                    if ck.get(k, 0) < v:
                        ck[k] = v
                if ck.get(d.key, 0) < d.val:
                    ck[d.key] = d.val
            self.clock[eng] = ck
        n.clock = ck
        self.ops[eng].append(n)
        for b in reads:
            b.r.append(n)
        for b in writes:
            if shared:
                b.wl.append(n)
            else:
                b.w = n
                b.wl = []
                b.r = []
        return n

    def op(self, eng, fn, reads=(), writes=()):
        return self._record(eng, fn, reads, writes, False)

    def dma(self, fn, reads=(), writes=(), q="sp", is_out=False, shared=False):
        n = self._record(q, fn, reads, writes, True, shared=shared)
        if is_out:
            self.out_nodes.append(n)
        return n

    def barrier(self):
        last = [self.ops[e][-1] for e in COMPUTE if self.ops[e]]
        for e in COMPUTE:
            for n in reversed(self.ops[e]):
                if not n.is_dma:
                    last.append(n)
                    break
        last += [d for d in self.dma_last if d is not None]
        for e in ("pe", "act", "dve", "pool", "sp"):
            self._record(e, lambda eng: eng.nop(), (), (), False, extra=last)

    def emit(self, es):
        nc = self.nc
        sems = {}
        for e in COMPUTE:
            sems[e] = es.enter_context(nc.semaphore("S_" + e))
        for i in range(NDMASEM):
            sems[("d", i)] = es.enter_context(nc.semaphore("D%d" % i))
        for e in COMPUTE:
            c = 0
            for n in self.ops[e]:
                if n.is_dma:
                    continue
                if n.needs_inc:
                    c += 1
                    n.val = c
                else:
                    n.val = None
        block = es.enter_context(nc.Block())

        def run(ename):
            def body(eng):
                for n in self.ops[ename]:
                    for d in n.waits:
                        eng.wait_ge(sems[d.key], d.val)
                    ins = n.fn(eng)
                    if n.is_dma:
                        ins.then_inc(sems[n.key], 16)
                    elif n.needs_inc:
                        ins.then_inc(sems[n.key], 1)
                if ename == "sp":
                    for d in self.dma_last:
                        if d is not None:
                            eng.wait_ge(sems[d.key], d.val)
            return body

        block.tensor(run("pe"))
        block.scalar(run("act"))
        block.vector(run("dve"))
        block.gpsimd(run("pool"))
        block.sync(run("sp"))


C_ID = 0
C_ONES = 128
C_TRIU = 256
C_MNEG = 384
C_MH = 512
C_MS = 1024
C_RCW = 1536
C_RC0 = 1538
C_PM = C_RC0 + 1024
C_N = C_PM + 12 * 128
C_G = 512


def make_consts():
    c = np.zeros((128, C_N), np.float32)
    k = np.arange(128)
    c[:, C_ID:C_ID + 128] = np.eye(128, dtype=np.float32)
    c[:, C_ONES:C_ONES + 128] = 1.0
    c[:, C_TRIU:C_TRIU + 128] = (k[:, None] <= k[None, :]).astype(np.float32)
    c[:, C_MNEG:C_MNEG + 128] = np.where(k[None, :] < k[:, None], -30000.0, 0.0)
    rj4, rg2 = k // 32, (k // 16) % 2
    cg2 = k // 64
    for j4 in range(4):
        c[:, C_MH + j4 * 128:C_MH + (j4 + 1) * 128] = ((rj4[:, None] == j4) & (rg2[:, None] == cg2[None, :])).astype(np.float32)
        c[:, C_MS + j4 * 128:C_MS + (j4 + 1) * 128] = ((cg2[:, None] == rg2[None, :]) & (rj4[None, :] == j4)).astype(np.float32)
    wins = np.array([2, 4, 8, 16], np.float32)
    for ch in range(2):
        w = wins[2 * ch + (k // 64)]
        c[:, C_RCW + ch] = 1.0 / w
        t = np.arange(512, dtype=np.float32)
        c[:, C_RC0 + ch * 512:C_RC0 + (ch + 1) * 512] = 1.0 / np.minimum(t[None, :] + 1.0, w[:, None])
    tp = k[:, None].astype(np.float64)
    tq = k[None, :].astype(np.float64)
    for g in range(4):
        W = float(wins[g])
        main = np.where((tp <= tq) & (tp > tq - W), 1.0 / W, 0.0) - np.eye(128)
        corner = np.where((tp - 128 > tq - W), 1.0 / W, 0.0)
        cnt = np.minimum(tq + 1.0, W)
        main0 = np.where((tp <= tq) & (tp > tq - W), 1.0 / cnt, 0.0) - np.eye(128)
        for i, mtx in enumerate((main, corner, main0)):
            o = C_PM + (g * 3 + i) * 128
            c[:, o:o + 128] = mtx.astype(np.float32)
    return c


def build(phases=("in", "ffn", "mix", "ple", "out"), depth=DEPTH, mix_parts=("att", "pool", "ssm", "conv"), debug=False, nq=NQ):
    nc = bass.Bass("TRN2", target_bir_lowering=False)
    dt_in = lambda name, shape: nc.dram_tensor(name, list(shape), F32, kind="ExternalInput").ap()
    x = dt_in("x", [T, D])
    p_in = dt_in("p", [DEPTH, T, PLE])
    norm_g = dt_in("norm_g", [DEPTH, 4, D])
    w_gate = dt_in("ffn_w_gate", [DEPTH, 2, D, DFF])
    w_up = dt_in("ffn_w_up", [DEPTH, 2, D, DFF])
    w_down = dt_in("ffn_w_down", [DEPTH, 2, DFF, D])
    w_in = dt_in("w_in", [DEPTH, D, INC])
    f_bias = dt_in("f_bias", [DEPTH, 4])
    pool_w = dt_in("pool_w", [DEPTH, 4, 64, 64])
    pool_scale = dt_in("pool_scale", [DEPTH, 256])
    lam_re = dt_in("ssm_lam_re", [DEPTH, 16, 64])
    lam_im = dt_in("ssm_lam_im", [DEPTH, 16, 64])
    log_dt = dt_in("ssm_log_dt", [DEPTH, 16])
    b_re = dt_in("ssm_b_re", [DEPTH, 16, 64, 16])
    b_im = dt_in("ssm_b_im", [DEPTH, 16, 64, 16])
    c_re = dt_in("ssm_c_re", [DEPTH, 16, 16, 64])
    c_im = dt_in("ssm_c_im", [DEPTH, 16, 16, 64])
    ssm_d = dt_in("ssm_d", [DEPTH, 256])
    w_glu = dt_in("ssm_w_glu", [DEPTH, 256, 512])
    conv_w = dt_in("conv_w", [DEPTH, 3, 256])
    w_branch = dt_in("w_branch", [DEPTH, 4, 256, D])
    w_out = dt_in("w_out", [DEPTH, D, D])
    ple_wg = dt_in("ple_w_gate", [DEPTH, D, D])
    ple_wp = dt_in("ple_w_proj", [DEPTH, PLE, D])
    final_g = dt_in("final_g", [D])
    consts = dt_in("consts", [128, C_N])
    y_out = nc.dram_tensor("y", [T, D], F32, kind="ExternalOutput").ap()

    dk = dict(kind="ExternalOutput") if debug else {}
    hT = nc.dram_tensor("hT_scr", [D, T], F32, **dk).ap()
    xnT = nc.dram_tensor("xnT_scr", [D, T], BF16, **dk).ap()
    ybr = nc.dram_tensor("ybr_scr", [4, 256, T], BF16, **dk).ap()
    qaug = nc.dram_tensor("qaug_scr", [128, 4, T], BF16, **dk).ap()
    vtok = nc.dram_tensor("vtok_scr", [T, 256], BF16, **dk).ap()
    hTv = hT.rearrange("(c p) t -> p c t", p=128)
    xnTv = xnT.rearrange("(c p) t -> p c t", p=128)

    es = contextlib.ExitStack()
    P = Prog(nc)
    OP = lambda eng, reads, writes, fn: P.op(eng, fn, reads, writes)

    uid = [0]

    def sb(stack, name, shape, dt):
        uid[0] += 1
        return stack.enter_context(nc.sbuf_tensor("%s_u%d" % (name, uid[0]), list(shape), dt))

    def dbg_dump(name, ap, shape, dt, reads):
        if not debug:
            return
        t = nc.dram_tensor("dbg_" + name, list(shape), dt, kind="ExternalOutput").ap()
        P.dma(lambda e: e.dma_start(out=t, in_=ap), reads=reads, writes=[Buf()])

    HT = [[Buf("hT%d_%d" % (q, c)) for c in range(8)] for q in range(NQ)]
    XN = [Buf("xnT%d" % q) for q in range(NQ)]
    YB = [[Buf("ybr%d_%d" % (b, q)) for q in range(NQ)] for b in range(4)]
    OUTB = Buf("out")

    psb = [es.enter_context(nc.psum_tensor("psb%d" % i, [128, 512], F32)) for i in range(8)]
    psB = [Buf("psb%d" % i) for i in range(8)]
    ps_rr = [0]

    def next_ps():
        i = ps_rr[0]
        ps_rr[0] = (i + 1) % 6
        return psb[i], psB[i]

    cst = sb(es, "cst", [128, 512], F32)
    cstb = sb(es, "cstb", [128, 512], BF16)
    Bc = Buf("cst")
    Bcb = Buf("cstb")
    P.dma(lambda e: e.dma_start(out=cst[:], in_=consts[:, 0:512]), writes=[Bc])
    OP("dve", [Bc], [Bcb], lambda e: e.tensor_copy(out=cstb[:], in_=cst[:, 0:512]))
    ident_f = cst[:, C_ID:C_ID + 128]
    ones_f = cst[:, C_ONES:C_ONES + 128]
    triu_f = cst[:, C_TRIU:C_TRIU + 128]
    ident_b = cstb[:, C_ID:C_ID + 128]
    ones_b = cstb[:, C_ONES:C_ONES + 128]
    mneg_b = cstb[:, C_MNEG:C_MNEG + 128]
    epsc = sb(es, "epsc", [128, 2], F32)
    Beps = Buf("eps")
    OP("dve", [], [Beps], lambda e: e.memset(epsc[:, 0:1], EPS))
    OP("dve", [Beps], [Beps], lambda e: e.memset(epsc[:, 1:2], 1.0))
    gcol = sb(es, "gcol", [128, 9, 8], F32)
    Bg = Buf("gcol")
    P.dma(lambda e: e.dma_start(out=gcol[:, 0:8, :], in_=norm_g.rearrange("l n (c p) -> p (l n) c", p=128), allow_slow_non_contiguous=True), writes=[Bg])
    P.dma(lambda e: e.dma_start(out=gcol[:, 8, :], in_=final_g.rearrange("(c p) -> p c", p=128), allow_slow_non_contiguous=True), writes=[Bg])

    def rmsnorm(h, hB, gi, xn, xnB, tmp):
        ps, pB = next_ps()
        for c in range(8):
            sq, sqB = tmp["sq"][c % 2]
            OP("act", [hB[c]], [sqB], lambda e, c=c, sq=sq: e.activation(out=sq, in_=h[:, c, :], func=AF.Square))
            OP("pe", [sqB, Bcb], [pB], lambda e, c=c, sq=sq, ps=ps: e.matmul(ps[:, 0:TT], lhsT=ones_b, rhs=sq, start=(c == 0), stop=(c == 7)))
        lnv, lnB = tmp["lnv"]
        rstd, rsB = tmp["rstd"]
        OP("act", [pB, Beps], [lnB], lambda e, ps=ps: e.activation(out=lnv, in_=ps[:, 0:TT], func=AF.Ln, scale=1.0 / D, bias=epsc[:, 0:1]))
        OP("act", [lnB], [rsB], lambda e: e.activation(out=rstd, in_=lnv, func=AF.Exp, scale=-0.5))
        for c in range(8):
            OP("dve", [hB[c], rsB, Bg], [xnB[c]],
               lambda e, c=c: e.scalar_tensor_tensor(out=xn[:, c, :], in0=h[:, c, :], scalar=gcol[:, gi, c:c + 1], in1=rstd,
                                                     op0=ALU.mult, op1=ALU.mult))

    def norm_tmp(stack, tag):
        sq0 = sb(stack, "sq0" + tag, [128, TT], BF16)
        sq1 = sb(stack, "sq1" + tag, [128, TT], BF16)
        lnv = sb(stack, "lnv" + tag, [128, TT], F32)
        rstd = sb(stack, "rstd" + tag, [128, TT], F32)
        return {"sq": [(sq0[:], Buf()), (sq1[:], Buf())], "lnv": (lnv[:], Buf()), "rstd": (rstd[:], Buf())}

    def load_h(tile, tB, q):
        P.dma(lambda e: e.dma_start(out=tile, in_=hTv[:, :, q * TT:(q + 1) * TT]), reads=HT[q], writes=tB)

    def wload(dst, src, wB):
        P.dma(lambda e: e.dma_start(out=dst, in_=src), writes=wB, q="pool")

    def phase_in():
        P.barrier()
        with contextlib.ExitStack() as st:
            xt = [sb(st, "xt%d" % i, [128, D], F32) for i in range(2)]
            xtB = [Buf() for _ in range(2)]
            ho = [sb(st, "ho%d" % i, [128, 8, TT], F32) for i in range(2)]
            hoB = [Buf() for _ in range(2)]
            k = 0
            for q in range(NQ):
                for s in range(4):
                    tt = q * 4 + s
                    xb, xB = xt[tt % 2], xtB[tt % 2]
                    P.dma(lambda e, xb=xb, tt=tt: e.dma_start(out=xb[:], in_=x[tt * 128:(tt + 1) * 128, :]), writes=[xB])
                    for half in range(2):
                        ps, pB = next_ps()
                        for cc in range(4):
                            c = half * 4 + cc
                            OP("pe", [xB, Bc], [pB], lambda e, ps=ps, cc=cc, c=c, xb=xb: e.transpose(out=ps[:, cc * 128:(cc + 1) * 128], in_=xb[:, c * 128:(c + 1) * 128], identity=ident_f))
                        eng = "act" if (k % 2 == 0) else "dve"
                        k += 1
                        dst = ho[q % 2][:, half * 4:(half + 1) * 4, s * 128:(s + 1) * 128]
                        src = ps[:, :].rearrange("p (c t) -> p c t", c=4)
                        if eng == "act":
                            OP("act", [pB], [hoB[q % 2]], lambda e, dst=dst, src=src: e.activation(out=dst, in_=src, func=AF.Copy))
                        else:
                            OP("dve", [pB], [hoB[q % 2]], lambda e, dst=dst, src=src: e.tensor_copy(out=dst, in_=src))
                P.dma(lambda e, q=q: e.dma_start(out=hTv[:, :, q * TT:(q + 1) * TT], in_=ho[q % 2][:]), reads=[hoB[q % 2]], writes=HT[q])

    def phase_out():
        P.barrier()
        with contextlib.ExitStack() as st:
            hn2 = [sb(st, "fo_hn%d" % i, [128, 8, TT], F32) for i in range(2)]
            hnB2 = [[Buf() for _ in range(8)] for _ in range(2)]
            yn2 = [sb(st, "fo_yn%d" % i, [128, 8, TT], F32) for i in range(2)]
            ynB2 = [[Buf() for _ in range(8)] for _ in range(2)]
            ot = [sb(st, "fo_ot%d" % i, [128, D], F32) for i in range(4)]
            otB = [Buf() for _ in range(4)]
            tmp2 = [norm_tmp(st, "fo%d" % i) for i in range(2)]
            k = 0
            for q in range(NQ):
                hn, hnB, yn, ynB = hn2[q % 2], hnB2[q % 2], yn2[q % 2], ynB2[q % 2]
                load_h(hn[:], hnB, q)
                rmsnorm(hn, hnB, 8, yn, ynB, tmp2[q % 2])
                for s in range(4):
                    tt = q * 4 + s
                    o, oB = ot[tt % 4], otB[tt % 4]
                    for half in range(2):
                        ps, pB = next_ps()
                        for cc in range(4):
                            c = half * 4 + cc
                            OP("pe", [ynB[c], Bc], [pB], lambda e, ps=ps, cc=cc, c=c, s=s, yn=yn: e.transpose(out=ps[:, cc * 128:(cc + 1) * 128], in_=yn[:, c, s * 128:(s + 1) * 128], identity=ident_f))
                        dst = o[:, half * 512:(half + 1) * 512]
                        if k % 2 == 0:
                            OP("act", [pB], [oB], lambda e, dst=dst, ps=ps: e.activation(out=dst, in_=ps[:, :], func=AF.Copy))
                        else:
                            OP("dve", [pB], [oB], lambda e, dst=dst, ps=ps: e.tensor_copy(out=dst, in_=ps[:, :]))
                        k += 1
                    P.dma(lambda e, o=o, tt=tt: e.dma_start(out=y_out[tt * 128:(tt + 1) * 128, :], in_=o[:]), reads=[oB], writes=[Buf()], is_out=True)

    def phase_ffn(l, f):
        P.barrier()
        with contextlib.ExitStack() as st:
            wg = sb(st, "wg", [128, 8, DFF], BF16)
            wu = sb(st, "wu", [128, 8, DFF], BF16)
            wd = sb(st, "wd", [128, NFC, D], BF16)
            CB = [(0, 256), (256, 1024), (1024, 2048), (2048, DFF)]
            wgB = [Buf() for _ in range(4)]
            wuB = [Buf() for _ in range(4)]
            wdB = [Buf() for _ in range(NFC)]
            wgv = w_gate[l, f].rearrange("(kc p) n -> p kc n", p=128)
            wuv = w_up[l, f].rearrange("(kc p) n -> p kc n", p=128)
            wdv = w_down[l, f].rearrange("(fc p) n -> p fc n", p=128)
            for cb, (c0, c1) in enumerate(CB):
                wload(wg[:, :, c0:c1], wgv[:, :, c0:c1], [wgB[cb]])
                wload(wu[:, :, c0:c1], wuv[:, :, c0:c1], [wuB[cb]])
            for f0 in range(0, NFC, 6):
                f1 = min(NFC, f0 + 6)
                wload(wd[:, f0:f1, :], wdv[:, f0:f1, :], wdB[f0:f1])
            hn = sb(st, "ff_hn", [128, 8, TT], F32)
            hnB = [Buf() for _ in range(8)]
            xn = [sb(st, "ff_xn%d" % i, [128, 8, TT], BF16) for i in range(2)]
            xnB = [[Buf() for _ in range(8)] for _ in range(2)]
            act = sb(st, "ff_act", [128, NFC, TT], BF16)
            actB = [Buf() for _ in range(NFC)]
            sg = [sb(st, "ff_sg%d" % i, [128, TT], F32) for i in range(2)]
            sgB = [Buf() for _ in range(2)]
            hr = [sb(st, "ff_hr%d" % i, [128, TT], F32) for i in range(3)]
            hrB = [Buf() for _ in range(3)]
            tmp = norm_tmp(st, "ff")
            gi = l * 4 + (0 if f == 0 else 2)
            k = 0
            load_h(hn[:], hnB, 0)
            rmsnorm(hn, hnB, gi, xn[0], xnB[0], tmp)
            for q in range(nq):
                X, XB = xn[q % 2], xnB[q % 2]
                if q + 1 < nq:
                    load_h(hn[:], hnB, q + 1)
                if q == 0 and l == 0 and f == 0:
                    dbg_dump("xn", X[:], [128, 8, TT], BF16, XB)
                    dbg_dump("wg", wg[:], [128, 8, DFF], BF16, wgB)
                    dbg_dump("wd", wd[:], [128, NFC, D], BF16, wdB)
                for fc in range(NFC):
                    pg, pgB = next_ps()
                    for kc in range(8):
                        OP("pe", [wgB[0 if fc < 2 else 1 + fc // 8], XB[kc]], [pgB], lambda e, pg=pg, kc=kc, fc=fc, X=X: e.matmul(pg[:, 0:TT], lhsT=wg[:, kc, fc * 128:(fc + 1) * 128], rhs=X[:, kc, :], start=(kc == 0), stop=(kc == 7)))
                    pu, puB = next_ps()
                    for kc in range(8):
                        OP("pe", [wuB[0 if fc < 2 else 1 + fc // 8], XB[kc]], [puB], lambda e, pu=pu, kc=kc, fc=fc, X=X: e.matmul(pu[:, 0:TT], lhsT=wu[:, kc, fc * 128:(fc + 1) * 128], rhs=X[:, kc, :], start=(kc == 0), stop=(kc == 7)))
                    s_, sB_ = sg[fc % 2], sgB[fc % 2]
                    OP("act", [pgB], [sB_], lambda e, s_=s_, pg=pg: e.activation(out=s_[:], in_=pg[:, 0:TT], func=AF.Silu))
                    OP("dve", [sB_, puB], [actB[fc]], lambda e, s_=s_, pu=pu, fc=fc: e.tensor_tensor(out=act[:, fc, :], in0=pu[:, 0:TT], in1=s_[:], op=ALU.mult))
                    if fc == 11 and q + 1 < nq:
                        rmsnorm(hn, hnB, gi, xn[(q + 1) % 2], xnB[(q + 1) % 2], tmp)
                if q == 0 and l == 0 and f == 0:
                    dbg_dump("act", act[:], [128, NFC, TT], BF16, actB)
                for dc in range(8):
                    po, poB = next_ps()
                    for fc in range(NFC):
                        OP("pe", [wdB[fc], actB[fc]], [poB], lambda e, po=po, fc=fc, dc=dc: e.matmul(po[:, 0:TT], lhsT=wd[:, fc, dc * 128:(dc + 1) * 128], rhs=act[:, fc, :], start=(fc == 0), stop=(fc == NFC - 1)))
                    r, rB = hr[k % 3], hrB[k % 3]
                    k += 1
                    P.dma(lambda e, r=r, dc=dc, q=q: e.dma_start(out=r[:], in_=hTv[:, dc, q * TT:(q + 1) * TT]), reads=[HT[q][dc]], writes=[rB])
                    OP("dve", [poB, rB], [rB], lambda e, r=r, po=po: e.scalar_tensor_tensor(out=r[:], in0=po[:, 0:TT], scalar=0.5, in1=r[:], op0=ALU.mult, op1=ALU.add))
                    P.dma(lambda e, r=r, dc=dc, q=q: e.dma_start(out=hTv[:, dc, q * TT:(q + 1) * TT], in_=r[:]), reads=[rB], writes=[HT[q][dc]])

    def phase_ple(l, fuse_out=False):
        P.barrier()
        with contextlib.ExitStack() as st:
            wpg = sb(st, "wpg", [128, 8, D], BF16)
            wpp = sb(st, "wpp", [128, 2, D], BF16)
            wpgB = [Buf()]
            wppB = [Buf()]
            wpgv = ple_wg[l].rearrange("(kc p) n -> p kc n", p=128)
            wpgB = [Buf(), Buf()]
            wload(wpg[:, :, 0:256], wpgv[:, :, 0:256], [wpgB[0]])
            wload(wpg[:, :, 256:D], wpgv[:, :, 256:D], [wpgB[1]])
            wload(wpp[:], ple_wp[l].rearrange("(kc p) n -> p kc n", p=128), wppB)
            NH = 3 if fuse_out else 2
            hn2 = [sb(st, "pl_hn%d" % i, [128, 8, TT], F32) for i in range(NH)]
            hnB2 = [[Buf() for _ in range(8)] for _ in range(NH)]
            xn = [sb(st, "pl_xn%d" % i, [128, 8, TT], BF16) for i in range(2)]
            xnB = [[Buf() for _ in range(8)] for _ in range(2)]
            pt = [sb(st, "pl_pt%d" % i, [128, 4, PLE], F32) for i in range(2)]
            ptB = [Buf() for _ in range(2)]
            pT = [sb(st, "pl_pT%d" % i, [128, 2, TT], BF16) for i in range(2)]
            pTB = [Buf() for _ in range(2)]
            sg = [sb(st, "pl_sg%d" % i, [128, TT], F32) for i in range(2)]
            sgB = [Buf() for _ in range(2)]
            tg = [sb(st, "pl_tg%d" % i, [128, TT], F32) for i in range(2)]
            tgB = [Buf() for _ in range(2)]
            hr = [sb(st, "pl_hr%d" % i, [128, TT], F32) for i in range(3)]
            hrB = [Buf() for _ in range(3)]
            tmp2 = [norm_tmp(st, "pl%d" % i) for i in range(2)]
            gi = l * 4 + 3

            def ple_load(q):
                load_h(hn2[q % NH][:], hnB2[q % NH], q)
                pt_, ptB_ = pt[q % 2], ptB[q % 2]
                P.dma(lambda e, pt_=pt_, q=q: e.dma_start(out=pt_[:], in_=p_in[l, q * TT:(q + 1) * TT, :].rearrange("(s p) c -> p s c", p=128)), writes=[ptB_])

            def ple_prep(q):
                rmsnorm(hn2[q % NH], hnB2[q % NH], gi, xn[q % 2], xnB[q % 2], tmp2[q % 2])
                pt_, ptB_ = pt[q % 2], ptB[q % 2]
                pT_, pTB_ = pT[q % 2], pTB[q % 2]
                for c2 in range(2):
                    ps, pB = next_ps()
                    for s in range(4):
                        OP("pe", [ptB_, Bc], [pB], lambda e, ps=ps, s=s, c2=c2, pt_=pt_: e.transpose(out=ps[:, s * 128:(s + 1) * 128], in_=pt_[:, s, c2 * 128:(c2 + 1) * 128], identity=ident_f))
                    OP("act", [pB], [pTB_], lambda e, ps=ps, c2=c2, pT_=pT_: e.activation(out=pT_[:, c2, :], in_=ps[:, :], func=AF.Copy))

            if fuse_out:
                yn2 = [sb(st, "fo_yn%d" % i, [128, 8, TT], F32) for i in range(2)]
                ynB2 = [[Buf() for _ in range(8)] for _ in range(2)]
                ot = [sb(st, "fo_ot%d" % i, [128, D], F32) for i in range(4)]
                otB = [Buf() for _ in range(4)]
                tmpo = [norm_tmp(st, "fo%d" % i) for i in range(2)]
            kev = [0]

            def out_tile(q):
                hn, hnB, yn, ynB = hn2[q % NH], hnB2[q % NH], yn2[q % 2], ynB2[q % 2]
                rmsnorm(hn, hnB, 8, yn, ynB, tmpo[q % 2])
                for s in range(4):
                    tt = q * 4 + s
                    o, oB = ot[tt % 4], otB[tt % 4]
                    for half in range(2):
                        ps, pB = next_ps()
                        for cc in range(4):
                            c_ = half * 4 + cc
                            OP("pe", [ynB[c_], Bc], [pB], lambda e, ps=ps, cc=cc, c_=c_, s=s, yn=yn: e.transpose(out=ps[:, cc * 128:(cc + 1) * 128], in_=yn[:, c_, s * 128:(s + 1) * 128], identity=ident_f))
                        dst = o[:, half * 512:(half + 1) * 512]
                        if kev[0] % 2 == 0:
                            OP("act", [pB], [oB], lambda e, dst=dst, ps=ps: e.activation(out=dst, in_=ps[:, :], func=AF.Copy))
                        else:
                            OP("dve", [pB], [oB], lambda e, dst=dst, ps=ps: e.tensor_copy(out=dst, in_=ps[:, :]))
                        kev[0] += 1
                    P.dma(lambda e, o=o, tt=tt: e.dma_start(out=y_out[tt * 128:(tt + 1) * 128, :], in_=o[:]), reads=[oB], writes=[Buf()], is_out=True)

            ple_load(0)
            ple_prep(0)
            for q in range(NQ):
                hn, hnB = hn2[q % NH], hnB2[q % NH]
                X, XB = xn[q % 2], xnB[q % 2]
                pT_, pTB_ = pT[q % 2], pTB[q % 2]
                if q + 1 < NQ:
                    ple_load(q + 1)
                for dc in range(8):
                    pg, pgB = next_ps()
                    for kc in range(8):
                        OP("pe", [wpgB[0 if dc < 2 else 1], XB[kc]], [pgB], lambda e, pg=pg, kc=kc, dc=dc, X=X: e.matmul(pg[:, 0:TT], lhsT=wpg[:, kc, dc * 128:(dc + 1) * 128], rhs=X[:, kc, :], start=(kc == 0), stop=(kc == 7)))
                    pe_, peB = next_ps()
                    for c2 in range(2):
                        OP("pe", [wppB[0], pTB_], [peB], lambda e, pe_=pe_, c2=c2, dc=dc, pT_=pT_: e.matmul(pe_[:, 0:TT], lhsT=wpp[:, c2, dc * 128:(dc + 1) * 128], rhs=pT_[:, c2, :], start=(c2 == 0), stop=(c2 == 1)))
                    s_, sB_ = sg[dc % 2], sgB[dc % 2]
                    OP("act", [pgB], [sB_], lambda e, s_=s_, pg=pg: e.activation(out=s_[:], in_=pg[:, 0:TT], func=AF.Sigmoid))
                    t_, tB_ = tg[dc % 2], tgB[dc % 2]
                    OP("dve", [sB_, peB], [tB_], lambda e, s_=s_, pe_=pe_, t_=t_: e.tensor_tensor(out=t_[:], in0=pe_[:, 0:TT], in1=s_[:], op=ALU.mult))
                    OP("pool", [tB_, hnB[dc]], [hnB[dc]], lambda e, hn=hn, t_=t_, dc=dc: e.tensor_tensor(out=hn[:, dc, :], in0=t_[:], in1=hn[:, dc, :], op=ALU.add))
                    if not fuse_out:
                        P.dma(lambda e, hn=hn, dc=dc, q=q: e.dma_start(out=hTv[:, dc, q * TT:(q + 1) * TT], in_=hn[:, dc, :]), reads=[hnB[dc]], writes=[HT[q][dc]])
                    if dc == 3 and q + 1 < NQ:
                        ple_prep(q + 1)
                    if fuse_out and dc == 3 and q >= 1:
                        out_tile(q - 1)
            if fuse_out:
                out_tile(NQ - 1)

    ctx = dict(nc=nc, P=P, OP=OP, sb=sb, es=es, next_ps=next_ps, rmsnorm=rmsnorm, norm_tmp=norm_tmp, load_h=load_h,
               wload=wload, HT=HT, XN=XN, YB=YB, hTv=hTv, xnTv=xnTv, ybr=ybr, cst=cst, cstb=cstb, Bc=Bc, Bcb=Bcb,
               epsc=epsc, Beps=Beps, ident_f=ident_f, ones_f=ones_f, triu_f=triu_f, ident_b=ident_b, ones_b=ones_b,
               mneg_b=mneg_b, gcol=gcol, Bg=Bg,
               w_in=w_in, f_bias=f_bias, pool_w=pool_w, pool_scale=pool_scale, lam_re=lam_re, lam_im=lam_im,
               log_dt=log_dt, b_re=b_re, b_im=b_im, c_re=c_re, c_im=c_im, ssm_d=ssm_d, w_glu=w_glu, conv_w=conv_w,
               w_branch=w_branch, w_out=w_out, consts=consts, qaug=qaug, vtok=vtok, psb=psb, psB=psB, dbg_dump=dbg_dump, nq=nq)

    if "in" in phases:
        phase_in()
    for l in range(depth):
        if "ffn" in phases:
            phase_ffn(l, 0)
        if "mix" in phases:
            phase_mixer(ctx, l, mix_parts)
        if "ffn" in phases:
            phase_ffn(l, 1)
        fuse = ("out" in phases) and (l == depth - 1)
        if "ple" in phases:
            phase_ple(l, fuse_out=fuse)
    if "out" in phases and "ple" not in phases:
        phase_out()
    P.emit(es)
    es.close()
    return nc


from types import SimpleNamespace


def phase_mixer(ctx, l, parts):
    c = SimpleNamespace(**ctx)
    P, OP, sb, nc = c.P, c.OP, c.sb, c.nc
    psb, psB = c.psb, c.psB
    ybrv = c.ybr.rearrange("b (c p) t -> b p c t", p=128)
    QA = [Buf() for _ in range(NQ)]
    VT = [Buf() for _ in range(NQ)]
    P.barrier()
    with contextlib.ExitStack() as so:
        u_ssm = sb(so, "u_ssm", [128, 2, 8 + T], BF16)
        uB = [[Buf() for _ in range(NQ)] for _ in range(2)]
        spar = ssm_param_load(c, l, so) if "ssm" in parts else None
        spre = ssm_s_alloc(c, so) if "ssm" in parts else None
        with contextlib.ExitStack() as s1:
            k_aug = sb(s1, "k_aug", [128, 4, T], BF16)
            kB = [[Buf() for _ in range(NQ)] for _ in range(4)]
            kcB = Buf()
            cabs = sb(s1, "cabs", [128, 32, 4], F32)
            tots = sb(s1, "tots", [128, 33, 4], F32)
            cabsB = [Buf() for _ in range(32)]
            totsB = [Buf() for _ in range(33)]
            def issue_params():
                if spar is not None:
                    for fn, rd, wr, kw in spar.dq:
                        P.dma(fn, reads=rd, writes=wr, **kw)
                    spar.dq.clear()
            mixer_m1(c, l, parts, u_ssm, uB, k_aug, kB, kcB, cabs, cabsB, tots, totsB, QA, VT, ybrv, issue_params)
            issue_params()
            P.barrier()
            sch = ssm_s_chain(c, l, spre, spar) if "ssm" in parts else None
            dq = sch.dq if sch is not None else []

            def drain(n):
                for _ in range(min(n, len(dq))):
                    eng, rd, wr, fn = dq.pop(0)
                    OP(eng, rd, wr, fn)
            if "att" in parts:
                mixer_m3(c, l, k_aug, kB, kcB, cabs, cabsB, tots, totsB, QA, VT, ybrv, drain)
            drain(len(dq))
        P.barrier()
        if "ssm" in parts:
            mixer_m2(c, l, u_ssm, uB, ybrv, spar, sch)
    P.barrier()
    mixer_m4(c, l, parts, ybrv)


def mixer_m1(c, l, parts, u_ssm, uB, k_aug, kB, kcB, cabs, cabsB, tots, totsB, QA, VT, ybrv, after_tile0=lambda: None):
    P, OP, sb, nc, next_ps = c.P, c.OP, c.sb, c.nc, c.next_ps
    with contextlib.ExitStack() as st:
        win = sb(st, "win", [128, 8, 2052], BF16)
        WBLK = [(0, 772), (772, 1796), (1796, 2052)]
        winB = [Buf() for _ in WBLK]
        wiv = c.w_in[l].rearrange("(kc p) n -> p kc n", p=128)
        c.wload(win[:, :, 0:772], wiv[:, :, 0:772], [winB[0]])

        def wB(col):
            for i, (c0, c1) in enumerate(WBLK):
                if c0 <= col < c1:
                    return winB[i]

        wf_sb = sb(st, "wf_sb", [128, 8, 4, 64], BF16)
        wfB = Buf()
        OP("dve", [winB[0]], [wfB], lambda e: e.tensor_copy(out=wf_sb[:], in_=win[:, :, 768:772].unsqueeze(3).to_broadcast([128, 8, 4, 64])))
        fb = sb(st, "fb", [128, 8], F32)
        fbB = Buf()
        P.dma(lambda e: e.dma_start(out=fb[:, 0:4], in_=c.f_bias[l:l + 1, :].partition_broadcast(128), allow_slow_non_contiguous=True), writes=[fbB])
        OP("dve", [fbB], [fbB], lambda e: e.tensor_scalar(out=fb[:, 4:8], in0=fb[:, 0:4], scalar1=-1.0, scalar2=None, op0=ALU.mult))
        pwb = sb(st, "pwb", [128, 2, 128], BF16)
        pwB = Buf()
        OP("pool", [], [pwB], lambda e: e.memset(pwb[:], 0.0))
        for g in range(4):
            r0 = (g % 2) * 64
            P.dma(lambda e, g=g, r0=r0: e.dma_start(out=pwb[r0:r0 + 64, g // 2, r0:r0 + 64], in_=c.pool_w[l, g]), reads=[pwB], writes=[pwB], q="pool")
        pmb = sb(st, "pmb", [128, 12, 128], BF16)
        pmB = Buf()
        P.dma(lambda e: e.dma_start(out=pmb[:], in_=c.consts[:, C_PM:C_PM + 12 * 128].rearrange("p (a b) -> p a b", a=12)), writes=[pmB], q="pool")
        for i in (1, 2):
            c.wload(win[:, :, WBLK[i][0]:WBLK[i][1]], wiv[:, :, WBLK[i][0]:WBLK[i][1]], [winB[i]])
        scol = sb(st, "scol", [128, 8], F32)
        scB = Buf()
        P.dma(lambda e: e.dma_start(out=scol[:, 0:2], in_=c.pool_scale[l].rearrange("(c p) -> p c", p=128), allow_slow_non_contiguous=True), writes=[scB])
        P.dma(lambda e: e.dma_start(out=scol[:, 2:8].rearrange("p (j c) -> p j c", j=3), in_=c.conv_w[l].rearrange("j (c p) -> p j c", p=128), allow_slow_non_contiguous=True), writes=[scB])

        OP("pool", [], [totsB[0]], lambda e: e.memset(tots[:, 0, :], 0.0))

        hn = sb(st, "m1_hn", [128, 8, TT], F32)
        hnB = [Buf() for _ in range(8)]
        xn2 = [sb(st, "m1_xn%d" % i, [128, 8, TT], BF16) for i in range(2)]
        xnB2 = [[Buf() for _ in range(8)] for _ in range(2)]
        tmp = c.norm_tmp(st, "m1")
        qa = [sb(st, "m1_qa%d" % i, [128, 4, TT], BF16) for i in range(2)]
        qaB = [Buf() for _ in range(2)]
        et = [sb(st, "m1_et%d" % i, [128, TT], F32) for i in range(2)]
        etB = [Buf() for _ in range(2)]
        spt = [sb(st, "m1_sp%d" % i, [128, TT], F32) for i in range(2)]
        spB = [Buf() for _ in range(2)]
        crn = [sb(st, "m1_crn%d" % i, [128, TT], F32) for i in range(2)]
        crnB = [Buf() for _ in range(2)]
        hit = [sb(st, "m1_hit%d" % i, [128, TT], BF16) for i in range(2)]
        hitB = [Buf() for _ in range(2)]
        vst = [sb(st, "m1_vst%d" % i, [128, 4, 256], BF16) for i in range(2)]
        vstB = [Buf() for _ in range(2)]
        ftk = [sb(st, "m1_ftk%d" % i, [128, 12], F32) for i in range(2)]
        ftkB = [Buf() for _ in range(2)]
        xpt = sb(st, "m1_xpt", [128, 5, 256], BF16)
        xptB = [Buf() for _ in range(5)]
        pld = sb(st, "m1_pld", [128, 2, TT], BF16)
        pldB = [Buf() for _ in range(2)]
        yps = [sb(st, "m1_yps%d" % i, [128, 2, TT], BF16) for i in range(2)]
        ypsB = [Buf() for _ in range(2)]
        ycs = [sb(st, "m1_ycs%d" % i, [128, 2, TT], BF16) for i in range(2)]
        ycsB = [Buf() for _ in range(2)]
        ccs = [sb(st, "m1_ccs%d" % i, [128, TT], F32) for i in range(2)]
        ccsB = [Buf() for _ in range(2)]
        cbs = [sb(st, "m1_cbs%d" % i, [128, TT], F32) for i in range(2)]
        cbsB = [Buf() for _ in range(2)]
        zt = [[sb(st, "m1_z%d_%d" % (cc, i), [128, TT + 2], F32) for i in range(2)] for cc in range(2)]
        ztB = [[Buf() for _ in range(2)] for _ in range(2)]
        y1 = [sb(st, "m1_y1%d" % i, [128, TT], F32) for i in range(2)]
        y1B = [Buf() for _ in range(2)]
        ones1 = c.cst[:, C_ONES:C_ONES + 1]

        def proj_fm(col0, M, ps, pB, pslice, X, XB, tp=None):
            for kc in range(8):
                kw = {} if tp is None else {"tile_position": tp}
                OP("pe", [wB(col0), XB[kc]], [pB], lambda e, kc=kc, kw=kw: e.matmul(ps[pslice, 0:TT], lhsT=win[:, kc, col0:col0 + M], rhs=X[:, kc, :], start=(kc == 0), stop=(kc == 7), **kw))

        c.load_h(hn[:], hnB, 0)
        c.rmsnorm(hn, hnB, l * 4 + 1, xn2[0], xnB2[0], tmp)
        if c.nq > 1:
            c.load_h(hn[:], hnB, 1)
        for q in range(c.nq):
            qs = slice(q * TT, (q + 1) * TT)
            xn, xnB = xn2[q % 2], xnB2[q % 2]
            P.dma(lambda e, qs=qs, xn=xn: e.dma_start(out=c.xnTv[:, :, qs], in_=xn[:]), reads=xnB, writes=[c.XN[q]])
            Q_, QB_ = qa[q % 2], qaB[q % 2]
            if "att" in parts:
                for h in range(4):
                    ps, pB = next_ps()
                    proj_fm(h * 64, 64, ps, pB, slice(0, 64), xn, xnB)
                    for kc in range(8):
                        OP("pe", [wfB, xnB[kc]], [pB], lambda e, kc=kc, h=h, ps=ps, xn=xn: e.matmul(ps[64:128, 0:TT], lhsT=wf_sb[:, kc, h, :], rhs=xn[:, kc, :], start=(kc == 0), stop=(kc == 7), tile_position=(0, 64)))
                    OP("act", [pB], [QB_], lambda e, ps=ps, h=h, Q_=Q_: e.activation(out=Q_[0:64, h, :], in_=ps[0:64, 0:TT], func=AF.Copy, scale=0.125))
                    e_, eB_ = et[h % 2], etB[h % 2]
                    OP("act", [pB, fbB], [eB_], lambda e, ps=ps, h=h, e_=e_: e.activation(out=e_[64:128, :], in_=ps[64:128, 0:TT], func=AF.Exp, scale=-1.0, bias=fb[64:128, 4 + h:5 + h]))
                    s_, sB_ = spt[h % 2], spB[h % 2]
                    OP("act", [eB_, c.Beps], [sB_], lambda e, e_=e_, s_=s_: e.activation(out=s_[64:128, :], in_=e_[64:128, :], func=AF.Ln, bias=c.epsc[64:128, 1:2]))
                    r_, rB_ = crn[h % 2], crnB[h % 2]
                    OP("dve", [sB_, c.Bc], [rB_], lambda e, s_=s_, r_=r_: e.tensor_tensor_scan(out=r_[64:128, :], data0=ones1[64:128, :].to_broadcast([64, TT]), data1=s_[64:128, :], initial=0.0, op0=ALU.mult, op1=ALU.add))
                    h_, hB_ = hit[h % 2], hitB[h % 2]
                    OP("dve", [rB_], [hB_], lambda e, r_=r_, h_=h_: e.tensor_scalar(out=h_[64:128, :], in0=r_[64:128, :], scalar1=-1.0, scalar2=None, op0=ALU.mult))
                    OP("pool", [hB_], [QB_], lambda e, h_=h_, h=h, Q_=Q_: e.tensor_copy(out=Q_[64:96, h, :], in_=h_[64:96, :]))
                    OP("dve", [rB_, hB_], [QB_], lambda e, r_=r_, h_=h_, h=h, Q_=Q_: e.scalar_tensor_tensor(out=Q_[96:128, h, :], in0=r_[96:128, :], scalar=-1.0, in1=h_[96:128, :], op0=ALU.mult, op1=ALU.subtract))
                P.dma(lambda e, qs=qs, Q_=Q_: e.dma_start(out=c.qaug[:, :, qs], in_=Q_[:]), reads=[QB_], writes=[QA[q]])
                for h in range(4):
                    ps, pB = next_ps()
                    proj_fm(256 + h * 64, 64, ps, pB, slice(0, 64), xn, xnB)
                    OP("dve", [pB], [kB[h][q]], lambda e, ps=ps, h=h, qs=qs: e.tensor_copy(out=k_aug[0:64, h, qs], in_=ps[0:64, 0:TT]))
            if q == 1:
                after_tile0()
            if q == 0:
                OP("pool", [], [kcB], lambda e: e.memset(k_aug[64:128, :, :], 0.0))
                OP("pool", [kcB], [kcB], lambda e: e.memset(k_aug[64:65, :, :], 1.0))
                OP("pool", [kcB], [kcB], lambda e: e.memset(k_aug[96:97, :, :], 1.0))
            if q + 1 < c.nq:
                c.rmsnorm(hn, hnB, l * 4 + 1, xn2[(q + 1) % 2], xnB2[(q + 1) % 2], tmp)
                if q + 2 < c.nq:
                    c.load_h(hn[:], hnB, q + 2)
            V_, VB_ = vst[q % 2], vstB[q % 2]
            ppool = [(c.psb[6], c.psB[6]), (c.psb[7], c.psB[7])]

            def stA(s):
                tt = q * 4 + s
                ts_ = slice(s * 128, (s + 1) * 128)
                if "att" not in parts:
                    return
                psA, pBA = next_ps()
                for kc in range(8):
                    OP("pe", [winB[0], xnB[kc]], [pBA], lambda e, kc=kc, psA=psA, ts_=ts_, xn=xn: e.matmul(psA[:, 0:260], lhsT=xn[:, kc, ts_], rhs=win[:, kc, 512:772], start=(kc == 0), stop=(kc == 7)))
                OP("act", [pBA], [VB_], lambda e, psA=psA, s=s, V_=V_: e.activation(out=V_[:, s, :], in_=psA[:, 0:256], func=AF.Copy))
                f_, fB_ = ftk[tt % 2], ftkB[tt % 2]
                OP("dve", [pBA, fbB], [fB_], lambda e, psA=psA, f_=f_: e.tensor_tensor(out=f_[:, 0:4], in0=psA[:, 256:260], in1=fb[:, 0:4], op=ALU.add))
                OP("act", [fB_], [fB_], lambda e, f_=f_: e.activation(out=f_[:, 4:8], in_=f_[:, 0:4], func=AF.Exp, scale=-1.0))
                OP("act", [fB_, c.Beps], [fB_], lambda e, f_=f_: e.activation(out=f_[:, 8:12], in_=f_[:, 4:8], func=AF.Ln, bias=c.epsc[:, 1:2]))

            def stB(s):
                ts_ = slice(s * 128, (s + 1) * 128)
                if "pool" not in parts:
                    return
                psP, pBP = next_ps()
                for kc in range(8):
                    OP("pe", [winB[1], xnB[kc]], [pBP], lambda e, kc=kc, psP=psP, ts_=ts_, xn=xn: e.matmul(psP[:, 0:256], lhsT=xn[:, kc, ts_], rhs=win[:, kc, 772:1028], start=(kc == 0), stop=(kc == 7)))
                OP("act", [pBP], [xptB[1 + s]], lambda e, psP=psP, s=s: e.activation(out=xpt[:, 1 + s, :], in_=psP[:, 0:256], func=AF.Copy))

            def stC(s):
                tt = q * 4 + s
                ts_ = slice(s * 128, (s + 1) * 128)
                if "pool" not in parts:
                    return
                for g in range(4):
                    pp, ppB = ppool[g // 2]
                    r0 = (g % 2) * 64
                    first = (tt == 0)
                    mi = g * 3 + (2 if first else 0)
                    OP("pe", [xptB[1 + s], pmB], [ppB], lambda e, pp=pp, r0=r0, g=g, s=s, mi=mi, ts_=ts_, first=first: e.matmul(pp[r0:r0 + 64, ts_], lhsT=xpt[:, 1 + s, g * 64:(g + 1) * 64], rhs=pmb[:, mi, :], start=True, stop=first, tile_position=(0, r0)))
                    if not first:
                        OP("pe", [xptB[s], pmB], [ppB], lambda e, pp=pp, r0=r0, g=g, s=s, ts_=ts_: e.matmul(pp[r0:r0 + 64, ts_], lhsT=xpt[:, s, g * 64:(g + 1) * 64], rhs=pmb[:, g * 3 + 1, :], start=False, stop=True, tile_position=(0, r0)))

            def stD(s):
                tt = q * 4 + s
                if "att" not in parts:
                    return
                f_, fB_ = ftk[tt % 2], ftkB[tt % 2]
                psc, pBc = next_ps()
                OP("pe", [fB_, c.Bc], [pBc], lambda e, psc=psc, f_=f_: e.matmul(psc[:, 0:4], lhsT=c.triu_f, rhs=f_[:, 8:12], start=True, stop=True))
                OP("pe", [fB_, c.Bc], [pBc], lambda e, psc=psc, f_=f_: e.matmul(psc[:, 8:12], lhsT=c.ones_f, rhs=f_[:, 8:12], start=True, stop=True))
                OP("dve", [pBc, totsB[tt]], [cabsB[tt]], lambda e, psc=psc, tt=tt: e.tensor_tensor(out=cabs[:, tt, :], in0=psc[:, 0:4], in1=tots[:, tt, :], op=ALU.add))
                OP("dve", [pBc, totsB[tt]], [totsB[tt + 1]], lambda e, psc=psc, tt=tt: e.tensor_tensor(out=tots[:, tt + 1, :], in0=psc[:, 8:12], in1=tots[:, tt, :], op=ALU.add))

            stA(0); stB(0); stA(1); stB(1); stC(0); stD(0); stA(2); stB(2); stC(1); stD(1); stA(3); stB(3); stC(2); stD(2); stC(3); stD(3)
            if "att" in parts:
                P.dma(lambda e, q=q, V_=V_: e.dma_start(out=c.vtok[q * TT:(q + 1) * TT, :].rearrange("(s p) c -> p s c", p=128), in_=V_[:]), reads=[VB_], writes=[VT[q]])
            if "pool" in parts:
                OP("pool", [xptB[4]], [xptB[0]], lambda e: e.tensor_copy(out=xpt[:, 0, :], in_=xpt[:, 4, :]))
                Y_, YB_ = yps[q % 2], ypsB[q % 2]
                for cc in range(2):
                    pp, ppB = ppool[cc]
                    OP("act", [ppB], [pldB[cc]], lambda e, pp=pp, cc=cc: e.activation(out=pld[:, cc, :], in_=pp[:, 0:TT], func=AF.Copy))
                    ps, pB = next_ps()
                    OP("pe", [pldB[cc], pwB], [pB], lambda e, ps=ps, cc=cc: e.matmul(ps[:, 0:TT], lhsT=pwb[:, cc, :], rhs=pld[:, cc, :], start=True, stop=True))
                    OP("dve", [pB, scB], [YB_], lambda e, ps=ps, cc=cc, Y_=Y_: e.tensor_scalar(out=Y_[:, cc, :], in0=ps[:, 0:TT], scalar1=scol[:, cc:cc + 1], scalar2=None, op0=ALU.mult))
                P.dma(lambda e, qs=qs, Y_=Y_: e.dma_start(out=ybrv[1][:, :, qs], in_=Y_[:]), reads=[YB_], writes=[c.YB[1][q]])
            if "ssm" in parts:
                for cc in range(2):
                    ps, pB = next_ps()
                    proj_fm(1028 + cc * 128, 128, ps, pB, slice(0, 128), xn, xnB)
                    OP("act", [pB], [uB[cc][q]], lambda e, ps=ps, cc=cc, q=q: e.activation(out=u_ssm[:, cc, 8 + q * TT:8 + (q + 1) * TT], in_=ps[:, 0:TT], func=AF.Copy))
            if "conv" in parts:
                Y_, YB_ = ycs[q % 2], ycsB[q % 2]
                for cc in range(2):
                    pcc, pccB = next_ps()
                    proj_fm(1540 + cc * 128, 128, pcc, pccB, slice(0, 128), xn, xnB)
                    pcx, pcxB = next_ps()
                    proj_fm(1796 + cc * 128, 128, pcx, pcxB, slice(0, 128), xn, xnB)
                    pcb, pcbB = next_ps()
                    proj_fm(1284 + cc * 128, 128, pcb, pcbB, slice(0, 128), xn, xnB)
                    a_, aB_ = ccs[cc], ccsB[cc]
                    b_, bB_ = cbs[cc], cbsB[cc]
                    z_, zB_ = zt[cc][q % 2], ztB[cc][q % 2]
                    zp_, zpB_ = zt[cc][(q + 1) % 2], ztB[cc][(q + 1) % 2]
                    y_, yB_ = y1[cc], y1B[cc]
                    OP("act", [pccB], [aB_], lambda e, pcc=pcc, a_=a_: e.activation(out=a_[:], in_=pcc[:, 0:TT], func=AF.Copy))
                    OP("act", [pcbB], [bB_], lambda e, pcb=pcb, b_=b_: e.activation(out=b_[:], in_=pcb[:, 0:TT], func=AF.Copy))
                    if q == 0:
                        OP("pool", [], [zB_], lambda e, z_=z_: e.memset(z_[:, 0:2], 0.0))
                    else:
                        OP("pool", [zpB_], [zB_], lambda e, z_=z_, zp_=zp_: e.tensor_copy(out=z_[:, 0:2], in_=zp_[:, TT:TT + 2]))
                    OP("dve", [pcxB, aB_], [zB_], lambda e, pcx=pcx, a_=a_, z_=z_: e.tensor_tensor(out=z_[:, 2:TT + 2], in0=pcx[:, 0:TT], in1=a_[:], op=ALU.mult))
                    OP("dve", [zB_, scB], [yB_], lambda e, z_=z_, y_=y_, cc=cc: e.tensor_scalar(out=y_[:], in0=z_[:, 0:TT], scalar1=scol[:, 2 + cc:3 + cc], scalar2=None, op0=ALU.mult))
                    OP("dve", [zB_, scB, yB_], [yB_], lambda e, z_=z_, y_=y_, cc=cc: e.scalar_tensor_tensor(out=y_[:], in0=z_[:, 1:TT + 1], scalar=scol[:, 4 + cc:5 + cc], in1=y_[:], op0=ALU.mult, op1=ALU.add))
                    OP("dve", [zB_, scB, yB_], [yB_], lambda e, z_=z_, y_=y_, cc=cc: e.scalar_tensor_tensor(out=y_[:], in0=z_[:, 2:TT + 2], scalar=scol[:, 6 + cc:7 + cc], in1=y_[:], op0=ALU.mult, op1=ALU.add))
                    OP("pool", [yB_, bB_], [YB_], lambda e, y_=y_, b_=b_, Y_=Y_, cc=cc: e.tensor_tensor(out=Y_[:, cc, :], in0=y_[:], in1=b_[:], op=ALU.mult))
                P.dma(lambda e, qs=qs, Y_=Y_: e.dma_start(out=ybrv[3][:, :, qs], in_=Y_[:]), reads=[YB_], writes=[c.YB[3][q]])


def mixer_m3(c, l, k_aug, kB, kcB, cabs, cabsB, tots, totsB, QA, VT, ybrv, drain=lambda n: None):
    P, OP, sb, nc = c.P, c.OP, c.sb, c.nc
    psb, psB = c.psb, c.psB
    with contextlib.ExitStack() as st:
        V_aug = sb(st, "V_aug", [128, 32, 4, 2, 64], BF16)
        VB = [Buf() for _ in range(NQ)]
        VoB = Buf()
        OP("pool", [], [VoB], lambda e: e.memset(V_aug[:, :, :, 1, :], 1.0))
        qt = [sb(st, "m3_qt%d" % i, [128, 4, TT], BF16) for i in range(2)]
        qtB = [Buf() for _ in range(2)]
        Pt = [sb(st, "m3_P%d" % i, [128, TT], BF16) for i in range(3)]
        PtB = [Buf() for _ in range(3)]
        bq = [sb(st, "m3_bq%d" % i, [128, 32, 4], F32) for i in range(2)]
        bqB = [Buf() for _ in range(2)]
        rd = [sb(st, "m3_rd%d" % i, [64, TT], F32) for i in range(2)]
        rdB = [Buf() for _ in range(2)]
        yst = [sb(st, "m3_y%d" % i, [64, 4, TT], BF16) for i in range(2)]
        ystB = [Buf() for _ in range(2)]
        yav = c.ybr[0].rearrange("(h p) t -> p h t", p=64)
        kP = 0
        kS = 0
        kO = 0
        for q in range(c.nq):
            Q_, QB_ = qt[q % 2], qtB[q % 2]
            P.dma(lambda e, q=q, Q_=Q_: e.dma_start(out=Q_[:], in_=c.qaug[:, :, q * TT:(q + 1) * TT]), reads=[QA[q]], writes=[QB_])
            for i4 in range(4):
                i = q * 4 + i4
                P.dma(lambda e, i=i: e.dma_start(out=V_aug[:, i, :, 0, :], in_=c.vtok[i * 128:(i + 1) * 128, :].rearrange("p (h d) -> p h d", h=4)), reads=[VT[q]], writes=[VB[q]], shared=True)
            n = 4 * q + 4
            b_, bB_ = bq[q % 2], bqB[q % 2]
            OP("dve", cabsB[0:n] + [totsB[4 * q]], [bB_], lambda e, b_=b_, n=n, q=q: e.tensor_tensor(out=b_[:, 0:n, :], in0=cabs[:, 0:n, :], in1=tots[:, 4 * q, :].unsqueeze(1).to_broadcast([128, n, 4]), op=ALU.subtract))
            Y_, YB_ = yst[q % 2], ystB[q % 2]
            for h in range(4):
                po, poB = psb[4 + kO % 2], psB[4 + kO % 2]
                kO += 1
                def emit_S(i):
                    d = i - 4 * q
                    c0 = max(0, d) * 128
                    ps, pB = psb[i % 4], psB[i % 4]
                    OP("pe", [kB[h][i // 4], kcB, QB_], [pB], lambda e, ps=ps, i=i, c0=c0, d=d, h=h, Q_=Q_: e.matmul(ps[:, c0:TT], lhsT=k_aug[:, h, i * 128:(i + 1) * 128], rhs=Q_[:, h, c0:TT], start=True, stop=(d < 0)))
                    if d >= 0:
                        OP("pe", [c.Bcb], [pB], lambda e, ps=ps, c0=c0: e.matmul(ps[:, c0:c0 + 128], lhsT=c.ident_b, rhs=c.mneg_b, start=False, stop=True))
                    return ps, pB, c0

                nxt = emit_S(0)
                for i in range(n):
                    ps, pB, c0 = nxt
                    if i + 1 < n:
                        nxt = emit_S(i + 1)
                    p_, pB_ = Pt[kP % 3], PtB[kP % 3]
                    kP += 1
                    OP("act", [pB, bB_], [pB_], lambda e, ps=ps, p_=p_, c0=c0, i=i, b_=b_, h=h: e.activation(out=p_[:, c0:TT], in_=ps[:, c0:TT], func=AF.Exp, bias=b_[:, i, h:h + 1]))
                    OP("pe", [VB[i // 4], VoB, pB_], [poB], lambda e, p_=p_, c0=c0, i=i, po=po, h=h, n=n: e.matmul(po[:, c0:TT], lhsT=V_aug[:, i, h, :, :].rearrange("p a b -> p (a b)"), rhs=p_[:, c0:TT], start=(i == 0), stop=(i == n - 1)))
                r_, rB_ = rd[h % 2], rdB[h % 2]
                OP("dve", [poB], [rB_], lambda e, po=po, r_=r_: e.reciprocal(out=r_[0:64, :], in_=po[64:128, 0:TT]))
                OP("dve", [poB, rB_], [YB_], lambda e, po=po, r_=r_, h=h, Y_=Y_: e.tensor_tensor(out=Y_[0:64, h, :], in0=po[0:64, 0:TT], in1=r_[0:64, :], op=ALU.mult))
                drain(12)
            P.dma(lambda e, q=q, Y_=Y_: e.dma_start(out=yav[:, :, q * TT:(q + 1) * TT], in_=Y_[:]), reads=[YB_], writes=[c.YB[0][q]])


def ssm_param_load(c, l, stack):
    sb = c.sb
    dq = []

    class _P:
        @staticmethod
        def dma(fn, reads=(), writes=(), **kw):
            dq.append((fn, list(reads), list(writes), kw))
    P = _P

    def mk_(name, shape, dt=F32):
        return sb(stack, "spl_" + name, shape, dt), Buf()
    lamS, lamSB = mk_("lamS", [128, 2, 8])
    for ri, src in enumerate((c.lam_re, c.lam_im)):
        P.dma(lambda e, ri=ri, src=src: e.dma_start(out=lamS[:, ri, :], in_=src[l].rearrange("(j g) p -> (g p) j", g=2), allow_slow_non_contiguous=True), writes=[lamSB], shared=True)
    ldtS, ldtSB = mk_("ldtS", [128, 8])
    for g in range(2):
        P.dma(lambda e, g=g: e.dma_start(out=ldtS[g * 64:(g + 1) * 64, :], in_=c.log_dt[l].rearrange("(j g) -> g j", g=2)[g:g + 1, :].partition_broadcast(64), allow_slow_non_contiguous=True), writes=[ldtSB], shared=True)
    BS, BSB = mk_("BS", [128, 2, 8, 16])
    for ri, src in enumerate((c.b_re, c.b_im)):
        P.dma(lambda e, ri=ri, src=src: e.dma_start(out=BS[:, ri, :, :], in_=src[l].rearrange("(j g) p h -> (g p) j h", g=2)), writes=[BSB], shared=True)
    Csrc, CsB = mk_("Csrc", [128, 2, 128])
    for ri, src in enumerate((c.c_re, c.c_im)):
        for j in range(8):
            P.dma(lambda e, ri=ri, src=src, j=j: e.dma_start(out=Csrc[16 * j:16 * j + 16, ri, :].rearrange("h (g p) -> h g p", g=2), in_=src[l, 2 * j:2 * j + 2].rearrange("g h p -> h g p")), writes=[CsB], shared=True)
    dcol, dcB = mk_("dcol", [128, 2])
    P.dma(lambda e: e.dma_start(out=dcol[:], in_=c.ssm_d[l].rearrange("(c p) -> p c", p=128), allow_slow_non_contiguous=True), writes=[dcB], shared=True)
    Bsrc, BsB = mk_("Bsrc", [128, 2, 2, 4, 2, 16])
    for ri, src in enumerate((c.b_re, c.b_im)):
        for hf in range(2):
            for g2p in range(2):
                P.dma(lambda e, ri=ri, src=src, hf=hf, g2p=g2p: e.dma_start(out=Bsrc[:, ri, hf, :, g2p, :], in_=src[l, 8 * hf:8 * hf + 8].rearrange("(j g) p h -> (g p) j h", g=2)), writes=[BsB], shared=True)
    return SimpleNamespace(lamS=lamS, lamSB=lamSB, ldtS=ldtS, ldtSB=ldtSB, BS=BS, BSB=BSB, Csrc=Csrc, CsB=CsB, dcol=dcol, dcB=dcB,
                           Bsrc=Bsrc, BsB=BsB, dq=dq)


def ssm_s_alloc(c, stack):
    sb = c.sb
    names = ["lr", "dt", "th", "sn", "cs", "lg", "mag", "ar", "ai", "t1", "t2", "t3", "den", "am1", "cr", "ci", "p0r", "p0i"]
    names += ["p%d%s" % (n, x) for n in range(2, 9) for x in "ri"]
    tl = {nm: sb(stack, "ppS_%s" % nm, [128, 8], F32)[:] for nm in names}
    hp = sb(stack, "ssc_halfpi", [128, 1], F32)
    LV = sb(stack, "ssm_LV", [128, 3, 9, 8], F32)
    return SimpleNamespace(hp=hp, LV=LV, tl=tl)


def ssm_s_chain(c, l, pre, spar):
    P, sb = c.P, c.sb
    dq = []
    OP = lambda eng, rd, wr, fn: dq.append((eng, list(rd), list(wr), fn))
    mul, add, sub = ALU.mult, ALU.add, ALU.subtract
    lamS, lamSB, ldtS, ldtSB = spar.lamS, spar.lamSB, spar.ldtS, spar.ldtSB

    def TT_(eng, out, a, b, op, rd, wr):
        OP(eng, rd, wr, lambda e: e.tensor_tensor(out=out, in0=a, in1=b, op=op))
    hp, LV, pre_tl = pre.hp, pre.LV, pre.tl
    hpB = Buf()
    OP("pool", [], [hpB], lambda e: e.memset(hp[:], float(np.pi / 2)))
    LVB = Buf()
    def cpow_prep(tag, F, lr_in, li, ldt_ap, deps, eng):
        tl = pre_tl

        def t_(nm):
            return tl[nm]
        B_ = Buf()
        D = list(deps) + [B_]
        OP(eng, D, [B_], lambda e: e.tensor_scalar(out=t_("lr"), in0=lr_in, scalar1=-1e-4, scalar2=None, op0=ALU.min))
        OP(eng, D, [B_], lambda e: e.tensor_copy(out=t_("dt"), in_=ldt_ap))
        OP("act", [B_], [B_], lambda e: e.activation(out=t_("dt"), in_=t_("dt"), func=AF.Exp))
        TT_(eng, t_("th"), li, t_("dt"), mul, D, [B_])
        OP("act", [B_], [B_], lambda e: e.activation(out=t_("sn"), in_=t_("th"), func=AF.Sin, scale=1.0 / 32))
        OP("act", [B_, hpB], [B_], lambda e: e.activation(out=t_("cs"), in_=t_("th"), func=AF.Sin, scale=1.0 / 32, bias=hp[:, 0:1]))
        TT_(eng, t_("lg"), t_("lr"), t_("dt"), mul, [B_], [B_])
        OP("act", [B_], [B_], lambda e: e.activation(out=t_("mag"), in_=t_("lg"), func=AF.Exp, scale=1.0 / 32))
        TT_(eng, t_("ar"), t_("mag"), t_("cs"), mul, [B_], [B_])
        TT_(eng, t_("ai"), t_("mag"), t_("sn"), mul, [B_], [B_])
        for _ in range(5):
            TT_(eng, t_("t1"), t_("ar"), t_("ar"), mul, [B_], [B_])
            TT_(eng, t_("t2"), t_("ai"), t_("ai"), mul, [B_], [B_])
            TT_(eng, t_("t3"), t_("ar"), t_("ai"), mul, [B_], [B_])
            TT_(eng, t_("ar"), t_("t1"), t_("t2"), sub, [B_], [B_])
            OP(eng, [B_], [B_], lambda e: e.tensor_scalar(out=t_("ai"), in0=t_("t3"), scalar1=2.0, scalar2=None, op0=mul))
        TT_(eng, t_("t1"), t_("lr"), t_("lr"), mul, [B_], [B_])
        TT_(eng, t_("t2"), li, li, mul, D, [B_])
        TT_(eng, t_("den"), t_("t1"), t_("t2"), add, [B_], [B_])
        OP("dve", [B_], [B_], lambda e: e.reciprocal(out=t_("den"), in_=t_("den")))
        OP(eng, [B_], [B_], lambda e: e.tensor_scalar(out=t_("am1"), in0=t_("ar"), scalar1=-1.0, scalar2=None, op0=add))
        TT_(eng, t_("t1"), t_("am1"), t_("lr"), mul, [B_], [B_])
        TT_(eng, t_("t2"), t_("ai"), li, mul, D, [B_])
        TT_(eng, t_("t3"), t_("t1"), t_("t2"), add, [B_], [B_])
        TT_(eng, t_("cr"), t_("t3"), t_("den"), mul, [B_], [B_])
        TT_(eng, t_("t1"), t_("ai"), t_("lr"), mul, [B_], [B_])
        TT_(eng, t_("t2"), t_("am1"), li, mul, D, [B_])
        TT_(eng, t_("t3"), t_("t1"), t_("t2"), sub, [B_], [B_])
        TT_(eng, t_("ci"), t_("t3"), t_("den"), mul, [B_], [B_])
        pw = []
        OP(eng, [B_], [B_], lambda e: e.memset(t_("p0r"), 1.0))
        OP(eng, [B_], [B_], lambda e: e.memset(t_("p0i"), 0.0))
        pw.append((t_("p0r"), t_("p0i")))
        pw.append((t_("ar"), t_("ai")))
        for n in range(2, 9):
            pr, pi = pw[-1]
            nr, ni = t_("p%dr" % n), t_("p%di" % n)
            cmul(nr, ni, pr, pi, t_("ar"), t_("ai"), t_("t1"), t_("t2"), [B_], [B_], eng)
            pw.append((nr, ni))
        return pw, (t_("cr"), t_("ci")), B_, t_

    def cmul(o_r, o_i, a_r, a_i, b_r, b_i, t1, t2, rd, wr, eng="dve"):
        TT_(eng, t1, a_r, b_r, mul, rd, wr)
        TT_(eng, t2, a_i, b_i, mul, rd, wr)
        TT_(eng, o_r, t1, t2, sub, rd, wr)
        TT_(eng, t1, a_r, b_i, mul, rd, wr)
        TT_(eng, t2, a_i, b_r, mul, rd, wr)
        TT_(eng, o_i, t1, t2, add, rd, wr)

    pwS, (crS, ciS), BS_, tS = cpow_prep("S", 8, lamS[:, 0, :], lamS[:, 1, :], ldtS[:], [lamSB, ldtSB], "dve")
    OP("dve", [BS_], [LVB], lambda e: e.tensor_copy(out=LV[:, 0, 0, :], in_=pwS[8][0]))
    OP("dve", [BS_], [LVB], lambda e: e.tensor_copy(out=LV[:, 1, 0, :], in_=pwS[8][1]))
    for k in range(1, 9):
        TT_("dve", tS("t1"), LV[:, 0, k - 1, :], LV[:, 0, k - 1, :], mul, [LVB, BS_], [BS_])
        TT_("dve", tS("t2"), LV[:, 1, k - 1, :], LV[:, 1, k - 1, :], mul, [LVB, BS_], [BS_])
        TT_("dve", tS("t3"), LV[:, 0, k - 1, :], LV[:, 1, k - 1, :], mul, [LVB, BS_], [BS_])
        TT_("dve", LV[:, 0, k, :], tS("t1"), tS("t2"), sub, [BS_], [LVB])
        OP("dve", [BS_], [LVB], lambda e, k=k: e.tensor_scalar(out=LV[:, 1, k, :], in0=tS("t3"), scalar1=2.0, scalar2=None, op0=mul))
    OP("dve", [LVB], [LVB], lambda e: e.tensor_scalar(out=LV[:, 2, :, :], in0=LV[:, 1, :, :], scalar1=-1.0, scalar2=None, op0=mul))
    return SimpleNamespace(pwS=pwS, crS=crS, ciS=ciS, BS_=BS_, tS=tS, LV=LV, LVB=LVB, cmul=cmul, dq=dq)


def mixer_m2(c, l, u_ssm, uB, ybrv, spar, sch):
    P, OP, sb, nc, next_ps = c.P, c.OP, c.sb, c.nc, c.next_ps
    mul, add, sub = ALU.mult, ALU.add, ALU.subtract
    NCH = T // 8

    def TT_(eng, out, a, b, op, rd, wr):
        OP(eng, rd, wr, lambda e: e.tensor_tensor(out=out, in0=a, in1=b, op=op))

    with contextlib.ExitStack() as st:
        WZ = sb(st, "ssm_WZ", [128, 8, 8, 2, 128], BF16)
        CA = sb(st, "ssm_CA", [128, 8, 9, 2, 128], BF16)
        BD = sb(st, "ssm_BD", [128, 2, 8, 128], BF16)
        LV, LVB = sch.LV, sch.LVB
        WZB, CAB, BDB = [Buf(), Buf()], Buf(), Buf()
        with contextlib.nullcontext():
            sp = st

            def mk_(name, shape, dt=F32):
                return sb(sp, "sp_" + name, shape, dt), Buf()
            mk, mkB = mk_("mk", [128, 8, 128])
            P.dma(lambda e: e.dma_start(out=mk[:], in_=c.consts[:, C_MH:C_MH + 1024].rearrange("p (a b) -> p a b", a=8)), writes=[mkB])
            msall, msB = mk_("msall", [128, 8, 128])
            for j in range(8):
                OP("pool", [mkB], [msB], lambda e, j=j: e.tensor_copy(out=msall[:, j, :], in_=mk[:, 4 + j % 4, :]))
            lamS, lamSB, ldtS, ldtSB, BS, BSB, Csrc, CsB = spar.lamS, spar.lamSB, spar.ldtS, spar.ldtSB, spar.BS, spar.BSB, spar.Csrc, spar.CsB
            dcol, dcB, Bsrc, BsB = spar.dcol, spar.dcB, spar.Bsrc, spar.BsB
            CT, CTB = mk_("CT", [128, 2, 128])
            ps, pB = next_ps()
            for ri in range(2):
                OP("pe", [CsB, c.Bc], [pB], lambda e, ps=ps, ri=ri: e.transpose(out=ps[:, ri * 128:(ri + 1) * 128], in_=Csrc[:, ri, :], identity=c.ident_f))
            OP("act", [pB], [CTB], lambda e, ps=ps: e.activation(out=CT[:], in_=ps[:, 0:256].rearrange("p (a b) -> p a b", a=2), func=AF.Copy))

            pwS, crS, ciS, BS_, tS = sch.pwS, sch.crS, sch.ciS, sch.BS_, sch.tS

            def cmul(o_r, o_i, a_r, a_i, b_r, b_i, t1, t2, rd, wr, eng="dve"):
                TT_(eng, t1, a_r, b_r, mul, rd, wr)
                TT_(eng, t2, a_i, b_i, mul, rd, wr)
                TT_(eng, o_r, t1, t2, sub, rd, wr)
                TT_(eng, t1, a_r, b_i, mul, rd, wr)
                TT_(eng, t2, a_i, b_r, mul, rd, wr)
                TT_(eng, o_i, t1, t2, add, rd, wr)
            cw1, cw1B = mk_("cw1", [128, 8, 16])
            cw2, cw2B = mk_("cw2", [128, 8, 16])
            cw3, cw3B = mk_("cw3", [128, 8, 16])
            CTr = CT[:, 0, :].rearrange("p (j h) -> p j h", j=8)
            CTi = CT[:, 1, :].rearrange("p (j h) -> p j h", j=8)
            bc16 = lambda ap: ap.unsqueeze(2).to_broadcast([128, 8, 16])
            msv = msall[:].rearrange("p j (a h) -> p j a h", a=8)
            for n in range(9):
                pr, pi = pwS[n]
                TT_("dve", cw1[:], CTr, bc16(pr), mul, [CTB, BS_, cw1B], [cw1B])
                TT_("dve", cw2[:], CTi, bc16(pi), mul, [CTB, BS_, cw2B], [cw2B])
                TT_("dve", cw3[:], cw1[:], cw2[:], sub, [cw1B, cw2B, cw3B], [cw3B])
                OP("dve", [cw3B, msB, CAB], [CAB], lambda e, n=n: e.tensor_tensor(out=CA[:, :, n, 0, :].rearrange("p j (a h) -> p j a h", a=8), in0=cw3[:].unsqueeze(2).to_broadcast([128, 8, 8, 16]), in1=msv, op=mul))
                TT_("dve", cw1[:], CTr, bc16(pi), mul, [CTB, BS_, cw1B], [cw1B])
                TT_("dve", cw2[:], CTi, bc16(pr), mul, [CTB, BS_, cw2B], [cw2B])
                OP("dve", [cw1B, cw2B, cw3B], [cw3B], lambda e: e.scalar_tensor_tensor(out=cw3[:], in0=cw1[:], scalar=-1.0, in1=cw2[:], op0=mul, op1=sub))
                OP("dve", [cw3B, msB, CAB], [CAB], lambda e, n=n: e.tensor_tensor(out=CA[:, :, n, 1, :].rearrange("p j (a h) -> p j a h", a=8), in0=cw3[:].unsqueeze(2).to_broadcast([128, 8, 8, 16]), in1=msv, op=mul))
            cB, cBB = mk_("cB", [128, 8, 2, 32], BF16)
            m2 = mk[:, 4, 0:32].rearrange("p (a h) -> p a h", a=2).unsqueeze(1).to_broadcast([128, 8, 2, 16])
            BSr, BSi = BS[:, 0, :, :], BS[:, 1, :, :]
            TT_("dve", cw1[:], BSr, bc16(crS), mul, [BSB, BS_, cw1B], [cw1B])
            TT_("dve", cw2[:], BSi, bc16(ciS), mul, [BSB, BS_, cw2B], [cw2B])
            TT_("dve", cw3[:], cw1[:], cw2[:], sub, [cw1B, cw2B, cw3B], [cw3B])
            OP("dve", [cw3B, mkB], [cBB], lambda e: e.tensor_tensor(out=cB[:, :, 0, :].rearrange("p j (a h) -> p j a h", a=2), in0=cw3[:].unsqueeze(2).to_broadcast([128, 8, 2, 16]), in1=m2, op=mul))
            TT_("dve", cw1[:], BSr, bc16(ciS), mul, [BSB, BS_, cw1B], [cw1B])
            TT_("dve", cw2[:], BSi, bc16(crS), mul, [BSB, BS_, cw2B], [cw2B])
            TT_("dve", cw3[:], cw1[:], cw2[:], add, [cw1B, cw2B, cw3B], [cw3B])
            OP("dve", [cw3B, mkB], [cBB], lambda e: e.tensor_tensor(out=cB[:, :, 1, :].rearrange("p j (a h) -> p j a h", a=2), in0=cw3[:].unsqueeze(2).to_broadcast([128, 8, 2, 16]), in1=m2, op=mul))
            for hf in range(2):
                for tau in range(8):
                    ps, pB = next_ps()
                    for j4 in range(4):
                        j = 4 * hf + j4
                        OP("pe", [cBB, CAB], [pB], lambda e, ps=ps, j=j, j4=j4, tau=tau: e.matmul(ps[32 * j4:32 * j4 + 32, 0:128], lhsT=cB[:, j, 0, :], rhs=CA[:, j, tau, 0, :], start=True, stop=False, tile_position=(0, 32 * j4)))
                        OP("pe", [cBB, CAB], [pB], lambda e, ps=ps, j=j, j4=j4, tau=tau: e.matmul(ps[32 * j4:32 * j4 + 32, 0:128], lhsT=cB[:, j, 1, :], rhs=CA[:, j, tau, 1, :], start=False, stop=True, tile_position=(0, 32 * j4)))
                    if tau == 0:
                        OP("dve", [pB, dcB, c.Bc], [BDB], lambda e, ps=ps, hf=hf: e.scalar_tensor_tensor(out=BD[:, hf, 0, :], in0=c.ident_f, scalar=dcol[:, hf:hf + 1], in1=ps[:, 0:128], op0=mul, op1=add))
                    else:
                        OP("act", [pB], [BDB], lambda e, ps=ps, hf=hf, tau=tau: e.activation(out=BD[:, hf, tau, :], in_=ps[:, 0:128], func=AF.Copy))
            Gt = [mk_("Gt%d" % i, [128, 2, 8]) for i in range(2)]
            Ws = [mk_("Ws%d" % i, [128, 2, 2, 128]) for i in range(2)]
            wt1, wt1B = mk_("wt1", [128, 2, 128])
            wt2, wt2B = mk_("wt2", [128, 2, 128])
            v4 = lambda ap: ap.rearrange("p a (b c) -> p a b c", b=4)
            Brv = Bsrc[:, 0].rearrange("p a b c d -> p a b (c d)")
            Biv = Bsrc[:, 1].rearrange("p a b c d -> p a b (c d)")
            gb = lambda ap: ap.rearrange("p (a b) -> p a b", a=2).unsqueeze(3).to_broadcast([128, 2, 4, 32])
            mh = mk[:, 0:4, :]
            for tau in range(8):
                G_, GB_ = Gt[tau % 2]
                if tau == 0:
                    OP("dve", [BS_, GB_], [GB_], lambda e, G_=G_: e.tensor_copy(out=G_[:, 0, :], in_=crS))
                    OP("dve", [BS_, GB_], [GB_], lambda e, G_=G_: e.tensor_copy(out=G_[:, 1, :], in_=ciS))
                else:
                    Gp, GpB = Gt[(tau - 1) % 2]
                    cmul(G_[:, 0, :], G_[:, 1, :], Gp[:, 0, :], Gp[:, 1, :], pwS[1][0], pwS[1][1], tS("t1"), tS("t2"), [BS_, GpB, GB_], [BS_, GB_])
                W_, WB_ = Ws[tau % 2]
                TT_("dve", v4(wt1[:]), Brv, gb(G_[:, 0, :]), mul, [BsB, GB_, wt1B], [wt1B])
                TT_("dve", v4(wt2[:]), Biv, gb(G_[:, 1, :]), mul, [BsB, GB_, wt2B], [wt2B])
                TT_("dve", W_[:, 0, :, :], wt1[:], wt2[:], sub, [wt1B, wt2B, WB_], [WB_])
                TT_("dve", v4(wt1[:]), Biv, gb(G_[:, 0, :]), mul, [BsB, GB_, wt1B], [wt1B])
                TT_("dve", v4(wt2[:]), Brv, gb(G_[:, 1, :]), mul, [BsB, GB_, wt2B], [wt2B])
                TT_("dve", W_[:, 1, :, :], wt1[:], wt2[:], add, [wt1B, wt2B, WB_], [WB_])
                ps, pB = next_ps()
                for ri in range(2):
                    for hf in range(2):
                        k4 = ri * 2 + hf
                        OP("pe", [WB_, c.Bc], [pB], lambda e, ps=ps, ri=ri, hf=hf, k4=k4, W_=W_: e.transpose(out=ps[:, k4 * 128:(k4 + 1) * 128], in_=W_[:, ri, hf, :], identity=c.ident_f))
                for ri in range(2):
                    for hf in range(2):
                        k4 = ri * 2 + hf
                        OP("dve", [pB, mkB, WZB[hf]], [WZB[hf]], lambda e, ps=ps, ri=ri, hf=hf, k4=k4, tau=tau: e.tensor_tensor(out=WZ[:, 4 * hf:4 * hf + 4, tau, ri, :], in0=ps[:, k4 * 128:(k4 + 1) * 128].unsqueeze(1).to_broadcast([128, 4, 128]), in1=mh, op=mul))
        with contextlib.nullcontext():
            sr = st
            Xb = [[[sb(sr, "X%d%d%d" % (s_, ri, ab), [128, 256 + NCH], F32) for ab in range(2)] for ri in range(2)] for s_ in range(2)]
            XB = [[[Buf() for ab in range(2)] for ri in range(2)] for s_ in range(2)]
            for s_ in range(2):
                for ri in range(2):
                    for ab in range(2):
                        OP("pool", [], [XB[s_][ri][ab]], lambda e, t=Xb[s_][ri][ab]: e.memset(t[:, 0:256], 0.0))
            Xp = sb(sr, "Xp", [128, 8, 2, NCH], BF16)
            XpB = [Buf() for _ in range(8)]
            ysT = sb(sr, "ysT", [128, 2, T], BF16)
            ysB = [Buf() for _ in range(2)]
            wgl = sb(sr, "wgl", [128, 2, 512], BF16)
            wglB = [Buf()]
            c.wload(wgl[:], c.w_glu[l].rearrange("(kc p) n -> p kc n", p=128), wglB)
            sg = [sb(sr, "s_sg%d" % i, [128, TT], F32) for i in range(2)]
            sgB = [Buf() for _ in range(2)]
            yst = [sb(sr, "s_yst%d" % i, [128, 2, TT], BF16) for i in range(2)]
            ystB = [Buf() for _ in range(2)]
            for j in range(8):
                hf = j // 4
                s_ = j % 2
                for ri in range(2):
                    ps, pB = next_ps()
                    for tau in range(8):
                        OP("pe", [WZB[hf]] + uB[hf], [pB], lambda e, ps=ps, j=j, tau=tau, ri=ri, hf=hf: e.matmul(ps[:, 0:NCH], lhsT=WZ[:, j, tau, ri, :], rhs=u_ssm[:, hf, 15 - tau:15 - tau + (NCH - 1) * 8 + 1:8], start=(tau == 0), stop=(tau == 7)))
                    OP("act", [pB], [XB[s_][ri][0]], lambda e, ps=ps, t=Xb[s_][ri][0]: e.activation(out=t[:, 256:256 + NCH], in_=ps[:, 0:NCH], func=AF.Copy))
                cur = 0
                for k in range(9):
                    sh = 1 << k
                    sr_, si_ = Xb[s_][0][cur], Xb[s_][1][cur]
                    dr_, di_ = Xb[s_][0][1 - cur], Xb[s_][1][1 - cur]
                    sBr, sBi = XB[s_][0][cur], XB[s_][1][cur]
                    dBr, dBi = XB[s_][0][1 - cur], XB[s_][1][1 - cur]
                    lo, hi = 256 - sh, 256 + NCH - sh
                    OP("dve", [sBr, LVB], [dBr], lambda e, sr_=sr_, dr_=dr_, lo=lo, hi=hi, k=k, j=j: e.scalar_tensor_tensor(out=dr_[:, 256:256 + NCH], in0=sr_[:, lo:hi], scalar=LV[:, 0, k, j:j + 1], in1=sr_[:, 256:256 + NCH], op0=mul, op1=add))
                    OP("dve", [sBi, LVB, dBr], [dBr], lambda e, si_=si_, dr_=dr_, lo=lo, hi=hi, k=k, j=j: e.scalar_tensor_tensor(out=dr_[:, 256:256 + NCH], in0=si_[:, lo:hi], scalar=LV[:, 2, k, j:j + 1], in1=dr_[:, 256:256 + NCH], op0=mul, op1=add))
                    OP("dve", [sBr, sBi, LVB], [dBi], lambda e, sr_=sr_, si_=si_, di_=di_, lo=lo, hi=hi, k=k, j=j: e.scalar_tensor_tensor(out=di_[:, 256:256 + NCH], in0=sr_[:, lo:hi], scalar=LV[:, 1, k, j:j + 1], in1=si_[:, 256:256 + NCH], op0=mul, op1=add))
                    OP("dve", [sBi, LVB, dBi], [dBi], lambda e, si_=si_, di_=di_, lo=lo, hi=hi, k=k, j=j: e.scalar_tensor_tensor(out=di_[:, 256:256 + NCH], in0=si_[:, lo:hi], scalar=LV[:, 0, k, j:j + 1], in1=di_[:, 256:256 + NCH], op0=mul, op1=add))
                    cur = 1 - cur
                for ri in range(2):
                    OP("act", [XB[s_][ri][cur]], [XpB[j]], lambda e, t=Xb[s_][ri][cur], j=j, ri=ri: e.activation(out=Xp[:, j, ri, :], in_=t[:, 255:255 + NCH], func=AF.Copy))
            kk = 0
            for hf in range(2):
                for s in range(8):
                    ps, pB = next_ps()
                    mms = []
                    for j4 in range(4):
                        j = 4 * hf + j4
                        for ri in range(2):
                            mms.append(([CAB, XpB[j]], CA[:, j, s + 1, ri, :], Xp[:, j, ri, :]))
                    for tau in range(s + 1):
                        mms.append(([BDB] + uB[hf], BD[:, hf, tau, :], u_ssm[:, hf, 8 + s - tau:8 + s - tau + (NCH - 1) * 8 + 1:8]))
                    for i, (rd, lh, rh) in enumerate(mms):
                        OP("pe", rd, [pB], lambda e, ps=ps, lh=lh, rh=rh, i=i, n=len(mms): e.matmul(ps[:, 0:NCH], lhsT=lh, rhs=rh, start=(i == 0), stop=(i == n - 1)))
                    if kk % 2 == 0:
                        OP("act", [pB], [ysB[hf]], lambda e, ps=ps, hf=hf, s=s: e.activation(out=ysT[:, hf, s:T:8], in_=ps[:, 0:NCH], func=AF.Copy))
                    else:
                        OP("dve", [pB], [ysB[hf]], lambda e, ps=ps, hf=hf, s=s: e.tensor_copy(out=ysT[:, hf, s:T:8], in_=ps[:, 0:NCH]))
                    kk += 1
            for q in range(NQ):
                qs = slice(q * TT, (q + 1) * TT)
                Y_, YB_ = yst[q % 2], ystB[q % 2]
                for cc in range(2):
                    pv, pvB = next_ps()
                    pg, pgB = next_ps()
                    for kc in range(2):
                        OP("pe", [wglB[0], ysB[kc]], [pvB], lambda e, pv=pv, kc=kc, cc=cc, qs=qs: e.matmul(pv[:, 0:TT], lhsT=wgl[:, kc, cc * 128:(cc + 1) * 128], rhs=ysT[:, kc, qs], start=(kc == 0), stop=(kc == 1)))
                    for kc in range(2):
                        OP("pe", [wglB[0], ysB[kc]], [pgB], lambda e, pg=pg, kc=kc, cc=cc, qs=qs: e.matmul(pg[:, 0:TT], lhsT=wgl[:, kc, 256 + cc * 128:256 + (cc + 1) * 128], rhs=ysT[:, kc, qs], start=(kc == 0), stop=(kc == 1)))
                    s2, s2B = sg[cc], sgB[cc]
                    OP("act", [pgB], [s2B], lambda e, pg=pg, s2=s2: e.activation(out=s2[:], in_=pg[:, 0:TT], func=AF.Sigmoid))
                    OP("dve", [pvB, s2B], [YB_], lambda e, pv=pv, s2=s2, Y_=Y_, cc=cc: e.tensor_tensor(out=Y_[:, cc, :], in0=pv[:, 0:TT], in1=s2[:], op=mul))
                P.dma(lambda e, qs=qs, Y_=Y_: e.dma_start(out=ybrv[2][:, :, qs], in_=Y_[:]), reads=[YB_], writes=[c.YB[2][q]])


def mixer_m4(c, l, parts, ybrv):
    P, OP, sb, nc, next_ps = c.P, c.OP, c.sb, c.nc, c.next_ps
    order = [b for b, nm in enumerate(("att", "pool", "ssm", "conv")) if nm in parts]
    with contextlib.ExitStack() as st:
        wgt = sb(st, "m4_wg", [128, 8, 4096], BF16)
        wgtB = [[Buf(), Buf()] for _ in range(4)]
        wiv = c.w_in[l].rearrange("(kc p) n -> p kc n", p=128)
        wbr = sb(st, "m4_wbr", [128, 4, 2, D], BF16)
        wbrB = [Buf() for _ in range(4)]
        for b in order:
            c.wload(wgt[:, :, b * 1024:b * 1024 + 256], wiv[:, :, 2052 + b * 1024:2052 + b * 1024 + 256], [wgtB[b][0]])
            c.wload(wbr[:, b, :, :], c.w_branch[l, b].rearrange("(kc p) n -> p kc n", p=128), [wbrB[b]])
        for b in order:
            c.wload(wgt[:, :, b * 1024 + 256:(b + 1) * 1024], wiv[:, :, 2052 + b * 1024 + 256:2052 + (b + 1) * 1024], [wgtB[b][1]])
        wo = sb(st, "m4_wo", [128, 8, D], BF16)
        woB = [Buf()]
        c.wload(wo[:], c.w_out[l].rearrange("(kc p) n -> p kc n", p=128), woB)
        xn = [sb(st, "m4_xn%d" % i, [128, 8, TT], BF16) for i in range(2)]
        xnB = [Buf() for _ in range(2)]
        ysb = [sb(st, "m4_ys%d" % i, [128, 4, 2, TT], BF16) for i in range(2)]
        ysB = [[Buf() for _ in range(4)] for _ in range(2)]
        m = sb(st, "m4_m", [128, 8, TT], F32)
        mB = [Buf() for _ in range(8)]
        mb = sb(st, "m4_mb", [128, 8, TT], BF16)
        mbB = [Buf() for _ in range(8)]
        sg = [sb(st, "m4_sg%d" % i, [128, TT], F32) for i in range(2)]
        sgB = [Buf() for _ in range(2)]
        tm = [sb(st, "m4_tm%d" % i, [128, TT], F32) for i in range(2)]
        tmB = [Buf() for _ in range(2)]
        hr = [sb(st, "m4_hr%d" % i, [128, TT], F32) for i in range(3)]
        hrB = [Buf() for _ in range(3)]
        k = 0
        kk = 0
        def m4_load(q):
            qs = slice(q * TT, (q + 1) * TT)
            X, XB = xn[q % 2], xnB[q % 2]
            P.dma(lambda e, qs=qs, X=X: e.dma_start(out=X[:], in_=c.xnTv[:, :, qs]), reads=[c.XN[q]], writes=[XB])
            Ys, YsB = ysb[q % 2], ysB[q % 2]
            for b in order:
                P.dma(lambda e, qs=qs, b=b, Ys=Ys: e.dma_start(out=Ys[:, b, :, :], in_=ybrv[b][:, :, qs]), reads=[c.YB[b][q]], writes=[YsB[b]])

        m4_load(0)
        for q in range(c.nq):
            qs = slice(q * TT, (q + 1) * TT)
            X, XB = xn[q % 2], xnB[q % 2]
            Ys, YsB = ysb[q % 2], ysB[q % 2]
            if q + 1 < c.nq:
                m4_load(q + 1)
            for dc in range(8):
                ds = slice(dc * 128, (dc + 1) * 128)
                for bi, b in enumerate(order):
                    pg, pgB = next_ps()
                    for kc in range(8):
                        OP("pe", [wgtB[b][0 if dc < 2 else 1], XB], [pgB], lambda e, pg=pg, kc=kc, b=b, dc=dc, X=X: e.matmul(pg[:, 0:TT], lhsT=wgt[:, kc, b * 1024 + dc * 128:b * 1024 + (dc + 1) * 128], rhs=X[:, kc, :], start=(kc == 0), stop=(kc == 7)))
                    pp, ppB = next_ps()
                    for kc in range(2):
                        OP("pe", [wbrB[b], YsB[b]], [ppB], lambda e, pp=pp, kc=kc, b=b, ds=ds, Ys=Ys: e.matmul(pp[:, 0:TT], lhsT=wbr[:, b, kc, ds], rhs=Ys[:, b, kc, :], start=(kc == 0), stop=(kc == 1)))
                    s_, sB_ = sg[kk % 2], sgB[kk % 2]
                    OP("act", [pgB], [sB_], lambda e, s_=s_, pg=pg: e.activation(out=s_[:], in_=pg[:, 0:TT], func=AF.Sigmoid))
                    last = (bi == len(order) - 1)
                    if bi == 0 and last:
                        OP("dve", [sB_, ppB], [mbB[dc]], lambda e, s_=s_, pp=pp, dc=dc: e.tensor_tensor(out=mb[:, dc, :], in0=pp[:, 0:TT], in1=s_[:], op=ALU.mult))
                    elif bi == 0:
                        OP("dve", [sB_, ppB], [mB[dc]], lambda e, s_=s_, pp=pp, dc=dc: e.tensor_tensor(out=m[:, dc, :], in0=pp[:, 0:TT], in1=s_[:], op=ALU.mult))
                    else:
                        t_, tB_ = tm[kk % 2], tmB[kk % 2]
                        OP("dve", [sB_, ppB], [tB_], lambda e, s_=s_, pp=pp, t_=t_: e.tensor_tensor(out=t_[:], in0=pp[:, 0:TT], in1=s_[:], op=ALU.mult))
                        if last:
                            OP("pool", [tB_, mB[dc]], [mbB[dc]], lambda e, t_=t_, dc=dc: e.tensor_tensor(out=mb[:, dc, :], in0=m[:, dc, :], in1=t_[:], op=ALU.add))
                        else:
                            OP("pool", [tB_, mB[dc]], [mB[dc]], lambda e, t_=t_, dc=dc: e.tensor_tensor(out=m[:, dc, :], in0=m[:, dc, :], in1=t_[:], op=ALU.add))
                    kk += 1
            for d2 in range(8):
                po, poB = next_ps()
                for dc in range(8):
                    OP("pe", [woB[0], mbB[dc]], [poB], lambda e, po=po, dc=dc, d2=d2: e.matmul(po[:, 0:TT], lhsT=wo[:, dc, d2 * 128:(d2 + 1) * 128], rhs=mb[:, dc, :], start=(dc == 0), stop=(dc == 7)))
                r, rB = hr[k % 3], hrB[k % 3]
                k += 1
                P.dma(lambda e, r=r, d2=d2, qs=qs: e.dma_start(out=r[:], in_=c.hTv[:, d2, qs]), reads=[c.HT[q][d2]], writes=[rB])
                OP("dve", [poB, rB], [rB], lambda e, r=r, po=po: e.tensor_tensor(out=r[:], in0=po[:, 0:TT], in1=r[:], op=ALU.add))
                P.dma(lambda e, r=r, d2=d2, qs=qs: e.dma_start(out=c.hTv[:, d2, qs], in_=r[:]), reads=[rB], writes=[c.HT[q][d2]])


_NAMES = ["norm_g", "ffn_w_gate", "ffn_w_up", "ffn_w_down", "w_in", "f_bias", "pool_w", "pool_scale", "ssm_lam_re",
          "ssm_lam_im", "ssm_log_dt", "ssm_b_re", "ssm_b_im", "ssm_c_re", "ssm_c_im", "ssm_d", "ssm_w_glu", "conv_w",
          "w_branch", "w_out", "ple_w_gate", "ple_w_proj", "final_g"]


def run(inputs, n_cores=8, ret_all=False, **bk):
    nc = build(**bk)
    cst = make_consts()
    shared = {k: np.ascontiguousarray(np.asarray(inputs[k], dtype=np.float32)) for k in _NAMES}
    xs = np.asarray(inputs["x"], dtype=np.float32)
    ps = np.asarray(inputs["p"], dtype=np.float32)
    in_maps = []
    for b in range(n_cores):
        m = dict(shared)
        m["x"] = np.ascontiguousarray(xs[b])
        m["p"] = np.ascontiguousarray(ps[:, b])
        m["consts"] = cst
        in_maps.append(m)
    res = run_bass_kernel_spmd(nc, in_maps, core_ids=list(range(n_cores)))
    if ret_all:
        return res.results
    return np.stack([np.asarray(r["y"]) for r in res.results], axis=0)


def kernel(**inputs):
    return run(inputs, n_cores=8).astype(np.float32)
```

```python
import contextlib
import numpy as np
import concourse.bass as bass
import concourse.mybir as mybir
from concourse.bass_utils import run_bass_kernel_spmd

F32 = mybir.dt.float32
BF16 = mybir.dt.bfloat16
AF = mybir.ActivationFunctionType
ALU = mybir.AluOpType

D = 1024
T = 4096
DEPTH = 2
DFF = 2816
NFC = DFF // 128
INC = 6148
PLE = 256
TT = 512
NQ = T // TT
EPS = 1e-6

COMPUTE = ("pe", "act", "dve", "pool")
NDMASEM = 56
NSP = 40


class Buf:
    __slots__ = ("name", "w", "r", "wl")

    def __init__(self, name=""):
        self.name = name
        self.w = None
        self.r = []
        self.wl = []


class Node:
    __slots__ = ("eng", "idx", "fn", "waits", "key", "val", "needs_inc", "clock", "is_dma")


class Prog:
    def __init__(self, nc):
        self.nc = nc
        self.ops = {e: [] for e in ("pe", "act", "dve", "pool", "sp")}
        self.clock = {e: {} for e in self.ops}
        self.dma_rr = 0
        self.dma_rr2 = 0
        self.dma_last = [None] * NDMASEM
        self.dma_cum = [0] * NDMASEM
        self.out_nodes = []

    def _record(self, eng, fn, reads, writes, is_dma, extra=(), shared=False):
        n = Node()
        n.eng = eng
        n.idx = len(self.ops[eng])
        n.fn = fn
        n.is_dma = is_dma
        n.needs_inc = False
        deps = []
        for b in reads:
            if b.w is not None:
                deps.append(b.w)
            deps.extend(b.wl)
        for b in writes:
            if not shared:
                if b.w is not None:
                    deps.append(b.w)
                deps.extend(b.wl)
            deps.extend(b.r)
        deps.extend(extra)
        if is_dma:
            if eng == "sp":
                s = self.dma_rr
                self.dma_rr = (self.dma_rr + 1) % NSP
            else:
                s = NSP + self.dma_rr2
                self.dma_rr2 = (self.dma_rr2 + 1) % (NDMASEM - NSP)
            if self.dma_last[s] is not None:
                deps.append(self.dma_last[s])
            self.dma_cum[s] += 16
            n.key = ("d", s)
            n.val = self.dma_cum[s]
            self.dma_last[s] = n
        else:
            n.key = eng
            n.val = n.idx + 1
        ck = self.clock[eng]
        waits = {}
        for d in deps:
            if (not d.is_dma) and d.eng == eng and eng == "pe":
                continue
            if ck.get(d.key, 0) >= d.val:
                continue
            if waits.get(d.key, (0, None))[0] < d.val:
                waits[d.key] = (d.val, d)
        n.waits = [w[1] for w in waits.values()]
        if n.waits:
            ck = dict(ck)
            for d in n.waits:
                d.needs_inc = True
                for k, v in d.clock.items():
                    if ck.get(k, 0) < v:
                        ck[k] = v
                if ck.get(d.key, 0) < d.val:
                    ck[d.key] = d.val
            self.clock[eng] = ck
        n.clock = ck
        self.ops[eng].append(n)
        for b in reads:
            b.r.append(n)
        for b in writes:
            if shared:
                b.wl.append(n)
            else:
                b.w = n
                b.wl = []
                b.r = []
        return n

    def op(self, eng, fn, reads=(), writes=()):
        return self._record(eng, fn, reads, writes, False)

    def dma(self, fn, reads=(), writes=(), q="sp", is_out=False, shared=False):
        n = self._record(q, fn, reads, writes, True, shared=shared)
        if is_out:
            self.out_nodes.append(n)
        return n

    def barrier(self):
        last = [self.ops[e][-1] for e in COMPUTE if self.ops[e]]
        for e in COMPUTE:
            for n in reversed(self.ops[e]):
                if not n.is_dma:
                    last.append(n)
                    break
        last += [d for d in self.dma_last if d is not None]
        for e in ("pe", "act", "dve", "pool", "sp"):
            self._record(e, lambda eng: eng.nop(), (), (), False, extra=last)

    def emit(self, es):
        nc = self.nc
        sems = {}
        for e in COMPUTE:
            sems[e] = es.enter_context(nc.semaphore("S_" + e))
        for i in range(NDMASEM):
            sems[("d", i)] = es.enter_context(nc.semaphore("D%d" % i))
        for e in COMPUTE:
            c = 0
            for n in self.ops[e]:
                if n.is_dma:
                    continue
                if n.needs_inc:
                    c += 1
                    n.val = c
                else:
                    n.val = None
        block = es.enter_context(nc.Block())

        def run(ename):
            def body(eng):
                for n in self.ops[ename]:
                    for d in n.waits:
                        eng.wait_ge(sems[d.key], d.val)
                    ins = n.fn(eng)
                    if n.is_dma:
                        ins.then_inc(sems[n.key], 16)
                    elif n.needs_inc:
                        ins.then_inc(sems[n.key], 1)
                if ename == "sp":
                    for d in self.dma_last:
                        if d is not None:
                            eng.wait_ge(sems[d.key], d.val)
            return body

        block.tensor(run("pe"))
        block.scalar(run("act"))
        block.vector(run("dve"))
        block.gpsimd(run("pool"))
        block.sync(run("sp"))


C_ID = 0
C_ONES = 128
C_TRIU = 256
C_MNEG = 384
C_MH = 512
C_MS = 1024
C_RCW = 1536
C_RC0 = 1538
C_PM = C_RC0 + 1024
C_N = C_PM + 12 * 128
C_G = 512


def make_consts():
    c = np.zeros((128, C_N), np.float32)
    k = np.arange(128)
    c[:, C_ID:C_ID + 128] = np.eye(128, dtype=np.float32)
    c[:, C_ONES:C_ONES + 128] = 1.0
    c[:, C_TRIU:C_TRIU + 128] = (k[:, None] <= k[None, :]).astype(np.float32)
    c[:, C_MNEG:C_MNEG + 128] = np.where(k[None, :] < k[:, None], -30000.0, 0.0)
    rj4, rg2 = k // 32, (k // 16) % 2
    cg2 = k // 64
    for j4 in range(4):
        c[:, C_MH + j4 * 128:C_MH + (j4 + 1) * 128] = ((rj4[:, None] == j4) & (rg2[:, None] == cg2[None, :])).astype(np.float32)
        c[:, C_MS + j4 * 128:C_MS + (j4 + 1) * 128] = ((cg2[:, None] == rg2[None, :]) & (rj4[None, :] == j4)).astype(np.float32)
    wins = np.array([2, 4, 8, 16], np.float32)
    for ch in range(2):
        w = wins[2 * ch + (k // 64)]
        c[:, C_RCW + ch] = 1.0 / w
        t = np.arange(512, dtype=np.float32)
        c[:, C_RC0 + ch * 512:C_RC0 + (ch + 1) * 512] = 1.0 / np.minimum(t[None, :] + 1.0, w[:, None])
    tp = k[:, None].astype(np.float64)
    tq = k[None, :].astype(np.float64)
    for g in range(4):
        W = float(wins[g])
        main = np.where((tp <= tq) & (tp > tq - W), 1.0 / W, 0.0) - np.eye(128)
        corner = np.where((tp - 128 > tq - W), 1.0 / W, 0.0)
        cnt = np.minimum(tq + 1.0, W)
        main0 = np.where((tp <= tq) & (tp > tq - W), 1.0 / cnt, 0.0) - np.eye(128)
        for i, mtx in enumerate((main, corner, main0)):
            o = C_PM + (g * 3 + i) * 128
            c[:, o:o + 128] = mtx.astype(np.float32)
    return c


def build(phases=("in", "ffn", "mix", "ple", "out"), depth=DEPTH, mix_parts=("att", "pool", "ssm", "conv"), debug=False, nq=NQ):
    nc = bass.Bass("TRN2", target_bir_lowering=False)
    dt_in = lambda name, shape: nc.dram_tensor(name, list(shape), F32, kind="ExternalInput").ap()
    x = dt_in("x", [T, D])
    p_in = dt_in("p", [DEPTH, T, PLE])
    norm_g = dt_in("norm_g", [DEPTH, 4, D])
    w_gate = dt_in("ffn_w_gate", [DEPTH, 2, D, DFF])
    w_up = dt_in("ffn_w_up", [DEPTH, 2, D, DFF])
    w_down = dt_in("ffn_w_down", [DEPTH, 2, DFF, D])
    w_in = dt_in("w_in", [DEPTH, D, INC])
    f_bias = dt_in("f_bias", [DEPTH, 4])
    pool_w = dt_in("pool_w", [DEPTH, 4, 64, 64])
    pool_scale = dt_in("pool_scale", [DEPTH, 256])
    lam_re = dt_in("ssm_lam_re", [DEPTH, 16, 64])
    lam_im = dt_in("ssm_lam_im", [DEPTH, 16, 64])
    log_dt = dt_in("ssm_log_dt", [DEPTH, 16])
    b_re = dt_in("ssm_b_re", [DEPTH, 16, 64, 16])
    b_im = dt_in("ssm_b_im", [DEPTH, 16, 64, 16])
    c_re = dt_in("ssm_c_re", [DEPTH, 16, 16, 64])
    c_im = dt_in("ssm_c_im", [DEPTH, 16, 16, 64])
    ssm_d = dt_in("ssm_d", [DEPTH, 256])
    w_glu = dt_in("ssm_w_glu", [DEPTH, 256, 512])
    conv_w = dt_in("conv_w", [DEPTH, 3, 256])
    w_branch = dt_in("w_branch", [DEPTH, 4, 256, D])
    w_out = dt_in("w_out", [DEPTH, D, D])
    ple_wg = dt_in("ple_w_gate", [DEPTH, D, D])
    ple_wp = dt_in("ple_w_proj", [DEPTH, PLE, D])
    final_g = dt_in("final_g", [D])
    consts = dt_in("consts", [128, C_N])
    y_out = nc.dram_tensor("y", [T, D], F32, kind="ExternalOutput").ap()

    dk = dict(kind="ExternalOutput") if debug else {}
    hT = nc.dram_tensor("hT_scr", [D, T], F32, **dk).ap()
    xnT = nc.dram_tensor("xnT_scr", [D, T], BF16, **dk).ap()
    ybr = nc.dram_tensor("ybr_scr", [4, 256, T], BF16, **dk).ap()
    qaug = nc.dram_tensor("qaug_scr", [128, 4, T], BF16, **dk).ap()
    vtok = nc.dram_tensor("vtok_scr", [T, 256], BF16, **dk).ap()
    hTv = hT.rearrange("(c p) t -> p c t", p=128)
    xnTv = xnT.rearrange("(c p) t -> p c t", p=128)

    es = contextlib.ExitStack()
    P = Prog(nc)
    OP = lambda eng, reads, writes, fn: P.op(eng, fn, reads, writes)

    uid = [0]

    def sb(stack, name, shape, dt):
        uid[0] += 1
        return stack.enter_context(nc.sbuf_tensor("%s_u%d" % (name, uid[0]), list(shape), dt))

    def dbg_dump(name, ap, shape, dt, reads):
        if not debug:
            return
        t = nc.dram_tensor("dbg_" + name, list(shape), dt, kind="ExternalOutput").ap()
        P.dma(lambda e: e.dma_start(out=t, in_=ap), reads=reads, writes=[Buf()])

    HT = [[Buf("hT%d_%d" % (q, c)) for c in range(8)] for q in range(NQ)]
    XN = [Buf("xnT%d" % q) for q in range(NQ)]
    YB = [[Buf("ybr%d_%d" % (b, q)) for q in range(NQ)] for b in range(4)]
    OUTB = Buf("out")

    psb = [es.enter_context(nc.psum_tensor("psb%d" % i, [128, 512], F32)) for i in range(8)]
    psB = [Buf("psb%d" % i) for i in range(8)]
    ps_rr = [0]

    def next_ps():
        i = ps_rr[0]
        ps_rr[0] = (i + 1) % 6
        return psb[i], psB[i]

    cst = sb(es, "cst", [128, 512], F32)
    cstb = sb(es, "cstb", [128, 512], BF16)
    Bc = Buf("cst")
    Bcb = Buf("cstb")
    P.dma(lambda e: e.dma_start(out=cst[:], in_=consts[:, 0:512]), writes=[Bc])
    OP("dve", [Bc], [Bcb], lambda e: e.tensor_copy(out=cstb[:], in_=cst[:, 0:512]))
    ident_f = cst[:, C_ID:C_ID + 128]
    ones_f = cst[:, C_ONES:C_ONES + 128]
    triu_f = cst[:, C_TRIU:C_TRIU + 128]
    ident_b = cstb[:, C_ID:C_ID + 128]
    ones_b = cstb[:, C_ONES:C_ONES + 128]
    mneg_b = cstb[:, C_MNEG:C_MNEG + 128]
    epsc = sb(es, "epsc", [128, 2], F32)
    Beps = Buf("eps")
    OP("dve", [], [Beps], lambda e: e.memset(epsc[:, 0:1], EPS))
    OP("dve", [Beps], [Beps], lambda e: e.memset(epsc[:, 1:2], 1.0))
    gcol = sb(es, "gcol", [128, 9, 8], F32)
    Bg = Buf("gcol")
    P.dma(lambda e: e.dma_start(out=gcol[:, 0:8, :], in_=norm_g.rearrange("l n (c p) -> p (l n) c", p=128), allow_slow_non_contiguous=True), writes=[Bg])
    P.dma(lambda e: e.dma_start(out=gcol[:, 8, :], in_=final_g.rearrange("(c p) -> p c", p=128), allow_slow_non_contiguous=True), writes=[Bg])

    def rmsnorm(h, hB, gi, xn, xnB, tmp):
        ps, pB = next_ps()
        for c in range(8):
            sq, sqB = tmp["sq"][c % 2]
            OP("act", [hB[c]], [sqB], lambda e, c=c, sq=sq: e.activation(out=sq, in_=h[:, c, :], func=AF.Square))
            OP("pe", [sqB, Bcb], [pB], lambda e, c=c, sq=sq, ps=ps: e.matmul(ps[:, 0:TT], lhsT=ones_b, rhs=sq, start=(c == 0), stop=(c == 7)))
        lnv, lnB = tmp["lnv"]
        rstd, rsB = tmp["rstd"]
        OP("act", [pB, Beps], [lnB], lambda e, ps=ps: e.activation(out=lnv, in_=ps[:, 0:TT], func=AF.Ln, scale=1.0 / D, bias=epsc[:, 0:1]))
        OP("act", [lnB], [rsB], lambda e: e.activation(out=rstd, in_=lnv, func=AF.Exp, scale=-0.5))
        for c in range(8):
            OP("dve", [hB[c], rsB, Bg], [xnB[c]],
               lambda e, c=c: e.scalar_tensor_tensor(out=xn[:, c, :], in0=h[:, c, :], scalar=gcol[:, gi, c:c + 1], in1=rstd,
                                                     op0=ALU.mult, op1=ALU.mult))

    def norm_tmp(stack, tag):
        sq0 = sb(stack, "sq0" + tag, [128, TT], BF16)
        sq1 = sb(stack, "sq1" + tag, [128, TT], BF16)
        lnv = sb(stack, "lnv" + tag, [128, TT], F32)
        rstd = sb(stack, "rstd" + tag, [128, TT], F32)
        return {"sq": [(sq0[:], Buf()), (sq1[:], Buf())], "lnv": (lnv[:], Buf()), "rstd": (rstd[:], Buf())}

    def load_h(tile, tB, q):
        P.dma(lambda e: e.dma_start(out=tile, in_=hTv[:, :, q * TT:(q + 1) * TT]), reads=HT[q], writes=tB)

    def wload(dst, src, wB):
        P.dma(lambda e: e.dma_start(out=dst, in_=src), writes=wB, q="pool")

    def phase_in():
        P.barrier()
        with contextlib.ExitStack() as st:
            xt = [sb(st, "xt%d" % i, [128, D], F32) for i in range(2)]
            xtB = [Buf() for _ in range(2)]
            ho = [sb(st, "ho%d" % i, [128, 8, TT], F32) for i in range(2)]
            hoB = [Buf() for _ in range(2)]
            k = 0
            for q in range(NQ):
                for s in range(4):
                    tt = q * 4 + s
                    xb, xB = xt[tt % 2], xtB[tt % 2]
                    P.dma(lambda e, xb=xb, tt=tt: e.dma_start(out=xb[:], in_=x[tt * 128:(tt + 1) * 128, :]), writes=[xB])
                    for half in range(2):
                        ps, pB = next_ps()
                        for cc in range(4):
                            c = half * 4 + cc
                            OP("pe", [xB, Bc], [pB], lambda e, ps=ps, cc=cc, c=c, xb=xb: e.transpose(out=ps[:, cc * 128:(cc + 1) * 128], in_=xb[:, c * 128:(c + 1) * 128], identity=ident_f))
                        eng = "act" if (k % 2 == 0) else "dve"
                        k += 1
                        dst = ho[q % 2][:, half * 4:(half + 1) * 4, s * 128:(s + 1) * 128]
                        src = ps[:, :].rearrange("p (c t) -> p c t", c=4)
                        if eng == "act":
                            OP("act", [pB], [hoB[q % 2]], lambda e, dst=dst, src=src: e.activation(out=dst, in_=src, func=AF.Copy))
                        else:
                            OP("dve", [pB], [hoB[q % 2]], lambda e, dst=dst, src=src: e.tensor_copy(out=dst, in_=src))
                P.dma(lambda e, q=q: e.dma_start(out=hTv[:, :, q * TT:(q + 1) * TT], in_=ho[q % 2][:]), reads=[hoB[q % 2]], writes=HT[q])

    def phase_out():
        P.barrier()
        with contextlib.ExitStack() as st:
            hn2 = [sb(st, "fo_hn%d" % i, [128, 8, TT], F32) for i in range(2)]
            hnB2 = [[Buf() for _ in range(8)] for _ in range(2)]
            yn2 = [sb(st, "fo_yn%d" % i, [128, 8, TT], F32) for i in range(2)]
            ynB2 = [[Buf() for _ in range(8)] for _ in range(2)]
            ot = [sb(st, "fo_ot%d" % i, [128, D], F32) for i in range(4)]
            otB = [Buf() for _ in range(4)]
            tmp2 = [norm_tmp(st, "fo%d" % i) for i in range(2)]
            k = 0
            for q in range(NQ):
                hn, hnB, yn, ynB = hn2[q % 2], hnB2[q % 2], yn2[q % 2], ynB2[q % 2]
                load_h(hn[:], hnB, q)
                rmsnorm(hn, hnB, 8, yn, ynB, tmp2[q % 2])
                for s in range(4):
                    tt = q * 4 + s
                    o, oB = ot[tt % 4], otB[tt % 4]
                    for half in range(2):
                        ps, pB = next_ps()
                        for cc in range(4):
                            c = half * 4 + cc
                            OP("pe", [ynB[c], Bc], [pB], lambda e, ps=ps, cc=cc, c=c, s=s, yn=yn: e.transpose(out=ps[:, cc * 128:(cc + 1) * 128], in_=yn[:, c, s * 128:(s + 1) * 128], identity=ident_f))
                        dst = o[:, half * 512:(half + 1) * 512]
                        if k % 2 == 0:
                            OP("act", [pB], [oB], lambda e, dst=dst, ps=ps: e.activation(out=dst, in_=ps[:, :], func=AF.Copy))
                        else:
                            OP("dve", [pB], [oB], lambda e, dst=dst, ps=ps: e.tensor_copy(out=dst, in_=ps[:, :]))
                        k += 1
                    P.dma(lambda e, o=o, tt=tt: e.dma_start(out=y_out[tt * 128:(tt + 1) * 128, :], in_=o[:]), reads=[oB], writes=[Buf()], is_out=True)

    def phase_ffn(l, f):
        P.barrier()
        with contextlib.ExitStack() as st:
            wg = sb(st, "wg", [128, 8, DFF], BF16)
            wu = sb(st, "wu", [128, 8, DFF], BF16)
            wd = sb(st, "wd", [128, NFC, D], BF16)
            CB = [(0, 256), (256, 1024), (1024, 2048), (2048, DFF)]
            wgB = [Buf() for _ in range(4)]
            wuB = [Buf() for _ in range(4)]
            wdB = [Buf() for _ in range(NFC)]
            wgv = w_gate[l, f].rearrange("(kc p) n -> p kc n", p=128)
            wuv = w_up[l, f].rearrange("(kc p) n -> p kc n", p=128)
            wdv = w_down[l, f].rearrange("(fc p) n -> p fc n", p=128)
            for cb, (c0, c1) in enumerate(CB):
                wload(wg[:, :, c0:c1], wgv[:, :, c0:c1], [wgB[cb]])
                wload(wu[:, :, c0:c1], wuv[:, :, c0:c1], [wuB[cb]])
            for f0 in range(0, NFC, 6):
                f1 = min(NFC, f0 + 6)
                wload(wd[:, f0:f1, :], wdv[:, f0:f1, :], wdB[f0:f1])
            hn = sb(st, "ff_hn", [128, 8, TT], F32)
            hnB = [Buf() for _ in range(8)]
            xn = [sb(st, "ff_xn%d" % i, [128, 8, TT], BF16) for i in range(2)]
            xnB = [[Buf() for _ in range(8)] for _ in range(2)]
            act = sb(st, "ff_act", [128, NFC, TT], BF16)
            actB = [Buf() for _ in range(NFC)]
            sg = [sb(st, "ff_sg%d" % i, [128, TT], F32) for i in range(2)]
            sgB = [Buf() for _ in range(2)]
            hr = [sb(st, "ff_hr%d" % i, [128, TT], F32) for i in range(3)]
            hrB = [Buf() for _ in range(3)]
            tmp = norm_tmp(st, "ff")
            gi = l * 4 + (0 if f == 0 else 2)
            k = 0
            load_h(hn[:], hnB, 0)
            rmsnorm(hn, hnB, gi, xn[0], xnB[0], tmp)
            for q in range(nq):
                X, XB = xn[q % 2], xnB[q % 2]
                if q + 1 < nq:
                    load_h(hn[:], hnB, q + 1)
                if q == 0 and l == 0 and f == 0:
                    dbg_dump("xn", X[:], [128, 8, TT], BF16, XB)
                    dbg_dump("wg", wg[:], [128, 8, DFF], BF16, wgB)
                    dbg_dump("wd", wd[:], [128, NFC, D], BF16, wdB)
                for fc in range(NFC):
                    pg, pgB = next_ps()
                    for kc in range(8):
                        OP("pe", [wgB[0 if fc < 2 else 1 + fc // 8], XB[kc]], [pgB], lambda e, pg=pg, kc=kc, fc=fc, X=X: e.matmul(pg[:, 0:TT], lhsT=wg[:, kc, fc * 128:(fc + 1) * 128], rhs=X[:, kc, :], start=(kc == 0), stop=(kc == 7)))
                    pu, puB = next_ps()
                    for kc in range(8):
                        OP("pe", [wuB[0 if fc < 2 else 1 + fc // 8], XB[kc]], [puB], lambda e, pu=pu, kc=kc, fc=fc, X=X: e.matmul(pu[:, 0:TT], lhsT=wu[:, kc, fc * 128:(fc + 1) * 128], rhs=X[:, kc, :], start=(kc == 0), stop=(kc == 7)))
                    s_, sB_ = sg[fc % 2], sgB[fc % 2]
                    OP("act", [pgB], [sB_], lambda e, s_=s_, pg=pg: e.activation(out=s_[:], in_=pg[:, 0:TT], func=AF.Silu))
                    OP("dve", [sB_, puB], [actB[fc]], lambda e, s_=s_, pu=pu, fc=fc: e.tensor_tensor(out=act[:, fc, :], in0=pu[:, 0:TT], in1=s_[:], op=ALU.mult))
                    if fc == 11 and q + 1 < nq:
                        rmsnorm(hn, hnB, gi, xn[(q + 1) % 2], xnB[(q + 1) % 2], tmp)
                if q == 0 and l == 0 and f == 0:
                    dbg_dump("act", act[:], [128, NFC, TT], BF16, actB)
                for dc in range(8):
                    po, poB = next_ps()
                    for fc in range(NFC):
                        OP("pe", [wdB[fc], actB[fc]], [poB], lambda e, po=po, fc=fc, dc=dc: e.matmul(po[:, 0:TT], lhsT=wd[:, fc, dc * 128:(dc + 1) * 128], rhs=act[:, fc, :], start=(fc == 0), stop=(fc == NFC - 1)))
                    r, rB = hr[k % 3], hrB[k % 3]
                    k += 1
                    P.dma(lambda e, r=r, dc=dc, q=q: e.dma_start(out=r[:], in_=hTv[:, dc, q * TT:(q + 1) * TT]), reads=[HT[q][dc]], writes=[rB])
                    OP("dve", [poB, rB], [rB], lambda e, r=r, po=po: e.scalar_tensor_tensor(out=r[:], in0=po[:, 0:TT], scalar=0.5, in1=r[:], op0=ALU.mult, op1=ALU.add))
                    P.dma(lambda e, r=r, dc=dc, q=q: e.dma_start(out=hTv[:, dc, q * TT:(q + 1) * TT], in_=r[:]), reads=[rB], writes=[HT[q][dc]])

    def phase_ple(l, fuse_out=False):
        P.barrier()
        with contextlib.ExitStack() as st:
            wpg = sb(st, "wpg", [128, 8, D], BF16)
            wpp = sb(st, "wpp", [128, 2, D], BF16)
            wpgB = [Buf()]
            wppB = [Buf()]
            wpgv = ple_wg[l].rearrange("(kc p) n -> p kc n", p=128)
            wpgB = [Buf(), Buf()]
            wload(wpg[:, :, 0:256], wpgv[:, :, 0:256], [wpgB[0]])
            wload(wpg[:, :, 256:D], wpgv[:, :, 256:D], [wpgB[1]])
            wload(wpp[:], ple_wp[l].rearrange("(kc p) n -> p kc n", p=128), wppB)
            NH = 3 if fuse_out else 2
            hn2 = [sb(st, "pl_hn%d" % i, [128, 8, TT], F32) for i in range(NH)]
            hnB2 = [[Buf() for _ in range(8)] for _ in range(NH)]
            xn = [sb(st, "pl_xn%d" % i, [128, 8, TT], BF16) for i in range(2)]
            xnB = [[Buf() for _ in range(8)] for _ in range(2)]
            pt = [sb(st, "pl_pt%d" % i, [128, 4, PLE], F32) for i in range(2)]
            ptB = [Buf() for _ in range(2)]
            pT = [sb(st, "pl_pT%d" % i, [128, 2, TT], BF16) for i in range(2)]
            pTB = [Buf() for _ in range(2)]
            sg = [sb(st, "pl_sg%d" % i, [128, TT], F32) for i in range(2)]
            sgB = [Buf() for _ in range(2)]
            tg = [sb(st, "pl_tg%d" % i, [128, TT], F32) for i in range(2)]
            tgB = [Buf() for _ in range(2)]
            hr = [sb(st, "pl_hr%d" % i, [128, TT], F32) for i in range(3)]
            hrB = [Buf() for _ in range(3)]
            tmp2 = [norm_tmp(st, "pl%d" % i) for i in range(2)]
            gi = l * 4 + 3

            def ple_load(q):
                load_h(hn2[q % NH][:], hnB2[q % NH], q)
                pt_, ptB_ = pt[q % 2], ptB[q % 2]
                P.dma(lambda e, pt_=pt_, q=q: e.dma_start(out=pt_[:], in_=p_in[l, q * TT:(q + 1) * TT, :].rearrange("(s p) c -> p s c", p=128)), writes=[ptB_])

            def ple_prep(q):
                rmsnorm(hn2[q % NH], hnB2[q % NH], gi, xn[q % 2], xnB[q % 2], tmp2[q % 2])
                pt_, ptB_ = pt[q % 2], ptB[q % 2]
                pT_, pTB_ = pT[q % 2], pTB[q % 2]
                for c2 in range(2):
                    ps, pB = next_ps()
                    for s in range(4):
                        OP("pe", [ptB_, Bc], [pB], lambda e, ps=ps, s=s, c2=c2, pt_=pt_: e.transpose(out=ps[:, s * 128:(s + 1) * 128], in_=pt_[:, s, c2 * 128:(c2 + 1) * 128], identity=ident_f))
                    OP("act", [pB], [pTB_], lambda e, ps=ps, c2=c2, pT_=pT_: e.activation(out=pT_[:, c2, :], in_=ps[:, :], func=AF.Copy))

            if fuse_out:
                yn2 = [sb(st, "fo_yn%d" % i, [128, 8, TT], F32) for i in range(2)]
                ynB2 = [[Buf() for _ in range(8)] for _ in range(2)]
                ot = [sb(st, "fo_ot%d" % i, [128, D], F32) for i in range(4)]
                otB = [Buf() for _ in range(4)]
                tmpo = [norm_tmp(st, "fo%d" % i) for i in range(2)]
            kev = [0]

            def out_tile(q):
                hn, hnB, yn, ynB = hn2[q % NH], hnB2[q % NH], yn2[q % 2], ynB2[q % 2]
                rmsnorm(hn, hnB, 8, yn, ynB, tmpo[q % 2])
                for s in range(4):
                    tt = q * 4 + s
                    o, oB = ot[tt % 4], otB[tt % 4]
                    for half in range(2):
                        ps, pB = next_ps()
                        for cc in range(4):
                            c_ = half * 4 + cc
                            OP("pe", [ynB[c_], Bc], [pB], lambda e, ps=ps, cc=cc, c_=c_, s=s, yn=yn: e.transpose(out=ps[:, cc * 128:(cc + 1) * 128], in_=yn[:, c_, s * 128:(s + 1) * 128], identity=ident_f))
                        dst = o[:, half * 512:(half + 1) * 512]
                        if kev[0] % 2 == 0:
                            OP("act", [pB], [oB], lambda e, dst=dst, ps=ps: e.activation(out=dst, in_=ps[:, :], func=AF.Copy))
                        else:
                            OP("dve", [pB], [oB], lambda e, dst=dst, ps=ps: e.tensor_copy(out=dst, in_=ps[:, :]))
                        kev[0] += 1
                    P.dma(lambda e, o=o, tt=tt: e.dma_start(out=y_out[tt * 128:(tt + 1) * 128, :], in_=o[:]), reads=[oB], writes=[Buf()], is_out=True)

            ple_load(0)
            ple_prep(0)
            for q in range(NQ):
                hn, hnB = hn2[q % NH], hnB2[q % NH]
                X, XB = xn[q % 2], xnB[q % 2]
                pT_, pTB_ = pT[q % 2], pTB[q % 2]
                if q + 1 < NQ:
                    ple_load(q + 1)
                for dc in range(8):
                    pg, pgB = next_ps()
                    for kc in range(8):
                        OP("pe", [wpgB[0 if dc < 2 else 1], XB[kc]], [pgB], lambda e, pg=pg, kc=kc, dc=dc, X=X: e.matmul(pg[:, 0:TT], lhsT=wpg[:, kc, dc * 128:(dc + 1) * 128], rhs=X[:, kc, :], start=(kc == 0), stop=(kc == 7)))
                    pe_, peB = next_ps()
                    for c2 in range(2):
                        OP("pe", [wppB[0], pTB_], [peB], lambda e, pe_=pe_, c2=c2, dc=dc, pT_=pT_: e.matmul(pe_[:, 0:TT], lhsT=wpp[:, c2, dc * 128:(dc + 1) * 128], rhs=pT_[:, c2, :], start=(c2 == 0), stop=(c2 == 1)))
                    s_, sB_ = sg[dc % 2], sgB[dc % 2]
                    OP("act", [pgB], [sB_], lambda e, s_=s_, pg=pg: e.activation(out=s_[:], in_=pg[:, 0:TT], func=AF.Sigmoid))
                    t_, tB_ = tg[dc % 2], tgB[dc % 2]
                    OP("dve", [sB_, peB], [tB_], lambda e, s_=s_, pe_=pe_, t_=t_: e.tensor_tensor(out=t_[:], in0=pe_[:, 0:TT], in1=s_[:], op=ALU.mult))
                    OP("pool", [tB_, hnB[dc]], [hnB[dc]], lambda e, hn=hn, t_=t_, dc=dc: e.tensor_tensor(out=hn[:, dc, :], in0=t_[:], in1=hn[:, dc, :], op=ALU.add))
                    if not fuse_out:
                        P.dma(lambda e, hn=hn, dc=dc, q=q: e.dma_start(out=hTv[:, dc, q * TT:(q + 1) * TT], in_=hn[:, dc, :]), reads=[hnB[dc]], writes=[HT[q][dc]])
                    if dc == 3 and q + 1 < NQ:
                        ple_prep(q + 1)
                    if fuse_out and dc == 3 and q >= 1:
                        out_tile(q - 1)
            if fuse_out:
                out_tile(NQ - 1)

    ctx = dict(nc=nc, P=P, OP=OP, sb=sb, es=es, next_ps=next_ps, rmsnorm=rmsnorm, norm_tmp=norm_tmp, load_h=load_h,
               wload=wload, HT=HT, XN=XN, YB=YB, hTv=hTv, xnTv=xnTv, ybr=ybr, cst=cst, cstb=cstb, Bc=Bc, Bcb=Bcb,
               epsc=epsc, Beps=Beps, ident_f=ident_f, ones_f=ones_f, triu_f=triu_f, ident_b=ident_b, ones_b=ones_b,
               mneg_b=mneg_b, gcol=gcol, Bg=Bg,
               w_in=w_in, f_bias=f_bias, pool_w=pool_w, pool_scale=pool_scale, lam_re=lam_re, lam_im=lam_im,
               log_dt=log_dt, b_re=b_re, b_im=b_im, c_re=c_re, c_im=c_im, ssm_d=ssm_d, w_glu=w_glu, conv_w=conv_w,
               w_branch=w_branch, w_out=w_out, consts=consts, qaug=qaug, vtok=vtok, psb=psb, psB=psB, dbg_dump=dbg_dump, nq=nq)

    if "in" in phases:
        phase_in()
    for l in range(depth):
        if "ffn" in phases:
            phase_ffn(l, 0)
        if "mix" in phases:
            phase_mixer(ctx, l, mix_parts)
        if "ffn" in phases:
            phase_ffn(l, 1)
        fuse = ("out" in phases) and (l == depth - 1)
        if "ple" in phases:
            phase_ple(l, fuse_out=fuse)
    if "out" in phases and "ple" not in phases:
        phase_out()
    P.emit(es)
    es.close()
    return nc


from types import SimpleNamespace


def phase_mixer(ctx, l, parts):
    c = SimpleNamespace(**ctx)
    P, OP, sb, nc = c.P, c.OP, c.sb, c.nc
    psb, psB = c.psb, c.psB
    ybrv = c.ybr.rearrange("b (c p) t -> b p c t", p=128)
    QA = [Buf() for _ in range(NQ)]
    VT = [Buf() for _ in range(NQ)]
    P.barrier()
    with contextlib.ExitStack() as so:
        u_ssm = sb(so, "u_ssm", [128, 2, 8 + T], BF16)
        uB = [[Buf() for _ in range(NQ)] for _ in range(2)]
        spar = ssm_param_load(c, l, so) if "ssm" in parts else None
        spre = ssm_s_alloc(c, so) if "ssm" in parts else None
        with contextlib.ExitStack() as s1:
            k_aug = sb(s1, "k_aug", [128, 4, T], BF16)
            kB = [[Buf() for _ in range(NQ)] for _ in range(4)]
            kcB = Buf()
            cabs = sb(s1, "cabs", [128, 32, 4], F32)
            tots = sb(s1, "tots", [128, 33, 4], F32)
            cabsB = [Buf() for _ in range(32)]
            totsB = [Buf() for _ in range(33)]
            def issue_params():
                if spar is not None:
                    for fn, rd, wr, kw in spar.dq:
                        P.dma(fn, reads=rd, writes=wr, **kw)
                    spar.dq.clear()
            mixer_m1(c, l, parts, u_ssm, uB, k_aug, kB, kcB, cabs, cabsB, tots, totsB, QA, VT, ybrv, issue_params)
            issue_params()
            P.barrier()
            sch = ssm_s_chain(c, l, spre, spar) if "ssm" in parts else None
            dq = sch.dq if sch is not None else []

            def drain(n):
                for _ in range(min(n, len(dq))):
                    eng, rd, wr, fn = dq.pop(0)
                    OP(eng, rd, wr, fn)
            if "att" in parts:
                mixer_m3(c, l, k_aug, kB, kcB, cabs, cabsB, tots, totsB, QA, VT, ybrv, drain)
            drain(len(dq))
        P.barrier()
        if "ssm" in parts:
            mixer_m2(c, l, u_ssm, uB, ybrv, spar, sch)
    P.barrier()
    mixer_m4(c, l, parts, ybrv)


def mixer_m1(c, l, parts, u_ssm, uB, k_aug, kB, kcB, cabs, cabsB, tots, totsB, QA, VT, ybrv, after_tile0=lambda: None):
    P, OP, sb, nc, next_ps = c.P, c.OP, c.sb, c.nc, c.next_ps
    with contextlib.ExitStack() as st:
        win = sb(st, "win", [128, 8, 2052], BF16)
        WBLK = [(0, 772), (772, 1796), (1796, 2052)]
        winB = [Buf() for _ in WBLK]
        wiv = c.w_in[l].rearrange("(kc p) n -> p kc n", p=128)
        c.wload(win[:, :, 0:772], wiv[:, :, 0:772], [winB[0]])

        def wB(col):
            for i, (c0, c1) in enumerate(WBLK):
                if c0 <= col < c1:
                    return winB[i]

        wf_sb = sb(st, "wf_sb", [128, 8, 4, 64], BF16)
        wfB = Buf()
        OP("dve", [winB[0]], [wfB], lambda e: e.tensor_copy(out=wf_sb[:], in_=win[:, :, 768:772].unsqueeze(3).to_broadcast([128, 8, 4, 64])))
        fb = sb(st, "fb", [128, 8], F32)
        fbB = Buf()
        P.dma(lambda e: e.dma_start(out=fb[:, 0:4], in_=c.f_bias[l:l + 1, :].partition_broadcast(128), allow_slow_non_contiguous=True), writes=[fbB])
        OP("dve", [fbB], [fbB], lambda e: e.tensor_scalar(out=fb[:, 4:8], in0=fb[:, 0:4], scalar1=-1.0, scalar2=None, op0=ALU.mult))
        pwb = sb(st, "pwb", [128, 2, 128], BF16)
        pwB = Buf()
        OP("pool", [], [pwB], lambda e: e.memset(pwb[:], 0.0))
        for g in range(4):
            r0 = (g % 2) * 64
            P.dma(lambda e, g=g, r0=r0: e.dma_start(out=pwb[r0:r0 + 64, g // 2, r0:r0 + 64], in_=c.pool_w[l, g]), reads=[pwB], writes=[pwB], q="pool")
        pmb = sb(st, "pmb", [128, 12, 128], BF16)
        pmB = Buf()
        P.dma(lambda e: e.dma_start(out=pmb[:], in_=c.consts[:, C_PM:C_PM + 12 * 128].rearrange("p (a b) -> p a b", a=12)), writes=[pmB], q="pool")
        for i in (1, 2):
            c.wload(win[:, :, WBLK[i][0]:WBLK[i][1]], wiv[:, :, WBLK[i][0]:WBLK[i][1]], [winB[i]])
        scol = sb(st, "scol", [128, 8], F32)
        scB = Buf()
        P.dma(lambda e: e.dma_start(out=scol[:, 0:2], in_=c.pool_scale[l].rearrange("(c p) -> p c", p=128), allow_slow_non_contiguous=True), writes=[scB])
        P.dma(lambda e: e.dma_start(out=scol[:, 2:8].rearrange("p (j c) -> p j c", j=3), in_=c.conv_w[l].rearrange("j (c p) -> p j c", p=128), allow_slow_non_contiguous=True), writes=[scB])

        OP("pool", [], [totsB[0]], lambda e: e.memset(tots[:, 0, :], 0.0))

        hn = sb(st, "m1_hn", [128, 8, TT], F32)
        hnB = [Buf() for _ in range(8)]
        xn2 = [sb(st, "m1_xn%d" % i, [128, 8, TT], BF16) for i in range(2)]
        xnB2 = [[Buf() for _ in range(8)] for _ in range(2)]
        tmp = c.norm_tmp(st, "m1")
        qa = [sb(st, "m1_qa%d" % i, [128, 4, TT], BF16) for i in range(2)]
        qaB = [Buf() for _ in range(2)]
        et = [sb(st, "m1_et%d" % i, [128, TT], F32) for i in range(2)]
        etB = [Buf() for _ in range(2)]
        spt = [sb(st, "m1_sp%d" % i, [128, TT], F32) for i in range(2)]
        spB = [Buf() for _ in range(2)]
        crn = [sb(st, "m1_crn%d" % i, [128, TT], F32) for i in range(2)]
        crnB = [Buf() for _ in range(2)]
        hit = [sb(st, "m1_hit%d" % i, [128, TT], BF16) for i in range(2)]
        hitB = [Buf() for _ in range(2)]
        vst = [sb(st, "m1_vst%d" % i, [128, 4, 256], BF16) for i in range(2)]
        vstB = [Buf() for _ in range(2)]
        ftk = [sb(st, "m1_ftk%d" % i, [128, 12], F32) for i in range(2)]
        ftkB = [Buf() for _ in range(2)]
        xpt = sb(st, "m1_xpt", [128, 5, 256], BF16)
        xptB = [Buf() for _ in range(5)]
        pld = sb(st, "m1_pld", [128, 2, TT], BF16)
        pldB = [Buf() for _ in range(2)]
        yps = [sb(st, "m1_yps%d" % i, [128, 2, TT], BF16) for i in range(2)]
        ypsB = [Buf() for _ in range(2)]
        ycs = [sb(st, "m1_ycs%d" % i, [128, 2, TT], BF16) for i in range(2)]
        ycsB = [Buf() for _ in range(2)]
        ccs = [sb(st, "m1_ccs%d" % i, [128, TT], F32) for i in range(2)]
        ccsB = [Buf() for _ in range(2)]
        cbs = [sb(st, "m1_cbs%d" % i, [128, TT], F32) for i in range(2)]
        cbsB = [Buf() for _ in range(2)]
        zt = [[sb(st, "m1_z%d_%d" % (cc, i), [128, TT + 2], F32) for i in range(2)] for cc in range(2)]
        ztB = [[Buf() for _ in range(2)] for _ in range(2)]
        y1 = [sb(st, "m1_y1%d" % i, [128, TT], F32) for i in range(2)]
        y1B = [Buf() for _ in range(2)]
        ones1 = c.cst[:, C_ONES:C_ONES + 1]

        def proj_fm(col0, M, ps, pB, pslice, X, XB, tp=None):
            for kc in range(8):
                kw = {} if tp is None else {"tile_position": tp}
                OP("pe", [wB(col0), XB[kc]], [pB], lambda e, kc=kc, kw=kw: e.matmul(ps[pslice, 0:TT], lhsT=win[:, kc, col0:col0 + M], rhs=X[:, kc, :], start=(kc == 0), stop=(kc == 7), **kw))

        c.load_h(hn[:], hnB, 0)
        c.rmsnorm(hn, hnB, l * 4 + 1, xn2[0], xnB2[0], tmp)
        if c.nq > 1:
            c.load_h(hn[:], hnB, 1)
        for q in range(c.nq):
            qs = slice(q * TT, (q + 1) * TT)
            xn, xnB = xn2[q % 2], xnB2[q % 2]
            P.dma(lambda e, qs=qs, xn=xn: e.dma_start(out=c.xnTv[:, :, qs], in_=xn[:]), reads=xnB, writes=[c.XN[q]])
            Q_, QB_ = qa[q % 2], qaB[q % 2]
            if "att" in parts:
                for h in range(4):
                    ps, pB = next_ps()
                    proj_fm(h * 64, 64, ps, pB, slice(0, 64), xn, xnB)
                    for kc in range(8):
                        OP("pe", [wfB, xnB[kc]], [pB], lambda e, kc=kc, h=h, ps=ps, xn=xn: e.matmul(ps[64:128, 0:TT], lhsT=wf_sb[:, kc, h, :], rhs=xn[:, kc, :], start=(kc == 0), stop=(kc == 7), tile_position=(0, 64)))
                    OP("act", [pB], [QB_], lambda e, ps=ps, h=h, Q_=Q_: e.activation(out=Q_[0:64, h, :], in_=ps[0:64, 0:TT], func=AF.Copy, scale=0.125))
                    e_, eB_ = et[h % 2], etB[h % 2]
                    OP("act", [pB, fbB], [eB_], lambda e, ps=ps, h=h, e_=e_: e.activation(out=e_[64:128, :], in_=ps[64:128, 0:TT], func=AF.Exp, scale=-1.0, bias=fb[64:128, 4 + h:5 + h]))
                    s_, sB_ = spt[h % 2], spB[h % 2]
                    OP("act", [eB_, c.Beps], [sB_], lambda e, e_=e_, s_=s_: e.activation(out=s_[64:128, :], in_=e_[64:128, :], func=AF.Ln, bias=c.epsc[64:128, 1:2]))
                    r_, rB_ = crn[h % 2], crnB[h % 2]
                    OP("dve", [sB_, c.Bc], [rB_], lambda e, s_=s_, r_=r_: e.tensor_tensor_scan(out=r_[64:128, :], data0=ones1[64:128, :].to_broadcast([64, TT]), data1=s_[64:128, :], initial=0.0, op0=ALU.mult, op1=ALU.add))
                    h_, hB_ = hit[h % 2], hitB[h % 2]
                    OP("dve", [rB_], [hB_], lambda e, r_=r_, h_=h_: e.tensor_scalar(out=h_[64:128, :], in0=r_[64:128, :], scalar1=-1.0, scalar2=None, op0=ALU.mult))
                    OP("pool", [hB_], [QB_], lambda e, h_=h_, h=h, Q_=Q_: e.tensor_copy(out=Q_[64:96, h, :], in_=h_[64:96, :]))
                    OP("dve", [rB_, hB_], [QB_], lambda e, r_=r_, h_=h_, h=h, Q_=Q_: e.scalar_tensor_tensor(out=Q_[96:128, h, :], in0=r_[96:128, :], scalar=-1.0, in1=h_[96:128, :], op0=ALU.mult, op1=ALU.subtract))
                P.dma(lambda e, qs=qs, Q_=Q_: e.dma_start(out=c.qaug[:, :, qs], in_=Q_[:]), reads=[QB_], writes=[QA[q]])
                for h in range(4):
                    ps, pB = next_ps()
                    proj_fm(256 + h * 64, 64, ps, pB, slice(0, 64), xn, xnB)
                    OP("dve", [pB], [kB[h][q]], lambda e, ps=ps, h=h, qs=qs: e.tensor_copy(out=k_aug[0:64, h, qs], in_=ps[0:64, 0:TT]))
            if q == 1:
                after_tile0()
            if q == 0:
                OP("pool", [], [kcB], lambda e: e.memset(k_aug[64:128, :, :], 0.0))
                OP("pool", [kcB], [kcB], lambda e: e.memset(k_aug[64:65, :, :], 1.0))
                OP("pool", [kcB], [kcB], lambda e: e.memset(k_aug[96:97, :, :], 1.0))
            if q + 1 < c.nq:
                c.rmsnorm(hn, hnB, l * 4 + 1, xn2[(q + 1) % 2], xnB2[(q + 1) % 2], tmp)
                if q + 2 < c.nq:
                    c.load_h(hn[:], hnB, q + 2)
            V_, VB_ = vst[q % 2], vstB[q % 2]
            ppool = [(c.psb[6], c.psB[6]), (c.psb[7], c.psB[7])]

            def stA(s):
                tt = q * 4 + s
                ts_ = slice(s * 128, (s + 1) * 128)
                if "att" not in parts:
                    return
                psA, pBA = next_ps()
                for kc in range(8):
                    OP("pe", [winB[0], xnB[kc]], [pBA], lambda e, kc=kc, psA=psA, ts_=ts_, xn=xn: e.matmul(psA[:, 0:260], lhsT=xn[:, kc, ts_], rhs=win[:, kc, 512:772], start=(kc == 0), stop=(kc == 7)))
                OP("act", [pBA], [VB_], lambda e, psA=psA, s=s, V_=V_: e.activation(out=V_[:, s, :], in_=psA[:, 0:256], func=AF.Copy))
                f_, fB_ = ftk[tt % 2], ftkB[tt % 2]
                OP("dve", [pBA, fbB], [fB_], lambda e, psA=psA, f_=f_: e.tensor_tensor(out=f_[:, 0:4], in0=psA[:, 256:260], in1=fb[:, 0:4], op=ALU.add))
                OP("act", [fB_], [fB_], lambda e, f_=f_: e.activation(out=f_[:, 4:8], in_=f_[:, 0:4], func=AF.Exp, scale=-1.0))
                OP("act", [fB_, c.Beps], [fB_], lambda e, f_=f_: e.activation(out=f_[:, 8:12], in_=f_[:, 4:8], func=AF.Ln, bias=c.epsc[:, 1:2]))

            def stB(s):
                ts_ = slice(s * 128, (s + 1) * 128)
                if "pool" not in parts:
                    return
                psP, pBP = next_ps()
                for kc in range(8):
                    OP("pe", [winB[1], xnB[kc]], [pBP], lambda e, kc=kc, psP=psP, ts_=ts_, xn=xn: e.matmul(psP[:, 0:256], lhsT=xn[:, kc, ts_], rhs=win[:, kc, 772:1028], start=(kc == 0), stop=(kc == 7)))
                OP("act", [pBP], [xptB[1 + s]], lambda e, psP=psP, s=s: e.activation(out=xpt[:, 1 + s, :], in_=psP[:, 0:256], func=AF.Copy))

            def stC(s):
                tt = q * 4 + s
                ts_ = slice(s * 128, (s + 1) * 128)
                if "pool" not in parts:
                    return
                for g in range(4):
                    pp, ppB = ppool[g // 2]
                    r0 = (g % 2) * 64
                    first = (tt == 0)
                    mi = g * 3 + (2 if first else 0)
                    OP("pe", [xptB[1 + s], pmB], [ppB], lambda e, pp=pp, r0=r0, g=g, s=s, mi=mi, ts_=ts_, first=first: e.matmul(pp[r0:r0 + 64, ts_], lhsT=xpt[:, 1 + s, g * 64:(g + 1) * 64], rhs=pmb[:, mi, :], start=True, stop=first, tile_position=(0, r0)))
                    if not first:
                        OP("pe", [xptB[s], pmB], [ppB], lambda e, pp=pp, r0=r0, g=g, s=s, ts_=ts_: e.matmul(pp[r0:r0 + 64, ts_], lhsT=xpt[:, s, g * 64:(g + 1) * 64], rhs=pmb[:, g * 3 + 1, :], start=False, stop=True, tile_position=(0, r0)))

            def stD(s):
                tt = q * 4 + s
                if "att" not in parts:
                    return
                f_, fB_ = ftk[tt % 2], ftkB[tt % 2]
                psc, pBc = next_ps()
                OP("pe", [fB_, c.Bc], [pBc], lambda e, psc=psc, f_=f_: e.matmul(psc[:, 0:4], lhsT=c.triu_f, rhs=f_[:, 8:12], start=True, stop=True))
                OP("pe", [fB_, c.Bc], [pBc], lambda e, psc=psc, f_=f_: e.matmul(psc[:, 8:12], lhsT=c.ones_f, rhs=f_[:, 8:12], start=True, stop=True))
                OP("dve", [pBc, totsB[tt]], [cabsB[tt]], lambda e, psc=psc, tt=tt: e.tensor_tensor(out=cabs[:, tt, :], in0=psc[:, 0:4], in1=tots[:, tt, :], op=ALU.add))
                OP("dve", [pBc, totsB[tt]], [totsB[tt + 1]], lambda e, psc=psc, tt=tt: e.tensor_tensor(out=tots[:, tt + 1, :], in0=psc[:, 8:12], in1=tots[:, tt, :], op=ALU.add))

            stA(0); stB(0); stA(1); stB(1); stC(0); stD(0); stA(2); stB(2); stC(1); stD(1); stA(3); stB(3); stC(2); stD(2); stC(3); stD(3)
            if "att" in parts:
                P.dma(lambda e, q=q, V_=V_: e.dma_start(out=c.vtok[q * TT:(q + 1) * TT, :].rearrange("(s p) c -> p s c", p=128), in_=V_[:]), reads=[VB_], writes=[VT[q]])
            if "pool" in parts:
                OP("pool", [xptB[4]], [xptB[0]], lambda e: e.tensor_copy(out=xpt[:, 0, :], in_=xpt[:, 4, :]))
                Y_, YB_ = yps[q % 2], ypsB[q % 2]
                for cc in range(2):
                    pp, ppB = ppool[cc]
                    OP("act", [ppB], [pldB[cc]], lambda e, pp=pp, cc=cc: e.activation(out=pld[:, cc, :], in_=pp[:, 0:TT], func=AF.Copy))
                    ps, pB = next_ps()
                    OP("pe", [pldB[cc], pwB], [pB], lambda e, ps=ps, cc=cc: e.matmul(ps[:, 0:TT], lhsT=pwb[:, cc, :], rhs=pld[:, cc, :], start=True, stop=True))
                    OP("dve", [pB, scB], [YB_], lambda e, ps=ps, cc=cc, Y_=Y_: e.tensor_scalar(out=Y_[:, cc, :], in0=ps[:, 0:TT], scalar1=scol[:, cc:cc + 1], scalar2=None, op0=ALU.mult))
                P.dma(lambda e, qs=qs, Y_=Y_: e.dma_start(out=ybrv[1][:, :, qs], in_=Y_[:]), reads=[YB_], writes=[c.YB[1][q]])
            if "ssm" in parts:
                for cc in range(2):
                    ps, pB = next_ps()
                    proj_fm(1028 + cc * 128, 128, ps, pB, slice(0, 128), xn, xnB)
                    OP("act", [pB], [uB[cc][q]], lambda e, ps=ps, cc=cc, q=q: e.activation(out=u_ssm[:, cc, 8 + q * TT:8 + (q + 1) * TT], in_=ps[:, 0:TT], func=AF.Copy))
            if "conv" in parts:
                Y_, YB_ = ycs[q % 2], ycsB[q % 2]
                for cc in range(2):
                    pcc, pccB = next_ps()
                    proj_fm(1540 + cc * 128, 128, pcc, pccB, slice(0, 128), xn, xnB)
                    pcx, pcxB = next_ps()
                    proj_fm(1796 + cc * 128, 128, pcx, pcxB, slice(0, 128), xn, xnB)
                    pcb, pcbB = next_ps()
                    proj_fm(1284 + cc * 128, 128, pcb, pcbB, slice(0, 128), xn, xnB)
                    a_, aB_ = ccs[cc], ccsB[cc]
                    b_, bB_ = cbs[cc], cbsB[cc]
                    z_, zB_ = zt[cc][q % 2], ztB[cc][q % 2]
                    zp_, zpB_ = zt[cc][(q + 1) % 2], ztB[cc][(q + 1) % 2]
                    y_, yB_ = y1[cc], y1B[cc]
                    OP("act", [pccB], [aB_], lambda e, pcc=pcc, a_=a_: e.activation(out=a_[:], in_=pcc[:, 0:TT], func=AF.Copy))
                    OP("act", [pcbB], [bB_], lambda e, pcb=pcb, b_=b_: e.activation(out=b_[:], in_=pcb[:, 0:TT], func=AF.Copy))
                    if q == 0:
                        OP("pool", [], [zB_], lambda e, z_=z_: e.memset(z_[:, 0:2], 0.0))
                    else:
                        OP("pool", [zpB_], [zB_], lambda e, z_=z_, zp_=zp_: e.tensor_copy(out=z_[:, 0:2], in_=zp_[:, TT:TT + 2]))
                    OP("dve", [pcxB, aB_], [zB_], lambda e, pcx=pcx, a_=a_, z_=z_: e.tensor_tensor(out=z_[:, 2:TT + 2], in0=pcx[:, 0:TT], in1=a_[:], op=ALU.mult))
                    OP("dve", [zB_, scB], [yB_], lambda e, z_=z_, y_=y_, cc=cc: e.tensor_scalar(out=y_[:], in0=z_[:, 0:TT], scalar1=scol[:, 2 + cc:3 + cc], scalar2=None, op0=ALU.mult))
                    OP("dve", [zB_, scB, yB_], [yB_], lambda e, z_=z_, y_=y_, cc=cc: e.scalar_tensor_tensor(out=y_[:], in0=z_[:, 1:TT + 1], scalar=scol[:, 4 + cc:5 + cc], in1=y_[:], op0=ALU.mult, op1=ALU.add))
                    OP("dve", [zB_, scB, yB_], [yB_], lambda e, z_=z_, y_=y_, cc=cc: e.scalar_tensor_tensor(out=y_[:], in0=z_[:, 2:TT + 2], scalar=scol[:, 6 + cc:7 + cc], in1=y_[:], op0=ALU.mult, op1=ALU.add))
                    OP("pool", [yB_, bB_], [YB_], lambda e, y_=y_, b_=b_, Y_=Y_, cc=cc: e.tensor_tensor(out=Y_[:, cc, :], in0=y_[:], in1=b_[:], op=ALU.mult))
                P.dma(lambda e, qs=qs, Y_=Y_: e.dma_start(out=ybrv[3][:, :, qs], in_=Y_[:]), reads=[YB_], writes=[c.YB[3][q]])


def mixer_m3(c, l, k_aug, kB, kcB, cabs, cabsB, tots, totsB, QA, VT, ybrv, drain=lambda n: None):
    P, OP, sb, nc = c.P, c.OP, c.sb, c.nc
    psb, psB = c.psb, c.psB
    with contextlib.ExitStack() as st:
        V_aug = sb(st, "V_aug", [128, 32, 4, 2, 64], BF16)
        VB = [Buf() for _ in range(NQ)]
        VoB = Buf()
        OP("pool", [], [VoB], lambda e: e.memset(V_aug[:, :, :, 1, :], 1.0))
        qt = [sb(st, "m3_qt%d" % i, [128, 4, TT], BF16) for i in range(2)]
        qtB = [Buf() for _ in range(2)]
        Pt = [sb(st, "m3_P%d" % i, [128, TT], BF16) for i in range(3)]
        PtB = [Buf() for _ in range(3)]
        bq = [sb(st, "m3_bq%d" % i, [128, 32, 4], F32) for i in range(2)]
        bqB = [Buf() for _ in range(2)]
        rd = [sb(st, "m3_rd%d" % i, [64, TT], F32) for i in range(2)]
        rdB = [Buf() for _ in range(2)]
        yst = [sb(st, "m3_y%d" % i, [64, 4, TT], BF16) for i in range(2)]
        ystB = [Buf() for _ in range(2)]
        yav = c.ybr[0].rearrange("(h p) t -> p h t", p=64)
        kP = 0
        kS = 0
        kO = 0
        for q in range(c.nq):
            Q_, QB_ = qt[q % 2], qtB[q % 2]
            P.dma(lambda e, q=q, Q_=Q_: e.dma_start(out=Q_[:], in_=c.qaug[:, :, q * TT:(q + 1) * TT]), reads=[QA[q]], writes=[QB_])
            for i4 in range(4):
                i = q * 4 + i4
                P.dma(lambda e, i=i: e.dma_start(out=V_aug[:, i, :, 0, :], in_=c.vtok[i * 128:(i + 1) * 128, :].rearrange("p (h d) -> p h d", h=4)), reads=[VT[q]], writes=[VB[q]], shared=True)
            n = 4 * q + 4
            b_, bB_ = bq[q % 2], bqB[q % 2]
            OP("dve", cabsB[0:n] + [totsB[4 * q]], [bB_], lambda e, b_=b_, n=n, q=q: e.tensor_tensor(out=b_[:, 0:n, :], in0=cabs[:, 0:n, :], in1=tots[:, 4 * q, :].unsqueeze(1).to_broadcast([128, n, 4]), op=ALU.subtract))
            Y_, YB_ = yst[q % 2], ystB[q % 2]
            for h in range(4):
                po, poB = psb[4 + kO % 2], psB[4 + kO % 2]
                kO += 1
                def emit_S(i):
                    d = i - 4 * q
                    c0 = max(0, d) * 128
                    ps, pB = psb[i % 4], psB[i % 4]
                    OP("pe", [kB[h][i // 4], kcB, QB_], [pB], lambda e, ps=ps, i=i, c0=c0, d=d, h=h, Q_=Q_: e.matmul(ps[:, c0:TT], lhsT=k_aug[:, h, i * 128:(i + 1) * 128], rhs=Q_[:, h, c0:TT], start=True, stop=(d < 0)))
                    if d >= 0:
                        OP("pe", [c.Bcb], [pB], lambda e, ps=ps, c0=c0: e.matmul(ps[:, c0:c0 + 128], lhsT=c.ident_b, rhs=c.mneg_b, start=False, stop=True))
                    return ps, pB, c0

                nxt = emit_S(0)
                for i in range(n):
                    ps, pB, c0 = nxt
                    if i + 1 < n:
                        nxt = emit_S(i + 1)
                    p_, pB_ = Pt[kP % 3], PtB[kP % 3]
                    kP += 1
                    OP("act", [pB, bB_], [pB_], lambda e, ps=ps, p_=p_, c0=c0, i=i, b_=b_, h=h: e.activation(out=p_[:, c0:TT], in_=ps[:, c0:TT], func=AF.Exp, bias=b_[:, i, h:h + 1]))
                    OP("pe", [VB[i // 4], VoB, pB_], [poB], lambda e, p_=p_, c0=c0, i=i, po=po, h=h, n=n: e.matmul(po[:, c0:TT], lhsT=V_aug[:, i, h, :, :].rearrange("p a b -> p (a b)"), rhs=p_[:, c0:TT], start=(i == 0), stop=(i == n - 1)))
                r_, rB_ = rd[h % 2], rdB[h % 2]
                OP("dve", [poB], [rB_], lambda e, po=po, r_=r_: e.reciprocal(out=r_[0:64, :], in_=po[64:128, 0:TT]))
                OP("dve", [poB, rB_], [YB_], lambda e, po=po, r_=r_, h=h, Y_=Y_: e.tensor_tensor(out=Y_[0:64, h, :], in0=po[0:64, 0:TT], in1=r_[0:64, :], op=ALU.mult))
                drain(12)
            P.dma(lambda e, q=q, Y_=Y_: e.dma_start(out=yav[:, :, q * TT:(q + 1) * TT], in_=Y_[:]), reads=[YB_], writes=[c.YB[0][q]])


def ssm_param_load(c, l, stack):
    sb = c.sb
    dq = []

    class _P:
        @staticmethod
        def dma(fn, reads=(), writes=(), **kw):
            dq.append((fn, list(reads), list(writes), kw))
    P = _P

    def mk_(name, shape, dt=F32):
        return sb(stack, "spl_" + name, shape, dt), Buf()
    lamS, lamSB = mk_("lamS", [128, 2, 8])
    for ri, src in enumerate((c.lam_re, c.lam_im)):
        P.dma(lambda e, ri=ri, src=src: e.dma_start(out=lamS[:, ri, :], in_=src[l].rearrange("(j g) p -> (g p) j", g=2), allow_slow_non_contiguous=True), writes=[lamSB], shared=True)
    ldtS, ldtSB = mk_("ldtS", [128, 8])
    for g in range(2):
        P.dma(lambda e, g=g: e.dma_start(out=ldtS[g * 64:(g + 1) * 64, :], in_=c.log_dt[l].rearrange("(j g) -> g j", g=2)[g:g + 1, :].partition_broadcast(64), allow_slow_non_contiguous=True), writes=[ldtSB], shared=True)
    BS, BSB = mk_("BS", [128, 2, 8, 16])
    for ri, src in enumerate((c.b_re, c.b_im)):
        P.dma(lambda e, ri=ri, src=src: e.dma_start(out=BS[:, ri, :, :], in_=src[l].rearrange("(j g) p h -> (g p) j h", g=2)), writes=[BSB], shared=True)
    Csrc, CsB = mk_("Csrc", [128, 2, 128])
    for ri, src in enumerate((c.c_re, c.c_im)):
        for j in range(8):
            P.dma(lambda e, ri=ri, src=src, j=j: e.dma_start(out=Csrc[16 * j:16 * j + 16, ri, :].rearrange("h (g p) -> h g p", g=2), in_=src[l, 2 * j:2 * j + 2].rearrange("g h p -> h g p")), writes=[CsB], shared=True)
    dcol, dcB = mk_("dcol", [128, 2])
    P.dma(lambda e: e.dma_start(out=dcol[:], in_=c.ssm_d[l].rearrange("(c p) -> p c", p=128), allow_slow_non_contiguous=True), writes=[dcB], shared=True)
    Bsrc, BsB = mk_("Bsrc", [128, 2, 2, 4, 2, 16])
    for ri, src in enumerate((c.b_re, c.b_im)):
        for hf in range(2):
            for g2p in range(2):
                P.dma(lambda e, ri=ri, src=src, hf=hf, g2p=g2p: e.dma_start(out=Bsrc[:, ri, hf, :, g2p, :], in_=src[l, 8 * hf:8 * hf + 8].rearrange("(j g) p h -> (g p) j h", g=2)), writes=[BsB], shared=True)
    return SimpleNamespace(lamS=lamS, lamSB=lamSB, ldtS=ldtS, ldtSB=ldtSB, BS=BS, BSB=BSB, Csrc=Csrc, CsB=CsB, dcol=dcol, dcB=dcB,
                           Bsrc=Bsrc, BsB=BsB, dq=dq)


def ssm_s_alloc(c, stack):
    sb = c.sb
    names = ["lr", "dt", "th", "sn", "cs", "lg", "mag", "ar", "ai", "t1", "t2", "t3", "den", "am1", "cr", "ci", "p0r", "p0i"]
    names += ["p%d%s" % (n, x) for n in range(2, 9) for x in "ri"]
    tl = {nm: sb(stack, "ppS_%s" % nm, [128, 8], F32)[:] for nm in names}
    hp = sb(stack, "ssc_halfpi", [128, 1], F32)
    LV = sb(stack, "ssm_LV", [128, 3, 9, 8], F32)
    return SimpleNamespace(hp=hp, LV=LV, tl=tl)


def ssm_s_chain(c, l, pre, spar):
    P, sb = c.P, c.sb
    dq = []
    OP = lambda eng, rd, wr, fn: dq.append((eng, list(rd), list(wr), fn))
    mul, add, sub = ALU.mult, ALU.add, ALU.subtract
    lamS, lamSB, ldtS, ldtSB = spar.lamS, spar.lamSB, spar.ldtS, spar.ldtSB

    def TT_(eng, out, a, b, op, rd, wr):
        OP(eng, rd, wr, lambda e: e.tensor_tensor(out=out, in0=a, in1=b, op=op))
    hp, LV, pre_tl = pre.hp, pre.LV, pre.tl
    hpB = Buf()
    OP("pool", [], [hpB], lambda e: e.memset(hp[:], float(np.pi / 2)))
    LVB = Buf()
    def cpow_prep(tag, F, lr_in, li, ldt_ap, deps, eng):
        tl = pre_tl

        def t_(nm):
            return tl[nm]
        B_ = Buf()
        D = list(deps) + [B_]
        OP(eng, D, [B_], lambda e: e.tensor_scalar(out=t_("lr"), in0=lr_in, scalar1=-1e-4, scalar2=None, op0=ALU.min))
        OP(eng, D, [B_], lambda e: e.tensor_copy(out=t_("dt"), in_=ldt_ap))
        OP("act", [B_], [B_], lambda e: e.activation(out=t_("dt"), in_=t_("dt"), func=AF.Exp))
        TT_(eng, t_("th"), li, t_("dt"), mul, D, [B_])
        OP("act", [B_], [B_], lambda e: e.activation(out=t_("sn"), in_=t_("th"), func=AF.Sin, scale=1.0 / 32))
        OP("act", [B_, hpB], [B_], lambda e: e.activation(out=t_("cs"), in_=t_("th"), func=AF.Sin, scale=1.0 / 32, bias=hp[:, 0:1]))
        TT_(eng, t_("lg"), t_("lr"), t_("dt"), mul, [B_], [B_])
        OP("act", [B_], [B_], lambda e: e.activation(out=t_("mag"), in_=t_("lg"), func=AF.Exp, scale=1.0 / 32))
        TT_(eng, t_("ar"), t_("mag"), t_("cs"), mul, [B_], [B_])
        TT_(eng, t_("ai"), t_("mag"), t_("sn"), mul, [B_], [B_])
        for _ in range(5):
            TT_(eng, t_("t1"), t_("ar"), t_("ar"), mul, [B_], [B_])
            TT_(eng, t_("t2"), t_("ai"), t_("ai"), mul, [B_], [B_])
            TT_(eng, t_("t3"), t_("ar"), t_("ai"), mul, [B_], [B_])
            TT_(eng, t_("ar"), t_("t1"), t_("t2"), sub, [B_], [B_])
            OP(eng, [B_], [B_], lambda e: e.tensor_scalar(out=t_("ai"), in0=t_("t3"), scalar1=2.0, scalar2=None, op0=mul))
        TT_(eng, t_("t1"), t_("lr"), t_("lr"), mul, [B_], [B_])
        TT_(eng, t_("t2"), li, li, mul, D, [B_])
        TT_(eng, t_("den"), t_("t1"), t_("t2"), add, [B_], [B_])
        OP("dve", [B_], [B_], lambda e: e.reciprocal(out=t_("den"), in_=t_("den")))
        OP(eng, [B_], [B_], lambda e: e.tensor_scalar(out=t_("am1"), in0=t_("ar"), scalar1=-1.0, scalar2=None, op0=add))
        TT_(eng, t_("t1"), t_("am1"), t_("lr"), mul, [B_], [B_])
        TT_(eng, t_("t2"), t_("ai"), li, mul, D, [B_])
        TT_(eng, t_("t3"), t_("t1"), t_("t2"), add, [B_], [B_])
        TT_(eng, t_("cr"), t_("t3"), t_("den"), mul, [B_], [B_])
        TT_(eng, t_("t1"), t_("ai"), t_("lr"), mul, [B_], [B_])
        TT_(eng, t_("t2"), t_("am1"), li, mul, D, [B_])
        TT_(eng, t_("t3"), t_("t1"), t_("t2"), sub, [B_], [B_])
        TT_(eng, t_("ci"), t_("t3"), t_("den"), mul, [B_], [B_])
        pw = []
        OP(eng, [B_], [B_], lambda e: e.memset(t_("p0r"), 1.0))
        OP(eng, [B_], [B_], lambda e: e.memset(t_("p0i"), 0.0))
        pw.append((t_("p0r"), t_("p0i")))
        pw.append((t_("ar"), t_("ai")))
        for n in range(2, 9):
            pr, pi = pw[-1]
            nr, ni = t_("p%dr" % n), t_("p%di" % n)
            cmul(nr, ni, pr, pi, t_("ar"), t_("ai"), t_("t1"), t_("t2"), [B_], [B_], eng)
            pw.append((nr, ni))
        return pw, (t_("cr"), t_("ci")), B_, t_

    def cmul(o_r, o_i, a_r, a_i, b_r, b_i, t1, t2, rd, wr, eng="dve"):
        TT_(eng, t1, a_r, b_r, mul, rd, wr)
        TT_(eng, t2, a_i, b_i, mul, rd, wr)
        TT_(eng, o_r, t1, t2, sub, rd, wr)
        TT_(eng, t1, a_r, b_i, mul, rd, wr)
        TT_(eng, t2, a_i, b_r, mul, rd, wr)
        TT_(eng, o_i, t1, t2, add, rd, wr)

    pwS, (crS, ciS), BS_, tS = cpow_prep("S", 8, lamS[:, 0, :], lamS[:, 1, :], ldtS[:], [lamSB, ldtSB], "dve")
    OP("dve", [BS_], [LVB], lambda e: e.tensor_copy(out=LV[:, 0, 0, :], in_=pwS[8][0]))
    OP("dve", [BS_], [LVB], lambda e: e.tensor_copy(out=LV[:, 1, 0, :], in_=pwS[8][1]))
    for k in range(1, 9):
        TT_("dve", tS("t1"), LV[:, 0, k - 1, :], LV[:, 0, k - 1, :], mul, [LVB, BS_], [BS_])
        TT_("dve", tS("t2"), LV[:, 1, k - 1, :], LV[:, 1, k - 1, :], mul, [LVB, BS_], [BS_])
        TT_("dve", tS("t3"), LV[:, 0, k - 1, :], LV[:, 1, k - 1, :], mul, [LVB, BS_], [BS_])
        TT_("dve", LV[:, 0, k, :], tS("t1"), tS("t2"), sub, [BS_], [LVB])
        OP("dve", [BS_], [LVB], lambda e, k=k: e.tensor_scalar(out=LV[:, 1, k, :], in0=tS("t3"), scalar1=2.0, scalar2=None, op0=mul))
    OP("dve", [LVB], [LVB], lambda e: e.tensor_scalar(out=LV[:, 2, :, :], in0=LV[:, 1, :, :], scalar1=-1.0, scalar2=None, op0=mul))
    return SimpleNamespace(pwS=pwS, crS=crS, ciS=ciS, BS_=BS_, tS=tS, LV=LV, LVB=LVB, cmul=cmul, dq=dq)


def mixer_m2(c, l, u_ssm, uB, ybrv, spar, sch):
    P, OP, sb, nc, next_ps = c.P, c.OP, c.sb, c.nc, c.next_ps
    mul, add, sub = ALU.mult, ALU.add, ALU.subtract
    NCH = T // 8

    def TT_(eng, out, a, b, op, rd, wr):
        OP(eng, rd, wr, lambda e: e.tensor_tensor(out=out, in0=a, in1=b, op=op))

    with contextlib.ExitStack() as st:
        WZ = sb(st, "ssm_WZ", [128, 8, 8, 2, 128], BF16)
        CA = sb(st, "ssm_CA", [128, 8, 9, 2, 128], BF16)
        BD = sb(st, "ssm_BD", [128, 2, 8, 128], BF16)
        LV, LVB = sch.LV, sch.LVB
        WZB, CAB, BDB = [Buf(), Buf()], Buf(), Buf()
        with contextlib.nullcontext():
            sp = st

            def mk_(name, shape, dt=F32):
                return sb(sp, "sp_" + name, shape, dt), Buf()
            mk, mkB = mk_("mk", [128, 8, 128])
            P.dma(lambda e: e.dma_start(out=mk[:], in_=c.consts[:, C_MH:C_MH + 1024].rearrange("p (a b) -> p a b", a=8)), writes=[mkB])
            msall, msB = mk_("msall", [128, 8, 128])
            for j in range(8):
                OP("pool", [mkB], [msB], lambda e, j=j: e.tensor_copy(out=msall[:, j, :], in_=mk[:, 4 + j % 4, :]))
            lamS, lamSB, ldtS, ldtSB, BS, BSB, Csrc, CsB = spar.lamS, spar.lamSB, spar.ldtS, spar.ldtSB, spar.BS, spar.BSB, spar.Csrc, spar.CsB
            dcol, dcB, Bsrc, BsB = spar.dcol, spar.dcB, spar.Bsrc, spar.BsB
            CT, CTB = mk_("CT", [128, 2, 128])
            ps, pB = next_ps()
            for ri in range(2):
                OP("pe", [CsB, c.Bc], [pB], lambda e, ps=ps, ri=ri: e.transpose(out=ps[:, ri * 128:(ri + 1) * 128], in_=Csrc[:, ri, :], identity=c.ident_f))
            OP("act", [pB], [CTB], lambda e, ps=ps: e.activation(out=CT[:], in_=ps[:, 0:256].rearrange("p (a b) -> p a b", a=2), func=AF.Copy))

            pwS, crS, ciS, BS_, tS = sch.pwS, sch.crS, sch.ciS, sch.BS_, sch.tS

            def cmul(o_r, o_i, a_r, a_i, b_r, b_i, t1, t2, rd, wr, eng="dve"):
                TT_(eng, t1, a_r, b_r, mul, rd, wr)
                TT_(eng, t2, a_i, b_i, mul, rd, wr)
                TT_(eng, o_r, t1, t2, sub, rd, wr)
                TT_(eng, t1, a_r, b_i, mul, rd, wr)
                TT_(eng, t2, a_i, b_r, mul, rd, wr)
                TT_(eng, o_i, t1, t2, add, rd, wr)
            Gt = [mk_("Gt%d" % i, [128, 2, 8]) for i in range(2)]
            Ws = [mk_("Ws%d" % i, [128, 2, 2, 128]) for i in range(2)]
            wt1, wt1B = mk_("wt1", [128, 2, 128])
            wt2, wt2B = mk_("wt2", [128, 2, 128])
            v4 = lambda ap: ap.rearrange("p a (b c) -> p a b c", b=4)
            Brv = Bsrc[:, 0].rearrange("p a b c d -> p a b (c d)")
            Biv = Bsrc[:, 1].rearrange("p a b c d -> p a b (c d)")
            gb = lambda ap: ap.rearrange("p (a b) -> p a b", a=2).unsqueeze(3).to_broadcast([128, 2, 4, 32])
            mh = mk[:, 0:4, :]
            for tau in range(8):
                G_, GB_ = Gt[tau % 2]
                if tau == 0:
                    OP("dve", [BS_, GB_], [GB_], lambda e, G_=G_: e.tensor_copy(out=G_[:, 0, :], in_=crS))
                    OP("dve", [BS_, GB_], [GB_], lambda e, G_=G_: e.tensor_copy(out=G_[:, 1, :], in_=ciS))
                else:
                    Gp, GpB = Gt[(tau - 1) % 2]
                    cmul(G_[:, 0, :], G_[:, 1, :], Gp[:, 0, :], Gp[:, 1, :], pwS[1][0], pwS[1][1], tS("t1"), tS("t2"), [BS_, GpB, GB_], [BS_, GB_])
                W_, WB_ = Ws[tau % 2]
                TT_("dve", v4(wt1[:]), Brv, gb(G_[:, 0, :]), mul, [BsB, GB_, wt1B], [wt1B])
                TT_("dve", v4(wt2[:]), Biv, gb(G_[:, 1, :]), mul, [BsB, GB_, wt2B], [wt2B])
                TT_("dve", W_[:, 0, :, :], wt1[:], wt2[:], sub, [wt1B, wt2B, WB_], [WB_])
                TT_("dve", v4(wt1[:]), Biv, gb(G_[:, 0, :]), mul, [BsB, GB_, wt1B], [wt1B])
                TT_("dve", v4(wt2[:]), Brv, gb(G_[:, 1, :]), mul, [BsB, GB_, wt2B], [wt2B])
                TT_("dve", W_[:, 1, :, :], wt1[:], wt2[:], add, [wt1B, wt2B, WB_], [WB_])
                ps, pB = next_ps()
                for ri in range(2):
                    for hf in range(2):
                        k4 = ri * 2 + hf
                        OP("pe", [WB_, c.Bc], [pB], lambda e, ps=ps, ri=ri, hf=hf, k4=k4, W_=W_: e.transpose(out=ps[:, k4 * 128:(k4 + 1) * 128], in_=W_[:, ri, hf, :], identity=c.ident_f))
                for ri in range(2):
                    for hf in range(2):
                        k4 = ri * 2 + hf
                        OP("dve", [pB, mkB, WZB[hf]], [WZB[hf]], lambda e, ps=ps, ri=ri, hf=hf, k4=k4, tau=tau: e.tensor_tensor(out=WZ[:, 4 * hf:4 * hf + 4, tau, ri, :], in0=ps[:, k4 * 128:(k4 + 1) * 128].unsqueeze(1).to_broadcast([128, 4, 128]), in1=mh, op=mul))
            cw1, cw1B = mk_("cw1", [128, 8, 16])
            cw2, cw2B = mk_("cw2", [128, 8, 16])
            cw3, cw3B = mk_("cw3", [128, 8, 16])
            CTr = CT[:, 0, :].rearrange("p (j h) -> p j h", j=8)
            CTi = CT[:, 1, :].rearrange("p (j h) -> p j h", j=8)
            bc16 = lambda ap: ap.unsqueeze(2).to_broadcast([128, 8, 16])
            msv = msall[:].rearrange("p j (a h) -> p j a h", a=8)
            msneg, msnB = mk_("msneg", [128, 8, 128])
            OP("pool", [msB], [msB], lambda e: e.tensor_scalar(out=msneg[:], in0=msall[:], scalar1=-1.0, scalar2=None, op0=mul))
            msvn = msneg[:].rearrange("p j (a h) -> p j a h", a=8)
            for n in range(9):
                pr, pi = pwS[n]
                TT_("pool", cw1[:], CTr, bc16(pr), mul, [CTB, BS_, cw1B], [cw1B])
                TT_("pool", cw2[:], CTi, bc16(pi), mul, [CTB, BS_, cw2B], [cw2B])
                TT_("pool", cw3[:], cw1[:], cw2[:], sub, [cw1B, cw2B, cw3B], [cw3B])
                OP("pool", [cw3B, msB, CAB], [CAB], lambda e, n=n: e.tensor_tensor(out=CA[:, :, n, 0, :].rearrange("p j (a h) -> p j a h", a=8), in0=cw3[:].unsqueeze(2).to_broadcast([128, 8, 8, 16]), in1=msv, op=mul))
                TT_("pool", cw1[:], CTr, bc16(pi), mul, [CTB, BS_, cw1B], [cw1B])
                TT_("pool", cw2[:], CTi, bc16(pr), mul, [CTB, BS_, cw2B], [cw2B])
                TT_("pool", cw3[:], cw1[:], cw2[:], add, [cw1B, cw2B, cw3B], [cw3B])
                OP("pool", [cw3B, msB, CAB], [CAB], lambda e, n=n: e.tensor_tensor(out=CA[:, :, n, 1, :].rearrange("p j (a h) -> p j a h", a=8), in0=cw3[:].unsqueeze(2).to_broadcast([128, 8, 8, 16]), in1=msvn, op=mul))
            cB, cBB = mk_("cB", [128, 8, 2, 32], BF16)
            m2 = mk[:, 4, 0:32].rearrange("p (a h) -> p a h", a=2).unsqueeze(1).to_broadcast([128, 8, 2, 16])
            BSr, BSi = BS[:, 0, :, :], BS[:, 1, :, :]
            TT_("pool", cw1[:], BSr, bc16(crS), mul, [BSB, BS_, cw1B], [cw1B])
            TT_("pool", cw2[:], BSi, bc16(ciS), mul, [BSB, BS_, cw2B], [cw2B])
            TT_("pool", cw3[:], cw1[:], cw2[:], sub, [cw1B, cw2B, cw3B], [cw3B])
            OP("pool", [cw3B, mkB], [cBB], lambda e: e.tensor_tensor(out=cB[:, :, 0, :].rearrange("p j (a h) -> p j a h", a=2), in0=cw3[:].unsqueeze(2).to_broadcast([128, 8, 2, 16]), in1=m2, op=mul))
            TT_("pool", cw1[:], BSr, bc16(ciS), mul, [BSB, BS_, cw1B], [cw1B])
            TT_("pool", cw2[:], BSi, bc16(crS), mul, [BSB, BS_, cw2B], [cw2B])
            TT_("pool", cw3[:], cw1[:], cw2[:], add, [cw1B, cw2B, cw3B], [cw3B])
            OP("pool", [cw3B, mkB], [cBB], lambda e: e.tensor_tensor(out=cB[:, :, 1, :].rearrange("p j (a h) -> p j a h", a=2), in0=cw3[:].unsqueeze(2).to_broadcast([128, 8, 2, 16]), in1=m2, op=mul))
        with contextlib.nullcontext():
            sr = st
            Xb = [[[sb(sr, "X%d%d%d" % (s_, ri, ab), [128, 256 + NCH], F32) for ab in range(2)] for ri in range(2)] for s_ in range(2)]
            XB = [[[Buf() for ab in range(2)] for ri in range(2)] for s_ in range(2)]
            for s_ in range(2):
                for ri in range(2):
                    for ab in range(2):
                        OP("pool", [], [XB[s_][ri][ab]], lambda e, t=Xb[s_][ri][ab]: e.memset(t[:, 0:256], 0.0))
            Xp = sb(sr, "Xp", [128, 8, 2, NCH], BF16)
            XpB = [Buf() for _ in range(8)]
            ysT = sb(sr, "ysT", [128, 2, T], BF16)
            ysB = [Buf() for _ in range(2)]
            wgl = sb(sr, "wgl", [128, 2, 512], BF16)
            wglB = [Buf()]
            c.wload(wgl[:], c.w_glu[l].rearrange("(kc p) n -> p kc n", p=128), wglB)
            sg = [sb(sr, "s_sg%d" % i, [128, TT], F32) for i in range(2)]
            sgB = [Buf() for _ in range(2)]
            yst = [sb(sr, "s_yst%d" % i, [128, 2, TT], BF16) for i in range(2)]
            ystB = [Buf() for _ in range(2)]
            for j in range(8):
                hf = j // 4
                s_ = j % 2
                for ri in range(2):
                    ps, pB = next_ps()
                    for tau in range(8):
                        OP("pe", [WZB[hf]] + uB[hf], [pB], lambda e, ps=ps, j=j, tau=tau, ri=ri, hf=hf: e.matmul(ps[:, 0:NCH], lhsT=WZ[:, j, tau, ri, :], rhs=u_ssm[:, hf, 15 - tau:15 - tau + (NCH - 1) * 8 + 1:8], start=(tau == 0), stop=(tau == 7)))
                    OP("act", [pB], [XB[s_][ri][0]], lambda e, ps=ps, t=Xb[s_][ri][0]: e.activation(out=t[:, 256:256 + NCH], in_=ps[:, 0:NCH], func=AF.Copy))
                cur = 0
                for k in range(9):
                    sh = 1 << k
                    sr_, si_ = Xb[s_][0][cur], Xb[s_][1][cur]
                    dr_, di_ = Xb[s_][0][1 - cur], Xb[s_][1][1 - cur]
                    sBr, sBi = XB[s_][0][cur], XB[s_][1][cur]
                    dBr, dBi = XB[s_][0][1 - cur], XB[s_][1][1 - cur]
                    lo, hi = 256 - sh, 256 + NCH - sh
                    OP("dve", [sBr, LVB], [dBr], lambda e, sr_=sr_, dr_=dr_, lo=lo, hi=hi, k=k, j=j: e.scalar_tensor_tensor(out=dr_[:, 256:256 + NCH], in0=sr_[:, lo:hi], scalar=LV[:, 0, k, j:j + 1], in1=sr_[:, 256:256 + NCH], op0=mul, op1=add))
                    OP("dve", [sBi, LVB, dBr], [dBr], lambda e, si_=si_, dr_=dr_, lo=lo, hi=hi, k=k, j=j: e.scalar_tensor_tensor(out=dr_[:, 256:256 + NCH], in0=si_[:, lo:hi], scalar=LV[:, 2, k, j:j + 1], in1=dr_[:, 256:256 + NCH], op0=mul, op1=add))
                    OP("dve", [sBr, sBi, LVB], [dBi], lambda e, sr_=sr_, si_=si_, di_=di_, lo=lo, hi=hi, k=k, j=j: e.scalar_tensor_tensor(out=di_[:, 256:256 + NCH], in0=sr_[:, lo:hi], scalar=LV[:, 1, k, j:j + 1], in1=si_[:, 256:256 + NCH], op0=mul, op1=add))
                    OP("dve", [sBi, LVB, dBi], [dBi], lambda e, si_=si_, di_=di_, lo=lo, hi=hi, k=k, j=j: e.scalar_tensor_tensor(out=di_[:, 256:256 + NCH], in0=si_[:, lo:hi], scalar=LV[:, 0, k, j:j + 1], in1=di_[:, 256:256 + NCH], op0=mul, op1=add))
                    cur = 1 - cur
                for ri in range(2):
                    OP("act", [XB[s_][ri][cur]], [XpB[j]], lambda e, t=Xb[s_][ri][cur], j=j, ri=ri: e.activation(out=Xp[:, j, ri, :], in_=t[:, 255:255 + NCH], func=AF.Copy))
            for hf in range(2):
                for tau in range(8):
                    ps, pB = next_ps()
                    for j4 in range(4):
                        j = 4 * hf + j4
                        OP("pe", [cBB, CAB], [pB], lambda e, ps=ps, j=j, j4=j4, tau=tau: e.matmul(ps[32 * j4:32 * j4 + 32, 0:128], lhsT=cB[:, j, 0, :], rhs=CA[:, j, tau, 0, :], start=True, stop=False, tile_position=(0, 32 * j4)))
                        OP("pe", [cBB, CAB], [pB], lambda e, ps=ps, j=j, j4=j4, tau=tau: e.matmul(ps[32 * j4:32 * j4 + 32, 0:128], lhsT=cB[:, j, 1, :], rhs=CA[:, j, tau, 1, :], start=False, stop=True, tile_position=(0, 32 * j4)))
                    if tau == 0:
                        OP("dve", [pB, dcB, c.Bc], [BDB], lambda e, ps=ps, hf=hf: e.scalar_tensor_tensor(out=BD[:, hf, 0, :], in0=c.ident_f, scalar=dcol[:, hf:hf + 1], in1=ps[:, 0:128], op0=mul, op1=add))
                    else:
                        OP("act", [pB], [BDB], lambda e, ps=ps, hf=hf, tau=tau: e.activation(out=BD[:, hf, tau, :], in_=ps[:, 0:128], func=AF.Copy))
            kk = 0
            for hf in range(2):
                for s in range(8):
                    ps, pB = next_ps()
                    mms = []
                    for j4 in range(4):
                        j = 4 * hf + j4
                        for ri in range(2):
                            mms.append(([CAB, XpB[j]], CA[:, j, s + 1, ri, :], Xp[:, j, ri, :]))
                    for tau in range(s + 1):
                        mms.append(([BDB] + uB[hf], BD[:, hf, tau, :], u_ssm[:, hf, 8 + s - tau:8 + s - tau + (NCH - 1) * 8 + 1:8]))
                    for i, (rd, lh, rh) in enumerate(mms):
                        OP("pe", rd, [pB], lambda e, ps=ps, lh=lh, rh=rh, i=i, n=len(mms): e.matmul(ps[:, 0:NCH], lhsT=lh, rhs=rh, start=(i == 0), stop=(i == n - 1)))
                    if kk % 2 == 0:
                        OP("act", [pB], [ysB[hf]], lambda e, ps=ps, hf=hf, s=s: e.activation(out=ysT[:, hf, s:T:8], in_=ps[:, 0:NCH], func=AF.Copy))
                    else:
                        OP("dve", [pB], [ysB[hf]], lambda e, ps=ps, hf=hf, s=s: e.tensor_copy(out=ysT[:, hf, s:T:8], in_=ps[:, 0:NCH]))
                    kk += 1
            for q in range(NQ):
                qs = slice(q * TT, (q + 1) * TT)
                Y_, YB_ = yst[q % 2], ystB[q % 2]
                for cc in range(2):
                    pv, pvB = next_ps()
                    pg, pgB = next_ps()
                    for kc in range(2):
                        OP("pe", [wglB[0], ysB[kc]], [pvB], lambda e, pv=pv, kc=kc, cc=cc, qs=qs: e.matmul(pv[:, 0:TT], lhsT=wgl[:, kc, cc * 128:(cc + 1) * 128], rhs=ysT[:, kc, qs], start=(kc == 0), stop=(kc == 1)))
                    for kc in range(2):
                        OP("pe", [wglB[0], ysB[kc]], [pgB], lambda e, pg=pg, kc=kc, cc=cc, qs=qs: e.matmul(pg[:, 0:TT], lhsT=wgl[:, kc, 256 + cc * 128:256 + (cc + 1) * 128], rhs=ysT[:, kc, qs], start=(kc == 0), stop=(kc == 1)))
                    s2, s2B = sg[cc], sgB[cc]
                    OP("act", [pgB], [s2B], lambda e, pg=pg, s2=s2: e.activation(out=s2[:], in_=pg[:, 0:TT], func=AF.Sigmoid))
                    OP("dve", [pvB, s2B], [YB_], lambda e, pv=pv, s2=s2, Y_=Y_, cc=cc: e.tensor_tensor(out=Y_[:, cc, :], in0=pv[:, 0:TT], in1=s2[:], op=mul))
                P.dma(lambda e, qs=qs, Y_=Y_: e.dma_start(out=ybrv[2][:, :, qs], in_=Y_[:]), reads=[YB_], writes=[c.YB[2][q]])


def mixer_m4(c, l, parts, ybrv):
    P, OP, sb, nc, next_ps = c.P, c.OP, c.sb, c.nc, c.next_ps
    order = [b for b, nm in enumerate(("att", "pool", "ssm", "conv")) if nm in parts]
    with contextlib.ExitStack() as st:
        wgt = sb(st, "m4_wg", [128, 8, 4096], BF16)
        wgtB = [[Buf(), Buf()] for _ in range(4)]
        wiv = c.w_in[l].rearrange("(kc p) n -> p kc n", p=128)
        wbr = sb(st, "m4_wbr", [128, 4, 2, D], BF16)
        wbrB = [Buf() for _ in range(4)]
        for b in order:
            c.wload(wgt[:, :, b * 1024:b * 1024 + 256], wiv[:, :, 2052 + b * 1024:2052 + b * 1024 + 256], [wgtB[b][0]])
            c.wload(wbr[:, b, :, :], c.w_branch[l, b].rearrange("(kc p) n -> p kc n", p=128), [wbrB[b]])
        for b in order:
            c.wload(wgt[:, :, b * 1024 + 256:(b + 1) * 1024], wiv[:, :, 2052 + b * 1024 + 256:2052 + (b + 1) * 1024], [wgtB[b][1]])
        wo = sb(st, "m4_wo", [128, 8, D], BF16)
        woB = [Buf()]
        c.wload(wo[:], c.w_out[l].rearrange("(kc p) n -> p kc n", p=128), woB)
        xn = [sb(st, "m4_xn%d" % i, [128, 8, TT], BF16) for i in range(2)]
        xnB = [Buf() for _ in range(2)]
        ysb = [sb(st, "m4_ys%d" % i, [128, 4, 2, TT], BF16) for i in range(2)]
        ysB = [[Buf() for _ in range(4)] for _ in range(2)]
        m = sb(st, "m4_m", [128, 8, TT], F32)
        mB = [Buf() for _ in range(8)]
        mb = sb(st, "m4_mb", [128, 8, TT], BF16)
        mbB = [Buf() for _ in range(8)]
        sg = [sb(st, "m4_sg%d" % i, [128, TT], F32) for i in range(2)]
        sgB = [Buf() for _ in range(2)]
        tm = [sb(st, "m4_tm%d" % i, [128, TT], F32) for i in range(2)]
        tmB = [Buf() for _ in range(2)]
        hr = [sb(st, "m4_hr%d" % i, [128, TT], F32) for i in range(3)]
        hrB = [Buf() for _ in range(3)]
        k = 0
        kk = 0
        def m4_load(q):
            qs = slice(q * TT, (q + 1) * TT)
            X, XB = xn[q % 2], xnB[q % 2]
            P.dma(lambda e, qs=qs, X=X: e.dma_start(out=X[:], in_=c.xnTv[:, :, qs]), reads=[c.XN[q]], writes=[XB])
            Ys, YsB = ysb[q % 2], ysB[q % 2]
            for b in order:
                P.dma(lambda e, qs=qs, b=b, Ys=Ys: e.dma_start(out=Ys[:, b, :, :], in_=ybrv[b][:, :, qs]), reads=[c.YB[b][q]], writes=[YsB[b]])

        m4_load(0)
        for q in range(c.nq):
            qs = slice(q * TT, (q + 1) * TT)
            X, XB = xn[q % 2], xnB[q % 2]
            Ys, YsB = ysb[q % 2], ysB[q % 2]
            if q + 1 < c.nq:
                m4_load(q + 1)
            for dc in range(8):
                ds = slice(dc * 128, (dc + 1) * 128)
                for bi, b in enumerate(order):
                    pg, pgB = next_ps()
                    for kc in range(8):
                        OP("pe", [wgtB[b][0 if dc < 2 else 1], XB], [pgB], lambda e, pg=pg, kc=kc, b=b, dc=dc, X=X: e.matmul(pg[:, 0:TT], lhsT=wgt[:, kc, b * 1024 + dc * 128:b * 1024 + (dc + 1) * 128], rhs=X[:, kc, :], start=(kc == 0), stop=(kc == 7)))
                    pp, ppB = next_ps()
                    for kc in range(2):
                        OP("pe", [wbrB[b], YsB[b]], [ppB], lambda e, pp=pp, kc=kc, b=b, ds=ds, Ys=Ys: e.matmul(pp[:, 0:TT], lhsT=wbr[:, b, kc, ds], rhs=Ys[:, b, kc, :], start=(kc == 0), stop=(kc == 1)))
                    s_, sB_ = sg[kk % 2], sgB[kk % 2]
                    OP("act", [pgB], [sB_], lambda e, s_=s_, pg=pg: e.activation(out=s_[:], in_=pg[:, 0:TT], func=AF.Sigmoid))
                    last = (bi == len(order) - 1)
                    if bi == 0 and last:
                        OP("dve", [sB_, ppB], [mbB[dc]], lambda e, s_=s_, pp=pp, dc=dc: e.tensor_tensor(out=mb[:, dc, :], in0=pp[:, 0:TT], in1=s_[:], op=ALU.mult))
                    elif bi == 0:
                        OP("dve", [sB_, ppB], [mB[dc]], lambda e, s_=s_, pp=pp, dc=dc: e.tensor_tensor(out=m[:, dc, :], in0=pp[:, 0:TT], in1=s_[:], op=ALU.mult))
                    else:
                        t_, tB_ = tm[kk % 2], tmB[kk % 2]
                        OP("dve", [sB_, ppB], [tB_], lambda e, s_=s_, pp=pp, t_=t_: e.tensor_tensor(out=t_[:], in0=pp[:, 0:TT], in1=s_[:], op=ALU.mult))
                        if last:
                            OP("pool", [tB_, mB[dc]], [mbB[dc]], lambda e, t_=t_, dc=dc: e.tensor_tensor(out=mb[:, dc, :], in0=m[:, dc, :], in1=t_[:], op=ALU.add))
                        else:
                            OP("pool", [tB_, mB[dc]], [mB[dc]], lambda e, t_=t_, dc=dc: e.tensor_tensor(out=m[:, dc, :], in0=m[:, dc, :], in1=t_[:], op=ALU.add))
                    kk += 1
            for d2 in range(8):
                po, poB = next_ps()
                for dc in range(8):
                    OP("pe", [woB[0], mbB[dc]], [poB], lambda e, po=po, dc=dc, d2=d2: e.matmul(po[:, 0:TT], lhsT=wo[:, dc, d2 * 128:(d2 + 1) * 128], rhs=mb[:, dc, :], start=(dc == 0), stop=(dc == 7)))
                r, rB = hr[k % 3], hrB[k % 3]
                k += 1
                P.dma(lambda e, r=r, d2=d2, qs=qs: e.dma_start(out=r[:], in_=c.hTv[:, d2, qs]), reads=[c.HT[q][d2]], writes=[rB])
                OP("dve", [poB, rB], [rB], lambda e, r=r, po=po: e.tensor_tensor(out=r[:], in0=po[:, 0:TT], in1=r[:], op=ALU.add))
                P.dma(lambda e, r=r, d2=d2, qs=qs: e.dma_start(out=c.hTv[:, d2, qs], in_=r[:]), reads=[rB], writes=[c.HT[q][d2]])


_NAMES = ["norm_g", "ffn_w_gate", "ffn_w_up", "ffn_w_down", "w_in", "f_bias", "pool_w", "pool_scale", "ssm_lam_re",
          "ssm_lam_im", "ssm_log_dt", "ssm_b_re", "ssm_b_im", "ssm_c_re", "ssm_c_im", "ssm_d", "ssm_w_glu", "conv_w",
          "w_branch", "w_out", "ple_w_gate", "ple_w_proj", "final_g"]


def run(inputs, n_cores=8, ret_all=False, **bk):
    nc = build(**bk)
    cst = make_consts()
    shared = {k: np.ascontiguousarray(np.asarray(inputs[k], dtype=np.float32)) for k in _NAMES}
    xs = np.asarray(inputs["x"], dtype=np.float32)
    ps = np.asarray(inputs["p"], dtype=np.float32)
    in_maps = []
    for b in range(n_cores):
        m = dict(shared)
        m["x"] = np.ascontiguousarray(xs[b])
        m["p"] = np.ascontiguousarray(ps[:, b])
        m["consts"] = cst
        in_maps.append(m)
    res = run_bass_kernel_spmd(nc, in_maps, core_ids=list(range(n_cores)))
    if ret_all:
        return res.results
    return np.stack([np.asarray(r["y"]) for r in res.results], axis=0)


def kernel(**inputs):
    return run(inputs, n_cores=8).astype(np.float32)
```

```python
import contextlib
import numpy as np
import concourse.bass as bass
import concourse.mybir as mybir
from concourse.bass_utils import run_bass_kernel_spmd

F32 = mybir.dt.float32
BF16 = mybir.dt.bfloat16
AF = mybir.ActivationFunctionType
ALU = mybir.AluOpType

D = 1024
T = 4096
DEPTH = 2
DFF = 2816
NFC = DFF // 128
INC = 6148
PLE = 256
TT = 512
NQ = T // TT
EPS = 1e-6

COMPUTE = ("pe", "act", "dve", "pool")
NDMASEM = 56
NSP = 40


class Buf:
    __slots__ = ("name", "w", "r", "wl")

    def __init__(self, name=""):
        self.name = name
        self.w = None
        self.r = []
        self.wl = []


class Node:
    __slots__ = ("eng", "idx", "fn", "waits", "key", "val", "needs_inc", "clock", "is_dma")


class Prog:
    def __init__(self, nc):
        self.nc = nc
        self.ops = {e: [] for e in ("pe", "act", "dve", "pool", "sp")}
        self.clock = {e: {} for e in self.ops}
        self.dma_rr = 0
        self.dma_rr2 = 0
        self.dma_last = [None] * NDMASEM
        self.dma_cum = [0] * NDMASEM
        self.out_nodes = []

    def _record(self, eng, fn, reads, writes, is_dma, extra=(), shared=False):
        n = Node()
        n.eng = eng
        n.idx = len(self.ops[eng])
        n.fn = fn
        n.is_dma = is_dma
        n.needs_inc = False
        deps = []
        for b in reads:
            if b.w is not None:
                deps.append(b.w)
            deps.extend(b.wl)
        for b in writes:
            if not shared:
                if b.w is not None:
                    deps.append(b.w)
                deps.extend(b.wl)
            deps.extend(b.r)
        deps.extend(extra)
        if is_dma:
            if eng == "sp":
                s = self.dma_rr
                self.dma_rr = (self.dma_rr + 1) % NSP
            else:
                s = NSP + self.dma_rr2
                self.dma_rr2 = (self.dma_rr2 + 1) % (NDMASEM - NSP)
            if self.dma_last[s] is not None:
                deps.append(self.dma_last[s])
            self.dma_cum[s] += 16
            n.key = ("d", s)
            n.val = self.dma_cum[s]
            self.dma_last[s] = n
        else:
            n.key = eng
            n.val = n.idx + 1
        ck = self.clock[eng]
        waits = {}
        for d in deps:
            if (not d.is_dma) and d.eng == eng and eng == "pe":
                continue
            if ck.get(d.key, 0) >= d.val:
                continue
            if waits.get(d.key, (0, None))[0] < d.val:
                waits[d.key] = (d.val, d)
        n.waits = [w[1] for w in waits.values()]
        if n.waits:
            ck = dict(ck)
            for d in n.waits:
                d.needs_inc = True
                for k, v in d.clock.items():
                    if ck.get(k, 0) < v:
                        ck[k] = v
                if ck.get(d.key, 0) < d.val:
                    ck[d.key] = d.val
            self.clock[eng] = ck
        n.clock = ck
        self.ops[eng].append(n)
        for b in reads:
            b.r.append(n)
        for b in writes:
            if shared:
                b.wl.append(n)
            else:
                b.w = n
                b.wl = []
                b.r = []
        return n

    def op(self, eng, fn, reads=(), writes=()):
        return self._record(eng, fn, reads, writes, False)

    def dma(self, fn, reads=(), writes=(), q="sp", is_out=False, shared=False):
        n = self._record(q, fn, reads, writes, True, shared=shared)
        if is_out:
            self.out_nodes.append(n)
        return n

    def barrier(self):
        last = [self.ops[e][-1] for e in COMPUTE if self.ops[e]]
        for e in COMPUTE:
            for n in reversed(self.ops[e]):
                if not n.is_dma:
                    last.append(n)
                    break
        last += [d for d in self.dma_last if d is not None]
        for e in ("pe", "act", "dve", "pool", "sp"):
            self._record(e, lambda eng: eng.nop(), (), (), False, extra=last)

    def emit(self, es):
        nc = self.nc
        sems = {}
        for e in COMPUTE:
            sems[e] = es.enter_context(nc.semaphore("S_" + e))
        for i in range(NDMASEM):
            sems[("d", i)] = es.enter_context(nc.semaphore("D%d" % i))
        for e in COMPUTE:
            c = 0
            for n in self.ops[e]:
                if n.is_dma:
                    continue
                if n.needs_inc:
                    c += 1
                    n.val = c
                else:
                    n.val = None
        block = es.enter_context(nc.Block())

        def run(ename):
            def body(eng):
                for n in self.ops[ename]:
                    for d in n.waits:
                        eng.wait_ge(sems[d.key], d.val)
                    ins = n.fn(eng)
                    if n.is_dma:
                        ins.then_inc(sems[n.key], 16)
                    elif n.needs_inc:
                        ins.then_inc(sems[n.key], 1)
                if ename == "sp":
                    for d in self.dma_last:
                        if d is not None:
                            eng.wait_ge(sems[d.key], d.val)
            return body

        block.tensor(run("pe"))
        block.scalar(run("act"))
        block.vector(run("dve"))
        block.gpsimd(run("pool"))
        block.sync(run("sp"))


C_ID = 0
C_ONES = 128
C_TRIU = 256
C_MNEG = 384
C_MH = 512
C_MS = 1024
C_RCW = 1536
C_RC0 = 1538
C_PM = C_RC0 + 1024
C_N = C_PM + 12 * 128
C_G = 512


def make_consts():
    c = np.zeros((128, C_N), np.float32)
    k = np.arange(128)
    c[:, C_ID:C_ID + 128] = np.eye(128, dtype=np.float32)
    c[:, C_ONES:C_ONES + 128] = 1.0
    c[:, C_TRIU:C_TRIU + 128] = (k[:, None] <= k[None, :]).astype(np.float32)
    c[:, C_MNEG:C_MNEG + 128] = np.where(k[None, :] < k[:, None], -30000.0, 0.0)
    rj4, rg2 = k // 32, (k // 16) % 2
    cg2 = k // 64
    for j4 in range(4):
        c[:, C_MH + j4 * 128:C_MH + (j4 + 1) * 128] = ((rj4[:, None] == j4) & (rg2[:, None] == cg2[None, :])).astype(np.float32)
        c[:, C_MS + j4 * 128:C_MS + (j4 + 1) * 128] = ((cg2[:, None] == rg2[None, :]) & (rj4[None, :] == j4)).astype(np.float32)
    wins = np.array([2, 4, 8, 16], np.float32)
    for ch in range(2):
        w = wins[2 * ch + (k // 64)]
        c[:, C_RCW + ch] = 1.0 / w
        t = np.arange(512, dtype=np.float32)
        c[:, C_RC0 + ch * 512:C_RC0 + (ch + 1) * 512] = 1.0 / np.minimum(t[None, :] + 1.0, w[:, None])
    tp = k[:, None].astype(np.float64)
    tq = k[None, :].astype(np.float64)
    for g in range(4):
        W = float(wins[g])
        main = np.where((tp <= tq) & (tp > tq - W), 1.0 / W, 0.0) - np.eye(128)
        corner = np.where((tp - 128 > tq - W), 1.0 / W, 0.0)
        cnt = np.minimum(tq + 1.0, W)
        main0 = np.where((tp <= tq) & (tp > tq - W), 1.0 / cnt, 0.0) - np.eye(128)
        for i, mtx in enumerate((main, corner, main0)):
            o = C_PM + (g * 3 + i) * 128
            c[:, o:o + 128] = mtx.astype(np.float32)
    return c


def build(phases=("in", "ffn", "mix", "ple", "out"), depth=DEPTH, mix_parts=("att", "pool", "ssm", "conv"), debug=False, nq=NQ):
    nc = bass.Bass("TRN2", target_bir_lowering=False)
    dt_in = lambda name, shape: nc.dram_tensor(name, list(shape), F32, kind="ExternalInput").ap()
    x = dt_in("x", [T, D])
    p_in = dt_in("p", [DEPTH, T, PLE])
    norm_g = dt_in("norm_g", [DEPTH, 4, D])
    w_gate = dt_in("ffn_w_gate", [DEPTH, 2, D, DFF])
    w_up = dt_in("ffn_w_up", [DEPTH, 2, D, DFF])
    w_down = dt_in("ffn_w_down", [DEPTH, 2, DFF, D])
    w_in = dt_in("w_in", [DEPTH, D, INC])
    f_bias = dt_in("f_bias", [DEPTH, 4])
    pool_w = dt_in("pool_w", [DEPTH, 4, 64, 64])
    pool_scale = dt_in("pool_scale", [DEPTH, 256])
    lam_re = dt_in("ssm_lam_re", [DEPTH, 16, 64])
    lam_im = dt_in("ssm_lam_im", [DEPTH, 16, 64])
    log_dt = dt_in("ssm_log_dt", [DEPTH, 16])
    b_re = dt_in("ssm_b_re", [DEPTH, 16, 64, 16])
    b_im = dt_in("ssm_b_im", [DEPTH, 16, 64, 16])
    c_re = dt_in("ssm_c_re", [DEPTH, 16, 16, 64])
    c_im = dt_in("ssm_c_im", [DEPTH, 16, 16, 64])
    ssm_d = dt_in("ssm_d", [DEPTH, 256])
    w_glu = dt_in("ssm_w_glu", [DEPTH, 256, 512])
    conv_w = dt_in("conv_w", [DEPTH, 3, 256])
    w_branch = dt_in("w_branch", [DEPTH, 4, 256, D])
    w_out = dt_in("w_out", [DEPTH, D, D])
    ple_wg = dt_in("ple_w_gate", [DEPTH, D, D])
    ple_wp = dt_in("ple_w_proj", [DEPTH, PLE, D])
    final_g = dt_in("final_g", [D])
    consts = dt_in("consts", [128, C_N])
    y_out = nc.dram_tensor("y", [T, D], F32, kind="ExternalOutput").ap()

    dk = dict(kind="ExternalOutput") if debug else {}
    hT = nc.dram_tensor("hT_scr", [D, T], F32, **dk).ap()
    xnT = nc.dram_tensor("xnT_scr", [D, T], BF16, **dk).ap()
    ybr = nc.dram_tensor("ybr_scr", [4, 256, T], BF16, **dk).ap()
    qaug = nc.dram_tensor("qaug_scr", [128, 4, T], BF16, **dk).ap()
    vtok = nc.dram_tensor("vtok_scr", [T, 256], BF16, **dk).ap()
    hTv = hT.rearrange("(c p) t -> p c t", p=128)
    xnTv = xnT.rearrange("(c p) t -> p c t", p=128)

    es = contextlib.ExitStack()
    P = Prog(nc)
    OP = lambda eng, reads, writes, fn: P.op(eng, fn, reads, writes)

    uid = [0]

    def sb(stack, name, shape, dt):
        uid[0] += 1
        return stack.enter_context(nc.sbuf_tensor("%s_u%d" % (name, uid[0]), list(shape), dt))

    def dbg_dump(name, ap, shape, dt, reads):
        if not debug:
            return
        t = nc.dram_tensor("dbg_" + name, list(shape), dt, kind="ExternalOutput").ap()
        P.dma(lambda e: e.dma_start(out=t, in_=ap), reads=reads, writes=[Buf()])

    HT = [[Buf("hT%d_%d" % (q, c)) for c in range(8)] for q in range(NQ)]
    XN = [Buf("xnT%d" % q) for q in range(NQ)]
    YB = [[Buf("ybr%d_%d" % (b, q)) for q in range(NQ)] for b in range(4)]
    OUTB = Buf("out")

    psb = [es.enter_context(nc.psum_tensor("psb%d" % i, [128, 512], F32)) for i in range(8)]
    psB = [Buf("psb%d" % i) for i in range(8)]
    ps_rr = [0]

    def next_ps():
        i = ps_rr[0]
        ps_rr[0] = (i + 1) % 6
        return psb[i], psB[i]

    cst = sb(es, "cst", [128, 512], F32)
    cstb = sb(es, "cstb", [128, 512], BF16)
    Bc = Buf("cst")
    Bcb = Buf("cstb")
    P.dma(lambda e: e.dma_start(out=cst[:], in_=consts[:, 0:512]), writes=[Bc])
    OP("dve", [Bc], [Bcb], lambda e: e.tensor_copy(out=cstb[:], in_=cst[:, 0:512]))
    ident_f = cst[:, C_ID:C_ID + 128]
    ones_f = cst[:, C_ONES:C_ONES + 128]
    triu_f = cst[:, C_TRIU:C_TRIU + 128]
    ident_b = cstb[:, C_ID:C_ID + 128]
    ones_b = cstb[:, C_ONES:C_ONES + 128]
    mneg_b = cstb[:, C_MNEG:C_MNEG + 128]
    epsc = sb(es, "epsc", [128, 2], F32)
    Beps = Buf("eps")
    OP("dve", [], [Beps], lambda e: e.memset(epsc[:, 0:1], EPS))
    OP("dve", [Beps], [Beps], lambda e: e.memset(epsc[:, 1:2], 1.0))
    gcol = sb(es, "gcol", [128, 9, 8], F32)
    Bg = Buf("gcol")
    P.dma(lambda e: e.dma_start(out=gcol[:, 0:8, :], in_=norm_g.rearrange("l n (c p) -> p (l n) c", p=128), allow_slow_non_contiguous=True), writes=[Bg])
    P.dma(lambda e: e.dma_start(out=gcol[:, 8, :], in_=final_g.rearrange("(c p) -> p c", p=128), allow_slow_non_contiguous=True), writes=[Bg])

    def rmsnorm(h, hB, gi, xn, xnB, tmp):
        ps, pB = next_ps()
        for c in range(8):
            sq, sqB = tmp["sq"][c % 2]
            OP("act", [hB[c]], [sqB], lambda e, c=c, sq=sq: e.activation(out=sq, in_=h[:, c, :], func=AF.Square))
            OP("pe", [sqB, Bcb], [pB], lambda e, c=c, sq=sq, ps=ps: e.matmul(ps[:, 0:TT], lhsT=ones_b, rhs=sq, start=(c == 0), stop=(c == 7)))
        lnv, lnB = tmp["lnv"]
        rstd, rsB = tmp["rstd"]
        OP("act", [pB, Beps], [lnB], lambda e, ps=ps: e.activation(out=lnv, in_=ps[:, 0:TT], func=AF.Ln, scale=1.0 / D, bias=epsc[:, 0:1]))
        OP("act", [lnB], [rsB], lambda e: e.activation(out=rstd, in_=lnv, func=AF.Exp, scale=-0.5))
        for c in range(8):
            OP("dve", [hB[c], rsB, Bg], [xnB[c]],
               lambda e, c=c: e.scalar_tensor_tensor(out=xn[:, c, :], in0=h[:, c, :], scalar=gcol[:, gi, c:c + 1], in1=rstd,
                                                     op0=ALU.mult, op1=ALU.mult))

    def norm_tmp(stack, tag):
        sq0 = sb(stack, "sq0" + tag, [128, TT], BF16)
        sq1 = sb(stack, "sq1" + tag, [128, TT], BF16)
        lnv = sb(stack, "lnv" + tag, [128, TT], F32)
        rstd = sb(stack, "rstd" + tag, [128, TT], F32)
        return {"sq": [(sq0[:], Buf()), (sq1[:], Buf())], "lnv": (lnv[:], Buf()), "rstd": (rstd[:], Buf())}

    def load_h(tile, tB, q):
        P.dma(lambda e: e.dma_start(out=tile, in_=hTv[:, :, q * TT:(q + 1) * TT]), reads=HT[q], writes=tB)

    def wload(dst, src, wB):
        P.dma(lambda e: e.dma_start(out=dst, in_=src), writes=wB, q="pool")

    def phase_in():
        P.barrier()
        with contextlib.ExitStack() as st:
            xt = [sb(st, "xt%d" % i, [128, D], F32) for i in range(2)]
            xtB = [Buf() for _ in range(2)]
            ho = [sb(st, "ho%d" % i, [128, 8, TT], F32) for i in range(2)]
            hoB = [Buf() for _ in range(2)]
            k = 0
            for q in range(NQ):
                for s in range(4):
                    tt = q * 4 + s
                    xb, xB = xt[tt % 2], xtB[tt % 2]
                    P.dma(lambda e, xb=xb, tt=tt: e.dma_start(out=xb[:], in_=x[tt * 128:(tt + 1) * 128, :]), writes=[xB])
                    for half in range(2):
                        ps, pB = next_ps()
                        for cc in range(4):
                            c = half * 4 + cc
                            OP("pe", [xB, Bc], [pB], lambda e, ps=ps, cc=cc, c=c, xb=xb: e.transpose(out=ps[:, cc * 128:(cc + 1) * 128], in_=xb[:, c * 128:(c + 1) * 128], identity=ident_f))
                        eng = "act" if (k % 2 == 0) else "dve"
                        k += 1
                        dst = ho[q % 2][:, half * 4:(half + 1) * 4, s * 128:(s + 1) * 128]
                        src = ps[:, :].rearrange("p (c t) -> p c t", c=4)
                        if eng == "act":
                            OP("act", [pB], [hoB[q % 2]], lambda e, dst=dst, src=src: e.activation(out=dst, in_=src, func=AF.Copy))
                        else:
                            OP("dve", [pB], [hoB[q % 2]], lambda e, dst=dst, src=src: e.tensor_copy(out=dst, in_=src))
                P.dma(lambda e, q=q: e.dma_start(out=hTv[:, :, q * TT:(q + 1) * TT], in_=ho[q % 2][:]), reads=[hoB[q % 2]], writes=HT[q])

    def phase_out():
        P.barrier()
        with contextlib.ExitStack() as st:
            hn2 = [sb(st, "fo_hn%d" % i, [128, 8, TT], F32) for i in range(2)]
            hnB2 = [[Buf() for _ in range(8)] for _ in range(2)]
            yn2 = [sb(st, "fo_yn%d" % i, [128, 8, TT], F32) for i in range(2)]
            ynB2 = [[Buf() for _ in range(8)] for _ in range(2)]
            ot = [sb(st, "fo_ot%d" % i, [128, D], F32) for i in range(4)]
            otB = [Buf() for _ in range(4)]
            tmp2 = [norm_tmp(st, "fo%d" % i) for i in range(2)]
            k = 0
            for q in range(NQ):
                hn, hnB, yn, ynB = hn2[q % 2], hnB2[q % 2], yn2[q % 2], ynB2[q % 2]
                load_h(hn[:], hnB, q)
                rmsnorm(hn, hnB, 8, yn, ynB, tmp2[q % 2])
                for s in range(4):
                    tt = q * 4 + s
                    o, oB = ot[tt % 4], otB[tt % 4]
                    for half in range(2):
                        ps, pB = next_ps()
                        for cc in range(4):
                            c = half * 4 + cc
                            OP("pe", [ynB[c], Bc], [pB], lambda e, ps=ps, cc=cc, c=c, s=s, yn=yn: e.transpose(out=ps[:, cc * 128:(cc + 1) * 128], in_=yn[:, c, s * 128:(s + 1) * 128], identity=ident_f))
                        dst = o[:, half * 512:(half + 1) * 512]
                        if k % 2 == 0:
                            OP("act", [pB], [oB], lambda e, dst=dst, ps=ps: e.activation(out=dst, in_=ps[:, :], func=AF.Copy))
                        else:
                            OP("dve", [pB], [oB], lambda e, dst=dst, ps=ps: e.tensor_copy(out=dst, in_=ps[:, :]))
                        k += 1
                    P.dma(lambda e, o=o, tt=tt: e.dma_start(out=y_out[tt * 128:(tt + 1) * 128, :], in_=o[:]), reads=[oB], writes=[Buf()], is_out=True)

    def phase_ffn(l, f):
        P.barrier()
        with contextlib.ExitStack() as st:
            wg = sb(st, "wg", [128, 8, DFF], BF16)
            wu = sb(st, "wu", [128, 8, DFF], BF16)
            wd = sb(st, "wd", [128, NFC, D], BF16)
            CB = [(0, 256), (256, 1024), (1024, 2048), (2048, DFF)]
            wgB = [Buf() for _ in range(4)]
            wuB = [Buf() for _ in range(4)]
            wdB = [Buf() for _ in range(NFC)]
            wgv = w_gate[l, f].rearrange("(kc p) n -> p kc n", p=128)
            wuv = w_up[l, f].rearrange("(kc p) n -> p kc n", p=128)
            wdv = w_down[l, f].rearrange("(fc p) n -> p fc n", p=128)
            for cb, (c0, c1) in enumerate(CB):
                wload(wg[:, :, c0:c1], wgv[:, :, c0:c1], [wgB[cb]])
                wload(wu[:, :, c0:c1], wuv[:, :, c0:c1], [wuB[cb]])
            for f0 in range(0, NFC, 6):
                f1 = min(NFC, f0 + 6)
                wload(wd[:, f0:f1, :], wdv[:, f0:f1, :], wdB[f0:f1])
            hn = sb(st, "ff_hn", [128, 8, TT], F32)
            hnB = [Buf() for _ in range(8)]
            xn = [sb(st, "ff_xn%d" % i, [128, 8, TT], BF16) for i in range(2)]
            xnB = [[Buf() for _ in range(8)] for _ in range(2)]
            act = sb(st, "ff_act", [128, NFC, TT], BF16)
            actB = [Buf() for _ in range(NFC)]
            sg = [sb(st, "ff_sg%d" % i, [128, TT], F32) for i in range(2)]
            sgB = [Buf() for _ in range(2)]
            hr = [sb(st, "ff_hr%d" % i, [128, TT], F32) for i in range(3)]
            hrB = [Buf() for _ in range(3)]
            tmp = norm_tmp(st, "ff")
            gi = l * 4 + (0 if f == 0 else 2)
            k = 0
            load_h(hn[:], hnB, 0)
            rmsnorm(hn, hnB, gi, xn[0], xnB[0], tmp)
            for q in range(nq):
                X, XB = xn[q % 2], xnB[q % 2]
                if q + 1 < nq:
                    load_h(hn[:], hnB, q + 1)
                if q == 0 and l == 0 and f == 0:
                    dbg_dump("xn", X[:], [128, 8, TT], BF16, XB)
                    dbg_dump("wg", wg[:], [128, 8, DFF], BF16, wgB)
                    dbg_dump("wd", wd[:], [128, NFC, D], BF16, wdB)
                for fc in range(NFC):
                    pg, pgB = next_ps()
                    for kc in range(8):
                        OP("pe", [wgB[0 if fc < 2 else 1 + fc // 8], XB[kc]], [pgB], lambda e, pg=pg, kc=kc, fc=fc, X=X: e.matmul(pg[:, 0:TT], lhsT=wg[:, kc, fc * 128:(fc + 1) * 128], rhs=X[:, kc, :], start=(kc == 0), stop=(kc == 7)))
                    pu, puB = next_ps()
                    for kc in range(8):
                        OP("pe", [wuB[0 if fc < 2 else 1 + fc // 8], XB[kc]], [puB], lambda e, pu=pu, kc=kc, fc=fc, X=X: e.matmul(pu[:, 0:TT], lhsT=wu[:, kc, fc * 128:(fc + 1) * 128], rhs=X[:, kc, :], start=(kc == 0), stop=(kc == 7)))
                    s_, sB_ = sg[fc % 2], sgB[fc % 2]
                    OP("act", [pgB], [sB_], lambda e, s_=s_, pg=pg: e.activation(out=s_[:], in_=pg[:, 0:TT], func=AF.Silu))
                    OP("dve", [sB_, puB], [actB[fc]], lambda e, s_=s_, pu=pu, fc=fc: e.tensor_tensor(out=act[:, fc, :], in0=pu[:, 0:TT], in1=s_[:], op=ALU.mult))
                    if fc == 11 and q + 1 < nq:
                        rmsnorm(hn, hnB, gi, xn[(q + 1) % 2], xnB[(q + 1) % 2], tmp)
                if q == 0 and l == 0 and f == 0:
                    dbg_dump("act", act[:], [128, NFC, TT], BF16, actB)
                for dc in range(8):
                    po, poB = next_ps()
                    for fc in range(NFC):
                        OP("pe", [wdB[fc], actB[fc]], [poB], lambda e, po=po, fc=fc, dc=dc: e.matmul(po[:, 0:TT], lhsT=wd[:, fc, dc * 128:(dc + 1) * 128], rhs=act[:, fc, :], start=(fc == 0), stop=(fc == NFC - 1)))
                    r, rB = hr[k % 3], hrB[k % 3]
                    k += 1
                    P.dma(lambda e, r=r, dc=dc, q=q: e.dma_start(out=r[:], in_=hTv[:, dc, q * TT:(q + 1) * TT]), reads=[HT[q][dc]], writes=[rB])
                    OP("dve", [poB, rB], [rB], lambda e, r=r, po=po: e.scalar_tensor_tensor(out=r[:], in0=po[:, 0:TT], scalar=0.5, in1=r[:], op0=ALU.mult, op1=ALU.add))
                    P.dma(lambda e, r=r, dc=dc, q=q: e.dma_start(out=hTv[:, dc, q * TT:(q + 1) * TT], in_=r[:]), reads=[rB], writes=[HT[q][dc]])

    def phase_ple(l, fuse_out=False):
        P.barrier()
        with contextlib.ExitStack() as st:
            wpg = sb(st, "wpg", [128, 8, D], BF16)
            wpp = sb(st, "wpp", [128, 2, D], BF16)
            wpgB = [Buf()]
            wppB = [Buf()]
            wpgv = ple_wg[l].rearrange("(kc p) n -> p kc n", p=128)
            wpgB = [Buf(), Buf()]
            wload(wpg[:, :, 0:256], wpgv[:, :, 0:256], [wpgB[0]])
            wload(wpg[:, :, 256:D], wpgv[:, :, 256:D], [wpgB[1]])
            wload(wpp[:], ple_wp[l].rearrange("(kc p) n -> p kc n", p=128), wppB)
            NH = 3 if fuse_out else 2
            hn2 = [sb(st, "pl_hn%d" % i, [128, 8, TT], F32) for i in range(NH)]
            hnB2 = [[Buf() for _ in range(8)] for _ in range(NH)]
            xn = [sb(st, "pl_xn%d" % i, [128, 8, TT], BF16) for i in range(2)]
            xnB = [[Buf() for _ in range(8)] for _ in range(2)]
            pt = [sb(st, "pl_pt%d" % i, [128, 4, PLE], F32) for i in range(2)]
            ptB = [Buf() for _ in range(2)]
            pT = [sb(st, "pl_pT%d" % i, [128, 2, TT], BF16) for i in range(2)]
            pTB = [Buf() for _ in range(2)]
            sg = [sb(st, "pl_sg%d" % i, [128, TT], F32) for i in range(2)]
            sgB = [Buf() for _ in range(2)]
            tg = [sb(st, "pl_tg%d" % i, [128, TT], F32) for i in range(2)]
            tgB = [Buf() for _ in range(2)]
            hr = [sb(st, "pl_hr%d" % i, [128, TT], F32) for i in range(3)]
            hrB = [Buf() for _ in range(3)]
            tmp2 = [norm_tmp(st, "pl%d" % i) for i in range(2)]
            gi = l * 4 + 3

            def ple_load(q):
                load_h(hn2[q % NH][:], hnB2[q % NH], q)
                pt_, ptB_ = pt[q % 2], ptB[q % 2]
                P.dma(lambda e, pt_=pt_, q=q: e.dma_start(out=pt_[:], in_=p_in[l, q * TT:(q + 1) * TT, :].rearrange("(s p) c -> p s c", p=128)), writes=[ptB_])

            def ple_prep(q):
                rmsnorm(hn2[q % NH], hnB2[q % NH], gi, xn[q % 2], xnB[q % 2], tmp2[q % 2])
                pt_, ptB_ = pt[q % 2], ptB[q % 2]
                pT_, pTB_ = pT[q % 2], pTB[q % 2]
                for c2 in range(2):
                    ps, pB = next_ps()
                    for s in range(4):
                        OP("pe", [ptB_, Bc], [pB], lambda e, ps=ps, s=s, c2=c2, pt_=pt_: e.transpose(out=ps[:, s * 128:(s + 1) * 128], in_=pt_[:, s, c2 * 128:(c2 + 1) * 128], identity=ident_f))
                    OP("act", [pB], [pTB_], lambda e, ps=ps, c2=c2, pT_=pT_: e.activation(out=pT_[:, c2, :], in_=ps[:, :], func=AF.Copy))

            if fuse_out:
                yn2 = [sb(st, "fo_yn%d" % i, [128, 8, TT], F32) for i in range(2)]
                ynB2 = [[Buf() for _ in range(8)] for _ in range(2)]
                ot = [sb(st, "fo_ot%d" % i, [128, D], F32) for i in range(4)]
                otB = [Buf() for _ in range(4)]
                tmpo = [norm_tmp(st, "fo%d" % i) for i in range(2)]
            kev = [0]

            def out_tile(q):
                hn, hnB, yn, ynB = hn2[q % NH], hnB2[q % NH], yn2[q % 2], ynB2[q % 2]
                rmsnorm(hn, hnB, 8, yn, ynB, tmpo[q % 2])
                for s in range(4):
                    tt = q * 4 + s
                    o, oB = ot[tt % 4], otB[tt % 4]
                    for half in range(2):
                        ps, pB = next_ps()
                        for cc in range(4):
                            c_ = half * 4 + cc
                            OP("pe", [ynB[c_], Bc], [pB], lambda e, ps=ps, cc=cc, c_=c_, s=s, yn=yn: e.transpose(out=ps[:, cc * 128:(cc + 1) * 128], in_=yn[:, c_, s * 128:(s + 1) * 128], identity=ident_f))
                        dst = o[:, half * 512:(half + 1) * 512]
                        if kev[0] % 2 == 0:
                            OP("act", [pB], [oB], lambda e, dst=dst, ps=ps: e.activation(out=dst, in_=ps[:, :], func=AF.Copy))
                        else:
                            OP("dve", [pB], [oB], lambda e, dst=dst, ps=ps: e.tensor_copy(out=dst, in_=ps[:, :]))
                        kev[0] += 1
                    P.dma(lambda e, o=o, tt=tt: e.dma_start(out=y_out[tt * 128:(tt + 1) * 128, :], in_=o[:]), reads=[oB], writes=[Buf()], is_out=True)

            ple_load(0)
            ple_prep(0)
            for q in range(NQ):
                hn, hnB = hn2[q % NH], hnB2[q % NH]
                X, XB = xn[q % 2], xnB[q % 2]
                pT_, pTB_ = pT[q % 2], pTB[q % 2]
                if q + 1 < NQ:
                    ple_load(q + 1)
                for dc in range(8):
                    pg, pgB = next_ps()
                    for kc in range(8):
                        OP("pe", [wpgB[0 if dc < 2 else 1], XB[kc]], [pgB], lambda e, pg=pg, kc=kc, dc=dc, X=X: e.matmul(pg[:, 0:TT], lhsT=wpg[:, kc, dc * 128:(dc + 1) * 128], rhs=X[:, kc, :], start=(kc == 0), stop=(kc == 7)))
                    pe_, peB = next_ps()
                    for c2 in range(2):
                        OP("pe", [wppB[0], pTB_], [peB], lambda e, pe_=pe_, c2=c2, dc=dc, pT_=pT_: e.matmul(pe_[:, 0:TT], lhsT=wpp[:, c2, dc * 128:(dc + 1) * 128], rhs=pT_[:, c2, :], start=(c2 == 0), stop=(c2 == 1)))
                    s_, sB_ = sg[dc % 2], sgB[dc % 2]
                    OP("act", [pgB], [sB_], lambda e, s_=s_, pg=pg: e.activation(out=s_[:], in_=pg[:, 0:TT], func=AF.Sigmoid))
                    t_, tB_ = tg[dc % 2], tgB[dc % 2]
                    OP("dve", [sB_, peB], [tB_], lambda e, s_=s_, pe_=pe_, t_=t_: e.tensor_tensor(out=t_[:], in0=pe_[:, 0:TT], in1=s_[:], op=ALU.mult))
                    OP("pool", [tB_, hnB[dc]], [hnB[dc]], lambda e, hn=hn, t_=t_, dc=dc: e.tensor_tensor(out=hn[:, dc, :], in0=t_[:], in1=hn[:, dc, :], op=ALU.add))
                    if not fuse_out:
                        P.dma(lambda e, hn=hn, dc=dc, q=q: e.dma_start(out=hTv[:, dc, q * TT:(q + 1) * TT], in_=hn[:, dc, :]), reads=[hnB[dc]], writes=[HT[q][dc]])
                    if dc == 3 and q + 1 < NQ:
                        ple_prep(q + 1)
                    if fuse_out and dc == 3 and q >= 1:
                        out_tile(q - 1)
            if fuse_out:
                out_tile(NQ - 1)

    ctx = dict(nc=nc, P=P, OP=OP, sb=sb, es=es, next_ps=next_ps, rmsnorm=rmsnorm, norm_tmp=norm_tmp, load_h=load_h,
               wload=wload, HT=HT, XN=XN, YB=YB, hTv=hTv, xnTv=xnTv, ybr=ybr, cst=cst, cstb=cstb, Bc=Bc, Bcb=Bcb,
               epsc=epsc, Beps=Beps, ident_f=ident_f, ones_f=ones_f, triu_f=triu_f, ident_b=ident_b, ones_b=ones_b,
               mneg_b=mneg_b, gcol=gcol, Bg=Bg,
               w_in=w_in, f_bias=f_bias, pool_w=pool_w, pool_scale=pool_scale, lam_re=lam_re, lam_im=lam_im,
               log_dt=log_dt, b_re=b_re, b_im=b_im, c_re=c_re, c_im=c_im, ssm_d=ssm_d, w_glu=w_glu, conv_w=conv_w,
               w_branch=w_branch, w_out=w_out, consts=consts, qaug=qaug, vtok=vtok, psb=psb, psB=psB, dbg_dump=dbg_dump, nq=nq)

    if "in" in phases:
        phase_in()
    for l in range(depth):
        if "ffn" in phases:
            phase_ffn(l, 0)
        if "mix" in phases:
            phase_mixer(ctx, l, mix_parts)
        if "ffn" in phases:
            phase_ffn(l, 1)
        fuse = ("out" in phases) and (l == depth - 1)
        if "ple" in phases:
            phase_ple(l, fuse_out=fuse)
    if "out" in phases and "ple" not in phases:
        phase_out()
    P.emit(es)
    es.close()
    return nc


from types import SimpleNamespace


def phase_mixer(ctx, l, parts):
    c = SimpleNamespace(**ctx)
    P, OP, sb, nc = c.P, c.OP, c.sb, c.nc
    psb, psB = c.psb, c.psB
    ybrv = c.ybr.rearrange("b (c p) t -> b p c t", p=128)
    QA = [Buf() for _ in range(NQ)]
    VT = [Buf() for _ in range(NQ)]
    P.barrier()
    with contextlib.ExitStack() as so:
        u_ssm = sb(so, "u_ssm", [128, 2, 8 + T], BF16)
        uB = [[Buf() for _ in range(NQ)] for _ in range(2)]
        spar = ssm_param_load(c, l, so) if "ssm" in parts else None
        spre = ssm_s_alloc(c, so) if "ssm" in parts else None
        with contextlib.ExitStack() as s1:
            k_aug = sb(s1, "k_aug", [128, 4, T], BF16)
            kB = [[Buf() for _ in range(NQ)] for _ in range(4)]
            kcB = Buf()
            cabs = sb(s1, "cabs", [128, 32, 4], F32)
            tots = sb(s1, "tots", [128, 33, 4], F32)
            cabsB = [Buf() for _ in range(32)]
            totsB = [Buf() for _ in range(33)]
            def issue_params():
                if spar is not None:
                    for fn, rd, wr, kw in spar.dq:
                        P.dma(fn, reads=rd, writes=wr, **kw)
                    spar.dq.clear()
            mixer_m1(c, l, parts, u_ssm, uB, k_aug, kB, kcB, cabs, cabsB, tots, totsB, QA, VT, ybrv, issue_params)
            issue_params()
            P.barrier()
            sch = ssm_s_chain(c, l, spre, spar) if "ssm" in parts else None
            dq = sch.dq if sch is not None else []

            def drain(n):
                for _ in range(min(n, len(dq))):
                    eng, rd, wr, fn = dq.pop(0)
                    OP(eng, rd, wr, fn)
            if "att" in parts:
                mixer_m3(c, l, k_aug, kB, kcB, cabs, cabsB, tots, totsB, QA, VT, ybrv, drain)
            drain(len(dq))
        P.barrier()
        if "ssm" in parts:
            mixer_m2(c, l, u_ssm, uB, ybrv, spar, sch)
    P.barrier()
    mixer_m4(c, l, parts, ybrv)


def mixer_m1(c, l, parts, u_ssm, uB, k_aug, kB, kcB, cabs, cabsB, tots, totsB, QA, VT, ybrv, after_tile0=lambda: None):
    P, OP, sb, nc, next_ps = c.P, c.OP, c.sb, c.nc, c.next_ps
    with contextlib.ExitStack() as st:
        win = sb(st, "win", [128, 8, 2052], BF16)
        WBLK = [(0, 772), (772, 1796), (1796, 2052)]
        winB = [Buf() for _ in WBLK]
        wiv = c.w_in[l].rearrange("(kc p) n -> p kc n", p=128)
        c.wload(win[:, :, 0:772], wiv[:, :, 0:772], [winB[0]])

        def wB(col):
            for i, (c0, c1) in enumerate(WBLK):
                if c0 <= col < c1:
                    return winB[i]

        wf_sb = sb(st, "wf_sb", [128, 8, 4, 64], BF16)
        wfB = Buf()
        OP("dve", [winB[0]], [wfB], lambda e: e.tensor_copy(out=wf_sb[:], in_=win[:, :, 768:772].unsqueeze(3).to_broadcast([128, 8, 4, 64])))
        fb = sb(st, "fb", [128, 8], F32)
        fbB = Buf()
        P.dma(lambda e: e.dma_start(out=fb[:, 0:4], in_=c.f_bias[l:l + 1, :].partition_broadcast(128), allow_slow_non_contiguous=True), writes=[fbB])
        OP("dve", [fbB], [fbB], lambda e: e.tensor_scalar(out=fb[:, 4:8], in0=fb[:, 0:4], scalar1=-1.0, scalar2=None, op0=ALU.mult))
        pwb = sb(st, "pwb", [128, 2, 128], BF16)
        pwB = Buf()
        OP("pool", [], [pwB], lambda e: e.memset(pwb[:], 0.0))
        for g in range(4):
            r0 = (g % 2) * 64
            P.dma(lambda e, g=g, r0=r0: e.dma_start(out=pwb[r0:r0 + 64, g // 2, r0:r0 + 64], in_=c.pool_w[l, g]), reads=[pwB], writes=[pwB], q="pool")
        pmb = sb(st, "pmb", [128, 12, 128], BF16)
        pmB = Buf()
        P.dma(lambda e: e.dma_start(out=pmb[:], in_=c.consts[:, C_PM:C_PM + 12 * 128].rearrange("p (a b) -> p a b", a=12)), writes=[pmB], q="pool")
        for i in (1, 2):
            c.wload(win[:, :, WBLK[i][0]:WBLK[i][1]], wiv[:, :, WBLK[i][0]:WBLK[i][1]], [winB[i]])
        scol = sb(st, "scol", [128, 8], F32)
        scB = Buf()
        P.dma(lambda e: e.dma_start(out=scol[:, 0:2], in_=c.pool_scale[l].rearrange("(c p) -> p c", p=128), allow_slow_non_contiguous=True), writes=[scB])
        P.dma(lambda e: e.dma_start(out=scol[:, 2:8].rearrange("p (j c) -> p j c", j=3), in_=c.conv_w[l].rearrange("j (c p) -> p j c", p=128), allow_slow_non_contiguous=True), writes=[scB])

        OP("pool", [], [totsB[0]], lambda e: e.memset(tots[:, 0, :], 0.0))

        hn = sb(st, "m1_hn", [128, 8, TT], F32)
        hnB = [Buf() for _ in range(8)]
        xn2 = [sb(st, "m1_xn%d" % i, [128, 8, TT], BF16) for i in range(2)]
        xnB2 = [[Buf() for _ in range(8)] for _ in range(2)]
        tmp = c.norm_tmp(st, "m1")
        qa = [sb(st, "m1_qa%d" % i, [128, 4, TT], BF16) for i in range(2)]
        qaB = [Buf() for _ in range(2)]
        et = [sb(st, "m1_et%d" % i, [128, TT], F32) for i in range(2)]
        etB = [Buf() for _ in range(2)]
        spt = [sb(st, "m1_sp%d" % i, [128, TT], F32) for i in range(2)]
        spB = [Buf() for _ in range(2)]
        crn = [sb(st, "m1_crn%d" % i, [128, TT], F32) for i in range(2)]
        crnB = [Buf() for _ in range(2)]
        hit = [sb(st, "m1_hit%d" % i, [128, TT], BF16) for i in range(2)]
        hitB = [Buf() for _ in range(2)]
        vst = [sb(st, "m1_vst%d" % i, [128, 4, 256], BF16) for i in range(2)]
        vstB = [Buf() for _ in range(2)]
        ftk = [sb(st, "m1_ftk%d" % i, [128, 12], F32) for i in range(2)]
        ftkB = [Buf() for _ in range(2)]
        xpt = sb(st, "m1_xpt", [128, 5, 256], BF16)
        xptB = [Buf() for _ in range(5)]
        pld = sb(st, "m1_pld", [128, 2, TT], BF16)
        pldB = [Buf() for _ in range(2)]
        yps = [sb(st, "m1_yps%d" % i, [128, 2, TT], BF16) for i in range(2)]
        ypsB = [Buf() for _ in range(2)]
        ycs = [sb(st, "m1_ycs%d" % i, [128, 2, TT], BF16) for i in range(2)]
        ycsB = [Buf() for _ in range(2)]
        ccs = [sb(st, "m1_ccs%d" % i, [128, TT], F32) for i in range(2)]
        ccsB = [Buf() for _ in range(2)]
        cbs = [sb(st, "m1_cbs%d" % i, [128, TT], F32) for i in range(2)]
        cbsB = [Buf() for _ in range(2)]
        zt = [[sb(st, "m1_z%d_%d" % (cc, i), [128, TT + 2], F32) for i in range(2)] for cc in range(2)]
        ztB = [[Buf() for _ in range(2)] for _ in range(2)]
        y1 = [sb(st, "m1_y1%d" % i, [128, TT], F32) for i in range(2)]
        y1B = [Buf() for _ in range(2)]
        ones1 = c.cst[:, C_ONES:C_ONES + 1]

        def proj_fm(col0, M, ps, pB, pslice, X, XB, tp=None):
            for kc in range(8):
                kw = {} if tp is None else {"tile_position": tp}
                OP("pe", [wB(col0), XB[kc]], [pB], lambda e, kc=kc, kw=kw: e.matmul(ps[pslice, 0:TT], lhsT=win[:, kc, col0:col0 + M], rhs=X[:, kc, :], start=(kc == 0), stop=(kc == 7), **kw))

        c.load_h(hn[:], hnB, 0)
        c.rmsnorm(hn, hnB, l * 4 + 1, xn2[0], xnB2[0], tmp)
        if c.nq > 1:
            c.load_h(hn[:], hnB, 1)
        for q in range(c.nq):
            qs = slice(q * TT, (q + 1) * TT)
            xn, xnB = xn2[q % 2], xnB2[q % 2]
            P.dma(lambda e, qs=qs, xn=xn: e.dma_start(out=c.xnTv[:, :, qs], in_=xn[:]), reads=xnB, writes=[c.XN[q]])
            Q_, QB_ = qa[q % 2], qaB[q % 2]
            if "att" in parts:
                for h in range(4):
                    ps, pB = next_ps()
                    proj_fm(h * 64, 64, ps, pB, slice(0, 64), xn, xnB)
                    for kc in range(8):
                        OP("pe", [wfB, xnB[kc]], [pB], lambda e, kc=kc, h=h, ps=ps, xn=xn: e.matmul(ps[64:128, 0:TT], lhsT=wf_sb[:, kc, h, :], rhs=xn[:, kc, :], start=(kc == 0), stop=(kc == 7), tile_position=(0, 64)))
                    OP("act", [pB], [QB_], lambda e, ps=ps, h=h, Q_=Q_: e.activation(out=Q_[0:64, h, :], in_=ps[0:64, 0:TT], func=AF.Copy, scale=0.125))
                    e_, eB_ = et[h % 2], etB[h % 2]
                    OP("act", [pB, fbB], [eB_], lambda e, ps=ps, h=h, e_=e_: e.activation(out=e_[64:128, :], in_=ps[64:128, 0:TT], func=AF.Exp, scale=-1.0, bias=fb[64:128, 4 + h:5 + h]))
                    s_, sB_ = spt[h % 2], spB[h % 2]
                    OP("act", [eB_, c.Beps], [sB_], lambda e, e_=e_, s_=s_: e.activation(out=s_[64:128, :], in_=e_[64:128, :], func=AF.Ln, bias=c.epsc[64:128, 1:2]))
                    r_, rB_ = crn[h % 2], crnB[h % 2]
                    OP("dve", [sB_, c.Bc], [rB_], lambda e, s_=s_, r_=r_: e.tensor_tensor_scan(out=r_[64:128, :], data0=ones1[64:128, :].to_broadcast([64, TT]), data1=s_[64:128, :], initial=0.0, op0=ALU.mult, op1=ALU.add))
                    h_, hB_ = hit[h % 2], hitB[h % 2]
                    OP("dve", [rB_], [hB_], lambda e, r_=r_, h_=h_: e.tensor_scalar(out=h_[64:128, :], in0=r_[64:128, :], scalar1=-1.0, scalar2=None, op0=ALU.mult))
                    OP("pool", [hB_], [QB_], lambda e, h_=h_, h=h, Q_=Q_: e.tensor_copy(out=Q_[64:96, h, :], in_=h_[64:96, :]))
                    OP("dve", [rB_, hB_], [QB_], lambda e, r_=r_, h_=h_, h=h, Q_=Q_: e.scalar_tensor_tensor(out=Q_[96:128, h, :], in0=r_[96:128, :], scalar=-1.0, in1=h_[96:128, :], op0=ALU.mult, op1=ALU.subtract))
                P.dma(lambda e, qs=qs, Q_=Q_: e.dma_start(out=c.qaug[:, :, qs], in_=Q_[:]), reads=[QB_], writes=[QA[q]])
                for h in range(4):
                    ps, pB = next_ps()
                    proj_fm(256 + h * 64, 64, ps, pB, slice(0, 64), xn, xnB)
                    OP("dve", [pB], [kB[h][q]], lambda e, ps=ps, h=h, qs=qs: e.tensor_copy(out=k_aug[0:64, h, qs], in_=ps[0:64, 0:TT]))
            if q == 1:
                after_tile0()
            if q == 0:
                OP("pool", [], [kcB], lambda e: e.memset(k_aug[64:128, :, :], 0.0))
                OP("pool", [kcB], [kcB], lambda e: e.memset(k_aug[64:65, :, :], 1.0))
                OP("pool", [kcB], [kcB], lambda e: e.memset(k_aug[96:97, :, :], 1.0))
            if q + 1 < c.nq:
                c.rmsnorm(hn, hnB, l * 4 + 1, xn2[(q + 1) % 2], xnB2[(q + 1) % 2], tmp)
                if q + 2 < c.nq:
                    c.load_h(hn[:], hnB, q + 2)
            V_, VB_ = vst[q % 2], vstB[q % 2]
            ppool = [(c.psb[6], c.psB[6]), (c.psb[7], c.psB[7])]

            def stA(s):
                tt = q * 4 + s
                ts_ = slice(s * 128, (s + 1) * 128)
                if "att" not in parts:
                    return
                psA, pBA = next_ps()
                for kc in range(8):
                    OP("pe", [winB[0], xnB[kc]], [pBA], lambda e, kc=kc, psA=psA, ts_=ts_, xn=xn: e.matmul(psA[:, 0:260], lhsT=xn[:, kc, ts_], rhs=win[:, kc, 512:772], start=(kc == 0), stop=(kc == 7)))
                OP("act", [pBA], [VB_], lambda e, psA=psA, s=s, V_=V_: e.activation(out=V_[:, s, :], in_=psA[:, 0:256], func=AF.Copy))
                f_, fB_ = ftk[tt % 2], ftkB[tt % 2]
                OP("dve", [pBA, fbB], [fB_], lambda e, psA=psA, f_=f_: e.tensor_tensor(out=f_[:, 0:4], in0=psA[:, 256:260], in1=fb[:, 0:4], op=ALU.add))
                OP("act", [fB_], [fB_], lambda e, f_=f_: e.activation(out=f_[:, 4:8], in_=f_[:, 0:4], func=AF.Exp, scale=-1.0))
                OP("act", [fB_, c.Beps], [fB_], lambda e, f_=f_: e.activation(out=f_[:, 8:12], in_=f_[:, 4:8], func=AF.Ln, bias=c.epsc[:, 1:2]))

            def stB(s):
                ts_ = slice(s * 128, (s + 1) * 128)
                if "pool" not in parts:
                    return
                psP, pBP = next_ps()
                for kc in range(8):
                    OP("pe", [winB[1], xnB[kc]], [pBP], lambda e, kc=kc, psP=psP, ts_=ts_, xn=xn: e.matmul(psP[:, 0:256], lhsT=xn[:, kc, ts_], rhs=win[:, kc, 772:1028], start=(kc == 0), stop=(kc == 7)))
                OP("act", [pBP], [xptB[1 + s]], lambda e, psP=psP, s=s: e.activation(out=xpt[:, 1 + s, :], in_=psP[:, 0:256], func=AF.Copy))

            def stC(s):
                tt = q * 4 + s
                ts_ = slice(s * 128, (s + 1) * 128)
                if "pool" not in parts:
                    return
                for g in range(4):
                    pp, ppB = ppool[g // 2]
                    r0 = (g % 2) * 64
                    first = (tt == 0)
                    mi = g * 3 + (2 if first else 0)
                    OP("pe", [xptB[1 + s], pmB], [ppB], lambda e, pp=pp, r0=r0, g=g, s=s, mi=mi, ts_=ts_, first=first: e.matmul(pp[r0:r0 + 64, ts_], lhsT=xpt[:, 1 + s, g * 64:(g + 1) * 64], rhs=pmb[:, mi, :], start=True, stop=first, tile_position=(0, r0)))
                    if not first:
                        OP("pe", [xptB[s], pmB], [ppB], lambda e, pp=pp, r0=r0, g=g, s=s, ts_=ts_: e.matmul(pp[r0:r0 + 64, ts_], lhsT=xpt[:, s, g * 64:(g + 1) * 64], rhs=pmb[:, g * 3 + 1, :], start=False, stop=True, tile_position=(0, r0)))

            def stD(s):
                tt = q * 4 + s
                if "att" not in parts:
                    return
                f_, fB_ = ftk[tt % 2], ftkB[tt % 2]
                psc, pBc = next_ps()
                OP("pe", [fB_, c.Bc], [pBc], lambda e, psc=psc, f_=f_: e.matmul(psc[:, 0:4], lhsT=c.triu_f, rhs=f_[:, 8:12], start=True, stop=True))
                OP("pe", [fB_, c.Bc], [pBc], lambda e, psc=psc, f_=f_: e.matmul(psc[:, 8:12], lhsT=c.ones_f, rhs=f_[:, 8:12], start=True, stop=True))
                OP("dve", [pBc, totsB[tt]], [cabsB[tt]], lambda e, psc=psc, tt=tt: e.tensor_tensor(out=cabs[:, tt, :], in0=psc[:, 0:4], in1=tots[:, tt, :], op=ALU.add))
                OP("dve", [pBc, totsB[tt]], [totsB[tt + 1]], lambda e, psc=psc, tt=tt: e.tensor_tensor(out=tots[:, tt + 1, :], in0=psc[:, 8:12], in1=tots[:, tt, :], op=ALU.add))

            stA(0); stB(0); stA(1); stB(1); stC(0); stD(0); stA(2); stB(2); stC(1); stD(1); stA(3); stB(3); stC(2); stD(2); stC(3); stD(3)
            if "att" in parts:
                P.dma(lambda e, q=q, V_=V_: e.dma_start(out=c.vtok[q * TT:(q + 1) * TT, :].rearrange("(s p) c -> p s c", p=128), in_=V_[:]), reads=[VB_], writes=[VT[q]])
            if "pool" in parts:
                OP("pool", [xptB[4]], [xptB[0]], lambda e: e.tensor_copy(out=xpt[:, 0, :], in_=xpt[:, 4, :]))
                Y_, YB_ = yps[q % 2], ypsB[q % 2]
                for cc in range(2):
                    pp, ppB = ppool[cc]
                    OP("act", [ppB], [pldB[cc]], lambda e, pp=pp, cc=cc: e.activation(out=pld[:, cc, :], in_=pp[:, 0:TT], func=AF.Copy))
                    ps, pB = next_ps()
                    OP("pe", [pldB[cc], pwB], [pB], lambda e, ps=ps, cc=cc: e.matmul(ps[:, 0:TT], lhsT=pwb[:, cc, :], rhs=pld[:, cc, :], start=True, stop=True))
                    OP("dve", [pB, scB], [YB_], lambda e, ps=ps, cc=cc, Y_=Y_: e.tensor_scalar(out=Y_[:, cc, :], in0=ps[:, 0:TT], scalar1=scol[:, cc:cc + 1], scalar2=None, op0=ALU.mult))
                P.dma(lambda e, qs=qs, Y_=Y_: e.dma_start(out=ybrv[1][:, :, qs], in_=Y_[:]), reads=[YB_], writes=[c.YB[1][q]])
            if "ssm" in parts:
                for cc in range(2):
                    ps, pB = next_ps()
                    proj_fm(1028 + cc * 128, 128, ps, pB, slice(0, 128), xn, xnB)
                    OP("act", [pB], [uB[cc][q]], lambda e, ps=ps, cc=cc, q=q: e.activation(out=u_ssm[:, cc, 8 + q * TT:8 + (q + 1) * TT], in_=ps[:, 0:TT], func=AF.Copy))
            if "conv" in parts:
                Y_, YB_ = ycs[q % 2], ycsB[q % 2]
                for cc in range(2):
                    pcc, pccB = next_ps()
                    proj_fm(1540 + cc * 128, 128, pcc, pccB, slice(0, 128), xn, xnB)
                    pcx, pcxB = next_ps()
                    proj_fm(1796 + cc * 128, 128, pcx, pcxB, slice(0, 128), xn, xnB)
                    pcb, pcbB = next_ps()
                    proj_fm(1284 + cc * 128, 128, pcb, pcbB, slice(0, 128), xn, xnB)
                    a_, aB_ = ccs[cc], ccsB[cc]
                    b_, bB_ = cbs[cc], cbsB[cc]
                    z_, zB_ = zt[cc][q % 2], ztB[cc][q % 2]
                    zp_, zpB_ = zt[cc][(q + 1) % 2], ztB[cc][(q + 1) % 2]
                    y_, yB_ = y1[cc], y1B[cc]
                    OP("act", [pccB], [aB_], lambda e, pcc=pcc, a_=a_: e.activation(out=a_[:], in_=pcc[:, 0:TT], func=AF.Copy))
                    OP("act", [pcbB], [bB_], lambda e, pcb=pcb, b_=b_: e.activation(out=b_[:], in_=pcb[:, 0:TT], func=AF.Copy))
                    if q == 0:
                        OP("pool", [], [zB_], lambda e, z_=z_: e.memset(z_[:, 0:2], 0.0))
                    else:
                        OP("pool", [zpB_], [zB_], lambda e, z_=z_, zp_=zp_: e.tensor_copy(out=z_[:, 0:2], in_=zp_[:, TT:TT + 2]))
                    OP("dve", [pcxB, aB_], [zB_], lambda e, pcx=pcx, a_=a_, z_=z_: e.tensor_tensor(out=z_[:, 2:TT + 2], in0=pcx[:, 0:TT], in1=a_[:], op=ALU.mult))
                    OP("dve", [zB_, scB], [yB_], lambda e, z_=z_, y_=y_, cc=cc: e.tensor_scalar(out=y_[:], in0=z_[:, 0:TT], scalar1=scol[:, 2 + cc:3 + cc], scalar2=None, op0=ALU.mult))
                    OP("dve", [zB_, scB, yB_], [yB_], lambda e, z_=z_, y_=y_, cc=cc: e.scalar_tensor_tensor(out=y_[:], in0=z_[:, 1:TT + 1], scalar=scol[:, 4 + cc:5 + cc], in1=y_[:], op0=ALU.mult, op1=ALU.add))
                    OP("dve", [zB_, scB, yB_], [yB_], lambda e, z_=z_, y_=y_, cc=cc: e.scalar_tensor_tensor(out=y_[:], in0=z_[:, 2:TT + 2], scalar=scol[:, 6 + cc:7 + cc], in1=y_[:], op0=ALU.mult, op1=ALU.add))
                    OP("pool", [yB_, bB_], [YB_], lambda e, y_=y_, b_=b_, Y_=Y_, cc=cc: e.tensor_tensor(out=Y_[:, cc, :], in0=y_[:], in1=b_[:], op=ALU.mult))
                P.dma(lambda e, qs=qs, Y_=Y_: e.dma_start(out=ybrv[3][:, :, qs], in_=Y_[:]), reads=[YB_], writes=[c.YB[3][q]])


def mixer_m3(c, l, k_aug, kB, kcB, cabs, cabsB, tots, totsB, QA, VT, ybrv, drain=lambda n: None):
    P, OP, sb, nc = c.P, c.OP, c.sb, c.nc
    psb, psB = c.psb, c.psB
    with contextlib.ExitStack() as st:
        V_aug = sb(st, "V_aug", [128, 32, 4, 2, 64], BF16)
        VB = [Buf() for _ in range(NQ)]
        VoB = Buf()
        OP("pool", [], [VoB], lambda e: e.memset(V_aug[:, :, :, 1, :], 1.0))
        qt = [sb(st, "m3_qt%d" % i, [128, 4, TT], BF16) for i in range(2)]
        qtB = [Buf() for _ in range(2)]
        Pt = [sb(st, "m3_P%d" % i, [128, TT], BF16) for i in range(3)]
        PtB = [Buf() for _ in range(3)]
        bq = [sb(st, "m3_bq%d" % i, [128, 32, 4], F32) for i in range(2)]
        bqB = [Buf() for _ in range(2)]
        rd = [sb(st, "m3_rd%d" % i, [64, TT], F32) for i in range(2)]
        rdB = [Buf() for _ in range(2)]
        yst = [sb(st, "m3_y%d" % i, [64, 4, TT], BF16) for i in range(2)]
        ystB = [Buf() for _ in range(2)]
        yav = c.ybr[0].rearrange("(h p) t -> p h t", p=64)
        kP = 0
        kS = 0
        kO = 0
        for q in range(c.nq):
            Q_, QB_ = qt[q % 2], qtB[q % 2]
            P.dma(lambda e, q=q, Q_=Q_: e.dma_start(out=Q_[:], in_=c.qaug[:, :, q * TT:(q + 1) * TT]), reads=[QA[q]], writes=[QB_])
            for i4 in range(4):
                i = q * 4 + i4
                P.dma(lambda e, i=i: e.dma_start(out=V_aug[:, i, :, 0, :], in_=c.vtok[i * 128:(i + 1) * 128, :].rearrange("p (h d) -> p h d", h=4)), reads=[VT[q]], writes=[VB[q]], shared=True)
            n = 4 * q + 4
            b_, bB_ = bq[q % 2], bqB[q % 2]
            OP("dve", cabsB[0:n] + [totsB[4 * q]], [bB_], lambda e, b_=b_, n=n, q=q: e.tensor_tensor(out=b_[:, 0:n, :], in0=cabs[:, 0:n, :], in1=tots[:, 4 * q, :].unsqueeze(1).to_broadcast([128, n, 4]), op=ALU.subtract))
            Y_, YB_ = yst[q % 2], ystB[q % 2]
            for h in range(4):
                po, poB = psb[4 + kO % 2], psB[4 + kO % 2]
                kO += 1
                def emit_S(i):
                    d = i - 4 * q
                    c0 = max(0, d) * 128
                    ps, pB = psb[i % 4], psB[i % 4]
                    OP("pe", [kB[h][i // 4], kcB, QB_], [pB], lambda e, ps=ps, i=i, c0=c0, d=d, h=h, Q_=Q_: e.matmul(ps[:, c0:TT], lhsT=k_aug[:, h, i * 128:(i + 1) * 128], rhs=Q_[:, h, c0:TT], start=True, stop=(d < 0)))
                    if d >= 0:
                        OP("pe", [c.Bcb], [pB], lambda e, ps=ps, c0=c0: e.matmul(ps[:, c0:c0 + 128], lhsT=c.ident_b, rhs=c.mneg_b, start=False, stop=True))
                    return ps, pB, c0

                pend = [emit_S(0)]
                if n > 1:
                    pend.append(emit_S(1))
                for i in range(n):
                    ps, pB, c0 = pend.pop(0)
                    if i + 2 < n:
                        pend.append(emit_S(i + 2))
                    p_, pB_ = Pt[kP % 3], PtB[kP % 3]
                    kP += 1
                    OP("act", [pB, bB_], [pB_], lambda e, ps=ps, p_=p_, c0=c0, i=i, b_=b_, h=h: e.activation(out=p_[:, c0:TT], in_=ps[:, c0:TT], func=AF.Exp, bias=b_[:, i, h:h + 1]))
                    OP("pe", [VB[i // 4], VoB, pB_], [poB], lambda e, p_=p_, c0=c0, i=i, po=po, h=h, n=n: e.matmul(po[:, c0:TT], lhsT=V_aug[:, i, h, :, :].rearrange("p a b -> p (a b)"), rhs=p_[:, c0:TT], start=(i == 0), stop=(i == n - 1)))
                r_, rB_ = rd[h % 2], rdB[h % 2]
                OP("dve", [poB], [rB_], lambda e, po=po, r_=r_: e.reciprocal(out=r_[0:64, :], in_=po[64:128, 0:TT]))
                OP("dve", [poB, rB_], [YB_], lambda e, po=po, r_=r_, h=h, Y_=Y_: e.tensor_tensor(out=Y_[0:64, h, :], in0=po[0:64, 0:TT], in1=r_[0:64, :], op=ALU.mult))
                drain(12)
            P.dma(lambda e, q=q, Y_=Y_: e.dma_start(out=yav[:, :, q * TT:(q + 1) * TT], in_=Y_[:]), reads=[YB_], writes=[c.YB[0][q]])


def ssm_param_load(c, l, stack):
    sb = c.sb
    dq = []

    class _P:
        @staticmethod
        def dma(fn, reads=(), writes=(), **kw):
            dq.append((fn, list(reads), list(writes), kw))
    P = _P

    def mk_(name, shape, dt=F32):
        return sb(stack, "spl_" + name, shape, dt), Buf()
    lamS, lamSB = mk_("lamS", [128, 2, 8])
    for ri, src in enumerate((c.lam_re, c.lam_im)):
        P.dma(lambda e, ri=ri, src=src: e.dma_start(out=lamS[:, ri, :], in_=src[l].rearrange("(j g) p -> (g p) j", g=2), allow_slow_non_contiguous=True), writes=[lamSB], shared=True)
    ldtS, ldtSB = mk_("ldtS", [128, 8])
    for g in range(2):
        P.dma(lambda e, g=g: e.dma_start(out=ldtS[g * 64:(g + 1) * 64, :], in_=c.log_dt[l].rearrange("(j g) -> g j", g=2)[g:g + 1, :].partition_broadcast(64), allow_slow_non_contiguous=True), writes=[ldtSB], shared=True)
    BS, BSB = mk_("BS", [128, 2, 8, 16])
    for ri, src in enumerate((c.b_re, c.b_im)):
        P.dma(lambda e, ri=ri, src=src: e.dma_start(out=BS[:, ri, :, :], in_=src[l].rearrange("(j g) p h -> (g p) j h", g=2)), writes=[BSB], shared=True)
    Csrc, CsB = mk_("Csrc", [128, 2, 128])
    for ri, src in enumerate((c.c_re, c.c_im)):
        for j in range(8):
            P.dma(lambda e, ri=ri, src=src, j=j: e.dma_start(out=Csrc[16 * j:16 * j + 16, ri, :].rearrange("h (g p) -> h g p", g=2), in_=src[l, 2 * j:2 * j + 2].rearrange("g h p -> h g p")), writes=[CsB], shared=True)
    dcol, dcB = mk_("dcol", [128, 2])
    P.dma(lambda e: e.dma_start(out=dcol[:], in_=c.ssm_d[l].rearrange("(c p) -> p c", p=128), allow_slow_non_contiguous=True), writes=[dcB], shared=True)
    Bsrc, BsB = mk_("Bsrc", [128, 2, 2, 4, 2, 16])
    for ri, src in enumerate((c.b_re, c.b_im)):
        for hf in range(2):
            for g2p in range(2):
                P.dma(lambda e, ri=ri, src=src, hf=hf, g2p=g2p: e.dma_start(out=Bsrc[:, ri, hf, :, g2p, :], in_=src[l, 8 * hf:8 * hf + 8].rearrange("(j g) p h -> (g p) j h", g=2)), writes=[BsB], shared=True)
    return SimpleNamespace(lamS=lamS, lamSB=lamSB, ldtS=ldtS, ldtSB=ldtSB, BS=BS, BSB=BSB, Csrc=Csrc, CsB=CsB, dcol=dcol, dcB=dcB,
                           Bsrc=Bsrc, BsB=BsB, dq=dq)


def ssm_s_alloc(c, stack):
    sb = c.sb
    names = ["lr", "dt", "th", "sn", "cs", "lg", "mag", "ar", "ai", "t1", "t2", "t3", "den", "am1", "cr", "ci", "p0r", "p0i"]
    names += ["p%d%s" % (n, x) for n in range(2, 9) for x in "ri"]
    tl = {nm: sb(stack, "ppS_%s" % nm, [128, 8], F32)[:] for nm in names}
    hp = sb(stack, "ssc_halfpi", [128, 1], F32)
    LV = sb(stack, "ssm_LV", [128, 3, 9, 8], F32)
    return SimpleNamespace(hp=hp, LV=LV, tl=tl)


def ssm_s_chain(c, l, pre, spar):
    P, sb = c.P, c.sb
    dq = []
    OP = lambda eng, rd, wr, fn: dq.append((eng, list(rd), list(wr), fn))
    mul, add, sub = ALU.mult, ALU.add, ALU.subtract
    lamS, lamSB, ldtS, ldtSB = spar.lamS, spar.lamSB, spar.ldtS, spar.ldtSB

    def TT_(eng, out, a, b, op, rd, wr):
        OP(eng, rd, wr, lambda e: e.tensor_tensor(out=out, in0=a, in1=b, op=op))
    hp, LV, pre_tl = pre.hp, pre.LV, pre.tl
    hpB = Buf()
    OP("pool", [], [hpB], lambda e: e.memset(hp[:], float(np.pi / 2)))
    LVB = Buf()
    def cpow_prep(tag, F, lr_in, li, ldt_ap, deps, eng):
        tl = pre_tl

        def t_(nm):
            return tl[nm]
        B_ = Buf()
        D = list(deps) + [B_]
        OP(eng, D, [B_], lambda e: e.tensor_scalar(out=t_("lr"), in0=lr_in, scalar1=-1e-4, scalar2=None, op0=ALU.min))
        OP(eng, D, [B_], lambda e: e.tensor_copy(out=t_("dt"), in_=ldt_ap))
        OP("act", [B_], [B_], lambda e: e.activation(out=t_("dt"), in_=t_("dt"), func=AF.Exp))
        TT_(eng, t_("th"), li, t_("dt"), mul, D, [B_])
        OP("act", [B_], [B_], lambda e: e.activation(out=t_("sn"), in_=t_("th"), func=AF.Sin, scale=1.0 / 32))
        OP("act", [B_, hpB], [B_], lambda e: e.activation(out=t_("cs"), in_=t_("th"), func=AF.Sin, scale=1.0 / 32, bias=hp[:, 0:1]))
        TT_(eng, t_("lg"), t_("lr"), t_("dt"), mul, [B_], [B_])
        OP("act", [B_], [B_], lambda e: e.activation(out=t_("mag"), in_=t_("lg"), func=AF.Exp, scale=1.0 / 32))
        TT_(eng, t_("ar"), t_("mag"), t_("cs"), mul, [B_], [B_])
        TT_(eng, t_("ai"), t_("mag"), t_("sn"), mul, [B_], [B_])
        for _ in range(5):
            TT_(eng, t_("t1"), t_("ar"), t_("ar"), mul, [B_], [B_])
            TT_(eng, t_("t2"), t_("ai"), t_("ai"), mul, [B_], [B_])
            TT_(eng, t_("t3"), t_("ar"), t_("ai"), mul, [B_], [B_])
            TT_(eng, t_("ar"), t_("t1"), t_("t2"), sub, [B_], [B_])
            OP(eng, [B_], [B_], lambda e: e.tensor_scalar(out=t_("ai"), in0=t_("t3"), scalar1=2.0, scalar2=None, op0=mul))
        TT_(eng, t_("t1"), t_("lr"), t_("lr"), mul, [B_], [B_])
        TT_(eng, t_("t2"), li, li, mul, D, [B_])
        TT_(eng, t_("den"), t_("t1"), t_("t2"), add, [B_], [B_])
        OP("dve", [B_], [B_], lambda e: e.reciprocal(out=t_("den"), in_=t_("den")))
        OP(eng, [B_], [B_], lambda e: e.tensor_scalar(out=t_("am1"), in0=t_("ar"), scalar1=-1.0, scalar2=None, op0=add))
        TT_(eng, t_("t1"), t_("am1"), t_("lr"), mul, [B_], [B_])
        TT_(eng, t_("t2"), t_("ai"), li, mul, D, [B_])
        TT_(eng, t_("t3"), t_("t1"), t_("t2"), add, [B_], [B_])
        TT_(eng, t_("cr"), t_("t3"), t_("den"), mul, [B_], [B_])
        TT_(eng, t_("t1"), t_("ai"), t_("lr"), mul, [B_], [B_])
        TT_(eng, t_("t2"), t_("am1"), li, mul, D, [B_])
        TT_(eng, t_("t3"), t_("t1"), t_("t2"), sub, [B_], [B_])
        TT_(eng, t_("ci"), t_("t3"), t_("den"), mul, [B_], [B_])
        pw = []
        OP(eng, [B_], [B_], lambda e: e.memset(t_("p0r"), 1.0))
        OP(eng, [B_], [B_], lambda e: e.memset(t_("p0i"), 0.0))
        pw.append((t_("p0r"), t_("p0i")))
        pw.append((t_("ar"), t_("ai")))
        for n in range(2, 9):
            pr, pi = pw[-1]
            nr, ni = t_("p%dr" % n), t_("p%di" % n)
            cmul(nr, ni, pr, pi, t_("ar"), t_("ai"), t_("t1"), t_("t2"), [B_], [B_], eng)
            pw.append((nr, ni))
        return pw, (t_("cr"), t_("ci")), B_, t_

    def cmul(o_r, o_i, a_r, a_i, b_r, b_i, t1, t2, rd, wr, eng="dve"):
        TT_(eng, t1, a_r, b_r, mul, rd, wr)
        TT_(eng, t2, a_i, b_i, mul, rd, wr)
        TT_(eng, o_r, t1, t2, sub, rd, wr)
        TT_(eng, t1, a_r, b_i, mul, rd, wr)
        TT_(eng, t2, a_i, b_r, mul, rd, wr)
        TT_(eng, o_i, t1, t2, add, rd, wr)

    pwS, (crS, ciS), BS_, tS = cpow_prep("S", 8, lamS[:, 0, :], lamS[:, 1, :], ldtS[:], [lamSB, ldtSB], "dve")
    OP("dve", [BS_], [LVB], lambda e: e.tensor_copy(out=LV[:, 0, 0, :], in_=pwS[8][0]))
    OP("dve", [BS_], [LVB], lambda e: e.tensor_copy(out=LV[:, 1, 0, :], in_=pwS[8][1]))
    for k in range(1, 9):
        TT_("dve", tS("t1"), LV[:, 0, k - 1, :], LV[:, 0, k - 1, :], mul, [LVB, BS_], [BS_])
        TT_("dve", tS("t2"), LV[:, 1, k - 1, :], LV[:, 1, k - 1, :], mul, [LVB, BS_], [BS_])
        TT_("dve", tS("t3"), LV[:, 0, k - 1, :], LV[:, 1, k - 1, :], mul, [LVB, BS_], [BS_])
        TT_("dve", LV[:, 0, k, :], tS("t1"), tS("t2"), sub, [BS_], [LVB])
        OP("dve", [BS_], [LVB], lambda e, k=k: e.tensor_scalar(out=LV[:, 1, k, :], in0=tS("t3"), scalar1=2.0, scalar2=None, op0=mul))
    OP("dve", [LVB], [LVB], lambda e: e.tensor_scalar(out=LV[:, 2, :, :], in0=LV[:, 1, :, :], scalar1=-1.0, scalar2=None, op0=mul))
    return SimpleNamespace(pwS=pwS, crS=crS, ciS=ciS, BS_=BS_, tS=tS, LV=LV, LVB=LVB, cmul=cmul, dq=dq)


def mixer_m2(c, l, u_ssm, uB, ybrv, spar, sch):
    P, OP, sb, nc, next_ps = c.P, c.OP, c.sb, c.nc, c.next_ps
    mul, add, sub = ALU.mult, ALU.add, ALU.subtract
    NCH = T // 8

    def TT_(eng, out, a, b, op, rd, wr):
        OP(eng, rd, wr, lambda e: e.tensor_tensor(out=out, in0=a, in1=b, op=op))

    with contextlib.ExitStack() as st:
        WZ = sb(st, "ssm_WZ", [128, 8, 8, 2, 128], BF16)
        CA = sb(st, "ssm_CA", [128, 8, 9, 2, 128], BF16)
        BD = sb(st, "ssm_BD", [128, 2, 8, 128], BF16)
        LV, LVB = sch.LV, sch.LVB
        WZB, CAB, BDB = [Buf(), Buf()], Buf(), Buf()
        with contextlib.nullcontext():
            sp = st

            def mk_(name, shape, dt=F32):
                return sb(sp, "sp_" + name, shape, dt), Buf()
            mk, mkB = mk_("mk", [128, 8, 128])
            P.dma(lambda e: e.dma_start(out=mk[:], in_=c.consts[:, C_MH:C_MH + 1024].rearrange("p (a b) -> p a b", a=8)), writes=[mkB])
            msall, msB = mk_("msall", [128, 8, 128])
            for j in range(8):
                OP("pool", [mkB], [msB], lambda e, j=j: e.tensor_copy(out=msall[:, j, :], in_=mk[:, 4 + j % 4, :]))
            lamS, lamSB, ldtS, ldtSB, BS, BSB, Csrc, CsB = spar.lamS, spar.lamSB, spar.ldtS, spar.ldtSB, spar.BS, spar.BSB, spar.Csrc, spar.CsB
            dcol, dcB, Bsrc, BsB = spar.dcol, spar.dcB, spar.Bsrc, spar.BsB
            CT, CTB = mk_("CT", [128, 2, 128])
            ps, pB = next_ps()
            for ri in range(2):
                OP("pe", [CsB, c.Bc], [pB], lambda e, ps=ps, ri=ri: e.transpose(out=ps[:, ri * 128:(ri + 1) * 128], in_=Csrc[:, ri, :], identity=c.ident_f))
            OP("act", [pB], [CTB], lambda e, ps=ps: e.activation(out=CT[:], in_=ps[:, 0:256].rearrange("p (a b) -> p a b", a=2), func=AF.Copy))

            pwS, crS, ciS, BS_, tS = sch.pwS, sch.crS, sch.ciS, sch.BS_, sch.tS

            def cmul(o_r, o_i, a_r, a_i, b_r, b_i, t1, t2, rd, wr, eng="dve"):
                TT_(eng, t1, a_r, b_r, mul, rd, wr)
                TT_(eng, t2, a_i, b_i, mul, rd, wr)
                TT_(eng, o_r, t1, t2, sub, rd, wr)
                TT_(eng, t1, a_r, b_i, mul, rd, wr)
                TT_(eng, t2, a_i, b_r, mul, rd, wr)
                TT_(eng, o_i, t1, t2, add, rd, wr)
            Gt = [mk_("Gt%d" % i, [128, 2, 8]) for i in range(2)]
            Ws = [mk_("Ws%d" % i, [128, 2, 2, 128]) for i in range(2)]
            wt1, wt1B = mk_("wt1", [128, 2, 128])
            wt2, wt2B = mk_("wt2", [128, 2, 128])
            v4 = lambda ap: ap.rearrange("p a (b c) -> p a b c", b=4)
            Brv = Bsrc[:, 0].rearrange("p a b c d -> p a b (c d)")
            Biv = Bsrc[:, 1].rearrange("p a b c d -> p a b (c d)")
            gb = lambda ap: ap.rearrange("p (a b) -> p a b", a=2).unsqueeze(3).to_broadcast([128, 2, 4, 32])
            mh = mk[:, 0:4, :]
            for tau in range(8):
                G_, GB_ = Gt[tau % 2]
                if tau == 0:
                    OP("dve", [BS_, GB_], [GB_], lambda e, G_=G_: e.tensor_copy(out=G_[:, 0, :], in_=crS))
                    OP("dve", [BS_, GB_], [GB_], lambda e, G_=G_: e.tensor_copy(out=G_[:, 1, :], in_=ciS))
                else:
                    Gp, GpB = Gt[(tau - 1) % 2]
                    cmul(G_[:, 0, :], G_[:, 1, :], Gp[:, 0, :], Gp[:, 1, :], pwS[1][0], pwS[1][1], tS("t1"), tS("t2"), [BS_, GpB, GB_], [BS_, GB_])
                W_, WB_ = Ws[tau % 2]
                TT_("dve", v4(wt1[:]), Brv, gb(G_[:, 0, :]), mul, [BsB, GB_, wt1B], [wt1B])
                TT_("dve", v4(wt2[:]), Biv, gb(G_[:, 1, :]), mul, [BsB, GB_, wt2B], [wt2B])
                TT_("dve", W_[:, 0, :, :], wt1[:], wt2[:], sub, [wt1B, wt2B, WB_], [WB_])
                TT_("dve", v4(wt1[:]), Biv, gb(G_[:, 0, :]), mul, [BsB, GB_, wt1B], [wt1B])
                TT_("dve", v4(wt2[:]), Brv, gb(G_[:, 1, :]), mul, [BsB, GB_, wt2B], [wt2B])
                TT_("dve", W_[:, 1, :, :], wt1[:], wt2[:], add, [wt1B, wt2B, WB_], [WB_])
                ps, pB = next_ps()
                for ri in range(2):
                    for hf in range(2):
                        k4 = ri * 2 + hf
                        OP("pe", [WB_, c.Bc], [pB], lambda e, ps=ps, ri=ri, hf=hf, k4=k4, W_=W_: e.transpose(out=ps[:, k4 * 128:(k4 + 1) * 128], in_=W_[:, ri, hf, :], identity=c.ident_f))
                for ri in range(2):
                    for hf in range(2):
                        k4 = ri * 2 + hf
                        OP("dve", [pB, mkB, WZB[hf]], [WZB[hf]], lambda e, ps=ps, ri=ri, hf=hf, k4=k4, tau=tau: e.tensor_tensor(out=WZ[:, 4 * hf:4 * hf + 4, tau, ri, :], in0=ps[:, k4 * 128:(k4 + 1) * 128].unsqueeze(1).to_broadcast([128, 4, 128]), in1=mh, op=mul))
            cw1, cw1B = mk_("cw1", [128, 8, 16])
            cw2, cw2B = mk_("cw2", [128, 8, 16])
            cw3, cw3B = mk_("cw3", [128, 8, 16])
            CTr = CT[:, 0, :].rearrange("p (j h) -> p j h", j=8)
            CTi = CT[:, 1, :].rearrange("p (j h) -> p j h", j=8)
            bc16 = lambda ap: ap.unsqueeze(2).to_broadcast([128, 8, 16])
            msv = msall[:].rearrange("p j (a h) -> p j a h", a=8)
            msneg, msnB = mk_("msneg", [128, 8, 128])
            OP("pool", [msB], [msB], lambda e: e.tensor_scalar(out=msneg[:], in0=msall[:], scalar1=-1.0, scalar2=None, op0=mul))
            msvn = msneg[:].rearrange("p j (a h) -> p j a h", a=8)
            for n in range(9):
                pr, pi = pwS[n]
                TT_("pool", cw1[:], CTr, bc16(pr), mul, [CTB, BS_, cw1B], [cw1B])
                TT_("pool", cw2[:], CTi, bc16(pi), mul, [CTB, BS_, cw2B], [cw2B])
                TT_("pool", cw3[:], cw1[:], cw2[:], sub, [cw1B, cw2B, cw3B], [cw3B])
                OP("pool", [cw3B, msB, CAB], [CAB], lambda e, n=n: e.tensor_tensor(out=CA[:, :, n, 0, :].rearrange("p j (a h) -> p j a h", a=8), in0=cw3[:].unsqueeze(2).to_broadcast([128, 8, 8, 16]), in1=msv, op=mul))
                TT_("pool", cw1[:], CTr, bc16(pi), mul, [CTB, BS_, cw1B], [cw1B])
                TT_("pool", cw2[:], CTi, bc16(pr), mul, [CTB, BS_, cw2B], [cw2B])
                TT_("pool", cw3[:], cw1[:], cw2[:], add, [cw1B, cw2B, cw3B], [cw3B])
                OP("pool", [cw3B, msB, CAB], [CAB], lambda e, n=n: e.tensor_tensor(out=CA[:, :, n, 1, :].rearrange("p j (a h) -> p j a h", a=8), in0=cw3[:].unsqueeze(2).to_broadcast([128, 8, 8, 16]), in1=msvn, op=mul))
            cB, cBB = mk_("cB", [128, 8, 2, 32], BF16)
            m2 = mk[:, 4, 0:32].rearrange("p (a h) -> p a h", a=2).unsqueeze(1).to_broadcast([128, 8, 2, 16])
            BSr, BSi = BS[:, 0, :, :], BS[:, 1, :, :]
            TT_("pool", cw1[:], BSr, bc16(crS), mul, [BSB, BS_, cw1B], [cw1B])
            TT_("pool", cw2[:], BSi, bc16(ciS), mul, [BSB, BS_, cw2B], [cw2B])
            TT_("pool", cw3[:], cw1[:], cw2[:], sub, [cw1B, cw2B, cw3B], [cw3B])
            OP("pool", [cw3B, mkB], [cBB], lambda e: e.tensor_tensor(out=cB[:, :, 0, :].rearrange("p j (a h) -> p j a h", a=2), in0=cw3[:].unsqueeze(2).to_broadcast([128, 8, 2, 16]), in1=m2, op=mul))
            TT_("pool", cw1[:], BSr, bc16(ciS), mul, [BSB, BS_, cw1B], [cw1B])
            TT_("pool", cw2[:], BSi, bc16(crS), mul, [BSB, BS_, cw2B], [cw2B])
            TT_("pool", cw3[:], cw1[:], cw2[:], add, [cw1B, cw2B, cw3B], [cw3B])
            OP("pool", [cw3B, mkB], [cBB], lambda e: e.tensor_tensor(out=cB[:, :, 1, :].rearrange("p j (a h) -> p j a h", a=2), in0=cw3[:].unsqueeze(2).to_broadcast([128, 8, 2, 16]), in1=m2, op=mul))
        with contextlib.nullcontext():
            sr = st
            Xb = [[[sb(sr, "X%d%d%d" % (s_, ri, ab), [128, 256 + NCH], F32) for ab in range(2)] for ri in range(2)] for s_ in range(2)]
            XB = [[[Buf() for ab in range(2)] for ri in range(2)] for s_ in range(2)]
            for s_ in range(2):
                for ri in range(2):
                    for ab in range(2):
                        OP("pool", [], [XB[s_][ri][ab]], lambda e, t=Xb[s_][ri][ab]: e.memset(t[:, 0:256], 0.0))
            Xp = sb(sr, "Xp", [128, 8, 2, NCH], BF16)
            XpB = [Buf() for _ in range(8)]
            ysT = sb(sr, "ysT", [128, 2, T], BF16)
            ysB = [Buf() for _ in range(2)]
            wgl = sb(sr, "wgl", [128, 2, 512], BF16)
            wglB = [Buf()]
            c.wload(wgl[:], c.w_glu[l].rearrange("(kc p) n -> p kc n", p=128), wglB)
            sg = [sb(sr, "s_sg%d" % i, [128, TT], F32) for i in range(2)]
            sgB = [Buf() for _ in range(2)]
            yst = [sb(sr, "s_yst%d" % i, [128, 2, TT], BF16) for i in range(2)]
            ystB = [Buf() for _ in range(2)]
            for j in range(8):
                hf = j // 4
                s_ = j % 2
                for ri in range(2):
                    ps, pB = next_ps()
                    for tau in range(8):
                        OP("pe", [WZB[hf]] + uB[hf], [pB], lambda e, ps=ps, j=j, tau=tau, ri=ri, hf=hf: e.matmul(ps[:, 0:NCH], lhsT=WZ[:, j, tau, ri, :], rhs=u_ssm[:, hf, 15 - tau:15 - tau + (NCH - 1) * 8 + 1:8], start=(tau == 0), stop=(tau == 7)))
                    OP("act", [pB], [XB[s_][ri][0]], lambda e, ps=ps, t=Xb[s_][ri][0]: e.activation(out=t[:, 256:256 + NCH], in_=ps[:, 0:NCH], func=AF.Copy))
                cur = 0
                for k in range(9):
                    sh = 1 << k
                    sr_, si_ = Xb[s_][0][cur], Xb[s_][1][cur]
                    dr_, di_ = Xb[s_][0][1 - cur], Xb[s_][1][1 - cur]
                    sBr, sBi = XB[s_][0][cur], XB[s_][1][cur]
                    dBr, dBi = XB[s_][0][1 - cur], XB[s_][1][1 - cur]
                    lo, hi = 256 - sh, 256 + NCH - sh
                    OP("dve", [sBr, LVB], [dBr], lambda e, sr_=sr_, dr_=dr_, lo=lo, hi=hi, k=k, j=j: e.scalar_tensor_tensor(out=dr_[:, 256:256 + NCH], in0=sr_[:, lo:hi], scalar=LV[:, 0, k, j:j + 1], in1=sr_[:, 256:256 + NCH], op0=mul, op1=add))
                    OP("dve", [sBi, LVB, dBr], [dBr], lambda e, si_=si_, dr_=dr_, lo=lo, hi=hi, k=k, j=j: e.scalar_tensor_tensor(out=dr_[:, 256:256 + NCH], in0=si_[:, lo:hi], scalar=LV[:, 2, k, j:j + 1], in1=dr_[:, 256:256 + NCH], op0=mul, op1=add))
                    OP("dve", [sBr, sBi, LVB], [dBi], lambda e, sr_=sr_, si_=si_, di_=di_, lo=lo, hi=hi, k=k, j=j: e.scalar_tensor_tensor(out=di_[:, 256:256 + NCH], in0=sr_[:, lo:hi], scalar=LV[:, 1, k, j:j + 1], in1=si_[:, 256:256 + NCH], op0=mul, op1=add))
                    OP("dve", [sBi, LVB, dBi], [dBi], lambda e, si_=si_, di_=di_, lo=lo, hi=hi, k=k, j=j: e.scalar_tensor_tensor(out=di_[:, 256:256 + NCH], in0=si_[:, lo:hi], scalar=LV[:, 0, k, j:j + 1], in1=di_[:, 256:256 + NCH], op0=mul, op1=add))
                    cur = 1 - cur
                for ri in range(2):
                    OP("act", [XB[s_][ri][cur]], [XpB[j]], lambda e, t=Xb[s_][ri][cur], j=j, ri=ri: e.activation(out=Xp[:, j, ri, :], in_=t[:, 255:255 + NCH], func=AF.Copy))
            for hf in range(2):
                for tau in range(8):
                    ps, pB = next_ps()
                    for j4 in range(4):
                        j = 4 * hf + j4
                        OP("pe", [cBB, CAB], [pB], lambda e, ps=ps, j=j, j4=j4, tau=tau: e.matmul(ps[32 * j4:32 * j4 + 32, 0:128], lhsT=cB[:, j, 0, :], rhs=CA[:, j, tau, 0, :], start=True, stop=False, tile_position=(0, 32 * j4)))
                        OP("pe", [cBB, CAB], [pB], lambda e, ps=ps, j=j, j4=j4, tau=tau: e.matmul(ps[32 * j4:32 * j4 + 32, 0:128], lhsT=cB[:, j, 1, :], rhs=CA[:, j, tau, 1, :], start=False, stop=True, tile_position=(0, 32 * j4)))
                    if tau == 0:
                        OP("dve", [pB, dcB, c.Bc], [BDB], lambda e, ps=ps, hf=hf: e.scalar_tensor_tensor(out=BD[:, hf, 0, :], in0=c.ident_f, scalar=dcol[:, hf:hf + 1], in1=ps[:, 0:128], op0=mul, op1=add))
                    else:
                        OP("act", [pB], [BDB], lambda e, ps=ps, hf=hf, tau=tau: e.activation(out=BD[:, hf, tau, :], in_=ps[:, 0:128], func=AF.Copy))
            kk = 0
            for hf in range(2):
                for s in range(8):
                    ps, pB = next_ps()
                    mms = []
                    for j4 in range(4):
                        j = 4 * hf + j4
                        for ri in range(2):
                            mms.append(([CAB, XpB[j]], CA[:, j, s + 1, ri, :], Xp[:, j, ri, :]))
                    for tau in range(s + 1):
                        mms.append(([BDB] + uB[hf], BD[:, hf, tau, :], u_ssm[:, hf, 8 + s - tau:8 + s - tau + (NCH - 1) * 8 + 1:8]))
                    for i, (rd, lh, rh) in enumerate(mms):
                        OP("pe", rd, [pB], lambda e, ps=ps, lh=lh, rh=rh, i=i, n=len(mms): e.matmul(ps[:, 0:NCH], lhsT=lh, rhs=rh, start=(i == 0), stop=(i == n - 1)))
                    if kk % 2 == 0:
                        OP("act", [pB], [ysB[hf]], lambda e, ps=ps, hf=hf, s=s: e.activation(out=ysT[:, hf, s:T:8], in_=ps[:, 0:NCH], func=AF.Copy))
                    else:
                        OP("dve", [pB], [ysB[hf]], lambda e, ps=ps, hf=hf, s=s: e.tensor_copy(out=ysT[:, hf, s:T:8], in_=ps[:, 0:NCH]))
                    kk += 1
            for q in range(NQ):
                qs = slice(q * TT, (q + 1) * TT)
                Y_, YB_ = yst[q % 2], ystB[q % 2]
                for cc in range(2):
                    pv, pvB = next_ps()
                    pg, pgB = next_ps()
                    for kc in range(2):
                        OP("pe", [wglB[0], ysB[kc]], [pvB], lambda e, pv=pv, kc=kc, cc=cc, qs=qs: e.matmul(pv[:, 0:TT], lhsT=wgl[:, kc, cc * 128:(cc + 1) * 128], rhs=ysT[:, kc, qs], start=(kc == 0), stop=(kc == 1)))
                    for kc in range(2):
                        OP("pe", [wglB[0], ysB[kc]], [pgB], lambda e, pg=pg, kc=kc, cc=cc, qs=qs: e.matmul(pg[:, 0:TT], lhsT=wgl[:, kc, 256 + cc * 128:256 + (cc + 1) * 128], rhs=ysT[:, kc, qs], start=(kc == 0), stop=(kc == 1)))
                    s2, s2B = sg[cc], sgB[cc]
                    OP("act", [pgB], [s2B], lambda e, pg=pg, s2=s2: e.activation(out=s2[:], in_=pg[:, 0:TT], func=AF.Sigmoid))
                    OP("dve", [pvB, s2B], [YB_], lambda e, pv=pv, s2=s2, Y_=Y_, cc=cc: e.tensor_tensor(out=Y_[:, cc, :], in0=pv[:, 0:TT], in1=s2[:], op=mul))
                P.dma(lambda e, qs=qs, Y_=Y_: e.dma_start(out=ybrv[2][:, :, qs], in_=Y_[:]), reads=[YB_], writes=[c.YB[2][q]])


def mixer_m4(c, l, parts, ybrv):
    P, OP, sb, nc, next_ps = c.P, c.OP, c.sb, c.nc, c.next_ps
    order = [b for b, nm in enumerate(("att", "pool", "ssm", "conv")) if nm in parts]
    with contextlib.ExitStack() as st:
        wgt = sb(st, "m4_wg", [128, 8, 4096], BF16)
        wgtB = [[Buf(), Buf()] for _ in range(4)]
        wiv = c.w_in[l].rearrange("(kc p) n -> p kc n", p=128)
        wbr = sb(st, "m4_wbr", [128, 4, 2, D], BF16)
        wbrB = [Buf() for _ in range(4)]
        for b in order:
            c.wload(wgt[:, :, b * 1024:b * 1024 + 256], wiv[:, :, 2052 + b * 1024:2052 + b * 1024 + 256], [wgtB[b][0]])
            c.wload(wbr[:, b, :, :], c.w_branch[l, b].rearrange("(kc p) n -> p kc n", p=128), [wbrB[b]])
        for b in order:
            c.wload(wgt[:, :, b * 1024 + 256:(b + 1) * 1024], wiv[:, :, 2052 + b * 1024 + 256:2052 + (b + 1) * 1024], [wgtB[b][1]])
        wo = sb(st, "m4_wo", [128, 8, D], BF16)
        woB = [Buf()]
        c.wload(wo[:], c.w_out[l].rearrange("(kc p) n -> p kc n", p=128), woB)
        xn = [sb(st, "m4_xn%d" % i, [128, 8, TT], BF16) for i in range(2)]
        xnB = [Buf() for _ in range(2)]
        ysb = [sb(st, "m4_ys%d" % i, [128, 4, 2, TT], BF16) for i in range(2)]
        ysB = [[Buf() for _ in range(4)] for _ in range(2)]
        m = sb(st, "m4_m", [128, 8, TT], F32)
        mB = [Buf() for _ in range(8)]
        mb = sb(st, "m4_mb", [128, 8, TT], BF16)
        mbB = [Buf() for _ in range(8)]
        sg = [sb(st, "m4_sg%d" % i, [128, TT], F32) for i in range(2)]
        sgB = [Buf() for _ in range(2)]
        tm = [sb(st, "m4_tm%d" % i, [128, TT], F32) for i in range(2)]
        tmB = [Buf() for _ in range(2)]
        hr = [sb(st, "m4_hr%d" % i, [128, TT], F32) for i in range(3)]
        hrB = [Buf() for _ in range(3)]
        k = 0
        kk = 0
        def m4_load(q):
            qs = slice(q * TT, (q + 1) * TT)
            X, XB = xn[q % 2], xnB[q % 2]
            P.dma(lambda e, qs=qs, X=X: e.dma_start(out=X[:], in_=c.xnTv[:, :, qs]), reads=[c.XN[q]], writes=[XB])
            Ys, YsB = ysb[q % 2], ysB[q % 2]
            for b in order:
                P.dma(lambda e, qs=qs, b=b, Ys=Ys: e.dma_start(out=Ys[:, b, :, :], in_=ybrv[b][:, :, qs]), reads=[c.YB[b][q]], writes=[YsB[b]])

        m4_load(0)
        for q in range(c.nq):
            qs = slice(q * TT, (q + 1) * TT)
            X, XB = xn[q % 2], xnB[q % 2]
            Ys, YsB = ysb[q % 2], ysB[q % 2]
            if q + 1 < c.nq:
                m4_load(q + 1)
            for dc in range(8):
                ds = slice(dc * 128, (dc + 1) * 128)
                for bi, b in enumerate(order):
                    pg, pgB = next_ps()
                    for kc in range(8):
                        OP("pe", [wgtB[b][0 if dc < 2 else 1], XB], [pgB], lambda e, pg=pg, kc=kc, b=b, dc=dc, X=X: e.matmul(pg[:, 0:TT], lhsT=wgt[:, kc, b * 1024 + dc * 128:b * 1024 + (dc + 1) * 128], rhs=X[:, kc, :], start=(kc == 0), stop=(kc == 7)))
                    pp, ppB = next_ps()
                    for kc in range(2):
                        OP("pe", [wbrB[b], YsB[b]], [ppB], lambda e, pp=pp, kc=kc, b=b, ds=ds, Ys=Ys: e.matmul(pp[:, 0:TT], lhsT=wbr[:, b, kc, ds], rhs=Ys[:, b, kc, :], start=(kc == 0), stop=(kc == 1)))
                    s_, sB_ = sg[kk % 2], sgB[kk % 2]
                    OP("act", [pgB], [sB_], lambda e, s_=s_, pg=pg: e.activation(out=s_[:], in_=pg[:, 0:TT], func=AF.Sigmoid))
                    last = (bi == len(order) - 1)
                    if bi == 0 and last:
                        OP("dve", [sB_, ppB], [mbB[dc]], lambda e, s_=s_, pp=pp, dc=dc: e.tensor_tensor(out=mb[:, dc, :], in0=pp[:, 0:TT], in1=s_[:], op=ALU.mult))
                    elif bi == 0:
                        OP("dve", [sB_, ppB], [mB[dc]], lambda e, s_=s_, pp=pp, dc=dc: e.tensor_tensor(out=m[:, dc, :], in0=pp[:, 0:TT], in1=s_[:], op=ALU.mult))
                    else:
                        t_, tB_ = tm[kk % 2], tmB[kk % 2]
                        OP("dve", [sB_, ppB], [tB_], lambda e, s_=s_, pp=pp, t_=t_: e.tensor_tensor(out=t_[:], in0=pp[:, 0:TT], in1=s_[:], op=ALU.mult))
                        if last:
                            OP("pool", [tB_, mB[dc]], [mbB[dc]], lambda e, t_=t_, dc=dc: e.tensor_tensor(out=mb[:, dc, :], in0=m[:, dc, :], in1=t_[:], op=ALU.add))
                        else:
                            OP("pool", [tB_, mB[dc]], [mB[dc]], lambda e, t_=t_, dc=dc: e.tensor_tensor(out=m[:, dc, :], in0=m[:, dc, :], in1=t_[:], op=ALU.add))
                    kk += 1
            for d2 in range(8):
                po, poB = next_ps()
                for dc in range(8):
                    OP("pe", [woB[0], mbB[dc]], [poB], lambda e, po=po, dc=dc, d2=d2: e.matmul(po[:, 0:TT], lhsT=wo[:, dc, d2 * 128:(d2 + 1) * 128], rhs=mb[:, dc, :], start=(dc == 0), stop=(dc == 7)))
                r, rB = hr[k % 3], hrB[k % 3]
                k += 1
                P.dma(lambda e, r=r, d2=d2, qs=qs: e.dma_start(out=r[:], in_=c.hTv[:, d2, qs]), reads=[c.HT[q][d2]], writes=[rB])
                OP("dve", [poB, rB], [rB], lambda e, r=r, po=po: e.tensor_tensor(out=r[:], in0=po[:, 0:TT], in1=r[:], op=ALU.add))
                P.dma(lambda e, r=r, d2=d2, qs=qs: e.dma_start(out=c.hTv[:, d2, qs], in_=r[:]), reads=[rB], writes=[c.HT[q][d2]])


_NAMES = ["norm_g", "ffn_w_gate", "ffn_w_up", "ffn_w_down", "w_in", "f_bias", "pool_w", "pool_scale", "ssm_lam_re",
          "ssm_lam_im", "ssm_log_dt", "ssm_b_re", "ssm_b_im", "ssm_c_re", "ssm_c_im", "ssm_d", "ssm_w_glu", "conv_w",
          "w_branch", "w_out", "ple_w_gate", "ple_w_proj", "final_g"]


def run(inputs, n_cores=8, ret_all=False, **bk):
    nc = build(**bk)
    cst = make_consts()
    shared = {k: np.ascontiguousarray(np.asarray(inputs[k], dtype=np.float32)) for k in _NAMES}
    xs = np.asarray(inputs["x"], dtype=np.float32)
    ps = np.asarray(inputs["p"], dtype=np.float32)
    in_maps = []
    for b in range(n_cores):
        m = dict(shared)
        m["x"] = np.ascontiguousarray(xs[b])
        m["p"] = np.ascontiguousarray(ps[:, b])
        m["consts"] = cst
        in_maps.append(m)
    res = run_bass_kernel_spmd(nc, in_maps, core_ids=list(range(n_cores)))
    if ret_all:
        return res.results
    return np.stack([np.asarray(r["y"]) for r in res.results], axis=0)


def kernel(**inputs):
    return run(inputs, n_cores=8).astype(np.float32)
```

```python
import contextlib
import numpy as np
import concourse.bass as bass
import concourse.mybir as mybir
from concourse.bass_utils import run_bass_kernel_spmd

F32 = mybir.dt.float32
BF16 = mybir.dt.bfloat16
AF = mybir.ActivationFunctionType
ALU = mybir.AluOpType

D = 1024
T = 4096
DEPTH = 2
DFF = 2816
NFC = DFF // 128
INC = 6148
PLE = 256
TT = 512
NQ = T // TT
EPS = 1e-6

COMPUTE = ("pe", "act", "dve", "pool")
NDMASEM = 56
NSP = 40


class Buf:
    __slots__ = ("name", "w", "r", "wl")

    def __init__(self, name=""):
        self.name = name
        self.w = None
        self.r = []
        self.wl = []


class Node:
    __slots__ = ("eng", "idx", "fn", "waits", "key", "val", "needs_inc", "clock", "is_dma")


class Prog:
    def __init__(self, nc):
        self.nc = nc
        self.ops = {e: [] for e in ("pe", "act", "dve", "pool", "sp")}
        self.clock = {e: {} for e in self.ops}
        self.dma_rr = 0
        self.dma_rr2 = 0
        self.dma_last = [None] * NDMASEM
        self.dma_cum = [0] * NDMASEM
        self.out_nodes = []

    def _record(self, eng, fn, reads, writes, is_dma, extra=(), shared=False):
        n = Node()
        n.eng = eng
        n.idx = len(self.ops[eng])
        n.fn = fn
        n.is_dma = is_dma
        n.needs_inc = False
        deps = []
        for b in reads:
            if b.w is not None:
                deps.append(b.w)
            deps.extend(b.wl)
        for b in writes:
            if not shared:
                if b.w is not None:
                    deps.append(b.w)
                deps.extend(b.wl)
            deps.extend(b.r)
        deps.extend(extra)
        if is_dma:
            if eng == "sp":
                s = self.dma_rr
                self.dma_rr = (self.dma_rr + 1) % NSP
            else:
                s = NSP + self.dma_rr2
                self.dma_rr2 = (self.dma_rr2 + 1) % (NDMASEM - NSP)
            if self.dma_last[s] is not None:
                deps.append(self.dma_last[s])
            self.dma_cum[s] += 16
            n.key = ("d", s)
            n.val = self.dma_cum[s]
            self.dma_last[s] = n
        else:
            n.key = eng
            n.val = n.idx + 1
        ck = self.clock[eng]
        waits = {}
        for d in deps:
            if (not d.is_dma) and d.eng == eng and eng == "pe":
                continue
            if ck.get(d.key, 0) >= d.val:
                continue
            if waits.get(d.key, (0, None))[0] < d.val:
                waits[d.key] = (d.val, d)
        n.waits = [w[1] for w in waits.values()]
        if n.waits:
            ck = dict(ck)
            for d in n.waits:
                d.needs_inc = True
                for k, v in d.clock.items():
                    if ck.get(k, 0) < v:
                        ck[k] = v
                if ck.get(d.key, 0) < d.val:
                    ck[d.key] = d.val
            self.clock[eng] = ck
        n.clock = ck
        self.ops[eng].append(n)
        for b in reads:
            b.r.append(n)
        for b in writes:
            if shared:
                b.wl.append(n)
            else:
                b.w = n
                b.wl = []
                b.r = []
        return n

    def op(self, eng, fn, reads=(), writes=()):
        return self._record(eng, fn, reads, writes, False)

    def dma(self, fn, reads=(), writes=(), q="sp", is_out=False, shared=False):
        n = self._record(q, fn, reads, writes, True, shared=shared)
        if is_out:
            self.out_nodes.append(n)
        return n

    def barrier(self):
        last = [self.ops[e][-1] for e in COMPUTE if self.ops[e]]
        for e in COMPUTE:
            for n in reversed(self.ops[e]):
                if not n.is_dma:
                    last.append(n)
                    break
        last += [d for d in self.dma_last if d is not None]
        for e in ("pe", "act", "dve", "pool", "sp"):
            self._record(e, lambda eng: eng.nop(), (), (), False, extra=last)

    def emit(self, es):
        nc = self.nc
        sems = {}
        for e in COMPUTE:
            sems[e] = es.enter_context(nc.semaphore("S_" + e))
        for i in range(NDMASEM):
            sems[("d", i)] = es.enter_context(nc.semaphore("D%d" % i))
        for e in COMPUTE:
            c = 0
            for n in self.ops[e]:
                if n.is_dma:
                    continue
                if n.needs_inc:
                    c += 1
                    n.val = c
                else:
                    n.val = None
        block = es.enter_context(nc.Block())

        def run(ename):
            def body(eng):
                for n in self.ops[ename]:
                    for d in n.waits:
                        eng.wait_ge(sems[d.key], d.val)
                    ins = n.fn(eng)
                    if n.is_dma:
                        ins.then_inc(sems[n.key], 16)
                    elif n.needs_inc:
                        ins.then_inc(sems[n.key], 1)
                if ename == "sp":
                    for d in self.dma_last:
                        if d is not None:
                            eng.wait_ge(sems[d.key], d.val)
            return body

        block.tensor(run("pe"))
        block.scalar(run("act"))
        block.vector(run("dve"))
        block.gpsimd(run("pool"))
        block.sync(run("sp"))


C_ID = 0
C_ONES = 128
C_TRIU = 256
C_MNEG = 384
C_MH = 512
C_MS = 1024
C_RCW = 1536
C_RC0 = 1538
C_PM = C_RC0 + 1024
C_N = C_PM + 12 * 128
C_G = 512


def make_consts():
    c = np.zeros((128, C_N), np.float32)
    k = np.arange(128)
    c[:, C_ID:C_ID + 128] = np.eye(128, dtype=np.float32)
    c[:, C_ONES:C_ONES + 128] = 1.0
    c[:, C_TRIU:C_TRIU + 128] = (k[:, None] <= k[None, :]).astype(np.float32)
    c[:, C_MNEG:C_MNEG + 128] = np.where(k[None, :] < k[:, None], -30000.0, 0.0)
    rj4, rg2 = k // 32, (k // 16) % 2
    cg2 = k // 64
    for j4 in range(4):
        c[:, C_MH + j4 * 128:C_MH + (j4 + 1) * 128] = ((rj4[:, None] == j4) & (rg2[:, None] == cg2[None, :])).astype(np.float32)
        c[:, C_MS + j4 * 128:C_MS + (j4 + 1) * 128] = ((cg2[:, None] == rg2[None, :]) & (rj4[None, :] == j4)).astype(np.float32)
    wins = np.array([2, 4, 8, 16], np.float32)
    for ch in range(2):
        w = wins[2 * ch + (k // 64)]
        c[:, C_RCW + ch] = 1.0 / w
        t = np.arange(512, dtype=np.float32)
        c[:, C_RC0 + ch * 512:C_RC0 + (ch + 1) * 512] = 1.0 / np.minimum(t[None, :] + 1.0, w[:, None])
    tp = k[:, None].astype(np.float64)
    tq = k[None, :].astype(np.float64)
    for g in range(4):
        W = float(wins[g])
        main = np.where((tp <= tq) & (tp > tq - W), 1.0 / W, 0.0) - np.eye(128)
        corner = np.where((tp - 128 > tq - W), 1.0 / W, 0.0)
        cnt = np.minimum(tq + 1.0, W)
        main0 = np.where((tp <= tq) & (tp > tq - W), 1.0 / cnt, 0.0) - np.eye(128)
        for i, mtx in enumerate((main, corner, main0)):
            o = C_PM + (g * 3 + i) * 128
            c[:, o:o + 128] = mtx.astype(np.float32)
    return c


def build(phases=("in", "ffn", "mix", "ple", "out"), depth=DEPTH, mix_parts=("att", "pool", "ssm", "conv"), debug=False, nq=NQ):
    nc = bass.Bass("TRN2", target_bir_lowering=False)
    dt_in = lambda name, shape: nc.dram_tensor(name, list(shape), F32, kind="ExternalInput").ap()
    x = dt_in("x", [T, D])
    p_in = dt_in("p", [DEPTH, T, PLE])
    norm_g = dt_in("norm_g", [DEPTH, 4, D])
    w_gate = dt_in("ffn_w_gate", [DEPTH, 2, D, DFF])
    w_up = dt_in("ffn_w_up", [DEPTH, 2, D, DFF])
    w_down = dt_in("ffn_w_down", [DEPTH, 2, DFF, D])
    w_in = dt_in("w_in", [DEPTH, D, INC])
    f_bias = dt_in("f_bias", [DEPTH, 4])
    pool_w = dt_in("pool_w", [DEPTH, 4, 64, 64])
    pool_scale = dt_in("pool_scale", [DEPTH, 256])
    lam_re = dt_in("ssm_lam_re", [DEPTH, 16, 64])
    lam_im = dt_in("ssm_lam_im", [DEPTH, 16, 64])
    log_dt = dt_in("ssm_log_dt", [DEPTH, 16])
    b_re = dt_in("ssm_b_re", [DEPTH, 16, 64, 16])
    b_im = dt_in("ssm_b_im", [DEPTH, 16, 64, 16])
    c_re = dt_in("ssm_c_re", [DEPTH, 16, 16, 64])
    c_im = dt_in("ssm_c_im", [DEPTH, 16, 16, 64])
    ssm_d = dt_in("ssm_d", [DEPTH, 256])
    w_glu = dt_in("ssm_w_glu", [DEPTH, 256, 512])
    conv_w = dt_in("conv_w", [DEPTH, 3, 256])
    w_branch = dt_in("w_branch", [DEPTH, 4, 256, D])
    w_out = dt_in("w_out", [DEPTH, D, D])
    ple_wg = dt_in("ple_w_gate", [DEPTH, D, D])
    ple_wp = dt_in("ple_w_proj", [DEPTH, PLE, D])
    final_g = dt_in("final_g", [D])
    consts = dt_in("consts", [128, C_N])
    y_out = nc.dram_tensor("y", [T, D], F32, kind="ExternalOutput").ap()

    dk = dict(kind="ExternalOutput") if debug else {}
    hT = nc.dram_tensor("hT_scr", [D, T], F32, **dk).ap()
    xnT = nc.dram_tensor("xnT_scr", [D, T], BF16, **dk).ap()
    ybr = nc.dram_tensor("ybr_scr", [4, 256, T], BF16, **dk).ap()
    qaug = nc.dram_tensor("qaug_scr", [128, 4, T], BF16, **dk).ap()
    vtok = nc.dram_tensor("vtok_scr", [T, 256], BF16, **dk).ap()
    hTv = hT.rearrange("(c p) t -> p c t", p=128)
    xnTv = xnT.rearrange("(c p) t -> p c t", p=128)

    es = contextlib.ExitStack()
    P = Prog(nc)
    OP = lambda eng, reads, writes, fn: P.op(eng, fn, reads, writes)

    uid = [0]

    def sb(stack, name, shape, dt):
        uid[0] += 1
        return stack.enter_context(nc.sbuf_tensor("%s_u%d" % (name, uid[0]), list(shape), dt))

    def dbg_dump(name, ap, shape, dt, reads):
        if not debug:
            return
        t = nc.dram_tensor("dbg_" + name, list(shape), dt, kind="ExternalOutput").ap()
        P.dma(lambda e: e.dma_start(out=t, in_=ap), reads=reads, writes=[Buf()])

    HT = [[Buf("hT%d_%d" % (q, c)) for c in range(8)] for q in range(NQ)]
    XN = [Buf("xnT%d" % q) for q in range(NQ)]
    YB = [[Buf("ybr%d_%d" % (b, q)) for q in range(NQ)] for b in range(4)]
    OUTB = Buf("out")

    psb = [es.enter_context(nc.psum_tensor("psb%d" % i, [128, 512], F32)) for i in range(8)]
    psB = [Buf("psb%d" % i) for i in range(8)]
    ps_rr = [0]

    def next_ps():
        i = ps_rr[0]
        ps_rr[0] = (i + 1) % 6
        return psb[i], psB[i]

    cst = sb(es, "cst", [128, 512], F32)
    cstb = sb(es, "cstb", [128, 512], BF16)
    Bc = Buf("cst")
    Bcb = Buf("cstb")
    P.dma(lambda e: e.dma_start(out=cst[:], in_=consts[:, 0:512]), writes=[Bc])
    OP("dve", [Bc], [Bcb], lambda e: e.tensor_copy(out=cstb[:], in_=cst[:, 0:512]))
    ident_f = cst[:, C_ID:C_ID + 128]
    ones_f = cst[:, C_ONES:C_ONES + 128]
    triu_f = cst[:, C_TRIU:C_TRIU + 128]
    ident_b = cstb[:, C_ID:C_ID + 128]
    ones_b = cstb[:, C_ONES:C_ONES + 128]
    mneg_b = cstb[:, C_MNEG:C_MNEG + 128]
    epsc = sb(es, "epsc", [128, 2], F32)
    Beps = Buf("eps")
    OP("dve", [], [Beps], lambda e: e.memset(epsc[:, 0:1], EPS))
    OP("dve", [Beps], [Beps], lambda e: e.memset(epsc[:, 1:2], 1.0))
    gcol = sb(es, "gcol", [128, 9, 8], F32)
    Bg = Buf("gcol")
    P.dma(lambda e: e.dma_start(out=gcol[:, 0:8, :], in_=norm_g.rearrange("l n (c p) -> p (l n) c", p=128), allow_slow_non_contiguous=True), writes=[Bg])
    P.dma(lambda e: e.dma_start(out=gcol[:, 8, :], in_=final_g.rearrange("(c p) -> p c", p=128), allow_slow_non_contiguous=True), writes=[Bg])

    def rmsnorm(h, hB, gi, xn, xnB, tmp):
        ps, pB = next_ps()
        for c in range(8):
            sq, sqB = tmp["sq"][c % 2]
            OP("act", [hB[c]], [sqB], lambda e, c=c, sq=sq: e.activation(out=sq, in_=h[:, c, :], func=AF.Square))
            OP("pe", [sqB, Bcb], [pB], lambda e, c=c, sq=sq, ps=ps: e.matmul(ps[:, 0:TT], lhsT=ones_b, rhs=sq, start=(c == 0), stop=(c == 7)))
        lnv, lnB = tmp["lnv"]
        rstd, rsB = tmp["rstd"]
        OP("act", [pB, Beps], [lnB], lambda e, ps=ps: e.activation(out=lnv, in_=ps[:, 0:TT], func=AF.Ln, scale=1.0 / D, bias=epsc[:, 0:1]))
        OP("act", [lnB], [rsB], lambda e: e.activation(out=rstd, in_=lnv, func=AF.Exp, scale=-0.5))
        for c in range(8):
            OP("dve", [hB[c], rsB, Bg], [xnB[c]],
               lambda e, c=c: e.scalar_tensor_tensor(out=xn[:, c, :], in0=h[:, c, :], scalar=gcol[:, gi, c:c + 1], in1=rstd,
                                                     op0=ALU.mult, op1=ALU.mult))

    def norm_tmp(stack, tag):
        sq0 = sb(stack, "sq0" + tag, [128, TT], BF16)
        sq1 = sb(stack, "sq1" + tag, [128, TT], BF16)
        lnv = sb(stack, "lnv" + tag, [128, TT], F32)
        rstd = sb(stack, "rstd" + tag, [128, TT], F32)
        return {"sq": [(sq0[:], Buf()), (sq1[:], Buf())], "lnv": (lnv[:], Buf()), "rstd": (rstd[:], Buf())}

    def load_h(tile, tB, q):
        P.dma(lambda e: e.dma_start(out=tile, in_=hTv[:, :, q * TT:(q + 1) * TT]), reads=HT[q], writes=tB)

    def wload(dst, src, wB):
        P.dma(lambda e: e.dma_start(out=dst, in_=src), writes=wB, q="pool")

    def phase_in():
        P.barrier()
        with contextlib.ExitStack() as st:
            xt = [sb(st, "xt%d" % i, [128, D], F32) for i in range(2)]
            xtB = [Buf() for _ in range(2)]
            ho = [sb(st, "ho%d" % i, [128, 8, TT], F32) for i in range(2)]
            hoB = [Buf() for _ in range(2)]
            k = 0
            for q in range(NQ):
                for s in range(4):
                    tt = q * 4 + s
                    xb, xB = xt[tt % 2], xtB[tt % 2]
                    P.dma(lambda e, xb=xb, tt=tt: e.dma_start(out=xb[:], in_=x[tt * 128:(tt + 1) * 128, :]), writes=[xB])
                    for half in range(2):
                        ps, pB = next_ps()
                        for cc in range(4):
                            c = half * 4 + cc
                            OP("pe", [xB, Bc], [pB], lambda e, ps=ps, cc=cc, c=c, xb=xb: e.transpose(out=ps[:, cc * 128:(cc + 1) * 128], in_=xb[:, c * 128:(c + 1) * 128], identity=ident_f))
                        eng = "act" if (k % 2 == 0) else "dve"
                        k += 1
                        dst = ho[q % 2][:, half * 4:(half + 1) * 4, s * 128:(s + 1) * 128]
                        src = ps[:, :].rearrange("p (c t) -> p c t", c=4)
                        if eng == "act":
                            OP("act", [pB], [hoB[q % 2]], lambda e, dst=dst, src=src: e.activation(out=dst, in_=src, func=AF.Copy))
                        else:
                            OP("dve", [pB], [hoB[q % 2]], lambda e, dst=dst, src=src: e.tensor_copy(out=dst, in_=src))
                P.dma(lambda e, q=q: e.dma_start(out=hTv[:, :, q * TT:(q + 1) * TT], in_=ho[q % 2][:]), reads=[hoB[q % 2]], writes=HT[q])

    def phase_out():
        P.barrier()
        with contextlib.ExitStack() as st:
            hn2 = [sb(st, "fo_hn%d" % i, [128, 8, TT], F32) for i in range(2)]
            hnB2 = [[Buf() for _ in range(8)] for _ in range(2)]
            yn2 = [sb(st, "fo_yn%d" % i, [128, 8, TT], F32) for i in range(2)]
            ynB2 = [[Buf() for _ in range(8)] for _ in range(2)]
            ot = [sb(st, "fo_ot%d" % i, [128, D], F32) for i in range(4)]
            otB = [Buf() for _ in range(4)]
            tmp2 = [norm_tmp(st, "fo%d" % i) for i in range(2)]
            k = 0
            for q in range(NQ):
                hn, hnB, yn, ynB = hn2[q % 2], hnB2[q % 2], yn2[q % 2], ynB2[q % 2]
                load_h(hn[:], hnB, q)
                rmsnorm(hn, hnB, 8, yn, ynB, tmp2[q % 2])
                for s in range(4):
                    tt = q * 4 + s
                    o, oB = ot[tt % 4], otB[tt % 4]
                    for half in range(2):
                        ps, pB = next_ps()
                        for cc in range(4):
                            c = half * 4 + cc
                            OP("pe", [ynB[c], Bc], [pB], lambda e, ps=ps, cc=cc, c=c, s=s, yn=yn: e.transpose(out=ps[:, cc * 128:(cc + 1) * 128], in_=yn[:, c, s * 128:(s + 1) * 128], identity=ident_f))
                        dst = o[:, half * 512:(half + 1) * 512]
                        if k % 2 == 0:
                            OP("act", [pB], [oB], lambda e, dst=dst, ps=ps: e.activation(out=dst, in_=ps[:, :], func=AF.Copy))
                        else:
                            OP("dve", [pB], [oB], lambda e, dst=dst, ps=ps: e.tensor_copy(out=dst, in_=ps[:, :]))
                        k += 1
                    P.dma(lambda e, o=o, tt=tt: e.dma_start(out=y_out[tt * 128:(tt + 1) * 128, :], in_=o[:]), reads=[oB], writes=[Buf()], is_out=True)

    def phase_ffn(l, f):
        P.barrier()
        with contextlib.ExitStack() as st:
            wg = sb(st, "wg", [128, 8, DFF], BF16)
            wu = sb(st, "wu", [128, 8, DFF], BF16)
            wd = sb(st, "wd", [128, NFC, D], BF16)
            CB = [(0, 256), (256, 1024), (1024, 2048), (2048, DFF)]
            wgB = [Buf() for _ in range(4)]
            wuB = [Buf() for _ in range(4)]
            wdB = [Buf() for _ in range(NFC)]
            wgv = w_gate[l, f].rearrange("(kc p) n -> p kc n", p=128)
            wuv = w_up[l, f].rearrange("(kc p) n -> p kc n", p=128)
            wdv = w_down[l, f].rearrange("(fc p) n -> p fc n", p=128)
            for cb, (c0, c1) in enumerate(CB):
                wload(wg[:, :, c0:c1], wgv[:, :, c0:c1], [wgB[cb]])
                wload(wu[:, :, c0:c1], wuv[:, :, c0:c1], [wuB[cb]])
            for f0 in range(0, NFC, 6):
                f1 = min(NFC, f0 + 6)
                wload(wd[:, f0:f1, :], wdv[:, f0:f1, :], wdB[f0:f1])
            hn = sb(st, "ff_hn", [128, 8, TT], F32)
            hnB = [Buf() for _ in range(8)]
            xn = [sb(st, "ff_xn%d" % i, [128, 8, TT], BF16) for i in range(2)]
            xnB = [[Buf() for _ in range(8)] for _ in range(2)]
            act = sb(st, "ff_act", [128, NFC, TT], BF16)
            actB = [Buf() for _ in range(NFC)]
            sg = [sb(st, "ff_sg%d" % i, [128, TT], F32) for i in range(2)]
            sgB = [Buf() for _ in range(2)]
            hr = [sb(st, "ff_hr%d" % i, [128, TT], F32) for i in range(3)]
            hrB = [Buf() for _ in range(3)]
            tmp = norm_tmp(st, "ff")
            gi = l * 4 + (0 if f == 0 else 2)
            k = 0
            load_h(hn[:], hnB, 0)
            rmsnorm(hn, hnB, gi, xn[0], xnB[0], tmp)
            for q in range(nq):
                X, XB = xn[q % 2], xnB[q % 2]
                if q + 1 < nq:
                    load_h(hn[:], hnB, q + 1)
                if q == 0 and l == 0 and f == 0:
                    dbg_dump("xn", X[:], [128, 8, TT], BF16, XB)
                    dbg_dump("wg", wg[:], [128, 8, DFF], BF16, wgB)
                    dbg_dump("wd", wd[:], [128, NFC, D], BF16, wdB)
                for fc in range(NFC):
                    pg, pgB = next_ps()
                    for kc in range(8):
                        OP("pe", [wgB[0 if fc < 2 else 1 + fc // 8], XB[kc]], [pgB], lambda e, pg=pg, kc=kc, fc=fc, X=X: e.matmul(pg[:, 0:TT], lhsT=wg[:, kc, fc * 128:(fc + 1) * 128], rhs=X[:, kc, :], start=(kc == 0), stop=(kc == 7)))
                    pu, puB = next_ps()
                    for kc in range(8):
                        OP("pe", [wuB[0 if fc < 2 else 1 + fc // 8], XB[kc]], [puB], lambda e, pu=pu, kc=kc, fc=fc, X=X: e.matmul(pu[:, 0:TT], lhsT=wu[:, kc, fc * 128:(fc + 1) * 128], rhs=X[:, kc, :], start=(kc == 0), stop=(kc == 7)))
                    s_, sB_ = sg[fc % 2], sgB[fc % 2]
                    OP("act", [pgB], [sB_], lambda e, s_=s_, pg=pg: e.activation(out=s_[:], in_=pg[:, 0:TT], func=AF.Silu))
                    OP("dve", [sB_, puB], [actB[fc]], lambda e, s_=s_, pu=pu, fc=fc: e.tensor_tensor(out=act[:, fc, :], in0=pu[:, 0:TT], in1=s_[:], op=ALU.mult))
                    if fc == 11 and q + 1 < nq:
                        rmsnorm(hn, hnB, gi, xn[(q + 1) % 2], xnB[(q + 1) % 2], tmp)
                if q == 0 and l == 0 and f == 0:
                    dbg_dump("act", act[:], [128, NFC, TT], BF16, actB)
                for dc in range(8):
                    po, poB = next_ps()
                    for fc in range(NFC):
                        OP("pe", [wdB[fc], actB[fc]], [poB], lambda e, po=po, fc=fc, dc=dc: e.matmul(po[:, 0:TT], lhsT=wd[:, fc, dc * 128:(dc + 1) * 128], rhs=act[:, fc, :], start=(fc == 0), stop=(fc == NFC - 1)))
                    r, rB = hr[k % 3], hrB[k % 3]
                    k += 1
                    P.dma(lambda e, r=r, dc=dc, q=q: e.dma_start(out=r[:], in_=hTv[:, dc, q * TT:(q + 1) * TT]), reads=[HT[q][dc]], writes=[rB])
                    OP("dve", [poB, rB], [rB], lambda e, r=r, po=po: e.scalar_tensor_tensor(out=r[:], in0=po[:, 0:TT], scalar=0.5, in1=r[:], op0=ALU.mult, op1=ALU.add))
                    P.dma(lambda e, r=r, dc=dc, q=q: e.dma_start(out=hTv[:, dc, q * TT:(q + 1) * TT], in_=r[:]), reads=[rB], writes=[HT[q][dc]])

    def phase_ple(l, fuse_out=False):
        P.barrier()
        with contextlib.ExitStack() as st:
            wpg = sb(st, "wpg", [128, 8, D], BF16)
            wpp = sb(st, "wpp", [128, 2, D], BF16)
            wpgB = [Buf()]
            wppB = [Buf()]
            wpgv = ple_wg[l].rearrange("(kc p) n -> p kc n", p=128)
            wpgB = [Buf(), Buf()]
            wload(wpg[:, :, 0:256], wpgv[:, :, 0:256], [wpgB[0]])
            wload(wpg[:, :, 256:D], wpgv[:, :, 256:D], [wpgB[1]])
            wload(wpp[:], ple_wp[l].rearrange("(kc p) n -> p kc n", p=128), wppB)
            NH = 4 if fuse_out else 3
            hn2 = [sb(st, "pl_hn%d" % i, [128, 8, TT], F32) for i in range(NH)]
            hnB2 = [[Buf() for _ in range(8)] for _ in range(NH)]
            xn = [sb(st, "pl_xn%d" % i, [128, 8, TT], BF16) for i in range(2)]
            xnB = [[Buf() for _ in range(8)] for _ in range(2)]
            pt = [sb(st, "pl_pt%d" % i, [128, 4, PLE], F32) for i in range(3)]
            ptB = [Buf() for _ in range(3)]
            pT = [sb(st, "pl_pT%d" % i, [128, 2, TT], BF16) for i in range(2)]
            pTB = [Buf() for _ in range(2)]
            sg = [sb(st, "pl_sg%d" % i, [128, TT], F32) for i in range(2)]
            sgB = [Buf() for _ in range(2)]
            tg = [sb(st, "pl_tg%d" % i, [128, TT], F32) for i in range(2)]
            tgB = [Buf() for _ in range(2)]
            hr = [sb(st, "pl_hr%d" % i, [128, TT], F32) for i in range(3)]
            hrB = [Buf() for _ in range(3)]
            tmp2 = [norm_tmp(st, "pl%d" % i) for i in range(2)]
            gi = l * 4 + 3

            def ple_load(q):
                load_h(hn2[q % NH][:], hnB2[q % NH], q)
                pt_, ptB_ = pt[q % 3], ptB[q % 3]
                P.dma(lambda e, pt_=pt_, q=q: e.dma_start(out=pt_[:], in_=p_in[l, q * TT:(q + 1) * TT, :].rearrange("(s p) c -> p s c", p=128)), writes=[ptB_])

            def ple_prep(q):
                rmsnorm(hn2[q % NH], hnB2[q % NH], gi, xn[q % 2], xnB[q % 2], tmp2[q % 2])
                pt_, ptB_ = pt[q % 3], ptB[q % 3]
                pT_, pTB_ = pT[q % 2], pTB[q % 2]
                for c2 in range(2):
                    ps, pB = next_ps()
                    for s in range(4):
                        OP("pe", [ptB_, Bc], [pB], lambda e, ps=ps, s=s, c2=c2, pt_=pt_: e.transpose(out=ps[:, s * 128:(s + 1) * 128], in_=pt_[:, s, c2 * 128:(c2 + 1) * 128], identity=ident_f))
                    OP("act", [pB], [pTB_], lambda e, ps=ps, c2=c2, pT_=pT_: e.activation(out=pT_[:, c2, :], in_=ps[:, :], func=AF.Copy))

            if fuse_out:
                yn2 = [sb(st, "fo_yn%d" % i, [128, 8, TT], F32) for i in range(2)]
                ynB2 = [[Buf() for _ in range(8)] for _ in range(2)]
                ot = [sb(st, "fo_ot%d" % i, [128, D], F32) for i in range(4)]
                otB = [Buf() for _ in range(4)]
                tmpo = [norm_tmp(st, "fo%d" % i) for i in range(2)]
            kev = [0]

            def out_tile(q):
                hn, hnB, yn, ynB = hn2[q % NH], hnB2[q % NH], yn2[q % 2], ynB2[q % 2]
                rmsnorm(hn, hnB, 8, yn, ynB, tmpo[q % 2])
                for s in range(4):
                    tt = q * 4 + s
                    o, oB = ot[tt % 4], otB[tt % 4]
                    for half in range(2):
                        ps, pB = next_ps()
                        for cc in range(4):
                            c_ = half * 4 + cc
                            OP("pe", [ynB[c_], Bc], [pB], lambda e, ps=ps, cc=cc, c_=c_, s=s, yn=yn: e.transpose(out=ps[:, cc * 128:(cc + 1) * 128], in_=yn[:, c_, s * 128:(s + 1) * 128], identity=ident_f))
                        dst = o[:, half * 512:(half + 1) * 512]
                        if kev[0] % 2 == 0:
                            OP("act", [pB], [oB], lambda e, dst=dst, ps=ps: e.activation(out=dst, in_=ps[:, :], func=AF.Copy))
                        else:
                            OP("dve", [pB], [oB], lambda e, dst=dst, ps=ps: e.tensor_copy(out=dst, in_=ps[:, :]))
                        kev[0] += 1
                    P.dma(lambda e, o=o, tt=tt: e.dma_start(out=y_out[tt * 128:(tt + 1) * 128, :], in_=o[:]), reads=[oB], writes=[Buf()], is_out=True)

            ple_load(0)
            ple_load(1)
            ple_prep(0)
            for q in range(NQ):
                hn, hnB = hn2[q % NH], hnB2[q % NH]
                X, XB = xn[q % 2], xnB[q % 2]
                pT_, pTB_ = pT[q % 2], pTB[q % 2]
                if q + 2 < NQ:
                    ple_load(q + 2)
                for dc in range(8):
                    pg, pgB = next_ps()
                    for kc in range(8):
                        OP("pe", [wpgB[0 if dc < 2 else 1], XB[kc]], [pgB], lambda e, pg=pg, kc=kc, dc=dc, X=X: e.matmul(pg[:, 0:TT], lhsT=wpg[:, kc, dc * 128:(dc + 1) * 128], rhs=X[:, kc, :], start=(kc == 0), stop=(kc == 7)))
                    pe_, peB = next_ps()
                    for c2 in range(2):
                        OP("pe", [wppB[0], pTB_], [peB], lambda e, pe_=pe_, c2=c2, dc=dc, pT_=pT_: e.matmul(pe_[:, 0:TT], lhsT=wpp[:, c2, dc * 128:(dc + 1) * 128], rhs=pT_[:, c2, :], start=(c2 == 0), stop=(c2 == 1)))
                    s_, sB_ = sg[dc % 2], sgB[dc % 2]
                    OP("act", [pgB], [sB_], lambda e, s_=s_, pg=pg: e.activation(out=s_[:], in_=pg[:, 0:TT], func=AF.Sigmoid))
                    t_, tB_ = tg[dc % 2], tgB[dc % 2]
                    OP("dve", [sB_, peB], [tB_], lambda e, s_=s_, pe_=pe_, t_=t_: e.tensor_tensor(out=t_[:], in0=pe_[:, 0:TT], in1=s_[:], op=ALU.mult))
                    OP("pool", [tB_, hnB[dc]], [hnB[dc]], lambda e, hn=hn, t_=t_, dc=dc: e.tensor_tensor(out=hn[:, dc, :], in0=t_[:], in1=hn[:, dc, :], op=ALU.add))
                    if not fuse_out:
                        P.dma(lambda e, hn=hn, dc=dc, q=q: e.dma_start(out=hTv[:, dc, q * TT:(q + 1) * TT], in_=hn[:, dc, :]), reads=[hnB[dc]], writes=[HT[q][dc]])
                    if dc == 3 and q + 1 < NQ:
                        ple_prep(q + 1)
                    if fuse_out and dc == 3 and q >= 1:
                        out_tile(q - 1)
            if fuse_out:
                out_tile(NQ - 1)

    ctx = dict(nc=nc, P=P, OP=OP, sb=sb, es=es, next_ps=next_ps, rmsnorm=rmsnorm, norm_tmp=norm_tmp, load_h=load_h,
               wload=wload, HT=HT, XN=XN, YB=YB, hTv=hTv, xnTv=xnTv, ybr=ybr, cst=cst, cstb=cstb, Bc=Bc, Bcb=Bcb,
               epsc=epsc, Beps=Beps, ident_f=ident_f, ones_f=ones_f, triu_f=triu_f, ident_b=ident_b, ones_b=ones_b,
               mneg_b=mneg_b, gcol=gcol, Bg=Bg,
               w_in=w_in, f_bias=f_bias, pool_w=pool_w, pool_scale=pool_scale, lam_re=lam_re, lam_im=lam_im,
               log_dt=log_dt, b_re=b_re, b_im=b_im, c_re=c_re, c_im=c_im, ssm_d=ssm_d, w_glu=w_glu, conv_w=conv_w,
               w_branch=w_branch, w_out=w_out, consts=consts, qaug=qaug, vtok=vtok, psb=psb, psB=psB, dbg_dump=dbg_dump, nq=nq)

    if "in" in phases:
        phase_in()
    for l in range(depth):
        if "ffn" in phases:
            phase_ffn(l, 0)
        if "mix" in phases:
            phase_mixer(ctx, l, mix_parts)
        if "ffn" in phases:
            phase_ffn(l, 1)
        fuse = ("out" in phases) and (l == depth - 1)
        if "ple" in phases:
            phase_ple(l, fuse_out=fuse)
    if "out" in phases and "ple" not in phases:
        phase_out()
    P.emit(es)
    es.close()
    return nc


from types import SimpleNamespace


def phase_mixer(ctx, l, parts):
    c = SimpleNamespace(**ctx)
    P, OP, sb, nc = c.P, c.OP, c.sb, c.nc
    psb, psB = c.psb, c.psB
    ybrv = c.ybr.rearrange("b (c p) t -> b p c t", p=128)
    QA = [Buf() for _ in range(NQ)]
    VT = [Buf() for _ in range(NQ)]
    P.barrier()
    with contextlib.ExitStack() as so:
        u_ssm = sb(so, "u_ssm", [128, 2, 8 + T], BF16)
        uB = [[Buf() for _ in range(NQ)] for _ in range(2)]
        spar = ssm_param_load(c, l, so) if "ssm" in parts else None
        spre = ssm_s_alloc(c, so) if "ssm" in parts else None
        with contextlib.ExitStack() as s1:
            k_aug = sb(s1, "k_aug", [128, 4, T], BF16)
            kB = [[Buf() for _ in range(NQ)] for _ in range(4)]
            kcB = Buf()
            cabs = sb(s1, "cabs", [128, 32, 4], F32)
            tots = sb(s1, "tots", [128, 33, 4], F32)
            cabsB = [Buf() for _ in range(32)]
            totsB = [Buf() for _ in range(33)]
            def issue_params():
                if spar is not None:
                    for fn, rd, wr, kw in spar.dq:
                        P.dma(fn, reads=rd, writes=wr, **kw)
                    spar.dq.clear()
            mixer_m1(c, l, parts, u_ssm, uB, k_aug, kB, kcB, cabs, cabsB, tots, totsB, QA, VT, ybrv, issue_params)
            issue_params()
            P.barrier()
            sch = ssm_s_chain(c, l, spre, spar) if "ssm" in parts else None
            dq = sch.dq if sch is not None else []

            def drain(n):
                for _ in range(min(n, len(dq))):
                    eng, rd, wr, fn = dq.pop(0)
                    OP(eng, rd, wr, fn)
            if "att" in parts:
                mixer_m3(c, l, k_aug, kB, kcB, cabs, cabsB, tots, totsB, QA, VT, ybrv, drain)
            drain(len(dq))
        P.barrier()
        if "ssm" in parts:
            mixer_m2(c, l, u_ssm, uB, ybrv, spar, sch)
    P.barrier()
    mixer_m4(c, l, parts, ybrv)


def mixer_m1(c, l, parts, u_ssm, uB, k_aug, kB, kcB, cabs, cabsB, tots, totsB, QA, VT, ybrv, after_tile0=lambda: None):
    P, OP, sb, nc, next_ps = c.P, c.OP, c.sb, c.nc, c.next_ps
    with contextlib.ExitStack() as st:
        win = sb(st, "win", [128, 8, 2052], BF16)
        WBLK = [(0, 772), (772, 1796), (1796, 2052)]
        winB = [Buf() for _ in WBLK]
        wiv = c.w_in[l].rearrange("(kc p) n -> p kc n", p=128)
        c.wload(win[:, :, 0:772], wiv[:, :, 0:772], [winB[0]])

        def wB(col):
            for i, (c0, c1) in enumerate(WBLK):
                if c0 <= col < c1:
                    return winB[i]

        wf_sb = sb(st, "wf_sb", [128, 8, 4, 64], BF16)
        wfB = Buf()
        OP("dve", [winB[0]], [wfB], lambda e: e.tensor_copy(out=wf_sb[:], in_=win[:, :, 768:772].unsqueeze(3).to_broadcast([128, 8, 4, 64])))
        fb = sb(st, "fb", [128, 8], F32)
        fbB = Buf()
        P.dma(lambda e: e.dma_start(out=fb[:, 0:4], in_=c.f_bias[l:l + 1, :].partition_broadcast(128), allow_slow_non_contiguous=True), writes=[fbB])
        OP("dve", [fbB], [fbB], lambda e: e.tensor_scalar(out=fb[:, 4:8], in0=fb[:, 0:4], scalar1=-1.0, scalar2=None, op0=ALU.mult))
        pwb = sb(st, "pwb", [128, 2, 128], BF16)
        pwB = Buf()
        OP("pool", [], [pwB], lambda e: e.memset(pwb[:], 0.0))
        for g in range(4):
            r0 = (g % 2) * 64
            P.dma(lambda e, g=g, r0=r0: e.dma_start(out=pwb[r0:r0 + 64, g // 2, r0:r0 + 64], in_=c.pool_w[l, g]), reads=[pwB], writes=[pwB], q="pool")
        pmb = sb(st, "pmb", [128, 12, 128], BF16)
        pmB = Buf()
        P.dma(lambda e: e.dma_start(out=pmb[:], in_=c.consts[:, C_PM:C_PM + 12 * 128].rearrange("p (a b) -> p a b", a=12)), writes=[pmB], q="pool")
        for i in (1, 2):
            c.wload(win[:, :, WBLK[i][0]:WBLK[i][1]], wiv[:, :, WBLK[i][0]:WBLK[i][1]], [winB[i]])
        scol = sb(st, "scol", [128, 8], F32)
        scB = Buf()
        P.dma(lambda e: e.dma_start(out=scol[:, 0:2], in_=c.pool_scale[l].rearrange("(c p) -> p c", p=128), allow_slow_non_contiguous=True), writes=[scB])
        P.dma(lambda e: e.dma_start(out=scol[:, 2:8].rearrange("p (j c) -> p j c", j=3), in_=c.conv_w[l].rearrange("j (c p) -> p j c", p=128), allow_slow_non_contiguous=True), writes=[scB])

        OP("pool", [], [totsB[0]], lambda e: e.memset(tots[:, 0, :], 0.0))

        hn = sb(st, "m1_hn", [128, 8, TT], F32)
        hnB = [Buf() for _ in range(8)]
        xn2 = [sb(st, "m1_xn%d" % i, [128, 8, TT], BF16) for i in range(2)]
        xnB2 = [[Buf() for _ in range(8)] for _ in range(2)]
        tmp = c.norm_tmp(st, "m1")
        qa = [sb(st, "m1_qa%d" % i, [128, 4, TT], BF16) for i in range(2)]
        qaB = [Buf() for _ in range(2)]
        et = [sb(st, "m1_et%d" % i, [128, TT], F32) for i in range(2)]
        etB = [Buf() for _ in range(2)]
        spt = [sb(st, "m1_sp%d" % i, [128, TT], F32) for i in range(2)]
        spB = [Buf() for _ in range(2)]
        crn = [sb(st, "m1_crn%d" % i, [128, TT], F32) for i in range(2)]
        crnB = [Buf() for _ in range(2)]
        hit = [sb(st, "m1_hit%d" % i, [128, TT], BF16) for i in range(2)]
        hitB = [Buf() for _ in range(2)]
        vst = [sb(st, "m1_vst%d" % i, [128, 4, 256], BF16) for i in range(2)]
        vstB = [Buf() for _ in range(2)]
        ftk = [sb(st, "m1_ftk%d" % i, [128, 12], F32) for i in range(2)]
        ftkB = [Buf() for _ in range(2)]
        xpt = sb(st, "m1_xpt", [128, 5, 256], BF16)
        xptB = [Buf() for _ in range(5)]
        pld = sb(st, "m1_pld", [128, 2, TT], BF16)
        pldB = [Buf() for _ in range(2)]
        yps = [sb(st, "m1_yps%d" % i, [128, 2, TT], BF16) for i in range(2)]
        ypsB = [Buf() for _ in range(2)]
        ycs = [sb(st, "m1_ycs%d" % i, [128, 2, TT], BF16) for i in range(2)]
        ycsB = [Buf() for _ in range(2)]
        ccs = [sb(st, "m1_ccs%d" % i, [128, TT], F32) for i in range(2)]
        ccsB = [Buf() for _ in range(2)]
        cbs = [sb(st, "m1_cbs%d" % i, [128, TT], F32) for i in range(2)]
        cbsB = [Buf() for _ in range(2)]
        zt = [[sb(st, "m1_z%d_%d" % (cc, i), [128, TT + 2], F32) for i in range(2)] for cc in range(2)]
        ztB = [[Buf() for _ in range(2)] for _ in range(2)]
        y1 = [sb(st, "m1_y1%d" % i, [128, TT], F32) for i in range(2)]
        y1B = [Buf() for _ in range(2)]
        ones1 = c.cst[:, C_ONES:C_ONES + 1]

        def proj_fm(col0, M, ps, pB, pslice, X, XB, tp=None):
            for kc in range(8):
                kw = {} if tp is None else {"tile_position": tp}
                OP("pe", [wB(col0), XB[kc]], [pB], lambda e, kc=kc, kw=kw: e.matmul(ps[pslice, 0:TT], lhsT=win[:, kc, col0:col0 + M], rhs=X[:, kc, :], start=(kc == 0), stop=(kc == 7), **kw))

        c.load_h(hn[:], hnB, 0)
        c.rmsnorm(hn, hnB, l * 4 + 1, xn2[0], xnB2[0], tmp)
        if c.nq > 1:
            c.load_h(hn[:], hnB, 1)
        for q in range(c.nq):
            qs = slice(q * TT, (q + 1) * TT)
            xn, xnB = xn2[q % 2], xnB2[q % 2]
            P.dma(lambda e, qs=qs, xn=xn: e.dma_start(out=c.xnTv[:, :, qs], in_=xn[:]), reads=xnB, writes=[c.XN[q]])
            Q_, QB_ = qa[q % 2], qaB[q % 2]
            if "att" in parts:
                for h in range(4):
                    ps, pB = next_ps()
                    proj_fm(h * 64, 64, ps, pB, slice(0, 64), xn, xnB)
                    for kc in range(8):
                        OP("pe", [wfB, xnB[kc]], [pB], lambda e, kc=kc, h=h, ps=ps, xn=xn: e.matmul(ps[64:128, 0:TT], lhsT=wf_sb[:, kc, h, :], rhs=xn[:, kc, :], start=(kc == 0), stop=(kc == 7), tile_position=(0, 64)))
                    OP("act", [pB], [QB_], lambda e, ps=ps, h=h, Q_=Q_: e.activation(out=Q_[0:64, h, :], in_=ps[0:64, 0:TT], func=AF.Copy, scale=0.125))
                    e_, eB_ = et[h % 2], etB[h % 2]
                    OP("act", [pB, fbB], [eB_], lambda e, ps=ps, h=h, e_=e_: e.activation(out=e_[64:128, :], in_=ps[64:128, 0:TT], func=AF.Exp, scale=-1.0, bias=fb[64:128, 4 + h:5 + h]))
                    s_, sB_ = spt[h % 2], spB[h % 2]
                    OP("act", [eB_, c.Beps], [sB_], lambda e, e_=e_, s_=s_: e.activation(out=s_[64:128, :], in_=e_[64:128, :], func=AF.Ln, bias=c.epsc[64:128, 1:2]))
                    r_, rB_ = crn[h % 2], crnB[h % 2]
                    OP("dve", [sB_, c.Bc], [rB_], lambda e, s_=s_, r_=r_: e.tensor_tensor_scan(out=r_[64:128, :], data0=ones1[64:128, :].to_broadcast([64, TT]), data1=s_[64:128, :], initial=0.0, op0=ALU.mult, op1=ALU.add))
                    h_, hB_ = hit[h % 2], hitB[h % 2]
                    OP("dve", [rB_], [hB_], lambda e, r_=r_, h_=h_: e.tensor_scalar(out=h_[64:128, :], in0=r_[64:128, :], scalar1=-1.0, scalar2=None, op0=ALU.mult))
                    OP("pool", [hB_], [QB_], lambda e, h_=h_, h=h, Q_=Q_: e.tensor_copy(out=Q_[64:96, h, :], in_=h_[64:96, :]))
                    OP("dve", [rB_, hB_], [QB_], lambda e, r_=r_, h_=h_, h=h, Q_=Q_: e.scalar_tensor_tensor(out=Q_[96:128, h, :], in0=r_[96:128, :], scalar=-1.0, in1=h_[96:128, :], op0=ALU.mult, op1=ALU.subtract))
                P.dma(lambda e, qs=qs, Q_=Q_: e.dma_start(out=c.qaug[:, :, qs], in_=Q_[:]), reads=[QB_], writes=[QA[q]])
                for h in range(4):
                    ps, pB = next_ps()
                    proj_fm(256 + h * 64, 64, ps, pB, slice(0, 64), xn, xnB)
                    OP("dve", [pB], [kB[h][q]], lambda e, ps=ps, h=h, qs=qs: e.tensor_copy(out=k_aug[0:64, h, qs], in_=ps[0:64, 0:TT]))
            if q == 1:
                after_tile0()
            if q == 0:
                OP("pool", [], [kcB], lambda e: e.memset(k_aug[64:128, :, :], 0.0))
                OP("pool", [kcB], [kcB], lambda e: e.memset(k_aug[64:65, :, :], 1.0))
                OP("pool", [kcB], [kcB], lambda e: e.memset(k_aug[96:97, :, :], 1.0))
            if q + 1 < c.nq:
                c.rmsnorm(hn, hnB, l * 4 + 1, xn2[(q + 1) % 2], xnB2[(q + 1) % 2], tmp)
                if q + 2 < c.nq:
                    c.load_h(hn[:], hnB, q + 2)
            V_, VB_ = vst[q % 2], vstB[q % 2]
            ppool = [(c.psb[6], c.psB[6]), (c.psb[7], c.psB[7])]

            def stA(s):
                tt = q * 4 + s
                ts_ = slice(s * 128, (s + 1) * 128)
                if "att" not in parts:
                    return
                psA, pBA = next_ps()
                for kc in range(8):
                    OP("pe", [winB[0], xnB[kc]], [pBA], lambda e, kc=kc, psA=psA, ts_=ts_, xn=xn: e.matmul(psA[:, 0:260], lhsT=xn[:, kc, ts_], rhs=win[:, kc, 512:772], start=(kc == 0), stop=(kc == 7)))
                OP("act", [pBA], [VB_], lambda e, psA=psA, s=s, V_=V_: e.activation(out=V_[:, s, :], in_=psA[:, 0:256], func=AF.Copy))
                f_, fB_ = ftk[tt % 2], ftkB[tt % 2]
                OP("dve", [pBA, fbB], [fB_], lambda e, psA=psA, f_=f_: e.tensor_tensor(out=f_[:, 0:4], in0=psA[:, 256:260], in1=fb[:, 0:4], op=ALU.add))
                OP("act", [fB_], [fB_], lambda e, f_=f_: e.activation(out=f_[:, 4:8], in_=f_[:, 0:4], func=AF.Exp, scale=-1.0))
                OP("act", [fB_, c.Beps], [fB_], lambda e, f_=f_: e.activation(out=f_[:, 8:12], in_=f_[:, 4:8], func=AF.Ln, bias=c.epsc[:, 1:2]))

            def stB(s):
                ts_ = slice(s * 128, (s + 1) * 128)
                if "pool" not in parts:
                    return
                psP, pBP = next_ps()
                for kc in range(8):
                    OP("pe", [winB[1], xnB[kc]], [pBP], lambda e, kc=kc, psP=psP, ts_=ts_, xn=xn: e.matmul(psP[:, 0:256], lhsT=xn[:, kc, ts_], rhs=win[:, kc, 772:1028], start=(kc == 0), stop=(kc == 7)))
                OP("act", [pBP], [xptB[1 + s]], lambda e, psP=psP, s=s: e.activation(out=xpt[:, 1 + s, :], in_=psP[:, 0:256], func=AF.Copy))

            def stC(s):
                tt = q * 4 + s
                ts_ = slice(s * 128, (s + 1) * 128)
                if "pool" not in parts:
                    return
                for g in range(4):
                    pp, ppB = ppool[g // 2]
                    r0 = (g % 2) * 64
                    first = (tt == 0)
                    mi = g * 3 + (2 if first else 0)
                    OP("pe", [xptB[1 + s], pmB], [ppB], lambda e, pp=pp, r0=r0, g=g, s=s, mi=mi, ts_=ts_, first=first: e.matmul(pp[r0:r0 + 64, ts_], lhsT=xpt[:, 1 + s, g * 64:(g + 1) * 64], rhs=pmb[:, mi, :], start=True, stop=first, tile_position=(0, r0)))
                    if not first:
                        OP("pe", [xptB[s], pmB], [ppB], lambda e, pp=pp, r0=r0, g=g, s=s, ts_=ts_: e.matmul(pp[r0:r0 + 64, ts_], lhsT=xpt[:, s, g * 64:(g + 1) * 64], rhs=pmb[:, g * 3 + 1, :], start=False, stop=True, tile_position=(0, r0)))

            def stD(s):
                tt = q * 4 + s
                if "att" not in parts:
                    return
                f_, fB_ = ftk[tt % 2], ftkB[tt % 2]
                psc, pBc = next_ps()
                OP("pe", [fB_, c.Bc], [pBc], lambda e, psc=psc, f_=f_: e.matmul(psc[:, 0:4], lhsT=c.triu_f, rhs=f_[:, 8:12], start=True, stop=True))
                OP("pe", [fB_, c.Bc], [pBc], lambda e, psc=psc, f_=f_: e.matmul(psc[:, 8:12], lhsT=c.ones_f, rhs=f_[:, 8:12], start=True, stop=True))
                OP("dve", [pBc, totsB[tt]], [cabsB[tt]], lambda e, psc=psc, tt=tt: e.tensor_tensor(out=cabs[:, tt, :], in0=psc[:, 0:4], in1=tots[:, tt, :], op=ALU.add))
                OP("dve", [pBc, totsB[tt]], [totsB[tt + 1]], lambda e, psc=psc, tt=tt: e.tensor_tensor(out=tots[:, tt + 1, :], in0=psc[:, 8:12], in1=tots[:, tt, :], op=ALU.add))

            stA(0); stB(0); stA(1); stB(1); stC(0); stD(0); stA(2); stB(2); stC(1); stD(1); stA(3); stB(3); stC(2); stD(2); stC(3); stD(3)
            if "att" in parts:
                P.dma(lambda e, q=q, V_=V_: e.dma_start(out=c.vtok[q * TT:(q + 1) * TT, :].rearrange("(s p) c -> p s c", p=128), in_=V_[:]), reads=[VB_], writes=[VT[q]])
            if "pool" in parts:
                OP("pool", [xptB[4]], [xptB[0]], lambda e: e.tensor_copy(out=xpt[:, 0, :], in_=xpt[:, 4, :]))
                Y_, YB_ = yps[q % 2], ypsB[q % 2]
                for cc in range(2):
                    pp, ppB = ppool[cc]
                    OP("act", [ppB], [pldB[cc]], lambda e, pp=pp, cc=cc: e.activation(out=pld[:, cc, :], in_=pp[:, 0:TT], func=AF.Copy))
                    ps, pB = next_ps()
                    OP("pe", [pldB[cc], pwB], [pB], lambda e, ps=ps, cc=cc: e.matmul(ps[:, 0:TT], lhsT=pwb[:, cc, :], rhs=pld[:, cc, :], start=True, stop=True))
                    OP("dve", [pB, scB], [YB_], lambda e, ps=ps, cc=cc, Y_=Y_: e.tensor_scalar(out=Y_[:, cc, :], in0=ps[:, 0:TT], scalar1=scol[:, cc:cc + 1], scalar2=None, op0=ALU.mult))
                P.dma(lambda e, qs=qs, Y_=Y_: e.dma_start(out=ybrv[1][:, :, qs], in_=Y_[:]), reads=[YB_], writes=[c.YB[1][q]])
            if "ssm" in parts:
                for cc in range(2):
                    ps, pB = next_ps()
                    proj_fm(1028 + cc * 128, 128, ps, pB, slice(0, 128), xn, xnB)
                    OP("act", [pB], [uB[cc][q]], lambda e, ps=ps, cc=cc, q=q: e.activation(out=u_ssm[:, cc, 8 + q * TT:8 + (q + 1) * TT], in_=ps[:, 0:TT], func=AF.Copy))
            if "conv" in parts:
                Y_, YB_ = ycs[q % 2], ycsB[q % 2]
                for cc in range(2):
                    pcc, pccB = next_ps()
                    proj_fm(1540 + cc * 128, 128, pcc, pccB, slice(0, 128), xn, xnB)
                    pcx, pcxB = next_ps()
                    proj_fm(1796 + cc * 128, 128, pcx, pcxB, slice(0, 128), xn, xnB)
                    pcb, pcbB = next_ps()
                    proj_fm(1284 + cc * 128, 128, pcb, pcbB, slice(0, 128), xn, xnB)
                    a_, aB_ = ccs[cc], ccsB[cc]
                    b_, bB_ = cbs[cc], cbsB[cc]
                    z_, zB_ = zt[cc][q % 2], ztB[cc][q % 2]
                    zp_, zpB_ = zt[cc][(q + 1) % 2], ztB[cc][(q + 1) % 2]
                    y_, yB_ = y1[cc], y1B[cc]
                    OP("act", [pccB], [aB_], lambda e, pcc=pcc, a_=a_: e.activation(out=a_[:], in_=pcc[:, 0:TT], func=AF.Copy))
                    OP("act", [pcbB], [bB_], lambda e, pcb=pcb, b_=b_: e.activation(out=b_[:], in_=pcb[:, 0:TT], func=AF.Copy))
                    if q == 0:
                        OP("pool", [], [zB_], lambda e, z_=z_: e.memset(z_[:, 0:2], 0.0))
                    else:
                        OP("pool", [zpB_], [zB_], lambda e, z_=z_, zp_=zp_: e.tensor_copy(out=z_[:, 0:2], in_=zp_[:, TT:TT + 2]))
                    OP("dve", [pcxB, aB_], [zB_], lambda e, pcx=pcx, a_=a_, z_=z_: e.tensor_tensor(out=z_[:, 2:TT + 2], in0=pcx[:, 0:TT], in1=a_[:], op=ALU.mult))
                    OP("dve", [zB_, scB], [yB_], lambda e, z_=z_, y_=y_, cc=cc: e.tensor_scalar(out=y_[:], in0=z_[:, 0:TT], scalar1=scol[:, 2 + cc:3 + cc], scalar2=None, op0=ALU.mult))
                    OP("dve", [zB_, scB, yB_], [yB_], lambda e, z_=z_, y_=y_, cc=cc: e.scalar_tensor_tensor(out=y_[:], in0=z_[:, 1:TT + 1], scalar=scol[:, 4 + cc:5 + cc], in1=y_[:], op0=ALU.mult, op1=ALU.add))
                    OP("dve", [zB_, scB, yB_], [yB_], lambda e, z_=z_, y_=y_, cc=cc: e.scalar_tensor_tensor(out=y_[:], in0=z_[:, 2:TT + 2], scalar=scol[:, 6 + cc:7 + cc], in1=y_[:], op0=ALU.mult, op1=ALU.add))
                    OP("pool", [yB_, bB_], [YB_], lambda e, y_=y_, b_=b_, Y_=Y_, cc=cc: e.tensor_tensor(out=Y_[:, cc, :], in0=y_[:], in1=b_[:], op=ALU.mult))
                P.dma(lambda e, qs=qs, Y_=Y_: e.dma_start(out=ybrv[3][:, :, qs], in_=Y_[:]), reads=[YB_], writes=[c.YB[3][q]])


def mixer_m3(c, l, k_aug, kB, kcB, cabs, cabsB, tots, totsB, QA, VT, ybrv, drain=lambda n: None):
    P, OP, sb, nc = c.P, c.OP, c.sb, c.nc
    psb, psB = c.psb, c.psB
    with contextlib.ExitStack() as st:
        V_aug = sb(st, "V_aug", [128, 32, 4, 2, 64], BF16)
        VB = [Buf() for _ in range(NQ)]
        VoB = Buf()
        OP("pool", [], [VoB], lambda e: e.memset(V_aug[:, :, :, 1, :], 1.0))
        qt = [sb(st, "m3_qt%d" % i, [128, 4, TT], BF16) for i in range(2)]
        qtB = [Buf() for _ in range(2)]
        Pt = [sb(st, "m3_P%d" % i, [128, TT], BF16) for i in range(3)]
        PtB = [Buf() for _ in range(3)]
        bq = [sb(st, "m3_bq%d" % i, [128, 32, 4], F32) for i in range(2)]
        bqB = [Buf() for _ in range(2)]
        rd = [sb(st, "m3_rd%d" % i, [64, TT], F32) for i in range(2)]
        rdB = [Buf() for _ in range(2)]
        yst = [sb(st, "m3_y%d" % i, [64, 4, TT], BF16) for i in range(2)]
        ystB = [Buf() for _ in range(2)]
        yav = c.ybr[0].rearrange("(h p) t -> p h t", p=64)
        kP = 0
        kS = 0
        kO = 0
        for q in range(c.nq):
            Q_, QB_ = qt[q % 2], qtB[q % 2]
            P.dma(lambda e, q=q, Q_=Q_: e.dma_start(out=Q_[:], in_=c.qaug[:, :, q * TT:(q + 1) * TT]), reads=[QA[q]], writes=[QB_])
            for i4 in range(4):
                i = q * 4 + i4
                P.dma(lambda e, i=i: e.dma_start(out=V_aug[:, i, :, 0, :], in_=c.vtok[i * 128:(i + 1) * 128, :].rearrange("p (h d) -> p h d", h=4)), reads=[VT[q]], writes=[VB[q]], shared=True)
            n = 4 * q + 4
            b_, bB_ = bq[q % 2], bqB[q % 2]
            OP("dve", cabsB[0:n] + [totsB[4 * q]], [bB_], lambda e, b_=b_, n=n, q=q: e.tensor_tensor(out=b_[:, 0:n, :], in0=cabs[:, 0:n, :], in1=tots[:, 4 * q, :].unsqueeze(1).to_broadcast([128, n, 4]), op=ALU.subtract))
            Y_, YB_ = yst[q % 2], ystB[q % 2]
            for h in range(4):
                po, poB = psb[4 + kO % 2], psB[4 + kO % 2]
                kO += 1
                def emit_S(i):
                    d = i - 4 * q
                    c0 = max(0, d) * 128
                    ps, pB = psb[i % 4], psB[i % 4]
                    OP("pe", [kB[h][i // 4], kcB, QB_], [pB], lambda e, ps=ps, i=i, c0=c0, d=d, h=h, Q_=Q_: e.matmul(ps[:, c0:TT], lhsT=k_aug[:, h, i * 128:(i + 1) * 128], rhs=Q_[:, h, c0:TT], start=True, stop=(d < 0)))
                    if d >= 0:
                        OP("pe", [c.Bcb], [pB], lambda e, ps=ps, c0=c0: e.matmul(ps[:, c0:c0 + 128], lhsT=c.ident_b, rhs=c.mneg_b, start=False, stop=True))
                    return ps, pB, c0

                pend = [emit_S(0)]
                if n > 1:
                    pend.append(emit_S(1))
                for i in range(n):
                    ps, pB, c0 = pend.pop(0)
                    if i + 2 < n:
                        pend.append(emit_S(i + 2))
                    p_, pB_ = Pt[kP % 3], PtB[kP % 3]
                    kP += 1
                    OP("act", [pB, bB_], [pB_], lambda e, ps=ps, p_=p_, c0=c0, i=i, b_=b_, h=h: e.activation(out=p_[:, c0:TT], in_=ps[:, c0:TT], func=AF.Exp, bias=b_[:, i, h:h + 1]))
                    OP("pe", [VB[i // 4], VoB, pB_], [poB], lambda e, p_=p_, c0=c0, i=i, po=po, h=h, n=n: e.matmul(po[:, c0:TT], lhsT=V_aug[:, i, h, :, :].rearrange("p a b -> p (a b)"), rhs=p_[:, c0:TT], start=(i == 0), stop=(i == n - 1)))
                r_, rB_ = rd[h % 2], rdB[h % 2]
                OP("dve", [poB], [rB_], lambda e, po=po, r_=r_: e.reciprocal(out=r_[0:64, :], in_=po[64:128, 0:TT]))
                OP("dve", [poB, rB_], [YB_], lambda e, po=po, r_=r_, h=h, Y_=Y_: e.tensor_tensor(out=Y_[0:64, h, :], in0=po[0:64, 0:TT], in1=r_[0:64, :], op=ALU.mult))
                drain(12)
            P.dma(lambda e, q=q, Y_=Y_: e.dma_start(out=yav[:, :, q * TT:(q + 1) * TT], in_=Y_[:]), reads=[YB_], writes=[c.YB[0][q]])


def ssm_param_load(c, l, stack):
    sb = c.sb
    dq = []

    class _P:
        @staticmethod
        def dma(fn, reads=(), writes=(), **kw):
            dq.append((fn, list(reads), list(writes), kw))
    P = _P

    def mk_(name, shape, dt=F32):
        return sb(stack, "spl_" + name, shape, dt), Buf()
    lamS, lamSB = mk_("lamS", [128, 2, 8])
    for ri, src in enumerate((c.lam_re, c.lam_im)):
        P.dma(lambda e, ri=ri, src=src: e.dma_start(out=lamS[:, ri, :], in_=src[l].rearrange("(j g) p -> (g p) j", g=2), allow_slow_non_contiguous=True), writes=[lamSB], shared=True)
    ldtS, ldtSB = mk_("ldtS", [128, 8])
    for g in range(2):
        P.dma(lambda e, g=g: e.dma_start(out=ldtS[g * 64:(g + 1) * 64, :], in_=c.log_dt[l].rearrange("(j g) -> g j", g=2)[g:g + 1, :].partition_broadcast(64), allow_slow_non_contiguous=True), writes=[ldtSB], shared=True)
    BS, BSB = mk_("BS", [128, 2, 8, 16])
    for ri, src in enumerate((c.b_re, c.b_im)):
        P.dma(lambda e, ri=ri, src=src: e.dma_start(out=BS[:, ri, :, :], in_=src[l].rearrange("(j g) p h -> (g p) j h", g=2)), writes=[BSB], shared=True)
    Csrc, CsB = mk_("Csrc", [128, 2, 128])
    for ri, src in enumerate((c.c_re, c.c_im)):
        for j in range(8):
            P.dma(lambda e, ri=ri, src=src, j=j: e.dma_start(out=Csrc[16 * j:16 * j + 16, ri, :].rearrange("h (g p) -> h g p", g=2), in_=src[l, 2 * j:2 * j + 2].rearrange("g h p -> h g p")), writes=[CsB], shared=True)
    dcol, dcB = mk_("dcol", [128, 2])
    P.dma(lambda e: e.dma_start(out=dcol[:], in_=c.ssm_d[l].rearrange("(c p) -> p c", p=128), allow_slow_non_contiguous=True), writes=[dcB], shared=True)
    Bsrc, BsB = mk_("Bsrc", [128, 2, 2, 4, 2, 16])
    for ri, src in enumerate((c.b_re, c.b_im)):
        for hf in range(2):
            for g2p in range(2):
                P.dma(lambda e, ri=ri, src=src, hf=hf, g2p=g2p: e.dma_start(out=Bsrc[:, ri, hf, :, g2p, :], in_=src[l, 8 * hf:8 * hf + 8].rearrange("(j g) p h -> (g p) j h", g=2)), writes=[BsB], shared=True)
    return SimpleNamespace(lamS=lamS, lamSB=lamSB, ldtS=ldtS, ldtSB=ldtSB, BS=BS, BSB=BSB, Csrc=Csrc, CsB=CsB, dcol=dcol, dcB=dcB,
                           Bsrc=Bsrc, BsB=BsB, dq=dq)


def ssm_s_alloc(c, stack):
    sb = c.sb
    names = ["lr", "dt", "th", "sn", "cs", "lg", "mag", "ar", "ai", "t1", "t2", "t3", "den", "am1", "cr", "ci", "p0r", "p0i"]
    names += ["p%d%s" % (n, x) for n in range(2, 9) for x in "ri"]
    tl = {nm: sb(stack, "ppS_%s" % nm, [128, 8], F32)[:] for nm in names}
    hp = sb(stack, "ssc_halfpi", [128, 1], F32)
    LV = sb(stack, "ssm_LV", [128, 3, 9, 8], F32)
    return SimpleNamespace(hp=hp, LV=LV, tl=tl)


def ssm_s_chain(c, l, pre, spar):
    P, sb = c.P, c.sb
    dq = []
    OP = lambda eng, rd, wr, fn: dq.append((eng, list(rd), list(wr), fn))
    mul, add, sub = ALU.mult, ALU.add, ALU.subtract
    lamS, lamSB, ldtS, ldtSB = spar.lamS, spar.lamSB, spar.ldtS, spar.ldtSB

    def TT_(eng, out, a, b, op, rd, wr):
        OP(eng, rd, wr, lambda e: e.tensor_tensor(out=out, in0=a, in1=b, op=op))
    hp, LV, pre_tl = pre.hp, pre.LV, pre.tl
    hpB = Buf()
    OP("pool", [], [hpB], lambda e: e.memset(hp[:], float(np.pi / 2)))
    LVB = Buf()
    def cpow_prep(tag, F, lr_in, li, ldt_ap, deps, eng):
        tl = pre_tl

        def t_(nm):
            return tl[nm]
        B_ = Buf()
        D = list(deps) + [B_]
        OP(eng, D, [B_], lambda e: e.tensor_scalar(out=t_("lr"), in0=lr_in, scalar1=-1e-4, scalar2=None, op0=ALU.min))
        OP(eng, D, [B_], lambda e: e.tensor_copy(out=t_("dt"), in_=ldt_ap))
        OP("act", [B_], [B_], lambda e: e.activation(out=t_("dt"), in_=t_("dt"), func=AF.Exp))
        TT_(eng, t_("th"), li, t_("dt"), mul, D, [B_])
        OP("act", [B_], [B_], lambda e: e.activation(out=t_("sn"), in_=t_("th"), func=AF.Sin, scale=1.0 / 32))
        OP("act", [B_, hpB], [B_], lambda e: e.activation(out=t_("cs"), in_=t_("th"), func=AF.Sin, scale=1.0 / 32, bias=hp[:, 0:1]))
        TT_(eng, t_("lg"), t_("lr"), t_("dt"), mul, [B_], [B_])
        OP("act", [B_], [B_], lambda e: e.activation(out=t_("mag"), in_=t_("lg"), func=AF.Exp, scale=1.0 / 32))
        TT_(eng, t_("ar"), t_("mag"), t_("cs"), mul, [B_], [B_])
        TT_(eng, t_("ai"), t_("mag"), t_("sn"), mul, [B_], [B_])
        for _ in range(5):
            TT_(eng, t_("t1"), t_("ar"), t_("ar"), mul, [B_], [B_])
            TT_(eng, t_("t2"), t_("ai"), t_("ai"), mul, [B_], [B_])
            TT_(eng, t_("t3"), t_("ar"), t_("ai"), mul, [B_], [B_])
            TT_(eng, t_("ar"), t_("t1"), t_("t2"), sub, [B_], [B_])
            OP(eng, [B_], [B_], lambda e: e.tensor_scalar(out=t_("ai"), in0=t_("t3"), scalar1=2.0, scalar2=None, op0=mul))
        TT_(eng, t_("t1"), t_("lr"), t_("lr"), mul, [B_], [B_])
        TT_(eng, t_("t2"), li, li, mul, D, [B_])
        TT_(eng, t_("den"), t_("t1"), t_("t2"), add, [B_], [B_])
        OP("dve", [B_], [B_], lambda e: e.reciprocal(out=t_("den"), in_=t_("den")))
        OP(eng, [B_], [B_], lambda e: e.tensor_scalar(out=t_("am1"), in0=t_("ar"), scalar1=-1.0, scalar2=None, op0=add))
        TT_(eng, t_("t1"), t_("am1"), t_("lr"), mul, [B_], [B_])
        TT_(eng, t_("t2"), t_("ai"), li, mul, D, [B_])
        TT_(eng, t_("t3"), t_("t1"), t_("t2"), add, [B_], [B_])
        TT_(eng, t_("cr"), t_("t3"), t_("den"), mul, [B_], [B_])
        TT_(eng, t_("t1"), t_("ai"), t_("lr"), mul, [B_], [B_])
        TT_(eng, t_("t2"), t_("am1"), li, mul, D, [B_])
        TT_(eng, t_("t3"), t_("t1"), t_("t2"), sub, [B_], [B_])
        TT_(eng, t_("ci"), t_("t3"), t_("den"), mul, [B_], [B_])
        pw = []
        OP(eng, [B_], [B_], lambda e: e.memset(t_("p0r"), 1.0))
        OP(eng, [B_], [B_], lambda e: e.memset(t_("p0i"), 0.0))
        pw.append((t_("p0r"), t_("p0i")))
        pw.append((t_("ar"), t_("ai")))
        for n in range(2, 9):
            pr, pi = pw[-1]
            nr, ni = t_("p%dr" % n), t_("p%di" % n)
            cmul(nr, ni, pr, pi, t_("ar"), t_("ai"), t_("t1"), t_("t2"), [B_], [B_], eng)
            pw.append((nr, ni))
        return pw, (t_("cr"), t_("ci")), B_, t_

    def cmul(o_r, o_i, a_r, a_i, b_r, b_i, t1, t2, rd, wr, eng="dve"):
        TT_(eng, t1, a_r, b_r, mul, rd, wr)
        TT_(eng, t2, a_i, b_i, mul, rd, wr)
        TT_(eng, o_r, t1, t2, sub, rd, wr)
        TT_(eng, t1, a_r, b_i, mul, rd, wr)
        TT_(eng, t2, a_i, b_r, mul, rd, wr)
        TT_(eng, o_i, t1, t2, add, rd, wr)

    pwS, (crS, ciS), BS_, tS = cpow_prep("S", 8, lamS[:, 0, :], lamS[:, 1, :], ldtS[:], [lamSB, ldtSB], "dve")
    OP("dve", [BS_], [LVB], lambda e: e.tensor_copy(out=LV[:, 0, 0, :], in_=pwS[8][0]))
    OP("dve", [BS_], [LVB], lambda e: e.tensor_copy(out=LV[:, 1, 0, :], in_=pwS[8][1]))
    for k in range(1, 9):
        TT_("dve", tS("t1"), LV[:, 0, k - 1, :], LV[:, 0, k - 1, :], mul, [LVB, BS_], [BS_])
        TT_("dve", tS("t2"), LV[:, 1, k - 1, :], LV[:, 1, k - 1, :], mul, [LVB, BS_], [BS_])
        TT_("dve", tS("t3"), LV[:, 0, k - 1, :], LV[:, 1, k - 1, :], mul, [LVB, BS_], [BS_])
        TT_("dve", LV[:, 0, k, :], tS("t1"), tS("t2"), sub, [BS_], [LVB])
        OP("dve", [BS_], [LVB], lambda e, k=k: e.tensor_scalar(out=LV[:, 1, k, :], in0=tS("t3"), scalar1=2.0, scalar2=None, op0=mul))
    OP("dve", [LVB], [LVB], lambda e: e.tensor_scalar(out=LV[:, 2, :, :], in0=LV[:, 1, :, :], scalar1=-1.0, scalar2=None, op0=mul))
    return SimpleNamespace(pwS=pwS, crS=crS, ciS=ciS, BS_=BS_, tS=tS, LV=LV, LVB=LVB, cmul=cmul, dq=dq)


def mixer_m2(c, l, u_ssm, uB, ybrv, spar, sch):
    P, OP, sb, nc, next_ps = c.P, c.OP, c.sb, c.nc, c.next_ps
    mul, add, sub = ALU.mult, ALU.add, ALU.subtract
    NCH = T // 8

    def TT_(eng, out, a, b, op, rd, wr):
        OP(eng, rd, wr, lambda e: e.tensor_tensor(out=out, in0=a, in1=b, op=op))

    with contextlib.ExitStack() as st:
        WZ = sb(st, "ssm_WZ", [128, 8, 8, 2, 128], BF16)
        CA = sb(st, "ssm_CA", [128, 8, 9, 2, 128], BF16)
        BD = sb(st, "ssm_BD", [128, 2, 8, 128], BF16)
        LV, LVB = sch.LV, sch.LVB
        WZB, CAB, BDB = [Buf(), Buf()], Buf(), Buf()
        with contextlib.nullcontext():
            sp = st

            def mk_(name, shape, dt=F32):
                return sb(sp, "sp_" + name, shape, dt), Buf()
            mk, mkB = mk_("mk", [128, 8, 128])
            P.dma(lambda e: e.dma_start(out=mk[:], in_=c.consts[:, C_MH:C_MH + 1024].rearrange("p (a b) -> p a b", a=8)), writes=[mkB])
            msall, msB = mk_("msall", [128, 8, 128])
            for j in range(8):
                OP("pool", [mkB], [msB], lambda e, j=j: e.tensor_copy(out=msall[:, j, :], in_=mk[:, 4 + j % 4, :]))
            lamS, lamSB, ldtS, ldtSB, BS, BSB, Csrc, CsB = spar.lamS, spar.lamSB, spar.ldtS, spar.ldtSB, spar.BS, spar.BSB, spar.Csrc, spar.CsB
            dcol, dcB, Bsrc, BsB = spar.dcol, spar.dcB, spar.Bsrc, spar.BsB
            CT, CTB = mk_("CT", [128, 2, 128])
            ps, pB = next_ps()
            for ri in range(2):
                OP("pe", [CsB, c.Bc], [pB], lambda e, ps=ps, ri=ri: e.transpose(out=ps[:, ri * 128:(ri + 1) * 128], in_=Csrc[:, ri, :], identity=c.ident_f))
            OP("act", [pB], [CTB], lambda e, ps=ps: e.activation(out=CT[:], in_=ps[:, 0:256].rearrange("p (a b) -> p a b", a=2), func=AF.Copy))

            pwS, crS, ciS, BS_, tS = sch.pwS, sch.crS, sch.ciS, sch.BS_, sch.tS

            def cmul(o_r, o_i, a_r, a_i, b_r, b_i, t1, t2, rd, wr, eng="dve"):
                TT_(eng, t1, a_r, b_r, mul, rd, wr)
                TT_(eng, t2, a_i, b_i, mul, rd, wr)
                TT_(eng, o_r, t1, t2, sub, rd, wr)
                TT_(eng, t1, a_r, b_i, mul, rd, wr)
                TT_(eng, t2, a_i, b_r, mul, rd, wr)
                TT_(eng, o_i, t1, t2, add, rd, wr)
            Gt = [mk_("Gt%d" % i, [128, 2, 8]) for i in range(2)]
            Ws = [mk_("Ws%d" % i, [128, 2, 2, 128]) for i in range(2)]
            wt1, wt1B = mk_("wt1", [128, 2, 128])
            wt2, wt2B = mk_("wt2", [128, 2, 128])
            v4 = lambda ap: ap.rearrange("p a (b c) -> p a b c", b=4)
            Brv = Bsrc[:, 0].rearrange("p a b c d -> p a b (c d)")
            Biv = Bsrc[:, 1].rearrange("p a b c d -> p a b (c d)")
            gb = lambda ap: ap.rearrange("p (a b) -> p a b", a=2).unsqueeze(3).to_broadcast([128, 2, 4, 32])
            mh = mk[:, 0:4, :]
            for tau in range(8):
                G_, GB_ = Gt[tau % 2]
                if tau == 0:
                    OP("dve", [BS_, GB_], [GB_], lambda e, G_=G_: e.tensor_copy(out=G_[:, 0, :], in_=crS))
                    OP("dve", [BS_, GB_], [GB_], lambda e, G_=G_: e.tensor_copy(out=G_[:, 1, :], in_=ciS))
                else:
                    Gp, GpB = Gt[(tau - 1) % 2]
                    cmul(G_[:, 0, :], G_[:, 1, :], Gp[:, 0, :], Gp[:, 1, :], pwS[1][0], pwS[1][1], tS("t1"), tS("t2"), [BS_, GpB, GB_], [BS_, GB_])
                W_, WB_ = Ws[tau % 2]
                TT_("dve", v4(wt1[:]), Brv, gb(G_[:, 0, :]), mul, [BsB, GB_, wt1B], [wt1B])
                TT_("dve", v4(wt2[:]), Biv, gb(G_[:, 1, :]), mul, [BsB, GB_, wt2B], [wt2B])
                TT_("dve", W_[:, 0, :, :], wt1[:], wt2[:], sub, [wt1B, wt2B, WB_], [WB_])
                TT_("dve", v4(wt1[:]), Biv, gb(G_[:, 0, :]), mul, [BsB, GB_, wt1B], [wt1B])
                TT_("dve", v4(wt2[:]), Brv, gb(G_[:, 1, :]), mul, [BsB, GB_, wt2B], [wt2B])
                TT_("dve", W_[:, 1, :, :], wt1[:], wt2[:], add, [wt1B, wt2B, WB_], [WB_])
                ps, pB = next_ps()
                for ri in range(2):
                    for hf in range(2):
                        k4 = ri * 2 + hf
                        OP("pe", [WB_, c.Bc], [pB], lambda e, ps=ps, ri=ri, hf=hf, k4=k4, W_=W_: e.transpose(out=ps[:, k4 * 128:(k4 + 1) * 128], in_=W_[:, ri, hf, :], identity=c.ident_f))
                for ri in range(2):
                    for hf in range(2):
                        k4 = ri * 2 + hf
                        OP("dve", [pB, mkB, WZB[hf]], [WZB[hf]], lambda e, ps=ps, ri=ri, hf=hf, k4=k4, tau=tau: e.tensor_tensor(out=WZ[:, 4 * hf:4 * hf + 4, tau, ri, :], in0=ps[:, k4 * 128:(k4 + 1) * 128].unsqueeze(1).to_broadcast([128, 4, 128]), in1=mh, op=mul))
            cw1, cw1B = mk_("cw1", [128, 8, 16])
            cw2, cw2B = mk_("cw2", [128, 8, 16])
            cw3, cw3B = mk_("cw3", [128, 8, 16])
            CTr = CT[:, 0, :].rearrange("p (j h) -> p j h", j=8)
            CTi = CT[:, 1, :].rearrange("p (j h) -> p j h", j=8)
            bc16 = lambda ap: ap.unsqueeze(2).to_broadcast([128, 8, 16])
            msv = msall[:].rearrange("p j (a h) -> p j a h", a=8)
            msneg, msnB = mk_("msneg", [128, 8, 128])
            OP("pool", [msB], [msB], lambda e: e.tensor_scalar(out=msneg[:], in0=msall[:], scalar1=-1.0, scalar2=None, op0=mul))
            msvn = msneg[:].rearrange("p j (a h) -> p j a h", a=8)
            for n in range(9):
                pr, pi = pwS[n]
                TT_("pool", cw1[:], CTr, bc16(pr), mul, [CTB, BS_, cw1B], [cw1B])
                TT_("pool", cw2[:], CTi, bc16(pi), mul, [CTB, BS_, cw2B], [cw2B])
                TT_("pool", cw3[:], cw1[:], cw2[:], sub, [cw1B, cw2B, cw3B], [cw3B])
                OP("pool", [cw3B, msB, CAB], [CAB], lambda e, n=n: e.tensor_tensor(out=CA[:, :, n, 0, :].rearrange("p j (a h) -> p j a h", a=8), in0=cw3[:].unsqueeze(2).to_broadcast([128, 8, 8, 16]), in1=msv, op=mul))
                TT_("pool", cw1[:], CTr, bc16(pi), mul, [CTB, BS_, cw1B], [cw1B])
                TT_("pool", cw2[:], CTi, bc16(pr), mul, [CTB, BS_, cw2B], [cw2B])
                TT_("pool", cw3[:], cw1[:], cw2[:], add, [cw1B, cw2B, cw3B], [cw3B])
                OP("pool", [cw3B, msB, CAB], [CAB], lambda e, n=n: e.tensor_tensor(out=CA[:, :, n, 1, :].rearrange("p j (a h) -> p j a h", a=8), in0=cw3[:].unsqueeze(2).to_broadcast([128, 8, 8, 16]), in1=msvn, op=mul))
            cB, cBB = mk_("cB", [128, 8, 2, 32], BF16)
            m2 = mk[:, 4, 0:32].rearrange("p (a h) -> p a h", a=2).unsqueeze(1).to_broadcast([128, 8, 2, 16])
            BSr, BSi = BS[:, 0, :, :], BS[:, 1, :, :]
            TT_("pool", cw1[:], BSr, bc16(crS), mul, [BSB, BS_, cw1B], [cw1B])
            TT_("pool", cw2[:], BSi, bc16(ciS), mul, [BSB, BS_, cw2B], [cw2B])
            TT_("pool", cw3[:], cw1[:], cw2[:], sub, [cw1B, cw2B, cw3B], [cw3B])
            OP("pool", [cw3B, mkB], [cBB], lambda e: e.tensor_tensor(out=cB[:, :, 0, :].rearrange("p j (a h) -> p j a h", a=2), in0=cw3[:].unsqueeze(2).to_broadcast([128, 8, 2, 16]), in1=m2, op=mul))
            TT_("pool", cw1[:], BSr, bc16(ciS), mul, [BSB, BS_, cw1B], [cw1B])
            TT_("pool", cw2[:], BSi, bc16(crS), mul, [BSB, BS_, cw2B], [cw2B])
            TT_("pool", cw3[:], cw1[:], cw2[:], add, [cw1B, cw2B, cw3B], [cw3B])
            OP("pool", [cw3B, mkB], [cBB], lambda e: e.tensor_tensor(out=cB[:, :, 1, :].rearrange("p j (a h) -> p j a h", a=2), in0=cw3[:].unsqueeze(2).to_broadcast([128, 8, 2, 16]), in1=m2, op=mul))
        with contextlib.nullcontext():
            sr = st
            Xb = [[[sb(sr, "X%d%d%d" % (s_, ri, ab), [128, 256 + NCH], F32) for ab in range(2)] for ri in range(2)] for s_ in range(2)]
            XB = [[[Buf() for ab in range(2)] for ri in range(2)] for s_ in range(2)]
            for s_ in range(2):
                for ri in range(2):
                    for ab in range(2):
                        OP("pool", [], [XB[s_][ri][ab]], lambda e, t=Xb[s_][ri][ab]: e.memset(t[:, 0:256], 0.0))
            Xp = sb(sr, "Xp", [128, 8, 2, NCH], BF16)
            XpB = [Buf() for _ in range(8)]
            ysT = sb(sr, "ysT", [128, 2, T], BF16)
            ysB = [Buf() for _ in range(2)]
            wgl = sb(sr, "wgl", [128, 2, 512], BF16)
            wglB = [Buf()]
            c.wload(wgl[:], c.w_glu[l].rearrange("(kc p) n -> p kc n", p=128), wglB)
            sg = [sb(sr, "s_sg%d" % i, [128, TT], F32) for i in range(2)]
            sgB = [Buf() for _ in range(2)]
            yst = [sb(sr, "s_yst%d" % i, [128, 2, TT], BF16) for i in range(2)]
            ystB = [Buf() for _ in range(2)]
            for j in range(8):
                hf = j // 4
                s_ = j % 2
                for ri in range(2):
                    ps, pB = next_ps()
                    for tau in range(8):
                        OP("pe", [WZB[hf]] + uB[hf], [pB], lambda e, ps=ps, j=j, tau=tau, ri=ri, hf=hf: e.matmul(ps[:, 0:NCH], lhsT=WZ[:, j, tau, ri, :], rhs=u_ssm[:, hf, 15 - tau:15 - tau + (NCH - 1) * 8 + 1:8], start=(tau == 0), stop=(tau == 7)))
                    OP("act", [pB], [XB[s_][ri][0]], lambda e, ps=ps, t=Xb[s_][ri][0]: e.activation(out=t[:, 256:256 + NCH], in_=ps[:, 0:NCH], func=AF.Copy))
                cur = 0
                for k in range(9):
                    sh = 1 << k
                    sr_, si_ = Xb[s_][0][cur], Xb[s_][1][cur]
                    dr_, di_ = Xb[s_][0][1 - cur], Xb[s_][1][1 - cur]
                    sBr, sBi = XB[s_][0][cur], XB[s_][1][cur]
                    dBr, dBi = XB[s_][0][1 - cur], XB[s_][1][1 - cur]
                    lo, hi = 256 - sh, 256 + NCH - sh
                    OP("dve", [sBr, LVB], [dBr], lambda e, sr_=sr_, dr_=dr_, lo=lo, hi=hi, k=k, j=j: e.scalar_tensor_tensor(out=dr_[:, 256:256 + NCH], in0=sr_[:, lo:hi], scalar=LV[:, 0, k, j:j + 1], in1=sr_[:, 256:256 + NCH], op0=mul, op1=add))
                    OP("dve", [sBi, LVB, dBr], [dBr], lambda e, si_=si_, dr_=dr_, lo=lo, hi=hi, k=k, j=j: e.scalar_tensor_tensor(out=dr_[:, 256:256 + NCH], in0=si_[:, lo:hi], scalar=LV[:, 2, k, j:j + 1], in1=dr_[:, 256:256 + NCH], op0=mul, op1=add))
                    OP("dve", [sBr, sBi, LVB], [dBi], lambda e, sr_=sr_, si_=si_, di_=di_, lo=lo, hi=hi, k=k, j=j: e.scalar_tensor_tensor(out=di_[:, 256:256 + NCH], in0=sr_[:, lo:hi], scalar=LV[:, 1, k, j:j + 1], in1=si_[:, 256:256 + NCH], op0=mul, op1=add))
                    OP("dve", [sBi, LVB, dBi], [dBi], lambda e, si_=si_, di_=di_, lo=lo, hi=hi, k=k, j=j: e.scalar_tensor_tensor(out=di_[:, 256:256 + NCH], in0=si_[:, lo:hi], scalar=LV[:, 0, k, j:j + 1], in1=di_[:, 256:256 + NCH], op0=mul, op1=add))
                    cur = 1 - cur
                for ri in range(2):
                    OP("act", [XB[s_][ri][cur]], [XpB[j]], lambda e, t=Xb[s_][ri][cur], j=j, ri=ri: e.activation(out=Xp[:, j, ri, :], in_=t[:, 255:255 + NCH], func=AF.Copy))
            for hf in range(2):
                for tau in range(8):
                    ps, pB = next_ps()
                    for j4 in range(4):
                        j = 4 * hf + j4
                        OP("pe", [cBB, CAB], [pB], lambda e, ps=ps, j=j, j4=j4, tau=tau: e.matmul(ps[32 * j4:32 * j4 + 32, 0:128], lhsT=cB[:, j, 0, :], rhs=CA[:, j, tau, 0, :], start=True, stop=False, tile_position=(0, 32 * j4)))
                        OP("pe", [cBB, CAB], [pB], lambda e, ps=ps, j=j, j4=j4, tau=tau: e.matmul(ps[32 * j4:32 * j4 + 32, 0:128], lhsT=cB[:, j, 1, :], rhs=CA[:, j, tau, 1, :], start=False, stop=True, tile_position=(0, 32 * j4)))
                    if tau == 0:
                        OP("dve", [pB, dcB, c.Bc], [BDB], lambda e, ps=ps, hf=hf: e.scalar_tensor_tensor(out=BD[:, hf, 0, :], in0=c.ident_f, scalar=dcol[:, hf:hf + 1], in1=ps[:, 0:128], op0=mul, op1=add))
                    else:
                        OP("act", [pB], [BDB], lambda e, ps=ps, hf=hf, tau=tau: e.activation(out=BD[:, hf, tau, :], in_=ps[:, 0:128], func=AF.Copy))
            kk = 0
            for hf in range(2):
                for s in range(8):
                    ps, pB = next_ps()
                    mms = []
                    for j4 in range(4):
                        j = 4 * hf + j4
                        for ri in range(2):
                            mms.append(([CAB, XpB[j]], CA[:, j, s + 1, ri, :], Xp[:, j, ri, :]))
                    for tau in range(s + 1):
                        mms.append(([BDB] + uB[hf], BD[:, hf, tau, :], u_ssm[:, hf, 8 + s - tau:8 + s - tau + (NCH - 1) * 8 + 1:8]))
                    for i, (rd, lh, rh) in enumerate(mms):
                        OP("pe", rd, [pB], lambda e, ps=ps, lh=lh, rh=rh, i=i, n=len(mms): e.matmul(ps[:, 0:NCH], lhsT=lh, rhs=rh, start=(i == 0), stop=(i == n - 1)))
                    if kk % 2 == 0:
                        OP("act", [pB], [ysB[hf]], lambda e, ps=ps, hf=hf, s=s: e.activation(out=ysT[:, hf, s:T:8], in_=ps[:, 0:NCH], func=AF.Copy))
                    else:
                        OP("dve", [pB], [ysB[hf]], lambda e, ps=ps, hf=hf, s=s: e.tensor_copy(out=ysT[:, hf, s:T:8], in_=ps[:, 0:NCH]))
                    kk += 1
            for q in range(NQ):
                qs = slice(q * TT, (q + 1) * TT)
                Y_, YB_ = yst[q % 2], ystB[q % 2]
                for cc in range(2):
                    pv, pvB = next_ps()
                    pg, pgB = next_ps()
                    for kc in range(2):
                        OP("pe", [wglB[0], ysB[kc]], [pvB], lambda e, pv=pv, kc=kc, cc=cc, qs=qs: e.matmul(pv[:, 0:TT], lhsT=wgl[:, kc, cc * 128:(cc + 1) * 128], rhs=ysT[:, kc, qs], start=(kc == 0), stop=(kc == 1)))
                    for kc in range(2):
                        OP("pe", [wglB[0], ysB[kc]], [pgB], lambda e, pg=pg, kc=kc, cc=cc, qs=qs: e.matmul(pg[:, 0:TT], lhsT=wgl[:, kc, 256 + cc * 128:256 + (cc + 1) * 128], rhs=ysT[:, kc, qs], start=(kc == 0), stop=(kc == 1)))
                    s2, s2B = sg[cc], sgB[cc]
                    OP("act", [pgB], [s2B], lambda e, pg=pg, s2=s2: e.activation(out=s2[:], in_=pg[:, 0:TT], func=AF.Sigmoid))
                    OP("dve", [pvB, s2B], [YB_], lambda e, pv=pv, s2=s2, Y_=Y_, cc=cc: e.tensor_tensor(out=Y_[:, cc, :], in0=pv[:, 0:TT], in1=s2[:], op=mul))
                P.dma(lambda e, qs=qs, Y_=Y_: e.dma_start(out=ybrv[2][:, :, qs], in_=Y_[:]), reads=[YB_], writes=[c.YB[2][q]])


def mixer_m4(c, l, parts, ybrv):
    P, OP, sb, nc, next_ps = c.P, c.OP, c.sb, c.nc, c.next_ps
    order = [b for b, nm in enumerate(("att", "pool", "ssm", "conv")) if nm in parts]
    with contextlib.ExitStack() as st:
        wgt = sb(st, "m4_wg", [128, 8, 4096], BF16)
        wgtB = [[Buf(), Buf()] for _ in range(4)]
        wiv = c.w_in[l].rearrange("(kc p) n -> p kc n", p=128)
        wbr = sb(st, "m4_wbr", [128, 4, 2, D], BF16)
        wbrB = [Buf() for _ in range(4)]
        for b in order:
            c.wload(wgt[:, :, b * 1024:b * 1024 + 256], wiv[:, :, 2052 + b * 1024:2052 + b * 1024 + 256], [wgtB[b][0]])
            c.wload(wbr[:, b, :, :], c.w_branch[l, b].rearrange("(kc p) n -> p kc n", p=128), [wbrB[b]])
        for b in order:
            c.wload(wgt[:, :, b * 1024 + 256:(b + 1) * 1024], wiv[:, :, 2052 + b * 1024 + 256:2052 + (b + 1) * 1024], [wgtB[b][1]])
        wo = sb(st, "m4_wo", [128, 8, D], BF16)
        woB = [Buf()]
        c.wload(wo[:], c.w_out[l].rearrange("(kc p) n -> p kc n", p=128), woB)
        xn = [sb(st, "m4_xn%d" % i, [128, 8, TT], BF16) for i in range(2)]
        xnB = [Buf() for _ in range(2)]
        ysb = [sb(st, "m4_ys%d" % i, [128, 4, 2, TT], BF16) for i in range(2)]
        ysB = [[Buf() for _ in range(4)] for _ in range(2)]
        m = sb(st, "m4_m", [128, 8, TT], F32)
        mB = [Buf() for _ in range(8)]
        mb = sb(st, "m4_mb", [128, 8, TT], BF16)
        mbB = [Buf() for _ in range(8)]
        sg = [sb(st, "m4_sg%d" % i, [128, TT], F32) for i in range(2)]
        sgB = [Buf() for _ in range(2)]
        tm = [sb(st, "m4_tm%d" % i, [128, TT], F32) for i in range(2)]
        tmB = [Buf() for _ in range(2)]
        hr = [sb(st, "m4_hr%d" % i, [128, TT], F32) for i in range(3)]
        hrB = [Buf() for _ in range(3)]
        k = 0
        kk = 0
        def m4_load(q):
            qs = slice(q * TT, (q + 1) * TT)
            X, XB = xn[q % 2], xnB[q % 2]
            P.dma(lambda e, qs=qs, X=X: e.dma_start(out=X[:], in_=c.xnTv[:, :, qs]), reads=[c.XN[q]], writes=[XB])
            Ys, YsB = ysb[q % 2], ysB[q % 2]
            for b in order:
                P.dma(lambda e, qs=qs, b=b, Ys=Ys: e.dma_start(out=Ys[:, b, :, :], in_=ybrv[b][:, :, qs]), reads=[c.YB[b][q]], writes=[YsB[b]])

        m4_load(0)
        for q in range(c.nq):
            qs = slice(q * TT, (q + 1) * TT)
            X, XB = xn[q % 2], xnB[q % 2]
            Ys, YsB = ysb[q % 2], ysB[q % 2]
            if q + 1 < c.nq:
                m4_load(q + 1)
            for dc in range(8):
                ds = slice(dc * 128, (dc + 1) * 128)
                for bi, b in enumerate(order):
                    pg, pgB = next_ps()
                    for kc in range(8):
                        OP("pe", [wgtB[b][0 if dc < 2 else 1], XB], [pgB], lambda e, pg=pg, kc=kc, b=b, dc=dc, X=X: e.matmul(pg[:, 0:TT], lhsT=wgt[:, kc, b * 1024 + dc * 128:b * 1024 + (dc + 1) * 128], rhs=X[:, kc, :], start=(kc == 0), stop=(kc == 7)))
                    pp, ppB = next_ps()
                    for kc in range(2):
                        OP("pe", [wbrB[b], YsB[b]], [ppB], lambda e, pp=pp, kc=kc, b=b, ds=ds, Ys=Ys: e.matmul(pp[:, 0:TT], lhsT=wbr[:, b, kc, ds], rhs=Ys[:, b, kc, :], start=(kc == 0), stop=(kc == 1)))
                    s_, sB_ = sg[kk % 2], sgB[kk % 2]
                    OP("act", [pgB], [sB_], lambda e, s_=s_, pg=pg: e.activation(out=s_[:], in_=pg[:, 0:TT], func=AF.Sigmoid))
                    last = (bi == len(order) - 1)
                    if bi == 0 and last:
                        OP("dve", [sB_, ppB], [mbB[dc]], lambda e, s_=s_, pp=pp, dc=dc: e.tensor_tensor(out=mb[:, dc, :], in0=pp[:, 0:TT], in1=s_[:], op=ALU.mult))
                    elif bi == 0:
                        OP("dve", [sB_, ppB], [mB[dc]], lambda e, s_=s_, pp=pp, dc=dc: e.tensor_tensor(out=m[:, dc, :], in0=pp[:, 0:TT], in1=s_[:], op=ALU.mult))
                    else:
                        t_, tB_ = tm[kk % 2], tmB[kk % 2]
                        OP("dve", [sB_, ppB], [tB_], lambda e, s_=s_, pp=pp, t_=t_: e.tensor_tensor(out=t_[:], in0=pp[:, 0:TT], in1=s_[:], op=ALU.mult))
                        if last:
                            OP("pool", [tB_, mB[dc]], [mbB[dc]], lambda e, t_=t_, dc=dc: e.tensor_tensor(out=mb[:, dc, :], in0=m[:, dc, :], in1=t_[:], op=ALU.add))
                        else:
                            OP("pool", [tB_, mB[dc]], [mB[dc]], lambda e, t_=t_, dc=dc: e.tensor_tensor(out=m[:, dc, :], in0=m[:, dc, :], in1=t_[:], op=ALU.add))
                    kk += 1
            for d2 in range(8):
                po, poB = next_ps()
                for dc in range(8):
                    OP("pe", [woB[0], mbB[dc]], [poB], lambda e, po=po, dc=dc, d2=d2: e.matmul(po[:, 0:TT], lhsT=wo[:, dc, d2 * 128:(d2 + 1) * 128], rhs=mb[:, dc, :], start=(dc == 0), stop=(dc == 7)))
                r, rB = hr[k % 3], hrB[k % 3]
                k += 1
                P.dma(lambda e, r=r, d2=d2, qs=qs: e.dma_start(out=r[:], in_=c.hTv[:, d2, qs]), reads=[c.HT[q][d2]], writes=[rB])
                OP("dve", [poB, rB], [rB], lambda e, r=r, po=po: e.tensor_tensor(out=r[:], in0=po[:, 0:TT], in1=r[:], op=ALU.add))
                P.dma(lambda e, r=r, d2=d2, qs=qs: e.dma_start(out=c.hTv[:, d2, qs], in_=r[:]), reads=[rB], writes=[c.HT[q][d2]])


_NAMES = ["norm_g", "ffn_w_gate", "ffn_w_up", "ffn_w_down", "w_in", "f_bias", "pool_w", "pool_scale", "ssm_lam_re",
          "ssm_lam_im", "ssm_log_dt", "ssm_b_re", "ssm_b_im", "ssm_c_re", "ssm_c_im", "ssm_d", "ssm_w_glu", "conv_w",
          "w_branch", "w_out", "ple_w_gate", "ple_w_proj", "final_g"]


def run(inputs, n_cores=8, ret_all=False, **bk):
    nc = build(**bk)
    cst = make_consts()
    shared = {k: np.ascontiguousarray(np.asarray(inputs[k], dtype=np.float32)) for k in _NAMES}
    xs = np.asarray(inputs["x"], dtype=np.float32)
    ps = np.asarray(inputs["p"], dtype=np.float32)
    in_maps = []
    for b in range(n_cores):
        m = dict(shared)
        m["x"] = np.ascontiguousarray(xs[b])
        m["p"] = np.ascontiguousarray(ps[:, b])
        m["consts"] = cst
        in_maps.append(m)
    res = run_bass_kernel_spmd(nc, in_maps, core_ids=list(range(n_cores)))
    if ret_all:
        return res.results
    return np.stack([np.asarray(r["y"]) for r in res.results], axis=0)


def kernel(**inputs):
    return run(inputs, n_cores=8).astype(np.float32)
```
